# Optimizing a Trainium2 kernel written in Bass

```python
import math
import jax
import jax.numpy as jnp
from jax import lax
import numpy as np

D_MODEL = 1024
BATCH = 8
SEQ = 2048
DEPTH = 1
DEC_BATCH = 128
DEC_SEQ = 4
PAST_LEN = 16384
PAGE_SIZE = 128

MIX_WIDTH = D_MODEL
S5_WIDTH = MIX_WIDTH // 2
S5_GROUP_CH = 16
S5_GROUPS = S5_WIDTH // S5_GROUP_CH
S5_STATE = 64
GLA_WIDTH = MIX_WIDTH - S5_WIDTH
GLA_HEADS = 4
GLA_DV = GLA_WIDTH // GLA_HEADS
GLA_DK = GLA_DV // 2
GLA_QK_WIDTH = GLA_HEADS * GLA_DK
GLA_GATE_RANK = 16
GLA_GATE_NORM = 16.0
GLA_CHUNK = 64
D_FF = 2816
EPS = 1e-6
DT_MIN = 1e-3
DT_MAX = 1e-1

OFF_U = 0
OFF_Q = OFF_U + S5_WIDTH
OFF_K = OFF_Q + GLA_QK_WIDTH
OFF_V = OFF_K + GLA_QK_WIDTH
OFF_G = OFF_V + GLA_WIDTH
OFF_A = OFF_G + GLA_WIDTH
IN_WIDTH = OFF_A + GLA_GATE_RANK

kernel_name = 'hymba_s5_gla_macaron_step'


def rmsnorm(x, g):
    xf = x.astype(jnp.float32)
    y = xf * lax.rsqrt(jnp.mean(xf * xf, axis=-1, keepdims=True) + EPS)
    return (y * g.astype(jnp.float32)).astype(x.dtype)


def swiglu(x, wg, wu, wd):
    return (jax.nn.silu(x @ wg) * (x @ wu)) @ wd


def _cplx_combine(e1, e2):
    a1r, a1i, b1r, b1i = e1
    a2r, a2i, b2r, b2i = e2
    return (a2r * a1r - a2i * a1i,
            a2r * a1i + a2i * a1r,
            a2r * b1r - a2i * b1i + b2r,
            a2r * b1i + a2i * b1r + b2i)


def s5_mixer(u, h0_re, h0_im, lam_re, lam_im, log_dt, b_re, b_im, c_re, c_im, d_skip):
    bsz, t, _ = u.shape
    uf = u.astype(jnp.float32).reshape(bsz, t, S5_GROUPS, S5_GROUP_CH)
    dt = jnp.exp(log_dt.astype(jnp.float32))[:, None]
    lr = lam_re.astype(jnp.float32)
    li = lam_im.astype(jnp.float32)
    mag = jnp.exp(lr * dt)
    abr = mag * jnp.cos(li * dt)
    abi = mag * jnp.sin(li * dt)
    den = lr * lr + li * li
    fr = ((abr - 1.0) * lr + abi * li) / den
    fi = (abi * lr - (abr - 1.0) * li) / den
    br = b_re.astype(jnp.float32)
    bi = b_im.astype(jnp.float32)
    bbr = fr[..., None] * br - fi[..., None] * bi
    bbi = fr[..., None] * bi + fi[..., None] * br
    xr = jnp.einsum('btgh,gph->btgp', uf, bbr)
    xi = jnp.einsum('btgh,gph->btgp', uf, bbi)
    h0r = h0_re.astype(jnp.float32)
    h0i = h0_im.astype(jnp.float32)
    xr = xr.at[:, 0].add(abr * h0r - abi * h0i)
    xi = xi.at[:, 0].add(abr * h0i + abi * h0r)
    ar = jnp.broadcast_to(abr, xr.shape)
    ai = jnp.broadcast_to(abi, xr.shape)
    _, _, hr, hi = lax.associative_scan(_cplx_combine, (ar, ai, xr, xi), axis=1)
    y = (jnp.einsum('btgp,ghp->btgh', hr, c_re.astype(jnp.float32))
         - jnp.einsum('btgp,ghp->btgh', hi, c_im.astype(jnp.float32))
         + d_skip.astype(jnp.float32).reshape(S5_GROUPS, S5_GROUP_CH) * uf)
    return y.reshape(bsz, t, S5_WIDTH), hr[:, -1], hi[:, -1]


def gla_mixer(q, k, v, log_f, s0):
    bsz, t = q.shape[:2]
    c = GLA_CHUNK if t % GLA_CHUNK == 0 else t
    n = t // c

    def blocks(z):
        return z.astype(jnp.float32).reshape(bsz, n, c, GLA_HEADS, -1).transpose(0, 3, 1, 2, 4)

    qb, kb, vb, fb = blocks(q), blocks(k), blocks(v), blocks(log_f)
    b = jnp.cumsum(fb, axis=3)
    b_last = b[:, :, :, -1:, :]
    qs = qb * (GLA_DK ** -0.5) * jnp.exp(b)
    k_intra = kb * jnp.exp(-b)
    k_end = kb * jnp.exp(b_last - b)
    mask = jnp.tril(jnp.ones((c, c), dtype=bool))
    att = jnp.where(mask, jnp.einsum('bhncd,bhnsd->bhncs', qs, k_intra), 0.0)
    o_intra = jnp.einsum('bhncs,bhnse->bhnce', att, vb)
    upd = jnp.einsum('bhnsd,bhnse->bhnde', k_end, vb)
    decay = jnp.exp(b_last[:, :, :, 0, :])

    def step(s, inp):
        dec, u = inp
        return dec[..., None] * s + u, s

    s_final, s_start = lax.scan(step, s0.astype(jnp.float32),
                                (decay.transpose(2, 0, 1, 3), upd.transpose(2, 0, 1, 3, 4)))
    s_start = s_start.transpose(1, 2, 0, 3, 4)
    o = o_intra + jnp.einsum('bhncd,bhnde->bhnce', qs, s_start)
    o = o.transpose(0, 2, 3, 1, 4).reshape(bsz, t, GLA_HEADS, GLA_DV)
    return o, s_final


def token_mixer(h, h0_re, h0_im, s0, w_in, lam_re, lam_im, log_dt, b_re, b_im, c_re, c_im, d_skip,
                glu_w, glu_b, gate_w, gate_b, gla_norm, w_out):
    bsz, t, _ = h.shape
    p = h @ w_in
    u = p[..., OFF_U:OFF_Q]
    q = p[..., OFF_Q:OFF_K]
    k = p[..., OFF_K:OFF_V]
    v = p[..., OFF_V:OFF_G]
    g = p[..., OFF_G:OFF_A]
    a = p[..., OFF_A:IN_WIDTH]
    y_s5, hr, hi = s5_mixer(u, h0_re, h0_im, lam_re, lam_im, log_dt, b_re, b_im, c_re, c_im, d_skip)
    z = jax.nn.gelu(y_s5.astype(h.dtype))
    z = z * jax.nn.sigmoid(z @ glu_w + glu_b)
    log_f = jax.nn.log_sigmoid((a @ gate_w + gate_b).astype(jnp.float32)) / GLA_GATE_NORM
    o, s_new = gla_mixer(q, k, v, log_f, s0)
    o = o * lax.rsqrt(jnp.mean(o * o, axis=-1, keepdims=True) + EPS) * gla_norm.astype(jnp.float32)
    o = o.reshape(bsz, t, GLA_WIDTH).astype(h.dtype) * jax.nn.silu(g)
    out = jnp.concatenate([z, o], axis=-1) @ w_out
    return out, hr, hi, s_new


def decoder(x, s5_re0, s5_im0, gla0, norm_ffn1, ffn1_gate, ffn1_up, ffn1_down, norm_mix, w_in,
            s5_lam_re, s5_lam_im, s5_log_dt, s5_b_re, s5_b_im, s5_c_re, s5_c_im, s5_d, s5_glu_w, s5_glu_b,
            gla_gate_w, gla_gate_b, gla_norm, w_out, norm_ffn2, ffn2_gate, ffn2_up, ffn2_down, norm_final):
    new_re, new_im, new_gla = [], [], []
    for l in range(DEPTH):
        x = x + 0.5 * swiglu(rmsnorm(x, norm_ffn1[l]), ffn1_gate[l], ffn1_up[l], ffn1_down[l])
        mix, hr, hi, s = token_mixer(rmsnorm(x, norm_mix[l]), s5_re0[l], s5_im0[l], gla0[l], w_in[l],
                                     s5_lam_re[l], s5_lam_im[l], s5_log_dt[l], s5_b_re[l], s5_b_im[l],
                                     s5_c_re[l], s5_c_im[l], s5_d[l], s5_glu_w[l], s5_glu_b[l],
                                     gla_gate_w[l], gla_gate_b[l], gla_norm[l], w_out[l])
        x = x + mix
        x = x + 0.5 * swiglu(rmsnorm(x, norm_ffn2[l]), ffn2_gate[l], ffn2_up[l], ffn2_down[l])
        new_re.append(hr)
        new_im.append(hi)
        new_gla.append(s)
    y = rmsnorm(x, norm_final)
    return y, jnp.stack(new_re), jnp.stack(new_im), jnp.stack(new_gla)


def setup_inputs(seed: int = 0) -> dict:
    key = jax.random.key(seed)
    ks = jax.random.split(key, 40)
    f32 = jnp.float32

    def nrm(i, shape, scale):
        return jax.random.normal(ks[i], shape, f32) * scale

    G, P, HS = S5_GROUPS, S5_STATE, S5_GROUP_CH
    lam_im = jnp.broadcast_to(math.pi * jnp.arange(P, dtype=f32), (DEPTH, G, P)) + nrm(5, (DEPTH, G, P), 0.01)
    return {
        'x_prompt': nrm(0, (BATCH, SEQ, D_MODEL), 1.0),
        'x_sample': nrm(1, (DEC_BATCH, DEC_SEQ, D_MODEL), 1.0),
        'state_s5_re': nrm(2, (DEPTH, DEC_BATCH, G, P), 0.5),
        'state_s5_im': nrm(3, (DEPTH, DEC_BATCH, G, P), 0.5),
        'state_gla': nrm(4, (DEPTH, DEC_BATCH, GLA_HEADS, GLA_DK, GLA_DV), 1.0),
        'norm_ffn1': 1.0 + nrm(6, (DEPTH, D_MODEL), 0.01),
        'ffn1_gate': nrm(7, (DEPTH, D_MODEL, D_FF), D_MODEL ** -0.5),
        'ffn1_up': nrm(8, (DEPTH, D_MODEL, D_FF), D_MODEL ** -0.5),
        'ffn1_down': nrm(9, (DEPTH, D_FF, D_MODEL), D_FF ** -0.5),
        'norm_mix': 1.0 + nrm(10, (DEPTH, D_MODEL), 0.01),
        'w_in': nrm(11, (DEPTH, D_MODEL, IN_WIDTH), D_MODEL ** -0.5),
        's5_lam_re': -0.5 + nrm(12, (DEPTH, G, P), 0.01),
        's5_lam_im': lam_im,
        's5_log_dt': jax.random.uniform(ks[13], (DEPTH, G), f32, math.log(DT_MIN), math.log(DT_MAX)),
        's5_b_re': nrm(14, (DEPTH, G, P, HS), (2 * HS) ** -0.5),
        's5_b_im': nrm(15, (DEPTH, G, P, HS), (2 * HS) ** -0.5),
        's5_c_re': nrm(16, (DEPTH, G, HS, P), (2 * P) ** -0.5),
        's5_c_im': nrm(17, (DEPTH, G, HS, P), (2 * P) ** -0.5),
        's5_d': nrm(18, (DEPTH, S5_WIDTH), 1.0),
        's5_glu_w': nrm(19, (DEPTH, S5_WIDTH, S5_WIDTH), S5_WIDTH ** -0.5),
        's5_glu_b': nrm(20, (DEPTH, S5_WIDTH), 0.01),
        'gla_gate_w': nrm(21, (DEPTH, GLA_GATE_RANK, GLA_QK_WIDTH), GLA_GATE_RANK ** -0.5),
        'gla_gate_b': nrm(22, (DEPTH, GLA_QK_WIDTH), 0.1),
        'gla_norm': 1.0 + nrm(23, (DEPTH, GLA_DV), 0.01),
        'w_out': nrm(24, (DEPTH, MIX_WIDTH, D_MODEL), MIX_WIDTH ** -0.5),
        'norm_ffn2': 1.0 + nrm(25, (DEPTH, D_MODEL), 0.01),
        'ffn2_gate': nrm(26, (DEPTH, D_MODEL, D_FF), D_MODEL ** -0.5),
        'ffn2_up': nrm(27, (DEPTH, D_MODEL, D_FF), D_MODEL ** -0.5),
        'ffn2_down': nrm(28, (DEPTH, D_FF, D_MODEL), D_FF ** -0.5),
        'norm_final': 1.0 + nrm(29, (D_MODEL,), 0.01),
    }


def reference(x_prompt, x_sample, state_s5_re, state_s5_im, state_gla, norm_ffn1, ffn1_gate, ffn1_up,
              ffn1_down, norm_mix, w_in, s5_lam_re, s5_lam_im, s5_log_dt, s5_b_re, s5_b_im, s5_c_re,
              s5_c_im, s5_d, s5_glu_w, s5_glu_b, gla_gate_w, gla_gate_b, gla_norm, w_out, norm_ffn2,
              ffn2_gate, ffn2_up, ffn2_down, norm_final):
    weights = (norm_ffn1, ffn1_gate, ffn1_up, ffn1_down, norm_mix, w_in, s5_lam_re, s5_lam_im, s5_log_dt,
               s5_b_re, s5_b_im, s5_c_re, s5_c_im, s5_d, s5_glu_w, s5_glu_b, gla_gate_w, gla_gate_b,
               gla_norm, w_out, norm_ffn2, ffn2_gate, ffn2_up, ffn2_down, norm_final)
    bp = x_prompt.shape[0]
    zero_s5 = jnp.zeros((DEPTH, bp, S5_GROUPS, S5_STATE), jnp.float32)
    zero_gla = jnp.zeros((DEPTH, bp, GLA_HEADS, GLA_DK, GLA_DV), jnp.float32)
    y_prompt, p_re, p_im, p_gla = decoder(x_prompt, zero_s5, zero_s5, zero_gla, *weights)
    y_sample, s_re, s_im, s_gla = decoder(x_sample, state_s5_re, state_s5_im, state_gla, *weights)
    return (y_prompt, y_sample, p_re, p_im, p_gla, s_re, s_im, s_gla)
```

```python
import contextlib
import math
import numpy as np
import concourse.bass as bass
import concourse.mybir as mybir
from concourse.bass_utils import run_bass_kernel_spmd

F32 = mybir.dt.float32
BF16 = mybir.dt.bfloat16
I32 = mybir.dt.int32
AF = mybir.ActivationFunctionType
ALU = mybir.AluOpType

ENGS = ["pe", "act", "dve", "pool", "sp"]
EPS = 1e-6
NCORES = 8
D = 1024
DFF = 2816
NFT = 22
SEQ = 2048
PT = 512
NPASS_FULL = SEQ // PT
NSMP = 64
INW = 2064
EVALS = [0, 1, 2, 3, 4, 5, 6, 7, 8,
         0, -1, -2, -3, -4, -5, -6, -7,
         7, 6, 5, 4, 3, 2, 1, 0,
         -4,
         16, 24, 32, 40, 48, 56, 64]
NEV = len(EVALS)
MERGE_SPAN_A = 0.5


class Op:
    __slots__ = ("eng", "fn", "r", "w", "dma", "semkey", "deps", "raw", "need_inc", "incval", "dmaval", "pos", "prewait")


class Prog:
    def __init__(self):
        self.ops = []

    def add(self, eng, fn, r=(), w=(), dma=False, semkey=None):
        op = Op()
        op.eng, op.fn, op.r, op.w, op.dma, op.semkey = eng, fn, list(r), list(w), dma, semkey
        op.deps, op.need_inc, op.incval, op.dmaval = [], False, 0, 0
        op.raw = set()
        op.prewait = 0
        op.pos = -1
        self.ops.append(op)
        return op

    def capture_begin(self):
        self._saved = getattr(self, "_saved", [])
        self._saved.append(self.ops)
        self.ops = []

    def capture_end(self):
        lst = self.ops
        self.ops = self._saved.pop()
        return lst

    def merge(self, lists, spans=None):
        if spans is None:
            spans = [1.0] * len(lists)
        keep = [i for i, l in enumerate(lists) if l]
        lists = [lists[i] for i in keep]
        spans = [spans[i] for i in keep]
        idx = [0] * len(lists)
        total = sum(len(l) for l in lists)
        for _ in range(total):
            best, bf = None, None
            for i, l in enumerate(lists):
                if idx[i] < len(l):
                    f = idx[i] / len(l) * spans[i]
                    if bf is None or f < bf:
                        best, bf = i, f
            self.ops.append(lists[best][idx[best]])
            idx[best] += 1

    def analyze(self):
        for i, op in enumerate(self.ops):
            op.pos = i
        last_w = {}
        rd_eng = {}
        rd_dma = {}
        for op in self.ops:
            deps = set()
            for k in op.r:
                if k in last_w:
                    deps.add(last_w[k])
                    op.raw.add(last_w[k])
            for k in op.w:
                if k in last_w:
                    deps.add(last_w[k])
                for p in rd_eng.get(k, {}).values():
                    deps.add(p)
                for p in rd_dma.get(k, ()):
                    deps.add(p)
            deps.discard(op.pos)
            op.deps = sorted(deps)
            for k in op.w:
                last_w[k] = op.pos
                rd_eng[k] = {}
                rd_dma[k] = []
            for k in op.r:
                if op.dma:
                    rd_dma.setdefault(k, []).append(op.pos)
                else:
                    rd_eng.setdefault(k, {})[op.eng] = op.pos
        for op in self.ops:
            for d in op.deps:
                a = self.ops[d]
                if a.dma:
                    continue
                if a.eng != op.eng or op.dma or a.eng != "pe":
                    a.need_inc = True
        cnt = {e: 0 for e in ENGS}
        dcnt = {}
        self.dma_hist = {}
        for op in self.ops:
            if op.dma:
                dcnt[op.semkey] = dcnt.get(op.semkey, 0) + 16
                op.dmaval = dcnt[op.semkey]
                self.dma_hist.setdefault(op.semkey, []).append((op.pos, op.dmaval))
            elif op.need_inc:
                cnt[op.eng] += 1
                op.incval = cnt[op.eng]
        self.semkeys = list(dcnt.keys())
        self.dma_total = dcnt

    def _dma_wait_val(self, semkey, pos):
        v = 0
        for p, c in self.dma_hist[semkey]:
            if p < pos:
                v = c
            else:
                break
        return v

    def emit(self, nc, final_semkeys=()):
        self.analyze()
        maxw = {}
        for op in self.ops:
            if op.dma:
                op.prewait = maxw.get(op.semkey, 0)
            for d in op.deps:
                a = self.ops[d]
                if a.dma:
                    v = self._dma_wait_val(a.semkey, op.pos)
                    if v > maxw.get(a.semkey, 0):
                        maxw[a.semkey] = v
        with contextlib.ExitStack() as st:
            esem = {e: st.enter_context(nc.semaphore("s_" + e)) for e in ENGS}
            dsem = {k: st.enter_context(nc.semaphore("d_%d" % i)) for i, k in enumerate(self.semkeys)}
            block = st.enter_context(nc.Block())
            per_eng = {e: [op for op in self.ops if op.eng == e] for e in ENGS}

            def run(ename, eobj):
                waited = {}
                for op in per_eng[ename]:
                    need = {}
                    for d in op.deps:
                        a = self.ops[d]
                        if a.dma:
                            key = ("d", a.semkey)
                            val = self._dma_wait_val(a.semkey, op.pos)
                            sem = dsem[a.semkey]
                        else:
                            if a.eng == ename and not op.dma and ename == "pe":
                                continue
                            key = ("e", a.eng)
                            val = a.incval
                            sem = esem[a.eng]
                        if val > need.get(key, (0, None))[0]:
                            need[key] = (val, sem)
                    for key, (val, sem) in need.items():
                        if waited.get(key, 0) >= val:
                            continue
                        eobj.wait_ge(sem, val)
                        waited[key] = val
                    if op.dma and op.prewait > waited.get(("d", op.semkey), 0):
                        eobj.wait_ge(dsem[op.semkey], op.prewait)
                        waited[("d", op.semkey)] = op.prewait
                    ins = op.fn(eobj)
                    if op.dma:
                        ins.then_inc(dsem[op.semkey], 16)
                    elif op.need_inc:
                        ins.then_inc(esem[ename], 1)
                if ename == "sp":
                    for k in final_semkeys:
                        if k in dsem:
                            eobj.wait_ge(dsem[k], self.dma_total[k])

            @block.tensor
            def _(e):
                run("pe", e)

            @block.scalar
            def _(e):
                run("act", e)

            @block.vector
            def _(e):
                run("dve", e)

            @block.gpsimd
            def _(e):
                run("pool", e)

            @block.sync
            def _(e):
                run("sp", e)


def build_program(npass=NPASS_FULL, do_mixer=True, do_ffn=True, taps=(), stage=9):
    nc = bass.Bass("TRN2", target_bir_lowering=False, dynamic_dma_scratch_size=4096)
    P = Prog()

    def din(name, shape):
        return nc.dram_tensor(name, list(shape), F32, kind="ExternalInput").ap()

    def dout(name, shape):
        return nc.dram_tensor(name, list(shape), F32, kind="ExternalOutput").ap()

    xp = din("xp", [SEQ, D])
    xs = din("xs", [NSMP, D])
    s5re_in = din("s5re_in", [16, 32, 64])
    s5im_in = din("s5im_in", [16, 32, 64])
    gla_in = din("gla_in", [16, 4, 64, 128])
    gains_d = din("gains", [4, D])
    w_ffn = [(din("ffn1_gate", [D, DFF]), din("ffn1_up", [D, DFF]), din("ffn1_down", [DFF, D])),
             (din("ffn2_gate", [D, DFF]), din("ffn2_up", [D, DFF]), din("ffn2_down", [DFF, D]))]
    w_in_d = din("w_in", [D, INW])
    w_out_d = din("w_out", [D, D])
    glu_w_d = din("glu_w", [512, 512])
    glu_b_d = din("glu_b", [512])
    gate_w_d = din("gate_w", [16, 256])
    gate_b_d = din("gate_b", [256])
    gla_norm_d = din("gla_norm", [128])
    lam_re_d = din("lam_re", [32, 64])
    lam_im_d = din("lam_im", [32, 64])
    log_dt_d = din("log_dt", [32])
    b_re_d = din("b_re", [32, 64, 16])
    b_im_d = din("b_im", [32, 64, 16])
    c_re_d = din("c_re", [32, 16, 64])
    c_im_d = din("c_im", [32, 16, 64])
    s5_d_d = din("s5_d", [512])
    ident_d = din("c_ident", [128, 128])
    tri64_d = din("c_tri64", [128, 128])
    triu64_d = din("c_triu64", [128, 128])
    tri4_d = din("c_tri4", [64, 64])
    triu4_d = din("c_triu4", [64, 64])
    sel64_d = din("c_sel64", [128, 2])
    sel4_d = din("c_sel4", [64, 16])
    maskw4_d = din("c_maskw4", [128, 128])
    evals_d = din("c_evals", [NEV])

    yp = dout("yp", [SEQ, D])
    ys = dout("ys", [NSMP, D])
    pre_o = dout("pre", [32, 64])
    pim_o = dout("pim", [32, 64])
    pgla_o = dout("pgla", [4, 64, 128])
    sre_o = dout("sre", [16, 32, 64])
    sim_o = dout("sim", [16, 32, 64])
    sgla_o = dout("sgla", [16, 4, 64, 128])
    tap_out = {}
    for (tname, tshape) in taps:
        tap_out[tname] = dout("tap_" + tname, tshape)

    st = contextlib.ExitStack()
    with st:
        def sb(name, shape, dt):
            return st.enter_context(nc.sbuf_tensor("sb_" + name, list(shape), dt))

        NMAX = PT + NSMP
        xT = sb("xT", [128, 8, NMAX], F32)
        xn = sb("xn", [128, 8, NMAX], BF16)
        NW = 4
        wslot = [sb("wslot%d" % i, [128, 2048], BF16) for i in range(NW)]
        W1 = sb("W1", [128, 32, 128], BF16)
        W3 = sb("W3", [128, 16, 2, 128], BF16)
        W4 = sb("W4", [128, 32, 128], BF16)
        A1 = sb("A1", [128, 2, 16], F32)
        A2 = sb("A2", [128, 2, 16], F32)
        L4 = sb("L4", [128, 2, 16], F32)
        Bst = sb("Bst", [128, 2, 16, 65], F32)
        ident_f = sb("ident_f", [128, 128], F32)
        ident_b = sb("ident_b", [128, 128], BF16)
        ones_b = sb("ones_b", [128, 128], BF16)
        tri64 = sb("tri64", [128, 128], F32)
        triu64 = sb("triu64", [128, 128], F32)
        tri4 = sb("tri4", [64, 64], F32)
        triu4 = sb("triu4", [64, 64], F32)
        sel64 = sb("sel64", [128, 2], F32)
        sel4 = sb("sel4", [64, 16], F32)
        gains = sb("gains", [128, 4, 8], F32)
        glub = sb("glub", [128, 4], F32)
        gnorm = sb("gnorm", [128, 1], F32)
        gatew = sb("gatew", [16, 256], F32)
        gatew_b = sb("gatew_b", [16, 256], BF16)
        gateb = sb("gateb", [128, 256], F32)
        S_f = sb("S_f", [128, 2, 128], F32)
        PA1 = sb("PA1", [128, 2, 16, 8], F32)
        PA2 = sb("PA2", [128, 2, 16, 8], F32)
        scr = sb("scr", [128, 8], F32)

        XW = 26368
        arena = sb("arena", [128, XW], F32)

        class Carver:
            def __init__(self, base=0):
                self.off = base

            def get(self, shape, dt):
                n = 1
                for s in shape:
                    n *= s
                words = (n * (2 if dt == BF16 else 4) + 3) // 4
                words = (words + 7) // 8 * 8
                a = arena[:, self.off:self.off + words]
                self.off += words
                assert self.off <= XW, "arena overflow %d" % self.off
                if dt == BF16:
                    a = a.bitcast(BF16)[:, 0:n]
                elif dt == I32:
                    a = a.bitcast(I32)[:, 0:n]
                else:
                    a = a[:, 0:n]
                if len(shape) == 2:
                    return a.rearrange("p (a b) -> p a b", a=shape[0])
                if len(shape) == 3:
                    return a.rearrange("p (a b c) -> p a b c", a=shape[0], b=shape[1])
                return a

        cf = Carver(0)
        hT = cf.get([NFT, NMAX], BF16)
        sgb = [cf.get([512], F32) for _ in range(2)]
        sqb = [cf.get([512], BF16) for _ in range(2)]
        sdb = cf.get([512], F32)
        xin = [cf.get([1024], F32) for _ in range(2)]
        yT = Carver(0).get([8, NMAX], F32)
        ffn_end = cf.off
        cst = Carver(ffn_end)
        xst = [cst.get([1024], F32) for _ in range(2)]
        cp = Carver(ffn_end)
        pc_lr = cp.get([16], F32)
        pc_li = cp.get([16], F32)
        pc_dt = cp.get([16], F32)
        pc_ev = cp.get([NEV], F32)
        pc_er = cp.get([16, NEV], F32)
        pc_ei = cp.get([16, NEV], F32)
        pc_t0 = cp.get([16, NEV], F32)
        pc_t1 = cp.get([16, NEV], F32)
        pc_t2 = cp.get([16, NEV], F32)
        pc_ti = cp.get([16, NEV], I32)
        pc_fr = cp.get([16], F32)
        pc_fi = cp.get([16], F32)
        pc_s0 = cp.get([16], F32)
        pc_s1 = cp.get([16], F32)
        pc_s2 = cp.get([16], F32)
        pc_pr = cp.get([16, NEV], F32)
        pc_pi = cp.get([16, NEV], F32)
        pc_br = cp.get([16, 16], F32)
        pc_bi = cp.get([16, 16], F32)
        pc_cr = cp.get([16, 16], F32)
        pc_ci = cp.get([16, 16], F32)
        pc_dcol = cp.get([32], F32)
        pc_A = cp.get([16, 128], F32)
        pc_B = cp.get([16, 128], F32)
        pc_C = cp.get([16, 128], F32)
        pc_D = cp.get([16, 128], F32)
        pc_E = cp.get([16, 128], F32)
        pc_w4t = cp.get([128], F32)
        pc_csb = [pc_E[:, 2 * i_:2 * i_ + 2, :].rearrange("p a (b c) -> p a b c", b=2) for i_ in range(2)]
        cm = Carver(0)
        ucm_flat = cm.get([4096], BF16)
        ucm = ucm_flat.rearrange("p (g j h) -> p g j h", g=32, j=8)
        zcm = ucm_flat.rearrange("p (j c) -> p j c", j=8)
        qT = cm.get([2, 512], F32)
        kT = cm.get([2, 512], F32)
        k_tm = cm.get([4, 256], F32)
        v_tm = cm.get([4, 512], BF16)
        sgT = cm.get([4, 512], BF16)
        aT = cm.get([512], BF16)
        lf = cm.get([4, 256], F32)
        Ug = cm.get([32, 64], BF16)
        cat = cm.get([8, 512], BF16)
        zT = cm.get([4, 512], BF16)
        Hbz = [cm.get([2, 16, 64], BF16) for _ in range(2)]
        ebb = [cm.get([2, 128], F32) for _ in range(2)]
        enb = [cm.get([2, 128], F32) for _ in range(2)]
        erem = [cm.get([256], F32) for _ in range(2)]
        qs = [cm.get([2, 128], BF16) for _ in range(2)]
        ki = [cm.get([2, 128], BF16) for _ in range(2)]
        kiz = [cm.get([4, 128], BF16) for _ in range(2)]
        Sz = [cm.get([4, 128], BF16) for _ in range(2)]
        kend = [cm.get([256], BF16) for _ in range(2)]
        kendm = [cm.get([256], BF16) for _ in range(4)]
        attT = [cm.get([4, 128], BF16) for _ in range(2)]
        o_sb = cm.get([512], F32)
        osq = cm.get([512], BF16)
        t1b = cm.get([512], F32)
        etmp = t1b[:, 0:256]
        sdm = cm.get([512], F32)
        sig0_ = cm.get([512], F32)
        sig = [sig0_, sig0_]
        S0b = [cm.get([2, 128], F32) for _ in range(2)]
        Sob = [cm.get([2, 128], F32) for _ in range(2)]
        sct1 = cm.get([2, 16, 8], F32)
        sct2 = cm.get([2, 16, 8], F32)
        sc_t1 = cm.get([2, 16], F32)
        sc_t2 = cm.get([2, 16], F32)
        Bsm = cm.get([2, 16, 16], F32)
        Ssm = cm.get([2, 16, 16], F32)
        Hs3 = cm.get([2, 16, 16], F32)
        sm_t = [cm.get([2, 16, 16], F32) for _ in range(2)]
        h0t = [cm.get([2, 64], F32) for _ in range(2)]
        hot = [cm.get([2, 64], F32) for _ in range(2)]

        ps = [st.enter_context(nc.psum_tensor("ps%d" % i, [128, 512], F32)) for i in range(8)]
        rot = {"A": [0, 1], "B": [2, 3], "C": [4, 5], "M": [6, 7], "U": [2, 3]}
        rot_i = {k: 0 for k in rot}

        rot_ovr = [None]
        ws_ovr = [None]

        def psum(group):
            banks = rot_ovr[0][group] if rot_ovr[0] is not None else rot[group]
            i = banks[rot_i[group] % len(banks)]
            rot_i[group] += 1
            return i

        def pk(i):
            return "ps%d" % i

        XB = "Xbar"

        def call(name, *a, **kw):
            return lambda e: getattr(e, name)(*a, **kw)

        def add(eng, fn, r=(), w=(), x=False):
            r = [k_ for k_ in r if k_ is not None]
            if x:
                r.append(XB)
            return P.add(eng, fn, r=r, w=w)

        def dma(eng, out, in_, r=(), w=(), semkey=None, x=False, slow=False):
            r = list(r)
            if x:
                r.append(XB)
            if slow:
                fn = call("dma_start", out=out, in_=in_, allow_slow_non_contiguous=True)
            else:
                fn = call("dma_start", out=out, in_=in_)
            return P.add(eng, fn, r=r, w=w, dma=True, semkey=semkey)

        def barrier():
            P.add("pool", call("memset", scr[:, 0:1], 0.0), r=[], w=[XB])

        ws_i = [0]

        ws_hist = []

        def wload(out_view_fn, in_ap):
            pool_ = ws_ovr[0] if ws_ovr[0] is not None else list(range(NW))
            s = pool_[ws_i[0] % len(pool_)]
            ws_i[0] += 1
            extra = ["ws%d" % ws_hist[-2]] if len(ws_hist) >= 2 and ws_hist[-2] != s else []
            ws_hist.append(s)
            dma("pool", out_view_fn(wslot[s]), in_ap, r=extra, w=["ws%d" % s], semkey="ws%d" % s)
            return s

        ev_i = [0]

        def evac_eng():
            ev_i[0] += 1
            return "act" if ev_i[0] % 2 == 0 else "dve"

        def copy_op(eng, out, in_, r, w, x=True):
            if eng == "act":
                add("act", call("copy", out=out, in_=in_), r=r, w=w, x=x)
            else:
                add(eng, call("tensor_copy", out=out, in_=in_), r=r, w=w, x=x)

        cload = [(ident_f[:], ident_d, "ident_f"), (tri64[:], tri64_d, "tri64"), (triu64[:], triu64_d, "triu64"),
                 (tri4[:], tri4_d, "tri4"), (triu4[:], triu4_d, "triu4"), (sel64[:], sel64_d, "sel64"),
                 (sel4[:], sel4_d, "sel4"), (gatew[:], gate_w_d, "gatew"),
                 (gateb[:], gate_b_d.partition_broadcast(128), "gateb")]
        for (o, i, k) in cload:
            dma("sp", o, i, w=[k], semkey="const")
        dma("sp", gains[:], gains_d.rearrange("n (k p) -> p n k", p=128), w=["gains"], semkey="const", slow=True)
        dma("sp", glub[:], glu_b_d.rearrange("(m p) -> p m", p=128), w=["glub"], semkey="const", slow=True)
        dma("sp", gnorm[:], gla_norm_d.rearrange("(p o) -> p o", o=1), w=["gnorm"], semkey="const", slow=True)
        add("dve", call("tensor_copy", out=ident_b[:], in_=ident_f[:]), r=["ident_f"], w=["ident_b"])
        add("dve", call("tensor_copy", out=gatew_b[:], in_=gatew[:]), r=["gatew"], w=["gatew_b"])
        add("act", call("activation", out=gateb[:], in_=gateb[:], func=AF.Exp, scale=-1.0), r=["gateb"], w=["gateb"])
        add("dve", call("memset", ones_b[:], 1.0), w=["ones_b"])
        add("dve", call("memset", S_f[:], 0.0), w=["S_f"])
        add("dve", call("memset", Bst[:], 0.0), w=["Bst"])

        xin_i = [0]

        def load_x(src_rows, ntok, col0):
            b = xin_i[0] % 2
            xin_i[0] += 1
            kx = "xin%d" % b
            dma("sp", xin[b][0:ntok, :], src_rows, w=[kx], semkey=kx, x=True)
            for half in range(2):
                pi = psum("M")
                for kk in range(4):
                    k = half * 4 + kk
                    add("pe", call("transpose",
                        out=ps[pi][:, kk * 128:kk * 128 + ntok], in_=xin[b][0:ntok, k * 128:(k + 1) * 128],
                        identity=ident_f[0:ntok, 0:ntok]), r=[kx, "ident_f"], w=[pk(pi)], x=True)
                src = ps[pi][:].rearrange("p (a b) -> p a b", a=4)[:, :, 0:ntok]
                dst = xT[:, half * 4:half * 4 + 4, col0:col0 + ntok]
                copy_op(evac_eng(), dst, src, r=[pk(pi)], w=["xT%d" % (half * 4 + q_) for q_ in range(4)], x=True)

        def rmsnorm(gi, chunks, dst, dst_key):
            for (c0, n) in chunks:
                pi = psum("M")
                for k in range(8):
                    b = k % 2
                    add("act", call("activation", out=sqb[b][:, 0:n], in_=xT[:, k, c0:c0 + n],
                                                               func=AF.Square),
                        r=["xT%d" % k], w=["sq%d" % b], x=True)
                    add("pe", call("matmul", ps[pi][:, 0:n], lhsT=ones_b[:], rhs=sqb[b][:, 0:n],
                                                                 start=(k == 0), stop=(k == 7)),
                        r=["sq%d" % b, "ones_b"], w=[pk(pi)], x=True)
                add("act", call("activation", out=sdb[:, 0:n], in_=ps[pi][:, 0:n], func=AF.Sqrt,
                                                        bias=EPS, scale=1.0 / D),
                    r=[pk(pi)], w=["sdb"], x=True)
                add("dve", call("reciprocal", out=ps[pi][:, 0:n], in_=sdb[:, 0:n]),
                    r=["sdb"], w=[pk(pi)], x=True)
                for k in range(8):
                    add("dve", call("scalar_tensor_tensor",
                        out=dst[:, k, c0:c0 + n], in0=xT[:, k, c0:c0 + n], scalar=gains[:, gi, k:k + 1],
                        in1=ps[pi][:, 0:n], op0=ALU.mult, op1=ALU.mult),
                        r=["xT%d" % k, "gains", pk(pi)], w=[dst_key + str(k)] + (["hT%d" % f_ for f_ in range(NFT)] if dst_key == "yT" else []), x=True)

        def ffn(wg, wu, wd, chunks, side=None):
            nside = [0]
            npts = NFT // 2 + 8

            side_dma = [o for o in side if o.dma] if side else []
            side_cmp = [o for o in side if not o.dma] if side else []
            if side_dma:
                P.ops.extend(side_dma)
            first_pt = 14

            def inject(ip):
                if side_cmp and ip >= first_pt:
                    hi = len(side_cmp) * (ip - first_pt + 1) // (npts - first_pt)
                    P.ops.extend(side_cmp[nside[0]:hi])
                    nside[0] = hi
            wgv = wg.rearrange("(k p) f -> p k f", p=128)
            wuv = wu.rearrange("(k p) f -> p k f", p=128)
            wdv = wd.rearrange("(t p) d -> p t d", p=128)
            v8 = lambda t: t[:].rearrange("p (k f) -> p k f", k=8)
            for fp in range(NFT // 2):
                if fp > 0:
                    inject(fp - 1)
                sg_ = wload(v8, wgv[:, :, fp * 256:(fp + 1) * 256])
                su_ = wload(v8, wuv[:, :, fp * 256:(fp + 1) * 256])
                for hf in range(2):
                    f = fp * 2 + hf
                    for (c0, n) in chunks:
                        pa = psum("A")
                        pb = psum("B")
                        for (pi, s_) in ((pa, sg_), (pb, su_)):
                            for k in range(8):
                                add("pe", call("matmul",
                                    ps[pi][:, 0:n], lhsT=v8(wslot[s_])[:, k, hf * 128:(hf + 1) * 128],
                                    rhs=xn[:, k, c0:c0 + n], start=(k == 0), stop=(k == 7)),
                                    r=["ws%d" % s_, "xn%d" % k], w=[pk(pi)], x=True)
                        b = f % 2
                        add("act", call("activation", out=sgb[b][:, 0:n], in_=ps[pa][:, 0:n],
                                                                          func=AF.Silu),
                            r=[pk(pa)], w=["sg%d" % b], x=True)
                        add("dve", call("tensor_tensor",
                            out=hT[:, f, c0:c0 + n], in0=sgb[b][:, 0:n], in1=ps[pb][:, 0:n], op=ALU.mult),
                            r=["sg%d" % b, pk(pb)], w=["hT%d" % f] + (["yT%d" % q_ for q_ in range(8)] if (f == 0 and c0 == 0) else []), x=True)
            v11 = lambda t: t[:, 0:1408].rearrange("p (t c) -> p t c", t=11)
            for d in range(8):
                inject(NFT // 2 + d)
                sl = [wload(v11, wdv[:, hh * 11:(hh + 1) * 11, d * 128:(d + 1) * 128]) for hh in range(2)]
                for (c0, n) in chunks:
                    pi = psum("C")
                    for f in range(NFT):
                        s_ = sl[f // 11]
                        add("pe", call("matmul",
                            ps[pi][:, 0:n], lhsT=v11(wslot[s_])[:, f % 11, :], rhs=hT[:, f, c0:c0 + n],
                            start=(f == 0), stop=(f == NFT - 1)),
                            r=["ws%d" % s_, "hT%d" % f], w=[pk(pi)], x=True)
                    add("dve", call("scalar_tensor_tensor",
                        out=xT[:, d, c0:c0 + n], in0=ps[pi][:, 0:n], scalar=0.5, in1=xT[:, d, c0:c0 + n],
                        op0=ALU.mult, op1=ALU.add),
                        r=[pk(pi), "xT%d" % d], w=["xT%d" % d], x=True)
            if side_cmp:
                P.ops.extend(side_cmp[nside[0]:])
                nside[0] = len(side_cmp)

        xst_i = [0]

        def store_out(dst_rows, ntok, col0):
            b = xst_i[0] % 2
            xst_i[0] += 1
            kx = "xst%d" % b
            for half in range(2):
                pi = psum("M")
                for kk in range(4):
                    k = half * 4 + kk
                    add("pe", call("transpose", out=ps[pi][0:ntok, kk * 128:(kk + 1) * 128], in_=yT[:, k, col0:col0 + ntok],
                                   identity=ident_f[:]), r=["yT%d" % k, "ident_f"], w=[pk(pi)], x=True)
                copy_op(evac_eng(), xst[b][0:ntok, half * 512:(half + 1) * 512], ps[pi][0:ntok, :],
                        r=[pk(pi)], w=[kx], x=True)
            dma("sp", dst_rows, xst[b][0:ntok, :], r=[kx], semkey="out", x=True)

        def precompute():
            x_ = True
            for hfp in range(2):
                pr = slice(hfp * 64, hfp * 64 + 64)
                gs = slice(hfp * 16, hfp * 16 + 16)
                dma("sp", pc_lr[pr, :], lam_re_d[gs, :].rearrange("g p -> p g"), w=["pc_lr"], semkey="pc", x=x_, slow=True)
                dma("sp", pc_li[pr, :], lam_im_d[gs, :].rearrange("g p -> p g"), w=["pc_li"], semkey="pc", x=x_, slow=True)
                dma("sp", pc_dt[pr, :], log_dt_d[gs].partition_broadcast(64), w=["pc_dt"], semkey="pc", x=x_, slow=True)
                dma("sp", pc_br[pr, :, :], b_re_d[gs, :, :].rearrange("g p h -> p g h"), w=["pc_br"], semkey="pc", x=x_, slow=True)
                dma("sp", pc_bi[pr, :, :], b_im_d[gs, :, :].rearrange("g p h -> p g h"), w=["pc_bi"], semkey="pc", x=x_, slow=True)
            piC = psum("A")
            for t_, src_d in enumerate((c_re_d, c_im_d)):
                for glhi in range(2):
                    dma("sp", pc_csb[t_][:, glhi, :, :], src_d.rearrange("g h p -> (g h) p").rearrange("(a b q) p -> q b a p", a=2, b=2)[:, glhi, :, :],
                        w=["pc_csb%d" % t_], semkey="pc", x=x_)
            for t_ in range(2):
                for glhi in range(2):
                    add("pe", call("transpose", out=ps[piC][:, t_ * 256 + glhi * 128:t_ * 256 + (glhi + 1) * 128],
                                   in_=pc_csb[t_][:, glhi, :, :].rearrange("q a p -> q (a p)"), identity=ident_f[:]),
                        r=["pc_csb%d" % t_, "ident_f"], w=[pk(piC)], x=True)
            copy_op("dve", pc_cr[:, :, :], ps[piC][:, 0:256].rearrange("p (g h) -> p g h", g=16), r=[pk(piC)], w=["pc_cr"])
            copy_op("dve", pc_ci[:, :, :], ps[piC][:, 256:512].rearrange("p (g h) -> p g h", g=16), r=[pk(piC)], w=["pc_ci"])
            dma("sp", pc_ev[:, :], evals_d.partition_broadcast(128), w=["pc_ev"], semkey="pc", x=x_, slow=True)
            for i in range(8):
                dma("sp", pc_dcol[i * 16:(i + 1) * 16, :], s5_d_d.rearrange("(g h) -> h g", h=16), w=["pc_dcol"],
                    semkey="pc", x=x_, slow=True)
            dma("sp", pc_w4t[:, :], maskw4_d, w=["maskw4"], semkey="pc", x=x_)

            def V(eng, fn, r, w):
                add(eng, fn, r=r, w=w, x=True)

            bc_g = lambda a: a.unsqueeze(2).to_broadcast([128, 16, NEV])
            bc_e = lambda a: a.unsqueeze(1).to_broadcast([128, 16, NEV])
            V("act", call("activation", out=pc_dt[:, :], in_=pc_dt[:, :], func=AF.Exp), ["pc_dt"], ["pc_dt"])
            V("dve", call("tensor_tensor", out=pc_s0[:, :], in0=pc_lr[:, :], in1=pc_dt[:, :], op=ALU.mult), ["pc_lr", "pc_dt"], ["pc_s0"])
            V("dve", call("tensor_tensor", out=pc_s1[:, :], in0=pc_li[:, :], in1=pc_dt[:, :], op=ALU.mult), ["pc_li", "pc_dt"], ["pc_s1"])
            V("dve", call("tensor_tensor", out=pc_t0[:, :, :], in0=bc_g(pc_s0[:, :]), in1=bc_e(pc_ev[:, :]), op=ALU.mult), ["pc_s0", "pc_ev"], ["pc_t0"])
            V("dve", call("tensor_tensor", out=pc_t1[:, :, :], in0=bc_g(pc_s1[:, :]), in1=bc_e(pc_ev[:, :]), op=ALU.mult), ["pc_s1", "pc_ev"], ["pc_t1"])
            V("act", call("activation", out=pc_t0[:, :, :], in_=pc_t0[:, :, :], func=AF.Exp), ["pc_t0"], ["pc_t0"])
            C1 = 6.28125
            C2 = 2.0 * math.pi - C1
            V("dve", call("tensor_scalar", out=pc_t2[:, :, :], in0=pc_t1[:, :, :], scalar1=1.0 / (2.0 * math.pi), scalar2=None, op0=ALU.mult), ["pc_t1"], ["pc_t2"])
            V("dve", call("tensor_copy", out=pc_ti[:, :, :], in_=pc_t2[:, :, :]), ["pc_t2"], ["pc_ti"])
            V("dve", call("tensor_copy", out=pc_t2[:, :, :], in_=pc_ti[:, :, :]), ["pc_ti"], ["pc_t2"])
            V("dve", call("scalar_tensor_tensor", out=pc_t1[:, :, :], in0=pc_t2[:, :, :], scalar=-C1, in1=pc_t1[:, :, :], op0=ALU.mult, op1=ALU.add), ["pc_t2", "pc_t1"], ["pc_t1"])
            V("dve", call("scalar_tensor_tensor", out=pc_t1[:, :, :], in0=pc_t2[:, :, :], scalar=-C2, in1=pc_t1[:, :, :], op0=ALU.mult, op1=ALU.add), ["pc_t2", "pc_t1"], ["pc_t1"])
            V("act", call("activation", out=pc_t2[:, :, :], in_=pc_t1[:, :, :], func=AF.Sin, scale=0.5), ["pc_t1"], ["pc_t2"])
            V("act", call("activation", out=pc_t1[:, :, :], in_=pc_t1[:, :, :], func=AF.Sin, scale=0.25), ["pc_t1"], ["pc_t1"])
            V("dve", call("tensor_tensor", out=pc_t1[:, :, :], in0=pc_t1[:, :, :], in1=pc_t1[:, :, :], op=ALU.mult), ["pc_t1"], ["pc_t1"])
            V("dve", call("tensor_scalar", out=pc_t1[:, :, :], in0=pc_t1[:, :, :], scalar1=-2.0, scalar2=1.0, op0=ALU.mult, op1=ALU.add), ["pc_t1"], ["pc_t1"])
            V("dve", call("tensor_tensor", out=pc_t1[:, :, :], in0=pc_t1[:, :, :], in1=pc_t2[:, :, :], op=ALU.mult), ["pc_t1", "pc_t2"], ["pc_t1"])
            V("dve", call("scalar_tensor_tensor", out=pc_ei[:, :, :], in0=pc_t1[:, :, :], scalar=2.0, in1=pc_t0[:, :, :], op0=ALU.mult, op1=ALU.mult), ["pc_t1", "pc_t0"], ["pc_ei"])
            V("dve", call("tensor_tensor", out=pc_t2[:, :, :], in0=pc_t2[:, :, :], in1=pc_t2[:, :, :], op=ALU.mult), ["pc_t2"], ["pc_t2"])
            V("dve", call("tensor_scalar", out=pc_t2[:, :, :], in0=pc_t2[:, :, :], scalar1=-2.0, scalar2=1.0, op0=ALU.mult, op1=ALU.add), ["pc_t2"], ["pc_t2"])
            V("dve", call("tensor_tensor", out=pc_er[:, :, :], in0=pc_t2[:, :, :], in1=pc_t0[:, :, :], op=ALU.mult), ["pc_t2", "pc_t0"], ["pc_er"])
            abr = pc_er[:, :, 1]
            abi = pc_ei[:, :, 1]
            V("dve", call("tensor_tensor", out=pc_s0[:, :], in0=pc_lr[:, :], in1=pc_lr[:, :], op=ALU.mult), ["pc_lr"], ["pc_s0"])
            V("dve", call("tensor_tensor", out=pc_s1[:, :], in0=pc_li[:, :], in1=pc_li[:, :], op=ALU.mult), ["pc_li"], ["pc_s1"])
            V("dve", call("tensor_tensor", out=pc_s0[:, :], in0=pc_s0[:, :], in1=pc_s1[:, :], op=ALU.add), ["pc_s0", "pc_s1"], ["pc_s0"])
            V("dve", call("reciprocal", out=pc_s0[:, :], in_=pc_s0[:, :]), ["pc_s0"], ["pc_s0"])
            V("dve", call("tensor_scalar", out=pc_s1[:, :], in0=abr, scalar1=-1.0, scalar2=None, op0=ALU.add), ["pc_er"], ["pc_s1"])
            V("dve", call("tensor_tensor", out=pc_fr[:, :], in0=pc_s1[:, :], in1=pc_lr[:, :], op=ALU.mult), ["pc_s1", "pc_lr"], ["pc_fr"])
            V("dve", call("tensor_tensor", out=pc_s2[:, :], in0=abi, in1=pc_li[:, :], op=ALU.mult), ["pc_ei", "pc_li"], ["pc_s2"])
            V("dve", call("tensor_tensor", out=pc_fr[:, :], in0=pc_fr[:, :], in1=pc_s2[:, :], op=ALU.add), ["pc_fr", "pc_s2"], ["pc_fr"])
            V("dve", call("tensor_tensor", out=pc_fr[:, :], in0=pc_fr[:, :], in1=pc_s0[:, :], op=ALU.mult), ["pc_fr", "pc_s0"], ["pc_fr"])
            V("dve", call("tensor_tensor", out=pc_fi[:, :], in0=abi, in1=pc_lr[:, :], op=ALU.mult), ["pc_ei", "pc_lr"], ["pc_fi"])
            V("dve", call("tensor_tensor", out=pc_s2[:, :], in0=pc_s1[:, :], in1=pc_li[:, :], op=ALU.mult), ["pc_s1", "pc_li"], ["pc_s2"])
            V("dve", call("tensor_tensor", out=pc_fi[:, :], in0=pc_fi[:, :], in1=pc_s2[:, :], op=ALU.subtract), ["pc_fi", "pc_s2"], ["pc_fi"])
            V("dve", call("tensor_tensor", out=pc_fi[:, :], in0=pc_fi[:, :], in1=pc_s0[:, :], op=ALU.mult), ["pc_fi", "pc_s0"], ["pc_fi"])
            V("dve", call("tensor_tensor", out=pc_pr[:, :, :], in0=pc_er[:, :, :], in1=bc_g(pc_fr[:, :]), op=ALU.mult), ["pc_er", "pc_fr"], ["pc_pr"])
            V("dve", call("tensor_tensor", out=pc_t0[:, :, :], in0=pc_ei[:, :, :], in1=bc_g(pc_fi[:, :]), op=ALU.mult), ["pc_ei", "pc_fi"], ["pc_t0"])
            V("dve", call("tensor_tensor", out=pc_pr[:, :, :], in0=pc_pr[:, :, :], in1=pc_t0[:, :, :], op=ALU.subtract), ["pc_pr", "pc_t0"], ["pc_pr"])
            V("dve", call("tensor_tensor", out=pc_pi[:, :, :], in0=pc_er[:, :, :], in1=bc_g(pc_fi[:, :]), op=ALU.mult), ["pc_er", "pc_fi"], ["pc_pi"])
            V("dve", call("tensor_tensor", out=pc_t0[:, :, :], in0=pc_ei[:, :, :], in1=bc_g(pc_fr[:, :]), op=ALU.mult), ["pc_ei", "pc_fr"], ["pc_t0"])
            V("dve", call("tensor_tensor", out=pc_pi[:, :, :], in0=pc_pi[:, :, :], in1=pc_t0[:, :, :], op=ALU.add), ["pc_pi", "pc_t0"], ["pc_pi"])
            V("dve", call("tensor_copy", out=A1[:, 0, :], in_=pc_er[:, :, 8]), ["pc_er"], ["A1"])
            V("dve", call("tensor_copy", out=A1[:, 1, :], in_=pc_er[:, :, 8]), ["pc_er"], ["A1"])
            V("dve", call("tensor_scalar", out=A2[:, 0, :], in0=pc_ei[:, :, 8], scalar1=-1.0, scalar2=None, op0=ALU.mult), ["pc_ei"], ["A2"])
            V("dve", call("tensor_copy", out=A2[:, 1, :], in_=pc_ei[:, :, 8]), ["pc_ei"], ["A2"])
            V("dve", call("tensor_copy", out=L4[:, 0, :], in_=pc_er[:, :, 25]), ["pc_er"], ["L4"])
            V("dve", call("tensor_copy", out=L4[:, 1, :], in_=pc_ei[:, :, 25]), ["pc_ei"], ["L4"])
            for k_, idx_ in enumerate([8, 26, 27, 28, 29, 30, 31, 32]):
                V("dve", call("tensor_copy", out=PA1[:, 0, :, k_], in_=pc_er[:, :, idx_]), ["pc_er"], ["PA"])
                V("dve", call("tensor_copy", out=PA1[:, 1, :, k_], in_=pc_er[:, :, idx_]), ["pc_er"], ["PA"])
                V("dve", call("tensor_scalar", out=PA2[:, 0, :, k_], in0=pc_ei[:, :, idx_], scalar1=-1.0, scalar2=None, op0=ALU.mult), ["pc_ei"], ["PA"])
                V("dve", call("tensor_copy", out=PA2[:, 1, :, k_], in_=pc_ei[:, :, idx_]), ["pc_ei"], ["PA"])

            def v4(t):
                return t[:, :, :].rearrange("p g (i h) -> p g i h", i=8)

            def bt(T, e0):
                return T[:, :, e0:e0 + 8].unsqueeze(3).to_broadcast([128, 16, 8, 16])

            def bv(Vv):
                return Vv[:, :, :].unsqueeze(2).to_broadcast([128, 16, 8, 16])

            def cplx(outr, outi, Tr, Ti, e0, Vr, Vi, keys_t, keys_v, neg_im=False, only=None):
                kr, ki_ = keys_t
                vr, vi = keys_v
                if outr is not None:
                    okr = outr[1]
                    V("dve", call("tensor_tensor", out=v4(outr[0]), in0=bt(Tr, e0), in1=bv(Vr), op=ALU.mult), [kr, vr], [okr])
                    V("dve", call("tensor_tensor", out=v4(pc_E), in0=bt(Ti, e0), in1=bv(Vi), op=ALU.mult), [ki_, vi], ["pc_E"])
                    V("dve", call("tensor_tensor", out=outr[0][:, :, :], in0=outr[0][:, :, :], in1=pc_E[:, :, :], op=ALU.subtract), [okr, "pc_E"], [okr])
                if outi is not None:
                    oki = outi[1]
                    V("dve", call("tensor_tensor", out=v4(outi[0]), in0=bt(Tr, e0), in1=bv(Vi), op=ALU.mult), [kr, vi], [oki])
                    V("dve", call("tensor_tensor", out=v4(pc_E), in0=bt(Ti, e0), in1=bv(Vr), op=ALU.mult), [ki_, vr], ["pc_E"])
                    if neg_im:
                        V("dve", call("scalar_tensor_tensor", out=outi[0][:, :, :], in0=outi[0][:, :, :], scalar=-1.0, in1=pc_E[:, :, :], op0=ALU.mult, op1=ALU.subtract), [oki, "pc_E"], [oki])
                    else:
                        V("dve", call("tensor_tensor", out=outi[0][:, :, :], in0=outi[0][:, :, :], in1=pc_E[:, :, :], op=ALU.add), [oki, "pc_E"], [oki])

            cplx((pc_A, "pc_A"), (pc_B, "pc_B"), pc_er, pc_ei, 1, pc_cr, pc_ci, ("pc_er", "pc_ei"), ("pc_cr", "pc_ci"), neg_im=True)
            V("dve", call("tensor_copy", out=W3[:, :, 0, :], in_=pc_A[:, :, :]), ["pc_A"], ["W3"])
            V("dve", call("tensor_copy", out=W3[:, :, 1, :], in_=pc_B[:, :, :]), ["pc_B"], ["W3"])
            cplx((pc_A, "pc_A"), (pc_B, "pc_B"), pc_er, pc_ei, 0, pc_cr, pc_ci, ("pc_er", "pc_ei"), ("pc_cr", "pc_ci"))
            cplx((pc_C, "pc_C"), (pc_D, "pc_D"), pc_pr, pc_pi, 9, pc_br, pc_bi, ("pc_pr", "pc_pi"), ("pc_br", "pc_bi"), neg_im=True)
            for g in range(32):
                hfp, gl = g // 16, g % 16
                pr = slice(hfp * 64, hfp * 64 + 64)
                pi = psum("A")
                add("pe", call("matmul", ps[pi][:, 0:128], lhsT=pc_C[pr, gl, :], rhs=pc_A[pr, gl, :], start=True, stop=False),
                    r=["pc_C", "pc_A"], w=[pk(pi)], x=True)
                add("pe", call("matmul", ps[pi][:, 0:128], lhsT=pc_D[pr, gl, :], rhs=pc_B[pr, gl, :], start=False, stop=True),
                    r=["pc_D", "pc_B"], w=[pk(pi)], x=True)
                add("dve", call("tensor_tensor", out=ps[pi][:, 128:256], in0=ps[pi][:, 0:128], in1=pc_w4t[:, :], op=ALU.mult),
                    r=[pk(pi), "maskw4"], w=[pk(pi)], x=True)
                add("dve", call("scalar_tensor_tensor", out=W4[:, g, :], in0=ident_f[:], scalar=pc_dcol[:, g:g + 1], in1=ps[pi][:, 128:256], op0=ALU.mult, op1=ALU.add),
                    r=[pk(pi), "ident_f", "pc_dcol"], w=["W4"], x=True)
            cplx((pc_C, "pc_C"), (pc_D, "pc_D"), pc_pr, pc_pi, 17, pc_br, pc_bi, ("pc_pr", "pc_pi"), ("pc_br", "pc_bi"))
            for g in range(32):
                hfp, gl = g // 16, g % 16
                pr = slice(hfp * 64, hfp * 64 + 64)
                pi = psum("B")
                for ri, (T, tk) in enumerate(((pc_C, "pc_C"), (pc_D, "pc_D"))):
                    add("pe", call("transpose", out=ps[pi][:, ri * 64:(ri + 1) * 64], in_=T[pr, gl, :], identity=ident_f[pr, pr]),
                        r=[tk, "ident_f"], w=[pk(pi)], x=True)
                copy_op(evac_eng(), W1[:, g, :], ps[pi][:, 0:128], r=[pk(pi)], w=["W1"], x=True)

        def mixer(kind, c0, n, pass_idx):
            sample = (kind == "sample")
            ntile = 1 if sample else n // 128
            ntok = 64 if sample else 128
            L = 4 if sample else 64
            nch = ntok // L
            ncc = 16 if sample else n // 8
            J = 4 if sample else 8
            TRI = tri4 if sample else tri64
            TRIU = triu4 if sample else triu64
            SEL = sel4 if sample else sel64
            rmsnorm(1, [(c0, n)], xn, "xn")
            if stage < 1:
                return
            v8 = lambda t: t[:].rearrange("p (k f) -> p k f", k=8)
            winv = w_in_d.rearrange("(k p) f -> p k f", p=128)

            def wl_in(col0, ncol):
                return wload(lambda t: t[:, 0:8 * ncol].rearrange("p (k f) -> p k f", k=8), winv[:, :, col0:col0 + ncol])

            def vin(s_, ncol):
                return wslot[s_][:, 0:8 * ncol].rearrange("p (k f) -> p k f", k=8)

            rot_ovr[0] = {"A": [0, 1, 2, 3], "B": [4, 5], "C": [4, 5], "M": [6, 7], "U": [2, 3]}
            if sample:
                add("dve", call("memset", ucm_flat[:, :], 0.0), w=["ucm"], x=True)
            for gq in range(2):
                s_ = wl_in(gq * 256, 256)
                for j in range(J):
                    pi = psum("A")
                    if sample:
                        lsel = lambda k, j=j: xn[:, k, c0 + j:c0 + n:4]
                    else:
                        lsel = lambda k, j=j: xn[:, k, c0 + j:c0 + n:8]
                    for k in range(8):
                        add("pe", call("matmul",
                            ps[pi][0:ncc, 0:256], lhsT=lsel(k), rhs=vin(s_, 256)[:, k, :], start=(k == 0), stop=(k == 7)),
                            r=["ws%d" % s_, "xn%d" % k], w=[pk(pi)], x=True)
                    copy_op(evac_eng(), ucm[0:ncc, gq * 16:(gq + 1) * 16, j, :], ps[pi][0:ncc, 0:256].rearrange("c (g h) -> c g h", g=16), r=[pk(pi)], w=["ucm"])
            P.capture_begin()
            rot_ovr[0] = {"A": [0, 1, 4], "B": [2, 3, 5], "C": [4], "M": [5]}
            ws_ovr[0] = [0, 1, 2]
            s_q = wl_in(512, 256)
            for m in range(2):
                pi = psum("A")
                for k in range(8):
                    add("pe", call("matmul", ps[pi][:, 0:n], lhsT=vin(s_q, 256)[:, k, m * 128:(m + 1) * 128],
                                                                   rhs=xn[:, k, c0:c0 + n], start=(k == 0), stop=(k == 7)),
                        r=["ws%d" % s_q, "xn%d" % k], w=[pk(pi)], x=True)
                copy_op(evac_eng(), qT[:, m, 0:n], ps[pi][:, 0:n], r=[pk(pi)], w=["qT%d" % m])
            s_k = wl_in(768, 256)
            for m in range(2):
                pi = psum("A")
                for k in range(8):
                    add("pe", call("matmul", ps[pi][:, 0:n], lhsT=vin(s_k, 256)[:, k, m * 128:(m + 1) * 128],
                                                                   rhs=xn[:, k, c0:c0 + n], start=(k == 0), stop=(k == 7)),
                        r=["ws%d" % s_k, "xn%d" % k], w=[pk(pi)], x=True)
                copy_op(evac_eng(), kT[:, m, 0:n], ps[pi][:, 0:n], r=[pk(pi)], w=["kT%d" % m])
            for tt in range(ntile):
                pi = psum("B")
                for k in range(8):
                    add("pe", call("matmul", ps[pi][0:ntok, 0:256], lhsT=xn[:, k, c0 + tt * ntok:c0 + (tt + 1) * ntok],
                                                                     rhs=vin(s_k, 256)[:, k, :], start=(k == 0), stop=(k == 7)),
                        r=["ws%d" % s_k, "xn%d" % k], w=[pk(pi)], x=True)
                copy_op(evac_eng(), k_tm[0:ntok, tt, :], ps[pi][0:ntok, 0:256], r=[pk(pi)], w=["k_tm%d" % tt])
            for gq in range(2):
                s_ = wl_in(1024 + gq * 256, 256)
                for tt in range(ntile):
                    pi = psum("B")
                    for k in range(8):
                        add("pe", call("matmul", ps[pi][0:ntok, 0:256], lhsT=xn[:, k, c0 + tt * ntok:c0 + (tt + 1) * ntok],
                                                                                rhs=vin(s_, 256)[:, k, :], start=(k == 0), stop=(k == 7)),
                            r=["ws%d" % s_, "xn%d" % k], w=[pk(pi)], x=True)
                    copy_op(evac_eng(), v_tm[0:ntok, tt, gq * 256:(gq + 1) * 256], ps[pi][0:ntok, 0:256], r=[pk(pi)], w=["v_tm%d_%d" % (tt, gq)])
            for gq in range(2):
                s_ = wl_in(1536 + gq * 256, 256)
                for mm in range(2):
                    m = gq * 2 + mm
                    pi = psum("A")
                    for k in range(8):
                        add("pe", call("matmul", ps[pi][:, 0:n], lhsT=vin(s_, 256)[:, k, mm * 128:(mm + 1) * 128],
                                                                                rhs=xn[:, k, c0:c0 + n], start=(k == 0), stop=(k == 7)),
                            r=["ws%d" % s_, "xn%d" % k], w=[pk(pi)], x=True)
                    add("act", call("activation", out=sgT[:, m, 0:n], in_=ps[pi][:, 0:n], func=AF.Silu),
                        r=[pk(pi)], w=["sgT%d" % m], x=True)
            s_a = wl_in(2048, 16)
            pi = psum("A")
            for k in range(8):
                add("pe", call("matmul", ps[pi][0:16, 0:n], lhsT=vin(s_a, 16)[:, k, :], rhs=xn[:, k, c0:c0 + n],
                                                          start=(k == 0), stop=(k == 7)),
                    r=["ws%d" % s_a, "xn%d" % k], w=[pk(pi)], x=True)
            copy_op(evac_eng(), aT[0:16, 0:n], ps[pi][0:16, 0:n], r=[pk(pi)], w=["aT"])
            for tt in range(ntile):
                pi = psum("B")
                add("pe", call("matmul", ps[pi][0:ntok, 0:256], lhsT=aT[0:16, tt * ntok:(tt + 1) * ntok], rhs=gatew_b[:, :], start=True, stop=True),
                    r=["aT", "gatew_b"], w=[pk(pi)], x=True)
                add("act", call("activation", out=etmp[0:ntok, :], in_=ps[pi][0:ntok, 0:256], func=AF.Exp, scale=-1.0),
                    r=[pk(pi)], w=["etmp"], x=True)
                add("dve", call("tensor_tensor", out=etmp[0:ntok, :], in0=etmp[0:ntok, :], in1=gateb[0:ntok, :], op=ALU.mult),
                    r=["etmp", "gateb"], w=["etmp"], x=True)
                add("act", call("activation", out=lf[0:ntok, tt, :], in_=etmp[0:ntok, :], func=AF.Ln, bias=1.0),
                    r=["etmp"], w=["lf%d" % tt], x=True)

            rot_ovr[0] = {"A": [0], "B": [1], "C": [2, 3], "U": [4, 5], "M": [1]}
            pOs, pUs = {}, {}

            def gla_part1(tt):
                bi = tt % 2
                eb = ebb[bi]
                ebk = "eb%d" % bi
                tc0 = tt * ntok
                pC = psum("A")
                for m in range(2):
                    add("pe", call("matmul", ps[pC][:, m * 128:m * 128 + ntok], lhsT=lf[0:ntok, tt, m * 128:(m + 1) * 128],
                                   rhs=TRI[0:ntok, 0:ntok], start=True, stop=True), r=["lf%d" % tt, "const"], w=[pk(pC)], x=True)
                pR = psum("B")
                add("pe", call("matmul", ps[pR][0:ntok, 0:256], lhsT=TRIU[0:ntok, 0:ntok], rhs=lf[0:ntok, tt, :], start=True, stop=True),
                    r=["lf%d" % tt, "const"], w=[pk(pR)], x=True)
                pCv = ps[pC][:, 0:256].rearrange("p (m t) -> p m t", m=2)[:, :, 0:ntok]
                add("act", call("activation", out=eb[:, :, 0:ntok], in_=pCv, func=AF.Exp, scale=-1.0 / 16.0), r=[pk(pC)], w=[ebk], x=True)
                add("act", call("activation", out=enb[bi][:, :, 0:ntok], in_=pCv, func=AF.Exp, scale=1.0 / 16.0), r=[pk(pC)], w=["enb%d" % bi], x=True)
                add("act", call("activation", out=erem[bi][0:ntok, :], in_=ps[pR][0:ntok, 0:256], func=AF.Exp, scale=-1.0 / 16.0),
                    r=[pk(pR)], w=["erem%d" % bi], x=True)
                add("dve", call("scalar_tensor_tensor", out=qs[bi][:, :, 0:ntok], in0=qT[:, :, tc0:tc0 + ntok], scalar=0.125, in1=eb[:, :, 0:ntok],
                                op0=ALU.mult, op1=ALU.mult), r=["qT0", "qT1", ebk], w=["qs%d" % bi], x=True)
                add("dve", call("tensor_tensor", out=ki[bi][:, :, 0:ntok], in0=kT[:, :, tc0:tc0 + ntok], in1=enb[bi][:, :, 0:ntok], op=ALU.mult),
                    r=["kT0", "kT1", "enb%d" % bi], w=["ki%d" % bi], x=True)
                add("dve", call("tensor_tensor", out=kend[bi][0:ntok, :], in0=k_tm[0:ntok, tt, :], in1=erem[bi][0:ntok, :], op=ALU.mult),
                    r=["k_tm%d" % tt, "erem%d" % bi], w=["kend%d" % bi], x=True)
                pA = psum("A")
                for h in range(4):
                    m = h // 2
                    add("dve", call("tensor_scalar", out=kiz[bi][:, h, 0:ntok], in0=ki[bi][:, m, 0:ntok], scalar1=sel64[:, (h % 2):(h % 2) + 1], scalar2=None, op0=ALU.mult),
                        r=["ki%d" % bi, "const"], w=["kiz%d" % bi], x=True)
                for h in range(4):
                    m = h // 2
                    add("pe", call("matmul", ps[pA][0:ntok, h * 128:h * 128 + ntok], lhsT=kiz[bi][:, h, 0:ntok], rhs=qs[bi][:, m, 0:ntok], start=True, stop=True),
                        r=["kiz%d" % bi, "qs%d" % bi], w=[pk(pA)], x=True)
                for h in range(4):
                    add("dve", call("tensor_tensor", out=attT[bi][0:ntok, h, 0:ntok], in0=ps[pA][0:ntok, h * 128:h * 128 + ntok], in1=TRI[0:ntok, 0:ntok], op=ALU.mult),
                        r=[pk(pA), "const"], w=["attT%d" % bi], x=True)
                pO = psum("C")
                pOs[tt] = pO
                for h in range(4):
                    add("pe", call("matmul", ps[pO][:, h * 128:h * 128 + ntok], lhsT=v_tm[0:ntok, tt, h * 128:(h + 1) * 128],
                                   rhs=attT[bi][0:ntok, h, 0:ntok], start=(h == 0), stop=False, skip_group_check=True),
                        r=["v_tm%d_%d" % (tt, h // 2), "attT%d" % bi], w=[pk(pO)], x=True)
                if not sample:
                    pU = psum("U")
                    pUs[tt] = pU
                    for c in range(nch):
                        km = kendm[bi * 2 + c]
                        kmk = "kendm%d" % (bi * 2 + c)
                        add("dve", call("tensor_scalar", out=km[0:ntok, :], in0=kend[bi][0:ntok, :], scalar1=SEL[0:ntok, c:c + 1], scalar2=None, op0=ALU.mult),
                            r=["kend%d" % bi, "const"], w=[kmk], x=True)
                        for h in range(4):
                            m, po = h // 2, 64 * (h % 2)
                            add("pe", call("matmul", ps[pU][po:po + 64, c * 256 + m * 128:c * 256 + (m + 1) * 128], lhsT=km[0:ntok, h * 64:(h + 1) * 64],
                                           rhs=v_tm[0:ntok, tt, h * 128:(h + 1) * 128], start=True, stop=True),
                                r=[kmk, "v_tm%d_%d" % (tt, h // 2)], w=[pk(pU)], x=True)

            def gla_part2(tt):
                bi = tt % 2
                eb = ebb[bi]
                ebk = "eb%d" % bi
                tc0 = tt * ntok
                pO = pOs[tt]
                for c in range(nch):
                    cb = c % 2
                    if sample:
                        for m in range(2):
                            dma("sp", S0b[cb][:, m, :], gla_in[c, 2 * m:2 * m + 2, :, :].rearrange("h d e -> (h d) e"),
                                w=["S0b%d" % cb], semkey="S0b%d" % cb, x=True)
                        Ssrc, Skey = S0b[cb], "S0b%d" % cb
                    else:
                        Ssrc, Skey = S_f, "S_f"
                    Szc = Sz[cb]
                    Szk = "Sz%d" % cb
                    for h in range(4):
                        m = h // 2
                        add("act", call("activation", out=Szc[:, h, :], in_=Ssrc[:, m, :], func=AF.Identity, scale=sel64[:, (h % 2):(h % 2) + 1]),
                            r=[Skey, "const"], w=[Szk], x=True)
                    for h in range(4):
                        m = h // 2
                        last = (c == nch - 1) and (h == 3)
                        add("pe", call("matmul", ps[pO][:, h * 128 + c * L:h * 128 + (c + 1) * L], lhsT=Szc[:, h, :],
                                       rhs=qs[bi][:, m, c * L:(c + 1) * L], start=False, stop=last, skip_group_check=True),
                            r=[Szk, "qs%d" % bi], w=[pk(pO)], x=True)
                    if sample:
                        km = kendm[cb]
                        kmk = "kendm%d" % cb
                        add("dve", call("tensor_scalar", out=km[0:ntok, :], in0=kend[bi][0:ntok, :], scalar1=SEL[0:ntok, c:c + 1], scalar2=None, op0=ALU.mult),
                            r=["kend%d" % bi, "const"], w=[kmk], x=True)
                        pU = psum("U")
                        ucol = 0
                        for h in range(4):
                            m, po = h // 2, 64 * (h % 2)
                            add("pe", call("matmul", ps[pU][po:po + 64, m * 128:(m + 1) * 128], lhsT=km[0:ntok, h * 64:(h + 1) * 64],
                                           rhs=v_tm[0:ntok, tt, h * 128:(h + 1) * 128], start=True, stop=True),
                                r=[kmk, "v_tm%d_%d" % (tt, h // 2)], w=[pk(pU)], x=True)
                        Sdst, Sdk = Sob[cb], "Sob%d" % cb
                    else:
                        pU = pUs[tt]
                        ucol = c * 256
                        Sdst, Sdk = S_f, "S_f"
                    col_last = c * L + L - 1
                    for m in range(2):
                        add("dve", call("scalar_tensor_tensor", out=Sdst[:, m, :], in0=Ssrc[:, m, :], scalar=eb[:, m, col_last:col_last + 1],
                                        in1=ps[pU][:, ucol + m * 128:ucol + (m + 1) * 128], op0=ALU.mult, op1=ALU.add),
                            r=[Skey, ebk, pk(pU)], w=[Sdk], x=True)
                    if sample:
                        for m in range(2):
                            dma("sp", sgla_o[c, 2 * m:2 * m + 2, :, :].rearrange("h d e -> (h d) e"), Sob[cb][:, m, :],
                                r=["Sob%d" % cb], semkey="out", x=True)
                pOv = ps[pO][:, :].rearrange("p (h t) -> p h t", h=4)[:, :, 0:ntok]
                o_v = o_sb[:, 0:4 * ntok].rearrange("p (h t) -> p h t", h=4)
                add("act", call("copy", out=o_v, in_=pOv), r=[pk(pO)], w=["o_sb"], x=True)
                add("act", call("activation", out=osq[:, 0:4 * ntok], in_=o_sb[:, 0:4 * ntok], func=AF.Square), r=["o_sb"], w=["osq"], x=True)
                pN = psum("M")
                add("pe", call("matmul", ps[pN][:, 0:4 * ntok], lhsT=ones_b[:], rhs=osq[:, 0:4 * ntok], start=True, stop=True),
                    r=["osq", "ones_b"], w=[pk(pN)], x=True)
                add("act", call("activation", out=sdm[:, 0:4 * ntok], in_=ps[pN][:, 0:4 * ntok], func=AF.Sqrt, bias=EPS, scale=1.0 / 128.0),
                    r=[pk(pN)], w=["sdm"], x=True)
                add("dve", call("reciprocal", out=ps[pN][:, 0:4 * ntok], in_=sdm[:, 0:4 * ntok]), r=["sdm"], w=[pk(pN)], x=True)
                add("dve", call("scalar_tensor_tensor", out=t1b[:, 0:4 * ntok], in0=o_sb[:, 0:4 * ntok], scalar=gnorm[:, 0:1], in1=ps[pN][:, 0:4 * ntok],
                                op0=ALU.mult, op1=ALU.mult), r=["o_sb", "gnorm", pk(pN)], w=["t1b"], x=True)
                t1v = t1b[:, 0:4 * ntok].rearrange("p (h t) -> p h t", h=4)
                add("dve", call("tensor_tensor", out=cat[:, 4:8, tc0:tc0 + ntok], in0=t1v, in1=sgT[:, :, tc0:tc0 + ntok], op=ALU.mult),
                    r=["t1b", "sgT0", "sgT1", "sgT2", "sgT3"], w=["cat_o%d" % tt], x=True)

            gla_part1(0)
            for tt in range(1, ntile):
                gla_part1(tt)
                gla_part2(tt - 1)
            gla_part2(ntile - 1)
            if (not sample) and pass_idx == npass - 1:
                for m in range(2):
                    dma("sp", pgla_o[2 * m:2 * m + 2, :, :].rearrange("h d e -> (h d) e"), S_f[:, m, :], r=["S_f"], semkey="out")

            listA = P.capture_end()
            P.capture_begin()
            rot_ovr[0] = {"A": [6], "B": [7], "C": [6, 7], "M": [7]}
            ws_ovr[0] = [3]
            for gq in range(4):
                pi = psum("A")
                pbf = ps[pi][:].bitcast(BF16)
                for gg in range(8):
                    g = gq * 8 + gg
                    add("pe", call("transpose", out=pbf[:, gg * 64:gg * 64 + ncc], in_=ucm[0:ncc, g, :, :].rearrange("c j h -> c (j h)"),
                                                                          identity=ident_b[0:ncc, 0:ncc]),
                        r=["ucm", "ident_b"], w=[pk(pi)], x=True)
                src = pbf[:, 0:512].rearrange("p (g c) -> p g c", g=8)[:, :, 0:ncc]
                copy_op(evac_eng(), Ug[:, gq * 8:(gq + 1) * 8, 0:ncc], src, r=[pk(pi)], w=["Ug"])
            Bdst = Ssm if sample else Bst
            Bdk = "Ssm" if sample else "Bst"
            for q in range(4):
                pi = psum("B")
                for hfp in range(2):
                    for g4 in range(4):
                        gl = q * 4 + g4
                        g = hfp * 16 + gl
                        for ri in range(2):
                            col = (ri * 4 + g4) * 64
                            add("pe", call("matmul",
                                ps[pi][hfp * 64:hfp * 64 + 64, col:col + ncc], lhsT=W1[:, g, ri * 64:(ri + 1) * 64], rhs=Ug[:, g, 0:ncc],
                                start=True, stop=True), r=["W1", "Ug"], w=[pk(pi)], x=True)
                src = ps[pi][:, :].rearrange("p (r g c) -> p r g c", r=2, g=4)[:, :, :, 0:ncc]
                if sample:
                    dst = Ssm[:, :, q * 4:(q + 1) * 4, 0:ncc]
                else:
                    dst = Bst[:, :, q * 4:(q + 1) * 4, 1:1 + ncc]
                copy_op(evac_eng(), dst, src, r=[pk(pi)], w=[Bdk])
            if sample:
                pcs = 0
                for ri, src_d in enumerate((s5re_in, s5im_in)):
                    srcv = src_d.rearrange("b (a g) p -> b g a p", a=2)
                    pi = psum("A")
                    for q4 in range(4):
                        bb = pcs % 2
                        pcs += 1
                        for g4 in range(4):
                            dma("sp", h0t[bb][g4 * 16:(g4 + 1) * 16, :, :], srcv[:, q4 * 4 + g4, :, :], w=["h0t%d" % bb], semkey="h0t%d" % bb, x=True)
                        add("pe", call("transpose", out=ps[pi][:, q4 * 64:(q4 + 1) * 64], in_=h0t[bb][0:64, :, :].rearrange("r a p -> r (a p)"),
                                       identity=ident_f[0:64, 0:64]), r=["h0t%d" % bb, "ident_f"], w=[pk(pi)], x=True)
                    copy_op("dve", Bsm[:, ri, :, :], ps[pi][:, 0:256].rearrange("p (g b) -> p g b", g=16), r=[pk(pi)], w=["Bsm"])
                bcb = lambda a: a.unsqueeze(3).to_broadcast([128, 2, 16, 16])
                sw = lambda t: (t[:, 1, :, :], t[:, 0, :, :])
                T0, T1 = sm_t

                def cmul(dst, dk, src, sk, cr_, ci_neg_pos, ck):
                    add("dve", call("tensor_tensor", out=dst[:, :, :, :], in0=src[:, :, :, :], in1=bcb(cr_[:, :, :]), op=ALU.mult), r=[sk, ck], w=[dk], x=True)
                    add("dve", call("tensor_tensor", out=T1[:, 0, :, :], in0=src[:, 1, :, :], in1=ci_neg_pos[:, 0, :].unsqueeze(2).to_broadcast([128, 16, 16]), op=ALU.mult), r=[sk, ck], w=["smT1"], x=True)
                    add("dve", call("tensor_tensor", out=T1[:, 1, :, :], in0=src[:, 0, :, :], in1=ci_neg_pos[:, 1, :].unsqueeze(2).to_broadcast([128, 16, 16]), op=ALU.mult), r=[sk, ck], w=["smT1"], x=True)
                    add("dve", call("tensor_tensor", out=dst[:, :, :, :], in0=dst[:, :, :, :], in1=T1[:, :, :, :], op=ALU.add), r=[dk, "smT1"], w=[dk], x=True)

                cmul(T0, "smT0", Bsm, "Bsm", A1, A2, "A1A2")
                add("dve", call("tensor_tensor", out=T0[:, :, :, :], in0=T0[:, :, :, :], in1=Ssm[:, :, :, :], op=ALU.add), r=["smT0", "Ssm"], w=["smT0"], x=True)
                add("dve", call("tensor_copy", out=sc_t1[:, 0, :], in_=L4[:, 0, :]), r=["L4"], w=["sc_t1"], x=True)
                add("dve", call("tensor_copy", out=sc_t1[:, 1, :], in_=L4[:, 0, :]), r=["L4"], w=["sc_t1"], x=True)
                add("dve", call("tensor_scalar", out=sc_t2[:, 0, :], in0=L4[:, 1, :], scalar1=-1.0, scalar2=None, op0=ALU.mult), r=["L4"], w=["sc_t2"], x=True)
                add("dve", call("tensor_copy", out=sc_t2[:, 1, :], in_=L4[:, 1, :]), r=["L4"], w=["sc_t2"], x=True)
                cmul(Hs3, "Hs3", T0, "smT0", sc_t1, sc_t2, "sc_t2")
                pcs = 0
                for ri, dst_d in enumerate((sre_o, sim_o)):
                    dstv = dst_d.rearrange("b (a g) p -> b g a p", a=2)
                    for q4 in range(4):
                        bb = pcs % 2
                        pcs += 1
                        pi = psum("B")
                        add("pe", call("transpose", out=ps[pi][0:64, 0:128], in_=Hs3[:, ri, q4 * 4:(q4 + 1) * 4, :].rearrange("p g b -> p (g b)"), identity=ident_f[:]),
                            r=["Hs3", "ident_f"], w=[pk(pi)], x=True)
                        copy_op(evac_eng(), hot[bb][0:64, :, :], ps[pi][0:64, 0:128].rearrange("r (a p) -> r a p", a=2), r=[pk(pi)], w=["hot%d" % bb])
                        for g4 in range(4):
                            dma("sp", dstv[:, q4 * 4 + g4, :, :], hot[bb][g4 * 16:(g4 + 1) * 16, :, :], r=["hot%d" % bb], semkey="out", x=True)
            else:
                nsb = ncc // 8
                bc8 = lambda a: a.unsqueeze(3).to_broadcast([128, 2, 16, nsb])
                bc8h = lambda a: a.unsqueeze(2).to_broadcast([128, 16, nsb])
                for j in range(1, 8):
                    src = Bst[:, :, :, j:j + 8 * (nsb - 1) + 1:8]
                    dst = Bst[:, :, :, j + 1:j + 1 + 8 * (nsb - 1) + 1:8]
                    add("dve", call("tensor_tensor", out=sct1[:, :, :, 0:nsb], in0=src, in1=bc8(A1[:, :, :]), op=ALU.mult), r=["Bst", "A1A2"], w=["sct1"], x=True)
                    add("dve", call("tensor_tensor", out=sct2[:, 0, :, 0:nsb], in0=Bst[:, 1, :, j:j + 8 * (nsb - 1) + 1:8], in1=bc8h(A2[:, 0, :]), op=ALU.mult), r=["Bst", "A1A2"], w=["sct2"], x=True)
                    add("dve", call("tensor_tensor", out=sct2[:, 1, :, 0:nsb], in0=Bst[:, 0, :, j:j + 8 * (nsb - 1) + 1:8], in1=bc8h(A2[:, 1, :]), op=ALU.mult), r=["Bst", "A1A2"], w=["sct2"], x=True)
                    add("dve", call("tensor_tensor", out=dst, in0=dst, in1=sct1[:, :, :, 0:nsb], op=ALU.add), r=["Bst", "sct1"], w=["Bst"], x=True)
                    add("dve", call("tensor_tensor", out=dst, in0=dst, in1=sct2[:, :, :, 0:nsb], op=ALU.add), r=["Bst", "sct2"], w=["Bst"], x=True)
                for sbk in range(nsb):
                    car = Bst[:, :, :, 8 * sbk]
                    dst = Bst[:, :, :, 8 * sbk + 1:8 * sbk + 9]
                    add("dve", call("tensor_tensor", out=sct1[:, :, :, 0:8], in0=PA1[:, :, :, :], in1=car.unsqueeze(3).to_broadcast([128, 2, 16, 8]), op=ALU.mult), r=["Bst", "PA"], w=["sct1"], x=True)
                    add("dve", call("tensor_tensor", out=sct2[:, 0, :, 0:8], in0=PA2[:, 0, :, :], in1=Bst[:, 1, :, 8 * sbk].unsqueeze(2).to_broadcast([128, 16, 8]), op=ALU.mult), r=["Bst", "PA"], w=["sct2"], x=True)
                    add("dve", call("tensor_tensor", out=sct2[:, 1, :, 0:8], in0=PA2[:, 1, :, :], in1=Bst[:, 0, :, 8 * sbk].unsqueeze(2).to_broadcast([128, 16, 8]), op=ALU.mult), r=["Bst", "PA"], w=["sct2"], x=True)
                    add("dve", call("tensor_tensor", out=dst, in0=dst, in1=sct1[:, :, :, 0:8], op=ALU.add), r=["Bst", "sct1"], w=["Bst"], x=True)
                    add("dve", call("tensor_tensor", out=dst, in0=dst, in1=sct2[:, :, :, 0:8], op=ALU.add), r=["Bst", "sct2"], w=["Bst"], x=True)
            for hz in range(2):
                hsrc, hkey = (Bsm[:, :, :, 0:ncc], "Bsm") if sample else (Bst[:, :, :, 0:ncc], "Bst")
                add("dve", call("tensor_scalar", out=Hbz[hz][:, :, :, 0:ncc], in0=hsrc, scalar1=sel64[:, hz:hz + 1], scalar2=None, op0=ALU.mult),
                    r=[hkey, "const"], w=["Hbz"], x=True)
            for gq in range(8):
                pi = psum("C")
                for g4 in range(4):
                    g = gq * 4 + g4
                    hfp, gl = g // 16, g % 16
                    pr = slice(hfp * 64, hfp * 64 + 64)
                    osl = ps[pi][0:ncc, g4 * 128:(g4 + 1) * 128]
                    add("pe", call("matmul", osl, lhsT=Hbz[hfp][:, 0, gl, 0:ncc], rhs=W3[:, gl, 0, :], start=True, stop=False),
                        r=["Hbz", "W3"], w=[pk(pi)], x=True)
                    add("pe", call("matmul", osl, lhsT=Hbz[hfp][:, 1, gl, 0:ncc], rhs=W3[:, gl, 1, :], start=False, stop=False),
                        r=["Hbz", "W3"], w=[pk(pi)], x=True)
                    add("pe", call("matmul", osl, lhsT=Ug[:, g, 0:ncc], rhs=W4[:, g, :], start=False, stop=True),
                        r=["Ug", "W4"], w=[pk(pi)], x=True)
                src = ps[pi][0:ncc, :].rearrange("c (g j h) -> c j g h", g=4, j=8)
                dst = zcm[0:ncc, :, gq * 64:(gq + 1) * 64].rearrange("c j (g h) -> c j g h", g=4)
                add("act", call("activation", out=dst, in_=src, func=AF.Gelu_apprx_tanh), r=[pk(pi), "Ug"], w=["ucm"], x=True)
            if (not sample) and pass_idx == npass - 1:
                for hfp in range(2):
                    pr = slice(hfp * 64, hfp * 64 + 64)
                    gs = slice(hfp * 16, hfp * 16 + 16)
                    dma("sp", pre_o[gs, :].rearrange("g p -> p g"), Bst[pr, 0, :, ncc], r=["Bst"], semkey="out", slow=True)
                    dma("sp", pim_o[gs, :].rearrange("g p -> p g"), Bst[pr, 1, :, ncc], r=["Bst"], semkey="out", slow=True)
            if not sample:
                add("dve", call("tensor_copy", out=Bst[:, :, :, 0], in_=Bst[:, :, :, ncc]), r=["Bst"], w=["Bst"], x=True)
            for m in range(4):
                pi = psum("A")
                pbf = ps[pi][:].bitcast(BF16)
                for j in range(J):
                    add("pe", call("transpose", out=pbf[:, j * 64:j * 64 + ncc], in_=zcm[0:ncc, j, m * 128:(m + 1) * 128],
                                                                        identity=ident_b[0:ncc, 0:ncc]),
                        r=["ucm", "ident_b"], w=[pk(pi)], x=True)
                src = pbf[:, 0:J * 64].rearrange("p (j c) -> p j c", j=J)[:, :, 0:ncc]
                dst = zT[:, m, 0:n].rearrange("p (c j) -> p j c", j=J)
                copy_op(evac_eng(), dst, src, r=[pk(pi)], w=["zT%d" % m])
            s_g = wload(lambda t: t[:].rearrange("p (k f) -> p k f", k=4), glu_w_d.rearrange("(k p) f -> p k f", p=128))
            gv = wslot[s_g][:].rearrange("p (k f) -> p k f", k=4)
            for m in range(4):
                pi = psum("A")
                for k in range(4):
                    add("pe", call("matmul", ps[pi][:, 0:n], lhsT=gv[:, k, m * 128:(m + 1) * 128], rhs=zT[:, k, 0:n], start=(k == 0), stop=(k == 3)),
                        r=["ws%d" % s_g, "zT%d" % k], w=[pk(pi)], x=True)
                b = m % 2
                add("act", call("activation", out=sig[b][:, 0:n], in_=ps[pi][:, 0:n], func=AF.Sigmoid, bias=glub[:, m:m + 1]),
                    r=[pk(pi), "glub"], w=["sig"], x=True)
                add("dve", call("tensor_tensor", out=cat[:, m, 0:n], in0=zT[:, m, 0:n], in1=sig[b][:, 0:n], op=ALU.mult),
                    r=["zT%d" % m, "sig"], w=["cat_z%d" % m], x=True)
            listB = P.capture_end()
            rot_ovr[0] = None
            ws_ovr[0] = None
            P.merge([listA, listB], spans=[MERGE_SPAN_A, 1.0])
            woutv = w_out_d.rearrange("(k p) f -> p k f", p=128)
            for dp in range(4):
                s_ = wload(v8, woutv[:, :, dp * 256:(dp + 1) * 256])
                for dd in range(2):
                    d = dp * 2 + dd
                    pi = psum("C")
                    for k in range(8):
                        add("pe", call("matmul", ps[pi][:, 0:n], lhsT=v8(wslot[s_])[:, k, dd * 128:(dd + 1) * 128], rhs=cat[:, k, 0:n],
                                                                                start=(k == 0), stop=(k == 7)),
                            r=["ws%d" % s_, ("cat_z%d" % k) if k < 4 else None] + (["cat_o%d" % t_ for t_ in range(ntile)] if k >= 4 else []), w=[pk(pi)], x=True)
                    add("dve", call("tensor_tensor", out=xT[:, d, c0:c0 + n], in0=ps[pi][:, 0:n], in1=xT[:, d, c0:c0 + n], op=ALU.add),
                        r=[pk(pi), "xT%d" % d], w=["xT%d" % d], x=True)

        P.ops
        add("dve", call("memset", scr[:, 1:2], 0.0), r=["tri64", "triu64", "tri4", "triu4", "sel64", "sel4"], w=["const"])
        side_pc = None
        if do_mixer:
            P.capture_begin()
            rot_ovr[0] = {"A": [6, 7], "B": [6, 7], "C": [6, 7], "M": [6, 7]}
            precompute()
            add("dve", call("memset", scr[:, 2:3], 0.0), r=["A1", "A2"], w=["A1A2"], x=True)
            side_pc = P.capture_end()
            rot_ovr[0] = None
            if not do_ffn:
                P.ops.extend(side_pc)
                side_pc = None
        pending_store = [None]
        for pidx in range(npass):
            chunks = [(0, PT)]
            if pidx == 0:
                chunks = [(0, (PT + NSMP) // 2), ((PT + NSMP) // 2, (PT + NSMP) // 2)]
            P.capture_begin()
            rot_ovr[0] = {"A": [0, 1], "B": [2, 3], "C": [4, 5], "M": [6], "U": [2, 3]}
            for tt in range(PT // 128):
                r0 = pidx * PT + tt * 128
                load_x(xp[r0:r0 + 128, :], 128, tt * 128)
            if pidx == 0:
                load_x(xs[:, :], NSMP, PT)
            if do_ffn:
                rmsnorm(0, chunks, xn, "xn")
            rot_ovr[0] = None
            l_load = P.capture_end()
            if pending_store[0]:
                P.merge([pending_store[0], l_load])
                pending_store[0] = None
            else:
                P.ops.extend(l_load)
            if do_ffn:
                ffn(*w_ffn[0], chunks, side=(side_pc if pidx == 0 else None))
            if do_mixer:
                barrier()
                mixer("prompt", 0, PT, pidx)
                if pidx == 0:
                    mixer("sample", PT, NSMP, pidx)
                barrier()
            if do_ffn:
                rmsnorm(2, chunks, xn, "xn")
                ffn(*w_ffn[1], chunks)
            rmsnorm(3, chunks, yT, "yT")
            P.capture_begin()
            rot_ovr[0] = {"A": [0, 1], "B": [2, 3], "C": [4, 5], "M": [7], "U": [2, 3]}
            for tt in range(PT // 128):
                r0 = pidx * PT + tt * 128
                store_out(yp[r0:r0 + 128, :], 128, tt * 128)
            if pidx == 0:
                store_out(ys[:, :], NSMP, PT)
            rot_ovr[0] = None
            pending_store[0] = P.capture_end()
        if pending_store[0]:
            P.ops.extend(pending_store[0])
        tapsrc = {"xT": (xT[:], ["xT%d" % q_ for q_ in range(8)]), "yT": (yT[:, :, :], ["yT%d" % q_ for q_ in range(8)]), "sdb": (sdb[:, :], ["sdb"]),
                  "A1": (A1[:], ["A1"]), "A2": (A2[:], ["A2"]), "L4": (L4[:], ["L4"]), "Bst": (Bst[:], ["Bst"]), "S_f": (S_f[:], ["S_f"]),
                  "qT": (qT[:, :, :], ["qT0"]), "kT": (kT[:, :, :], ["kT0"]), "lf": (lf[:, :, :], ["lf0"]), "k_tm": (k_tm[:, :, :], ["k_tm0"]),
                  "xin0": (xin[0][:, :], ["xin0"]), "xin1": (xin[1][:, :], ["xin1"]), "pc_er": (pc_er[:, :, :], ["pc_er"]), "pc_ei": (pc_ei[:, :, :], ["pc_ei"])}
        for (tname, tshape) in taps:
            src, keys = tapsrc[tname]
            dma("sp", tap_out[tname], src, r=keys, semkey="out")
        P.emit(nc, final_semkeys=["out"])
    return nc


def _consts():
    c = {}
    c["c_ident"] = np.eye(128, dtype=np.float32)
    s = np.arange(128)
    same64 = (s[:, None] // 64) == (s[None, :] // 64)
    c["c_tri64"] = (same64 & (s[:, None] <= s[None, :])).astype(np.float32)
    c["c_triu64"] = (same64 & (s[:, None] > s[None, :])).astype(np.float32)
    s4 = np.arange(64)
    same4 = (s4[:, None] // 4) == (s4[None, :] // 4)
    c["c_tri4"] = (same4 & (s4[:, None] <= s4[None, :])).astype(np.float32)
    c["c_triu4"] = (same4 & (s4[:, None] > s4[None, :])).astype(np.float32)
    c["c_sel64"] = (s[:, None] // 64 == np.arange(2)[None, :]).astype(np.float32)
    c["c_sel4"] = (s4[:, None] // 4 == np.arange(16)[None, :]).astype(np.float32)
    i_ = s // 16
    c["c_maskw4"] = (i_[:, None] <= i_[None, :]).astype(np.float32)
    c["c_evals"] = np.array(EVALS, dtype=np.float32)
    return c


_NC_CACHE = {}


def kernel(x_prompt, x_sample, state_s5_re, state_s5_im, state_gla, norm_ffn1, ffn1_gate, ffn1_up,
           ffn1_down, norm_mix, w_in, s5_lam_re, s5_lam_im, s5_log_dt, s5_b_re, s5_b_im, s5_c_re,
           s5_c_im, s5_d, s5_glu_w, s5_glu_b, gla_gate_w, gla_gate_b, gla_norm, w_out, norm_ffn2,
           ffn2_gate, ffn2_up, ffn2_down, norm_final, _npass=NPASS_FULL, _do_mixer=True, _do_ffn=True, _taps=(), _stage=9):
    f = lambda a: np.ascontiguousarray(np.asarray(a, dtype=np.float32))
    key = (_npass, _do_mixer, _do_ffn, str(_taps), _stage)
    if key not in _NC_CACHE:
        _NC_CACHE[key] = build_program(npass=_npass, do_mixer=_do_mixer, do_ffn=_do_ffn, taps=_taps, stage=_stage)
    nc = _NC_CACHE[key]
    shared = {
        "gains": f(np.stack([np.asarray(norm_ffn1)[0], np.asarray(norm_mix)[0], np.asarray(norm_ffn2)[0], np.asarray(norm_final)])),
        "ffn1_gate": f(ffn1_gate[0]), "ffn1_up": f(ffn1_up[0]), "ffn1_down": f(ffn1_down[0]),
        "ffn2_gate": f(ffn2_gate[0]), "ffn2_up": f(ffn2_up[0]), "ffn2_down": f(ffn2_down[0]),
        "w_in": f(w_in[0]), "w_out": f(w_out[0]), "glu_w": f(s5_glu_w[0]), "glu_b": f(s5_glu_b[0]),
        "gate_w": f(gla_gate_w[0]), "gate_b": f(gla_gate_b[0]), "gla_norm": f(gla_norm[0]),
        "lam_re": f(s5_lam_re[0]), "lam_im": f(s5_lam_im[0]), "log_dt": f(s5_log_dt[0]),
        "b_re": f(s5_b_re[0]), "b_im": f(s5_b_im[0]), "c_re": f(s5_c_re[0]), "c_im": f(s5_c_im[0]),
        "s5_d": f(s5_d[0]),
    }
    shared.update(_consts())
    xp_ = np.asarray(x_prompt, dtype=np.float32)
    xs_ = np.asarray(x_sample, dtype=np.float32)
    sre = np.asarray(state_s5_re, dtype=np.float32)[0]
    sim = np.asarray(state_s5_im, dtype=np.float32)[0]
    sgl = np.asarray(state_gla, dtype=np.float32)[0]
    in_maps = []
    for i in range(NCORES):
        m = dict(shared)
        m["xp"] = f(xp_[i])
        m["xs"] = f(xs_[16 * i:16 * i + 16].reshape(NSMP, D))
        m["s5re_in"] = f(sre[16 * i:16 * i + 16])
        m["s5im_in"] = f(sim[16 * i:16 * i + 16])
        m["gla_in"] = f(sgl[16 * i:16 * i + 16])
        in_maps.append(m)
    res = run_bass_kernel_spmd(nc, in_maps, core_ids=list(range(NCORES)))
    R = res.results
    y_prompt = np.stack([R[i]["yp"] for i in range(NCORES)]).astype(np.float32)
    y_sample = np.concatenate([R[i]["ys"].reshape(16, 4, D) for i in range(NCORES)]).astype(np.float32)
    p_re = np.stack([R[i]["pre"] for i in range(NCORES)])[None].astype(np.float32)
    p_im = np.stack([R[i]["pim"] for i in range(NCORES)])[None].astype(np.float32)
    p_gla = np.stack([R[i]["pgla"] for i in range(NCORES)])[None].astype(np.float32)
    s_re = np.concatenate([R[i]["sre"] for i in range(NCORES)])[None].astype(np.float32)
    s_im = np.concatenate([R[i]["sim"] for i in range(NCORES)])[None].astype(np.float32)
    s_gla = np.concatenate([R[i]["sgla"] for i in range(NCORES)])[None].astype(np.float32)
    if _taps:
        return (y_prompt, y_sample, p_re, p_im, p_gla, s_re, s_im, s_gla), {t[0]: R[0]["tap_" + t[0]] for t in _taps}
    return (y_prompt, y_sample, p_re, p_im, p_gla, s_re, s_im, s_gla)
```

```python
import contextlib
import math
import numpy as np
import concourse.bass as bass
import concourse.mybir as mybir
from concourse.bass_utils import run_bass_kernel_spmd

F32 = mybir.dt.float32
BF16 = mybir.dt.bfloat16
I32 = mybir.dt.int32
AF = mybir.ActivationFunctionType
ALU = mybir.AluOpType

ENGS = ["pe", "act", "dve", "pool", "sp"]
EPS = 1e-6
NCORES = 8
D = 1024
DFF = 2816
NFT = 22
SEQ = 2048
PT = 512
NPASS_FULL = SEQ // PT
NSMP = 64
INW = 2064
EVALS = [0, 1, 2, 3, 4, 5, 6, 7, 8,
         0, -1, -2, -3, -4, -5, -6, -7,
         7, 6, 5, 4, 3, 2, 1, 0,
         -4,
         16, 24, 32, 40, 48, 56, 64]
NEV = len(EVALS)
MERGE_SPAN_A = 0.5


class Op:
    __slots__ = ("eng", "fn", "r", "w", "dma", "semkey", "deps", "raw", "need_inc", "incval", "dmaval", "pos", "prewait")


class Prog:
    def __init__(self):
        self.ops = []

    def add(self, eng, fn, r=(), w=(), dma=False, semkey=None):
        op = Op()
        op.eng, op.fn, op.r, op.w, op.dma, op.semkey = eng, fn, list(r), list(w), dma, semkey
        op.deps, op.need_inc, op.incval, op.dmaval = [], False, 0, 0
        op.raw = set()
        op.prewait = 0
        op.pos = -1
        self.ops.append(op)
        return op

    def capture_begin(self):
        self._saved = getattr(self, "_saved", [])
        self._saved.append(self.ops)
        self.ops = []

    def capture_end(self):
        lst = self.ops
        self.ops = self._saved.pop()
        return lst

    def merge(self, lists, spans=None):
        if spans is None:
            spans = [1.0] * len(lists)
        keep = [i for i, l in enumerate(lists) if l]
        lists = [lists[i] for i in keep]
        spans = [spans[i] for i in keep]
        idx = [0] * len(lists)
        total = sum(len(l) for l in lists)
        for _ in range(total):
            best, bf = None, None
            for i, l in enumerate(lists):
                if idx[i] < len(l):
                    f = idx[i] / len(l) * spans[i]
                    if bf is None or f < bf:
                        best, bf = i, f
            self.ops.append(lists[best][idx[best]])
            idx[best] += 1

    def analyze(self):
        for i, op in enumerate(self.ops):
            op.pos = i
        last_w = {}
        rd_eng = {}
        rd_dma = {}
        for op in self.ops:
            deps = set()
            for k in op.r:
                if k in last_w:
                    deps.add(last_w[k])
                    op.raw.add(last_w[k])
            for k in op.w:
                if k in last_w:
                    deps.add(last_w[k])
                for p in rd_eng.get(k, {}).values():
                    deps.add(p)
                for p in rd_dma.get(k, ()):
                    deps.add(p)
            deps.discard(op.pos)
            op.deps = sorted(deps)
            for k in op.w:
                last_w[k] = op.pos
                rd_eng[k] = {}
                rd_dma[k] = []
            for k in op.r:
                if op.dma:
                    rd_dma.setdefault(k, []).append(op.pos)
                else:
                    rd_eng.setdefault(k, {})[op.eng] = op.pos
        for op in self.ops:
            for d in op.deps:
                a = self.ops[d]
                if a.dma:
                    continue
                if a.eng != op.eng or op.dma or a.eng != "pe":
                    a.need_inc = True
        cnt = {e: 0 for e in ENGS}
        dcnt = {}
        self.dma_hist = {}
        for op in self.ops:
            if op.dma:
                dcnt[op.semkey] = dcnt.get(op.semkey, 0) + 16
                op.dmaval = dcnt[op.semkey]
                self.dma_hist.setdefault(op.semkey, []).append((op.pos, op.dmaval))
            elif op.need_inc:
                cnt[op.eng] += 1
                op.incval = cnt[op.eng]
        self.semkeys = list(dcnt.keys())
        self.dma_total = dcnt

    def _dma_wait_val(self, semkey, pos):
        v = 0
        for p, c in self.dma_hist[semkey]:
            if p < pos:
                v = c
            else:
                break
        return v

    def emit(self, nc, final_semkeys=()):
        self.analyze()
        maxw = {}
        for op in self.ops:
            if op.dma:
                op.prewait = maxw.get(op.semkey, 0)
            for d in op.deps:
                a = self.ops[d]
                if a.dma:
                    v = self._dma_wait_val(a.semkey, op.pos)
                    if v > maxw.get(a.semkey, 0):
                        maxw[a.semkey] = v
        with contextlib.ExitStack() as st:
            esem = {e: st.enter_context(nc.semaphore("s_" + e)) for e in ENGS}
            dsem = {k: st.enter_context(nc.semaphore("d_%d" % i)) for i, k in enumerate(self.semkeys)}
            block = st.enter_context(nc.Block())
            per_eng = {e: [op for op in self.ops if op.eng == e] for e in ENGS}

            def run(ename, eobj):
                waited = {}
                for op in per_eng[ename]:
                    need = {}
                    for d in op.deps:
                        a = self.ops[d]
                        if a.dma:
                            key = ("d", a.semkey)
                            val = self._dma_wait_val(a.semkey, op.pos)
                            sem = dsem[a.semkey]
                        else:
                            if a.eng == ename and not op.dma and ename == "pe":
                                continue
                            key = ("e", a.eng)
                            val = a.incval
                            sem = esem[a.eng]
                        if val > need.get(key, (0, None))[0]:
                            need[key] = (val, sem)
                    for key, (val, sem) in need.items():
                        if waited.get(key, 0) >= val:
                            continue
                        eobj.wait_ge(sem, val)
                        waited[key] = val
                    if op.dma and op.prewait > waited.get(("d", op.semkey), 0):
                        eobj.wait_ge(dsem[op.semkey], op.prewait)
                        waited[("d", op.semkey)] = op.prewait
                    ins = op.fn(eobj)
                    if op.dma:
                        ins.then_inc(dsem[op.semkey], 16)
                    elif op.need_inc:
                        ins.then_inc(esem[ename], 1)
                if ename == "sp":
                    for k in final_semkeys:
                        if k in dsem:
                            eobj.wait_ge(dsem[k], self.dma_total[k])

            @block.tensor
            def _(e):
                run("pe", e)

            @block.scalar
            def _(e):
                run("act", e)

            @block.vector
            def _(e):
                run("dve", e)

            @block.gpsimd
            def _(e):
                run("pool", e)

            @block.sync
            def _(e):
                run("sp", e)


def build_program(npass=NPASS_FULL, do_mixer=True, do_ffn=True, taps=(), stage=9):
    nc = bass.Bass("TRN2", target_bir_lowering=False, dynamic_dma_scratch_size=4096)
    P = Prog()

    def din(name, shape):
        return nc.dram_tensor(name, list(shape), F32, kind="ExternalInput").ap()

    def dout(name, shape):
        return nc.dram_tensor(name, list(shape), F32, kind="ExternalOutput").ap()

    xp = din("xp", [SEQ, D])
    xs = din("xs", [NSMP, D])
    s5re_in = din("s5re_in", [16, 32, 64])
    s5im_in = din("s5im_in", [16, 32, 64])
    gla_in = din("gla_in", [16, 4, 64, 128])
    gains_d = din("gains", [4, D])
    w_ffn = [(din("ffn1_gate", [D, DFF]), din("ffn1_up", [D, DFF]), din("ffn1_down", [DFF, D])),
             (din("ffn2_gate", [D, DFF]), din("ffn2_up", [D, DFF]), din("ffn2_down", [DFF, D]))]
    w_in_d = din("w_in", [D, INW])
    w_out_d = din("w_out", [D, D])
    glu_w_d = din("glu_w", [512, 512])
    glu_b_d = din("glu_b", [512])
    gate_w_d = din("gate_w", [16, 256])
    gate_b_d = din("gate_b", [256])
    gla_norm_d = din("gla_norm", [128])
    lam_re_d = din("lam_re", [32, 64])
    lam_im_d = din("lam_im", [32, 64])
    log_dt_d = din("log_dt", [32])
    b_re_d = din("b_re", [32, 64, 16])
    b_im_d = din("b_im", [32, 64, 16])
    c_re_d = din("c_re", [32, 16, 64])
    c_im_d = din("c_im", [32, 16, 64])
    s5_d_d = din("s5_d", [512])
    ident_d = din("c_ident", [128, 128])
    tri64_d = din("c_tri64", [128, 128])
    triu64_d = din("c_triu64", [128, 128])
    tri4_d = din("c_tri4", [64, 64])
    triu4_d = din("c_triu4", [64, 64])
    sel64_d = din("c_sel64", [128, 2])
    sel4_d = din("c_sel4", [64, 16])
    maskw4_d = din("c_maskw4", [128, 128])
    evals_d = din("c_evals", [NEV])

    yp = dout("yp", [SEQ, D])
    ys = dout("ys", [NSMP, D])
    pre_o = dout("pre", [32, 64])
    pim_o = dout("pim", [32, 64])
    pgla_o = dout("pgla", [4, 64, 128])
    sre_o = dout("sre", [16, 32, 64])
    sim_o = dout("sim", [16, 32, 64])
    sgla_o = dout("sgla", [16, 4, 64, 128])
    tap_out = {}
    for (tname, tshape) in taps:
        tap_out[tname] = dout("tap_" + tname, tshape)

    st = contextlib.ExitStack()
    with st:
        def sb(name, shape, dt):
            return st.enter_context(nc.sbuf_tensor("sb_" + name, list(shape), dt))

        NMAX = PT + NSMP
        xT = sb("xT", [128, 8, NMAX], F32)
        xn = sb("xn", [128, 8, NMAX], BF16)
        NW = 4
        wslot = [sb("wslot%d" % i, [128, 2048], BF16) for i in range(NW)]
        W1 = sb("W1", [128, 32, 128], BF16)
        W3 = sb("W3", [128, 16, 2, 128], BF16)
        W4 = sb("W4", [128, 32, 128], BF16)
        A1 = sb("A1", [128, 2, 16], F32)
        A2 = sb("A2", [128, 2, 16], F32)
        L4 = sb("L4", [128, 2, 16], F32)
        Bst = sb("Bst", [128, 2, 16, 65], F32)
        ident_f = sb("ident_f", [128, 128], F32)
        ident_b = sb("ident_b", [128, 128], BF16)
        ones_b = sb("ones_b", [128, 128], BF16)
        tri64 = sb("tri64", [128, 128], F32)
        triu64 = sb("triu64", [128, 128], F32)
        tri4 = sb("tri4", [64, 64], F32)
        triu4 = sb("triu4", [64, 64], F32)
        sel64 = sb("sel64", [128, 2], F32)
        sel4 = sb("sel4", [64, 16], F32)
        gains = sb("gains", [128, 4, 8], F32)
        glub = sb("glub", [128, 4], F32)
        gnorm = sb("gnorm", [128, 1], F32)
        gatew = sb("gatew", [16, 256], F32)
        gatew_b = sb("gatew_b", [16, 256], BF16)
        gateb = sb("gateb", [128, 256], F32)
        S_f = sb("S_f", [128, 2, 128], F32)
        PA1 = sb("PA1", [128, 2, 16, 8], F32)
        PA2 = sb("PA2", [128, 2, 16, 8], F32)
        scr = sb("scr", [128, 8], F32)

        XW = 26368
        arena = sb("arena", [128, XW], F32)

        class Carver:
            def __init__(self, base=0):
                self.off = base

            def get(self, shape, dt):
                n = 1
                for s in shape:
                    n *= s
                words = (n * (2 if dt == BF16 else 4) + 3) // 4
                words = (words + 7) // 8 * 8
                a = arena[:, self.off:self.off + words]
                self.off += words
                assert self.off <= XW, "arena overflow %d" % self.off
                if dt == BF16:
                    a = a.bitcast(BF16)[:, 0:n]
                elif dt == I32:
                    a = a.bitcast(I32)[:, 0:n]
                else:
                    a = a[:, 0:n]
                if len(shape) == 2:
                    return a.rearrange("p (a b) -> p a b", a=shape[0])
                if len(shape) == 3:
                    return a.rearrange("p (a b c) -> p a b c", a=shape[0], b=shape[1])
                return a

        cf = Carver(0)
        hT = cf.get([NFT, NMAX], BF16)
        sgb = [cf.get([512], F32) for _ in range(2)]
        sqb = [cf.get([512], BF16) for _ in range(2)]
        sdb = cf.get([512], F32)
        xin = [cf.get([1024], F32) for _ in range(2)]
        yT = Carver(0).get([8, NMAX], F32)
        ffn_end = cf.off
        cst = Carver(ffn_end)
        xst = [cst.get([1024], F32) for _ in range(2)]
        cp = Carver(ffn_end)
        pc_lr = cp.get([16], F32)
        pc_li = cp.get([16], F32)
        pc_dt = cp.get([16], F32)
        pc_ev = cp.get([NEV], F32)
        pc_er = cp.get([16, NEV], F32)
        pc_ei = cp.get([16, NEV], F32)
        pc_t0 = cp.get([16, NEV], F32)
        pc_t1 = cp.get([16, NEV], F32)
        pc_t2 = cp.get([16, NEV], F32)
        pc_ti = cp.get([16, NEV], I32)
        pc_fr = cp.get([16], F32)
        pc_fi = cp.get([16], F32)
        pc_s0 = cp.get([16], F32)
        pc_s1 = cp.get([16], F32)
        pc_s2 = cp.get([16], F32)
        pc_pr = cp.get([16, NEV], F32)
        pc_pi = cp.get([16, NEV], F32)
        pc_br = cp.get([16, 16], F32)
        pc_bi = cp.get([16, 16], F32)
        pc_cr = cp.get([16, 16], F32)
        pc_ci = cp.get([16, 16], F32)
        pc_dcol = cp.get([32], F32)
        pc_A = cp.get([16, 128], F32)
        pc_B = cp.get([16, 128], F32)
        pc_C = cp.get([16, 128], F32)
        pc_D = cp.get([16, 128], F32)
        pc_E = cp.get([16, 128], F32)
        pc_w4t = cp.get([128], F32)
        pc_csb = [pc_E[:, 2 * i_:2 * i_ + 2, :].rearrange("p a (b c) -> p a b c", b=2) for i_ in range(2)]
        cm = Carver(0)
        ucm_flat = cm.get([4096], BF16)
        ucm = ucm_flat.rearrange("p (g j h) -> p g j h", g=32, j=8)
        zcm = ucm_flat.rearrange("p (j c) -> p j c", j=8)
        qT = cm.get([2, 512], F32)
        kT = cm.get([2, 512], F32)
        k_tm = cm.get([4, 256], F32)
        v_tm = cm.get([4, 512], BF16)
        sgT = cm.get([4, 512], BF16)
        aT = cm.get([512], BF16)
        lf = cm.get([4, 256], F32)
        Ug = cm.get([32, 64], BF16)
        cat = cm.get([8, 512], BF16)
        zT = cm.get([4, 512], BF16)
        Hbz = [cm.get([2, 16, 64], BF16) for _ in range(2)]
        ebb = [cm.get([2, 128], F32) for _ in range(2)]
        enb = [cm.get([2, 128], F32) for _ in range(2)]
        erem = [cm.get([256], F32) for _ in range(2)]
        qs = [cm.get([2, 128], BF16) for _ in range(2)]
        ki = [cm.get([2, 128], BF16) for _ in range(2)]
        kiz = [cm.get([4, 128], BF16) for _ in range(2)]
        Sz = [cm.get([4, 128], BF16) for _ in range(2)]
        kend = [cm.get([256], BF16) for _ in range(2)]
        kendm = [cm.get([256], BF16) for _ in range(4)]
        attT = [cm.get([4, 128], BF16) for _ in range(2)]
        o_sb = cm.get([512], F32)
        osq = cm.get([512], BF16)
        t1b = cm.get([512], F32)
        etmp = t1b[:, 0:256]
        sdm = cm.get([512], F32)
        sig0_ = cm.get([512], F32)
        sig = [sig0_, sig0_]
        S0b = [cm.get([2, 128], F32) for _ in range(2)]
        Sob = [cm.get([2, 128], F32) for _ in range(2)]
        sct1 = cm.get([2, 16, 8], F32)
        sct2 = cm.get([2, 16, 8], F32)
        sc_t1 = cm.get([2, 16], F32)
        sc_t2 = cm.get([2, 16], F32)
        Bsm = cm.get([2, 16, 16], F32)
        Ssm = cm.get([2, 16, 16], F32)
        Hs3 = cm.get([2, 16, 16], F32)
        sm_t = [cm.get([2, 16, 16], F32) for _ in range(2)]
        h0t = [cm.get([2, 64], F32) for _ in range(2)]
        hot = [cm.get([2, 64], F32) for _ in range(2)]

        ps = [st.enter_context(nc.psum_tensor("ps%d" % i, [128, 512], F32)) for i in range(8)]
        rot = {"A": [0, 1], "B": [2, 3], "C": [4, 5], "M": [6, 7], "U": [2, 3]}
        rot_i = {k: 0 for k in rot}

        rot_ovr = [None]
        ws_ovr = [None]

        def psum(group):
            banks = rot_ovr[0][group] if rot_ovr[0] is not None else rot[group]
            i = banks[rot_i[group] % len(banks)]
            rot_i[group] += 1
            return i

        def pk(i):
            return "ps%d" % i

        XB = "Xbar"

        def call(name, *a, **kw):
            return lambda e: getattr(e, name)(*a, **kw)

        def add(eng, fn, r=(), w=(), x=False):
            r = [k_ for k_ in r if k_ is not None]
            if x:
                r.append(XB)
            return P.add(eng, fn, r=r, w=w)

        def dma(eng, out, in_, r=(), w=(), semkey=None, x=False, slow=False):
            r = list(r)
            if x:
                r.append(XB)
            if slow:
                fn = call("dma_start", out=out, in_=in_, allow_slow_non_contiguous=True)
            else:
                fn = call("dma_start", out=out, in_=in_)
            return P.add(eng, fn, r=r, w=w, dma=True, semkey=semkey)

        def barrier():
            P.add("pool", call("memset", scr[:, 0:1], 0.0), r=[], w=[XB])

        ws_i = [0]

        ws_hist = []

        def wload(out_view_fn, in_ap):
            pool_ = ws_ovr[0] if ws_ovr[0] is not None else list(range(NW))
            s = pool_[ws_i[0] % len(pool_)]
            ws_i[0] += 1
            extra = ["ws%d" % ws_hist[-2]] if len(ws_hist) >= 2 and ws_hist[-2] != s else []
            ws_hist.append(s)
            dma("pool", out_view_fn(wslot[s]), in_ap, r=extra, w=["ws%d" % s], semkey="ws%d" % s)
            return s

        ev_i = [0]

        def evac_eng():
            ev_i[0] += 1
            return "act" if ev_i[0] % 2 == 0 else "dve"

        def copy_op(eng, out, in_, r, w, x=True):
            if eng == "act":
                add("act", call("copy", out=out, in_=in_), r=r, w=w, x=x)
            else:
                add(eng, call("tensor_copy", out=out, in_=in_), r=r, w=w, x=x)

        cload = [(ident_f[:], ident_d, "ident_f"), (tri64[:], tri64_d, "tri64"), (triu64[:], triu64_d, "triu64"),
                 (tri4[:], tri4_d, "tri4"), (triu4[:], triu4_d, "triu4"), (sel64[:], sel64_d, "sel64"),
                 (sel4[:], sel4_d, "sel4"), (gatew[:], gate_w_d, "gatew"),
                 (gateb[:], gate_b_d.partition_broadcast(128), "gateb")]
        for (o, i, k) in cload:
            dma("sp", o, i, w=[k], semkey="const")
        dma("sp", gains[:], gains_d.rearrange("n (k p) -> p n k", p=128), w=["gains"], semkey="const", slow=True)
        dma("sp", glub[:], glu_b_d.rearrange("(m p) -> p m", p=128), w=["glub"], semkey="const", slow=True)
        dma("sp", gnorm[:], gla_norm_d.rearrange("(p o) -> p o", o=1), w=["gnorm"], semkey="const", slow=True)
        add("dve", call("tensor_copy", out=ident_b[:], in_=ident_f[:]), r=["ident_f"], w=["ident_b"])
        add("dve", call("tensor_copy", out=gatew_b[:], in_=gatew[:]), r=["gatew"], w=["gatew_b"])
        add("act", call("activation", out=gateb[:], in_=gateb[:], func=AF.Exp, scale=-1.0), r=["gateb"], w=["gateb"])
        add("dve", call("memset", ones_b[:], 1.0), w=["ones_b"])
        add("dve", call("memset", S_f[:], 0.0), w=["S_f"])
        add("dve", call("memset", Bst[:], 0.0), w=["Bst"])

        xin_i = [0]

        def load_x(src_rows, ntok, col0):
            b = xin_i[0] % 2
            xin_i[0] += 1
            kx = "xin%d" % b
            dma("sp", xin[b][0:ntok, :], src_rows, w=[kx], semkey=kx, x=True)
            for half in range(2):
                pi = psum("M")
                for kk in range(4):
                    k = half * 4 + kk
                    add("pe", call("transpose",
                        out=ps[pi][:, kk * 128:kk * 128 + ntok], in_=xin[b][0:ntok, k * 128:(k + 1) * 128],
                        identity=ident_f[0:ntok, 0:ntok]), r=[kx, "ident_f"], w=[pk(pi)], x=True)
                src = ps[pi][:].rearrange("p (a b) -> p a b", a=4)[:, :, 0:ntok]
                dst = xT[:, half * 4:half * 4 + 4, col0:col0 + ntok]
                copy_op(evac_eng(), dst, src, r=[pk(pi)], w=["xT%d" % (half * 4 + q_) for q_ in range(4)], x=True)

        def rmsnorm(gi, chunks, dst, dst_key):
            for (c0, n) in chunks:
                pi = psum("M")
                for k in range(8):
                    b = k % 2
                    add("act", call("activation", out=sqb[b][:, 0:n], in_=xT[:, k, c0:c0 + n],
                                                               func=AF.Square),
                        r=["xT%d" % k], w=["sq%d" % b], x=True)
                    add("pe", call("matmul", ps[pi][:, 0:n], lhsT=ones_b[:], rhs=sqb[b][:, 0:n],
                                                                 start=(k == 0), stop=(k == 7)),
                        r=["sq%d" % b, "ones_b"], w=[pk(pi)], x=True)
                add("act", call("activation", out=sdb[:, 0:n], in_=ps[pi][:, 0:n], func=AF.Sqrt,
                                                        bias=EPS, scale=1.0 / D),
                    r=[pk(pi)], w=["sdb"], x=True)
                add("dve", call("reciprocal", out=ps[pi][:, 0:n], in_=sdb[:, 0:n]),
                    r=["sdb"], w=[pk(pi)], x=True)
                for k in range(8):
                    add("dve", call("scalar_tensor_tensor",
                        out=dst[:, k, c0:c0 + n], in0=xT[:, k, c0:c0 + n], scalar=gains[:, gi, k:k + 1],
                        in1=ps[pi][:, 0:n], op0=ALU.mult, op1=ALU.mult),
                        r=["xT%d" % k, "gains", pk(pi)], w=[dst_key + str(k)] + (["hT%d" % f_ for f_ in range(NFT)] if dst_key == "yT" else []), x=True)

        def ffn(wg, wu, wd, chunks, side=None):
            nside = [0]
            npts = NFT // 2 + 8

            side_dma = [o for o in side if o.dma] if side else []
            side_cmp = [o for o in side if not o.dma] if side else []
            if side_dma:
                P.ops.extend(side_dma)
            first_pt = npts // 2 + 2

            def inject(ip):
                if side_cmp and ip >= first_pt:
                    hi = len(side_cmp) * (ip - first_pt + 1) // (npts - first_pt)
                    P.ops.extend(side_cmp[nside[0]:hi])
                    nside[0] = hi
            wgv = wg.rearrange("(k p) f -> p k f", p=128)
            wuv = wu.rearrange("(k p) f -> p k f", p=128)
            wdv = wd.rearrange("(t p) d -> p t d", p=128)
            v8 = lambda t: t[:].rearrange("p (k f) -> p k f", k=8)
            for fp in range(NFT // 2):
                if fp > 0:
                    inject(fp - 1)
                sg_ = wload(v8, wgv[:, :, fp * 256:(fp + 1) * 256])
                su_ = wload(v8, wuv[:, :, fp * 256:(fp + 1) * 256])
                for hf in range(2):
                    f = fp * 2 + hf
                    for (c0, n) in chunks:
                        pa = psum("A")
                        pb = psum("B")
                        for (pi, s_) in ((pa, sg_), (pb, su_)):
                            for k in range(8):
                                add("pe", call("matmul",
                                    ps[pi][:, 0:n], lhsT=v8(wslot[s_])[:, k, hf * 128:(hf + 1) * 128],
                                    rhs=xn[:, k, c0:c0 + n], start=(k == 0), stop=(k == 7)),
                                    r=["ws%d" % s_, "xn%d" % k], w=[pk(pi)], x=True)
                        b = f % 2
                        add("act", call("activation", out=sgb[b][:, 0:n], in_=ps[pa][:, 0:n],
                                                                          func=AF.Silu),
                            r=[pk(pa)], w=["sg%d" % b], x=True)
                        add("dve", call("tensor_tensor",
                            out=hT[:, f, c0:c0 + n], in0=sgb[b][:, 0:n], in1=ps[pb][:, 0:n], op=ALU.mult),
                            r=["sg%d" % b, pk(pb)], w=["hT%d" % f] + (["yT%d" % q_ for q_ in range(8)] if (f == 0 and c0 == 0) else []), x=True)
            v11 = lambda t: t[:, 0:1408].rearrange("p (t c) -> p t c", t=11)
            for d in range(8):
                inject(NFT // 2 + d)
                sl = [wload(v11, wdv[:, hh * 11:(hh + 1) * 11, d * 128:(d + 1) * 128]) for hh in range(2)]
                for (c0, n) in chunks:
                    pi = psum("C")
                    for f in range(NFT):
                        s_ = sl[f // 11]
                        add("pe", call("matmul",
                            ps[pi][:, 0:n], lhsT=v11(wslot[s_])[:, f % 11, :], rhs=hT[:, f, c0:c0 + n],
                            start=(f == 0), stop=(f == NFT - 1)),
                            r=["ws%d" % s_, "hT%d" % f], w=[pk(pi)], x=True)
                    add("dve", call("scalar_tensor_tensor",
                        out=xT[:, d, c0:c0 + n], in0=ps[pi][:, 0:n], scalar=0.5, in1=xT[:, d, c0:c0 + n],
                        op0=ALU.mult, op1=ALU.add),
                        r=[pk(pi), "xT%d" % d], w=["xT%d" % d], x=True)
            if side_cmp:
                P.ops.extend(side_cmp[nside[0]:])
                nside[0] = len(side_cmp)

        xst_i = [0]

        def store_out(dst_rows, ntok, col0):
            b = xst_i[0] % 2
            xst_i[0] += 1
            kx = "xst%d" % b
            for half in range(2):
                pi = psum("M")
                for kk in range(4):
                    k = half * 4 + kk
                    add("pe", call("transpose", out=ps[pi][0:ntok, kk * 128:(kk + 1) * 128], in_=yT[:, k, col0:col0 + ntok],
                                   identity=ident_f[:]), r=["yT%d" % k, "ident_f"], w=[pk(pi)], x=True)
                copy_op(evac_eng(), xst[b][0:ntok, half * 512:(half + 1) * 512], ps[pi][0:ntok, :],
                        r=[pk(pi)], w=[kx], x=True)
            dma("sp", dst_rows, xst[b][0:ntok, :], r=[kx], semkey="out", x=True)

        def precompute():
            x_ = True
            for hfp in range(2):
                pr = slice(hfp * 64, hfp * 64 + 64)
                gs = slice(hfp * 16, hfp * 16 + 16)
                dma("sp", pc_lr[pr, :], lam_re_d[gs, :].rearrange("g p -> p g"), w=["pc_lr"], semkey="pc", x=x_, slow=True)
                dma("sp", pc_li[pr, :], lam_im_d[gs, :].rearrange("g p -> p g"), w=["pc_li"], semkey="pc", x=x_, slow=True)
                dma("sp", pc_dt[pr, :], log_dt_d[gs].partition_broadcast(64), w=["pc_dt"], semkey="pc", x=x_, slow=True)
                dma("sp", pc_br[pr, :, :], b_re_d[gs, :, :].rearrange("g p h -> p g h"), w=["pc_br"], semkey="pc", x=x_, slow=True)
                dma("sp", pc_bi[pr, :, :], b_im_d[gs, :, :].rearrange("g p h -> p g h"), w=["pc_bi"], semkey="pc", x=x_, slow=True)
            piC = psum("A")
            for t_, src_d in enumerate((c_re_d, c_im_d)):
                for glhi in range(2):
                    dma("sp", pc_csb[t_][:, glhi, :, :], src_d.rearrange("g h p -> (g h) p").rearrange("(a b q) p -> q b a p", a=2, b=2)[:, glhi, :, :],
                        w=["pc_csb%d" % t_], semkey="pc", x=x_)
            for t_ in range(2):
                for glhi in range(2):
                    add("pe", call("transpose", out=ps[piC][:, t_ * 256 + glhi * 128:t_ * 256 + (glhi + 1) * 128],
                                   in_=pc_csb[t_][:, glhi, :, :].rearrange("q a p -> q (a p)"), identity=ident_f[:]),
                        r=["pc_csb%d" % t_, "ident_f"], w=[pk(piC)], x=True)
            copy_op("dve", pc_cr[:, :, :], ps[piC][:, 0:256].rearrange("p (g h) -> p g h", g=16), r=[pk(piC)], w=["pc_cr"])
            copy_op("dve", pc_ci[:, :, :], ps[piC][:, 256:512].rearrange("p (g h) -> p g h", g=16), r=[pk(piC)], w=["pc_ci"])
            dma("sp", pc_ev[:, :], evals_d.partition_broadcast(128), w=["pc_ev"], semkey="pc", x=x_, slow=True)
            for i in range(8):
                dma("sp", pc_dcol[i * 16:(i + 1) * 16, :], s5_d_d.rearrange("(g h) -> h g", h=16), w=["pc_dcol"],
                    semkey="pc", x=x_, slow=True)
            dma("sp", pc_w4t[:, :], maskw4_d, w=["maskw4"], semkey="pc", x=x_)

            def V(eng, fn, r, w):
                add(eng, fn, r=r, w=w, x=True)

            bc_g = lambda a: a.unsqueeze(2).to_broadcast([128, 16, NEV])
            bc_e = lambda a: a.unsqueeze(1).to_broadcast([128, 16, NEV])
            V("act", call("activation", out=pc_dt[:, :], in_=pc_dt[:, :], func=AF.Exp), ["pc_dt"], ["pc_dt"])
            V("dve", call("tensor_tensor", out=pc_s0[:, :], in0=pc_lr[:, :], in1=pc_dt[:, :], op=ALU.mult), ["pc_lr", "pc_dt"], ["pc_s0"])
            V("dve", call("tensor_tensor", out=pc_s1[:, :], in0=pc_li[:, :], in1=pc_dt[:, :], op=ALU.mult), ["pc_li", "pc_dt"], ["pc_s1"])
            V("dve", call("tensor_tensor", out=pc_t0[:, :, :], in0=bc_g(pc_s0[:, :]), in1=bc_e(pc_ev[:, :]), op=ALU.mult), ["pc_s0", "pc_ev"], ["pc_t0"])
            V("dve", call("tensor_tensor", out=pc_t1[:, :, :], in0=bc_g(pc_s1[:, :]), in1=bc_e(pc_ev[:, :]), op=ALU.mult), ["pc_s1", "pc_ev"], ["pc_t1"])
            V("act", call("activation", out=pc_t0[:, :, :], in_=pc_t0[:, :, :], func=AF.Exp), ["pc_t0"], ["pc_t0"])
            C1 = 6.28125
            C2 = 2.0 * math.pi - C1
            V("dve", call("tensor_scalar", out=pc_t2[:, :, :], in0=pc_t1[:, :, :], scalar1=1.0 / (2.0 * math.pi), scalar2=None, op0=ALU.mult), ["pc_t1"], ["pc_t2"])
            V("dve", call("tensor_copy", out=pc_ti[:, :, :], in_=pc_t2[:, :, :]), ["pc_t2"], ["pc_ti"])
            V("dve", call("tensor_copy", out=pc_t2[:, :, :], in_=pc_ti[:, :, :]), ["pc_ti"], ["pc_t2"])
            V("dve", call("scalar_tensor_tensor", out=pc_t1[:, :, :], in0=pc_t2[:, :, :], scalar=-C1, in1=pc_t1[:, :, :], op0=ALU.mult, op1=ALU.add), ["pc_t2", "pc_t1"], ["pc_t1"])
            V("dve", call("scalar_tensor_tensor", out=pc_t1[:, :, :], in0=pc_t2[:, :, :], scalar=-C2, in1=pc_t1[:, :, :], op0=ALU.mult, op1=ALU.add), ["pc_t2", "pc_t1"], ["pc_t1"])
            V("act", call("activation", out=pc_t2[:, :, :], in_=pc_t1[:, :, :], func=AF.Sin, scale=0.5), ["pc_t1"], ["pc_t2"])
            V("act", call("activation", out=pc_t1[:, :, :], in_=pc_t1[:, :, :], func=AF.Sin, scale=0.25), ["pc_t1"], ["pc_t1"])
            V("dve", call("tensor_tensor", out=pc_t1[:, :, :], in0=pc_t1[:, :, :], in1=pc_t1[:, :, :], op=ALU.mult), ["pc_t1"], ["pc_t1"])
            V("dve", call("tensor_scalar", out=pc_t1[:, :, :], in0=pc_t1[:, :, :], scalar1=-2.0, scalar2=1.0, op0=ALU.mult, op1=ALU.add), ["pc_t1"], ["pc_t1"])
            V("dve", call("tensor_tensor", out=pc_t1[:, :, :], in0=pc_t1[:, :, :], in1=pc_t2[:, :, :], op=ALU.mult), ["pc_t1", "pc_t2"], ["pc_t1"])
            V("dve", call("scalar_tensor_tensor", out=pc_ei[:, :, :], in0=pc_t1[:, :, :], scalar=2.0, in1=pc_t0[:, :, :], op0=ALU.mult, op1=ALU.mult), ["pc_t1", "pc_t0"], ["pc_ei"])
            V("dve", call("tensor_tensor", out=pc_t2[:, :, :], in0=pc_t2[:, :, :], in1=pc_t2[:, :, :], op=ALU.mult), ["pc_t2"], ["pc_t2"])
            V("dve", call("tensor_scalar", out=pc_t2[:, :, :], in0=pc_t2[:, :, :], scalar1=-2.0, scalar2=1.0, op0=ALU.mult, op1=ALU.add), ["pc_t2"], ["pc_t2"])
            V("dve", call("tensor_tensor", out=pc_er[:, :, :], in0=pc_t2[:, :, :], in1=pc_t0[:, :, :], op=ALU.mult), ["pc_t2", "pc_t0"], ["pc_er"])
            abr = pc_er[:, :, 1]
            abi = pc_ei[:, :, 1]
            V("dve", call("tensor_tensor", out=pc_s0[:, :], in0=pc_lr[:, :], in1=pc_lr[:, :], op=ALU.mult), ["pc_lr"], ["pc_s0"])
            V("dve", call("tensor_tensor", out=pc_s1[:, :], in0=pc_li[:, :], in1=pc_li[:, :], op=ALU.mult), ["pc_li"], ["pc_s1"])
            V("dve", call("tensor_tensor", out=pc_s0[:, :], in0=pc_s0[:, :], in1=pc_s1[:, :], op=ALU.add), ["pc_s0", "pc_s1"], ["pc_s0"])
            V("dve", call("reciprocal", out=pc_s0[:, :], in_=pc_s0[:, :]), ["pc_s0"], ["pc_s0"])
            V("dve", call("tensor_scalar", out=pc_s1[:, :], in0=abr, scalar1=-1.0, scalar2=None, op0=ALU.add), ["pc_er"], ["pc_s1"])
            V("dve", call("tensor_tensor", out=pc_fr[:, :], in0=pc_s1[:, :], in1=pc_lr[:, :], op=ALU.mult), ["pc_s1", "pc_lr"], ["pc_fr"])
            V("dve", call("tensor_tensor", out=pc_s2[:, :], in0=abi, in1=pc_li[:, :], op=ALU.mult), ["pc_ei", "pc_li"], ["pc_s2"])
            V("dve", call("tensor_tensor", out=pc_fr[:, :], in0=pc_fr[:, :], in1=pc_s2[:, :], op=ALU.add), ["pc_fr", "pc_s2"], ["pc_fr"])
            V("dve", call("tensor_tensor", out=pc_fr[:, :], in0=pc_fr[:, :], in1=pc_s0[:, :], op=ALU.mult), ["pc_fr", "pc_s0"], ["pc_fr"])
            V("dve", call("tensor_tensor", out=pc_fi[:, :], in0=abi, in1=pc_lr[:, :], op=ALU.mult), ["pc_ei", "pc_lr"], ["pc_fi"])
            V("dve", call("tensor_tensor", out=pc_s2[:, :], in0=pc_s1[:, :], in1=pc_li[:, :], op=ALU.mult), ["pc_s1", "pc_li"], ["pc_s2"])
            V("dve", call("tensor_tensor", out=pc_fi[:, :], in0=pc_fi[:, :], in1=pc_s2[:, :], op=ALU.subtract), ["pc_fi", "pc_s2"], ["pc_fi"])
            V("dve", call("tensor_tensor", out=pc_fi[:, :], in0=pc_fi[:, :], in1=pc_s0[:, :], op=ALU.mult), ["pc_fi", "pc_s0"], ["pc_fi"])
            V("dve", call("tensor_tensor", out=pc_pr[:, :, :], in0=pc_er[:, :, :], in1=bc_g(pc_fr[:, :]), op=ALU.mult), ["pc_er", "pc_fr"], ["pc_pr"])
            V("dve", call("tensor_tensor", out=pc_t0[:, :, :], in0=pc_ei[:, :, :], in1=bc_g(pc_fi[:, :]), op=ALU.mult), ["pc_ei", "pc_fi"], ["pc_t0"])
            V("dve", call("tensor_tensor", out=pc_pr[:, :, :], in0=pc_pr[:, :, :], in1=pc_t0[:, :, :], op=ALU.subtract), ["pc_pr", "pc_t0"], ["pc_pr"])
            V("dve", call("tensor_tensor", out=pc_pi[:, :, :], in0=pc_er[:, :, :], in1=bc_g(pc_fi[:, :]), op=ALU.mult), ["pc_er", "pc_fi"], ["pc_pi"])
            V("dve", call("tensor_tensor", out=pc_t0[:, :, :], in0=pc_ei[:, :, :], in1=bc_g(pc_fr[:, :]), op=ALU.mult), ["pc_ei", "pc_fr"], ["pc_t0"])
            V("dve", call("tensor_tensor", out=pc_pi[:, :, :], in0=pc_pi[:, :, :], in1=pc_t0[:, :, :], op=ALU.add), ["pc_pi", "pc_t0"], ["pc_pi"])
            V("dve", call("tensor_copy", out=A1[:, 0, :], in_=pc_er[:, :, 8]), ["pc_er"], ["A1"])
            V("dve", call("tensor_copy", out=A1[:, 1, :], in_=pc_er[:, :, 8]), ["pc_er"], ["A1"])
            V("dve", call("tensor_scalar", out=A2[:, 0, :], in0=pc_ei[:, :, 8], scalar1=-1.0, scalar2=None, op0=ALU.mult), ["pc_ei"], ["A2"])
            V("dve", call("tensor_copy", out=A2[:, 1, :], in_=pc_ei[:, :, 8]), ["pc_ei"], ["A2"])
            V("dve", call("tensor_copy", out=L4[:, 0, :], in_=pc_er[:, :, 25]), ["pc_er"], ["L4"])
            V("dve", call("tensor_copy", out=L4[:, 1, :], in_=pc_ei[:, :, 25]), ["pc_ei"], ["L4"])
            for k_, idx_ in enumerate([8, 26, 27, 28, 29, 30, 31, 32]):
                V("dve", call("tensor_copy", out=PA1[:, 0, :, k_], in_=pc_er[:, :, idx_]), ["pc_er"], ["PA"])
                V("dve", call("tensor_copy", out=PA1[:, 1, :, k_], in_=pc_er[:, :, idx_]), ["pc_er"], ["PA"])
                V("dve", call("tensor_scalar", out=PA2[:, 0, :, k_], in0=pc_ei[:, :, idx_], scalar1=-1.0, scalar2=None, op0=ALU.mult), ["pc_ei"], ["PA"])
                V("dve", call("tensor_copy", out=PA2[:, 1, :, k_], in_=pc_ei[:, :, idx_]), ["pc_ei"], ["PA"])

            def v4(t):
                return t[:, :, :].rearrange("p g (i h) -> p g i h", i=8)

            def bt(T, e0):
                return T[:, :, e0:e0 + 8].unsqueeze(3).to_broadcast([128, 16, 8, 16])

            def bv(Vv):
                return Vv[:, :, :].unsqueeze(2).to_broadcast([128, 16, 8, 16])

            def cplx(outr, outi, Tr, Ti, e0, Vr, Vi, keys_t, keys_v, neg_im=False, only=None):
                kr, ki_ = keys_t
                vr, vi = keys_v
                if outr is not None:
                    okr = outr[1]
                    V("dve", call("tensor_tensor", out=v4(outr[0]), in0=bt(Tr, e0), in1=bv(Vr), op=ALU.mult), [kr, vr], [okr])
                    V("dve", call("tensor_tensor", out=v4(pc_E), in0=bt(Ti, e0), in1=bv(Vi), op=ALU.mult), [ki_, vi], ["pc_E"])
                    V("dve", call("tensor_tensor", out=outr[0][:, :, :], in0=outr[0][:, :, :], in1=pc_E[:, :, :], op=ALU.subtract), [okr, "pc_E"], [okr])
                if outi is not None:
                    oki = outi[1]
                    V("dve", call("tensor_tensor", out=v4(outi[0]), in0=bt(Tr, e0), in1=bv(Vi), op=ALU.mult), [kr, vi], [oki])
                    V("dve", call("tensor_tensor", out=v4(pc_E), in0=bt(Ti, e0), in1=bv(Vr), op=ALU.mult), [ki_, vr], ["pc_E"])
                    if neg_im:
                        V("dve", call("scalar_tensor_tensor", out=outi[0][:, :, :], in0=outi[0][:, :, :], scalar=-1.0, in1=pc_E[:, :, :], op0=ALU.mult, op1=ALU.subtract), [oki, "pc_E"], [oki])
                    else:
                        V("dve", call("tensor_tensor", out=outi[0][:, :, :], in0=outi[0][:, :, :], in1=pc_E[:, :, :], op=ALU.add), [oki, "pc_E"], [oki])

            cplx((pc_A, "pc_A"), (pc_B, "pc_B"), pc_er, pc_ei, 1, pc_cr, pc_ci, ("pc_er", "pc_ei"), ("pc_cr", "pc_ci"), neg_im=True)
            V("dve", call("tensor_copy", out=W3[:, :, 0, :], in_=pc_A[:, :, :]), ["pc_A"], ["W3"])
            V("dve", call("tensor_copy", out=W3[:, :, 1, :], in_=pc_B[:, :, :]), ["pc_B"], ["W3"])
            cplx((pc_A, "pc_A"), (pc_B, "pc_B"), pc_er, pc_ei, 0, pc_cr, pc_ci, ("pc_er", "pc_ei"), ("pc_cr", "pc_ci"))
            cplx((pc_C, "pc_C"), (pc_D, "pc_D"), pc_pr, pc_pi, 9, pc_br, pc_bi, ("pc_pr", "pc_pi"), ("pc_br", "pc_bi"), neg_im=True)
            for g in range(32):
                hfp, gl = g // 16, g % 16
                pr = slice(hfp * 64, hfp * 64 + 64)
                pi = psum("A")
                add("pe", call("matmul", ps[pi][:, 0:128], lhsT=pc_C[pr, gl, :], rhs=pc_A[pr, gl, :], start=True, stop=False),
                    r=["pc_C", "pc_A"], w=[pk(pi)], x=True)
                add("pe", call("matmul", ps[pi][:, 0:128], lhsT=pc_D[pr, gl, :], rhs=pc_B[pr, gl, :], start=False, stop=True),
                    r=["pc_D", "pc_B"], w=[pk(pi)], x=True)
                add("dve", call("tensor_tensor", out=ps[pi][:, 128:256], in0=ps[pi][:, 0:128], in1=pc_w4t[:, :], op=ALU.mult),
                    r=[pk(pi), "maskw4"], w=[pk(pi)], x=True)
                add("dve", call("scalar_tensor_tensor", out=W4[:, g, :], in0=ident_f[:], scalar=pc_dcol[:, g:g + 1], in1=ps[pi][:, 128:256], op0=ALU.mult, op1=ALU.add),
                    r=[pk(pi), "ident_f", "pc_dcol"], w=["W4"], x=True)
            cplx((pc_C, "pc_C"), (pc_D, "pc_D"), pc_pr, pc_pi, 17, pc_br, pc_bi, ("pc_pr", "pc_pi"), ("pc_br", "pc_bi"))
            for g in range(32):
                hfp, gl = g // 16, g % 16
                pr = slice(hfp * 64, hfp * 64 + 64)
                pi = psum("B")
                for ri, (T, tk) in enumerate(((pc_C, "pc_C"), (pc_D, "pc_D"))):
                    add("pe", call("transpose", out=ps[pi][:, ri * 64:(ri + 1) * 64], in_=T[pr, gl, :], identity=ident_f[pr, pr]),
                        r=[tk, "ident_f"], w=[pk(pi)], x=True)
                copy_op(evac_eng(), W1[:, g, :], ps[pi][:, 0:128], r=[pk(pi)], w=["W1"], x=True)

        def mixer(kind, c0, n, pass_idx):
            sample = (kind == "sample")
            ntile = 1 if sample else n // 128
            ntok = 64 if sample else 128
            L = 4 if sample else 64
            nch = ntok // L
            ncc = 16 if sample else n // 8
            J = 4 if sample else 8
            TRI = tri4 if sample else tri64
            TRIU = triu4 if sample else triu64
            SEL = sel4 if sample else sel64
            rmsnorm(1, [(c0, n)], xn, "xn")
            if stage < 1:
                return
            v8 = lambda t: t[:].rearrange("p (k f) -> p k f", k=8)
            winv = w_in_d.rearrange("(k p) f -> p k f", p=128)

            def wl_in(col0, ncol):
                return wload(lambda t: t[:, 0:8 * ncol].rearrange("p (k f) -> p k f", k=8), winv[:, :, col0:col0 + ncol])

            def vin(s_, ncol):
                return wslot[s_][:, 0:8 * ncol].rearrange("p (k f) -> p k f", k=8)

            rot_ovr[0] = {"A": [0, 1, 2, 3], "B": [4, 5], "C": [4, 5], "M": [6, 7], "U": [2, 3]}
            if sample:
                add("dve", call("memset", ucm_flat[:, :], 0.0), w=["ucm"], x=True)
            for gq in range(2):
                s_ = wl_in(gq * 256, 256)
                for j in range(J):
                    pi = psum("A")
                    if sample:
                        lsel = lambda k, j=j: xn[:, k, c0 + j:c0 + n:4]
                    else:
                        lsel = lambda k, j=j: xn[:, k, c0 + j:c0 + n:8]
                    for k in range(8):
                        add("pe", call("matmul",
                            ps[pi][0:ncc, 0:256], lhsT=lsel(k), rhs=vin(s_, 256)[:, k, :], start=(k == 0), stop=(k == 7)),
                            r=["ws%d" % s_, "xn%d" % k], w=[pk(pi)], x=True)
                    copy_op(evac_eng(), ucm[0:ncc, gq * 16:(gq + 1) * 16, j, :], ps[pi][0:ncc, 0:256].rearrange("c (g h) -> c g h", g=16), r=[pk(pi)], w=["ucm"])
            P.capture_begin()
            rot_ovr[0] = {"A": [0, 1, 4], "B": [2, 3, 5], "C": [4], "M": [5]}
            ws_ovr[0] = [0, 1, 2]
            s_q = wl_in(512, 256)
            for m in range(2):
                pi = psum("A")
                for k in range(8):
                    add("pe", call("matmul", ps[pi][:, 0:n], lhsT=vin(s_q, 256)[:, k, m * 128:(m + 1) * 128],
                                                                   rhs=xn[:, k, c0:c0 + n], start=(k == 0), stop=(k == 7)),
                        r=["ws%d" % s_q, "xn%d" % k], w=[pk(pi)], x=True)
                copy_op(evac_eng(), qT[:, m, 0:n], ps[pi][:, 0:n], r=[pk(pi)], w=["qT%d" % m])
            s_k = wl_in(768, 256)
            for m in range(2):
                pi = psum("A")
                for k in range(8):
                    add("pe", call("matmul", ps[pi][:, 0:n], lhsT=vin(s_k, 256)[:, k, m * 128:(m + 1) * 128],
                                                                   rhs=xn[:, k, c0:c0 + n], start=(k == 0), stop=(k == 7)),
                        r=["ws%d" % s_k, "xn%d" % k], w=[pk(pi)], x=True)
                copy_op(evac_eng(), kT[:, m, 0:n], ps[pi][:, 0:n], r=[pk(pi)], w=["kT%d" % m])
            for tt in range(ntile):
                pi = psum("B")
                for k in range(8):
                    add("pe", call("matmul", ps[pi][0:ntok, 0:256], lhsT=xn[:, k, c0 + tt * ntok:c0 + (tt + 1) * ntok],
                                                                     rhs=vin(s_k, 256)[:, k, :], start=(k == 0), stop=(k == 7)),
                        r=["ws%d" % s_k, "xn%d" % k], w=[pk(pi)], x=True)
                copy_op(evac_eng(), k_tm[0:ntok, tt, :], ps[pi][0:ntok, 0:256], r=[pk(pi)], w=["k_tm%d" % tt])
            for gq in range(2):
                s_ = wl_in(1024 + gq * 256, 256)
                for tt in range(ntile):
                    pi = psum("B")
                    for k in range(8):
                        add("pe", call("matmul", ps[pi][0:ntok, 0:256], lhsT=xn[:, k, c0 + tt * ntok:c0 + (tt + 1) * ntok],
                                                                                rhs=vin(s_, 256)[:, k, :], start=(k == 0), stop=(k == 7)),
                            r=["ws%d" % s_, "xn%d" % k], w=[pk(pi)], x=True)
                    copy_op(evac_eng(), v_tm[0:ntok, tt, gq * 256:(gq + 1) * 256], ps[pi][0:ntok, 0:256], r=[pk(pi)], w=["v_tm%d_%d" % (tt, gq)])
            for gq in range(2):
                s_ = wl_in(1536 + gq * 256, 256)
                for mm in range(2):
                    m = gq * 2 + mm
                    pi = psum("A")
                    for k in range(8):
                        add("pe", call("matmul", ps[pi][:, 0:n], lhsT=vin(s_, 256)[:, k, mm * 128:(mm + 1) * 128],
                                                                                rhs=xn[:, k, c0:c0 + n], start=(k == 0), stop=(k == 7)),
                            r=["ws%d" % s_, "xn%d" % k], w=[pk(pi)], x=True)
                    add("act", call("activation", out=sgT[:, m, 0:n], in_=ps[pi][:, 0:n], func=AF.Silu),
                        r=[pk(pi)], w=["sgT%d" % m], x=True)
            s_a = wl_in(2048, 16)
            pi = psum("A")
            for k in range(8):
                add("pe", call("matmul", ps[pi][0:16, 0:n], lhsT=vin(s_a, 16)[:, k, :], rhs=xn[:, k, c0:c0 + n],
                                                          start=(k == 0), stop=(k == 7)),
                    r=["ws%d" % s_a, "xn%d" % k], w=[pk(pi)], x=True)
            copy_op(evac_eng(), aT[0:16, 0:n], ps[pi][0:16, 0:n], r=[pk(pi)], w=["aT"])
            for tt in range(ntile):
                pi = psum("B")
                add("pe", call("matmul", ps[pi][0:ntok, 0:256], lhsT=aT[0:16, tt * ntok:(tt + 1) * ntok], rhs=gatew_b[:, :], start=True, stop=True),
                    r=["aT", "gatew_b"], w=[pk(pi)], x=True)
                add("act", call("activation", out=etmp[0:ntok, :], in_=ps[pi][0:ntok, 0:256], func=AF.Exp, scale=-1.0),
                    r=[pk(pi)], w=["etmp"], x=True)
                add("dve", call("tensor_tensor", out=etmp[0:ntok, :], in0=etmp[0:ntok, :], in1=gateb[0:ntok, :], op=ALU.mult),
                    r=["etmp", "gateb"], w=["etmp"], x=True)
                add("act", call("activation", out=lf[0:ntok, tt, :], in_=etmp[0:ntok, :], func=AF.Ln, bias=1.0),
                    r=["etmp"], w=["lf%d" % tt], x=True)

            rot_ovr[0] = {"A": [0], "B": [1], "C": [2, 3], "U": [4, 5], "M": [1]}
            pOs, pUs = {}, {}

            def gla_part1(tt):
                bi = tt % 2
                eb = ebb[bi]
                ebk = "eb%d" % bi
                tc0 = tt * ntok
                pC = psum("A")
                for m in range(2):
                    add("pe", call("matmul", ps[pC][:, m * 128:m * 128 + ntok], lhsT=lf[0:ntok, tt, m * 128:(m + 1) * 128],
                                   rhs=TRI[0:ntok, 0:ntok], start=True, stop=True), r=["lf%d" % tt, "const"], w=[pk(pC)], x=True)
                pR = psum("B")
                add("pe", call("matmul", ps[pR][0:ntok, 0:256], lhsT=TRIU[0:ntok, 0:ntok], rhs=lf[0:ntok, tt, :], start=True, stop=True),
                    r=["lf%d" % tt, "const"], w=[pk(pR)], x=True)
                pCv = ps[pC][:, 0:256].rearrange("p (m t) -> p m t", m=2)[:, :, 0:ntok]
                add("act", call("activation", out=eb[:, :, 0:ntok], in_=pCv, func=AF.Exp, scale=-1.0 / 16.0), r=[pk(pC)], w=[ebk], x=True)
                add("act", call("activation", out=enb[bi][:, :, 0:ntok], in_=pCv, func=AF.Exp, scale=1.0 / 16.0), r=[pk(pC)], w=["enb%d" % bi], x=True)
                add("act", call("activation", out=erem[bi][0:ntok, :], in_=ps[pR][0:ntok, 0:256], func=AF.Exp, scale=-1.0 / 16.0),
                    r=[pk(pR)], w=["erem%d" % bi], x=True)
                add("dve", call("scalar_tensor_tensor", out=qs[bi][:, :, 0:ntok], in0=qT[:, :, tc0:tc0 + ntok], scalar=0.125, in1=eb[:, :, 0:ntok],
                                op0=ALU.mult, op1=ALU.mult), r=["qT0", "qT1", ebk], w=["qs%d" % bi], x=True)
                add("dve", call("tensor_tensor", out=ki[bi][:, :, 0:ntok], in0=kT[:, :, tc0:tc0 + ntok], in1=enb[bi][:, :, 0:ntok], op=ALU.mult),
                    r=["kT0", "kT1", "enb%d" % bi], w=["ki%d" % bi], x=True)
                add("dve", call("tensor_tensor", out=kend[bi][0:ntok, :], in0=k_tm[0:ntok, tt, :], in1=erem[bi][0:ntok, :], op=ALU.mult),
                    r=["k_tm%d" % tt, "erem%d" % bi], w=["kend%d" % bi], x=True)
                pA = psum("A")
                for h in range(4):
                    m = h // 2
                    add("dve", call("tensor_scalar", out=kiz[bi][:, h, 0:ntok], in0=ki[bi][:, m, 0:ntok], scalar1=sel64[:, (h % 2):(h % 2) + 1], scalar2=None, op0=ALU.mult),
                        r=["ki%d" % bi, "const"], w=["kiz%d" % bi], x=True)
                for h in range(4):
                    m = h // 2
                    add("pe", call("matmul", ps[pA][0:ntok, h * 128:h * 128 + ntok], lhsT=kiz[bi][:, h, 0:ntok], rhs=qs[bi][:, m, 0:ntok], start=True, stop=True),
                        r=["kiz%d" % bi, "qs%d" % bi], w=[pk(pA)], x=True)
                for h in range(4):
                    add("dve", call("tensor_tensor", out=attT[bi][0:ntok, h, 0:ntok], in0=ps[pA][0:ntok, h * 128:h * 128 + ntok], in1=TRI[0:ntok, 0:ntok], op=ALU.mult),
                        r=[pk(pA), "const"], w=["attT%d" % bi], x=True)
                pO = psum("C")
                pOs[tt] = pO
                for h in range(4):
                    add("pe", call("matmul", ps[pO][:, h * 128:h * 128 + ntok], lhsT=v_tm[0:ntok, tt, h * 128:(h + 1) * 128],
                                   rhs=attT[bi][0:ntok, h, 0:ntok], start=(h == 0), stop=False, skip_group_check=True),
                        r=["v_tm%d_%d" % (tt, h // 2), "attT%d" % bi], w=[pk(pO)], x=True)
                if not sample:
                    pU = psum("U")
                    pUs[tt] = pU
                    for c in range(nch):
                        km = kendm[bi * 2 + c]
                        kmk = "kendm%d" % (bi * 2 + c)
                        add("dve", call("tensor_scalar", out=km[0:ntok, :], in0=kend[bi][0:ntok, :], scalar1=SEL[0:ntok, c:c + 1], scalar2=None, op0=ALU.mult),
                            r=["kend%d" % bi, "const"], w=[kmk], x=True)
                        for h in range(4):
                            m, po = h // 2, 64 * (h % 2)
                            add("pe", call("matmul", ps[pU][po:po + 64, c * 256 + m * 128:c * 256 + (m + 1) * 128], lhsT=km[0:ntok, h * 64:(h + 1) * 64],
                                           rhs=v_tm[0:ntok, tt, h * 128:(h + 1) * 128], start=True, stop=True),
                                r=[kmk, "v_tm%d_%d" % (tt, h // 2)], w=[pk(pU)], x=True)

            def gla_part2(tt):
                bi = tt % 2
                eb = ebb[bi]
                ebk = "eb%d" % bi
                tc0 = tt * ntok
                pO = pOs[tt]
                for c in range(nch):
                    cb = c % 2
                    if sample:
                        for m in range(2):
                            dma("sp", S0b[cb][:, m, :], gla_in[c, 2 * m:2 * m + 2, :, :].rearrange("h d e -> (h d) e"),
                                w=["S0b%d" % cb], semkey="S0b%d" % cb, x=True)
                        Ssrc, Skey = S0b[cb], "S0b%d" % cb
                    else:
                        Ssrc, Skey = S_f, "S_f"
                    Szc = Sz[cb]
                    Szk = "Sz%d" % cb
                    for h in range(4):
                        m = h // 2
                        add("act", call("activation", out=Szc[:, h, :], in_=Ssrc[:, m, :], func=AF.Identity, scale=sel64[:, (h % 2):(h % 2) + 1]),
                            r=[Skey, "const"], w=[Szk], x=True)
                    for h in range(4):
                        m = h // 2
                        last = (c == nch - 1) and (h == 3)
                        add("pe", call("matmul", ps[pO][:, h * 128 + c * L:h * 128 + (c + 1) * L], lhsT=Szc[:, h, :],
                                       rhs=qs[bi][:, m, c * L:(c + 1) * L], start=False, stop=last, skip_group_check=True),
                            r=[Szk, "qs%d" % bi], w=[pk(pO)], x=True)
                    if sample:
                        km = kendm[cb]
                        kmk = "kendm%d" % cb
                        add("dve", call("tensor_scalar", out=km[0:ntok, :], in0=kend[bi][0:ntok, :], scalar1=SEL[0:ntok, c:c + 1], scalar2=None, op0=ALU.mult),
                            r=["kend%d" % bi, "const"], w=[kmk], x=True)
                        pU = psum("U")
                        ucol = 0
                        for h in range(4):
                            m, po = h // 2, 64 * (h % 2)
                            add("pe", call("matmul", ps[pU][po:po + 64, m * 128:(m + 1) * 128], lhsT=km[0:ntok, h * 64:(h + 1) * 64],
                                           rhs=v_tm[0:ntok, tt, h * 128:(h + 1) * 128], start=True, stop=True),
                                r=[kmk, "v_tm%d_%d" % (tt, h // 2)], w=[pk(pU)], x=True)
                        Sdst, Sdk = Sob[cb], "Sob%d" % cb
                    else:
                        pU = pUs[tt]
                        ucol = c * 256
                        Sdst, Sdk = S_f, "S_f"
                    col_last = c * L + L - 1
                    for m in range(2):
                        add("dve", call("scalar_tensor_tensor", out=Sdst[:, m, :], in0=Ssrc[:, m, :], scalar=eb[:, m, col_last:col_last + 1],
                                        in1=ps[pU][:, ucol + m * 128:ucol + (m + 1) * 128], op0=ALU.mult, op1=ALU.add),
                            r=[Skey, ebk, pk(pU)], w=[Sdk], x=True)
                    if sample:
                        for m in range(2):
                            dma("sp", sgla_o[c, 2 * m:2 * m + 2, :, :].rearrange("h d e -> (h d) e"), Sob[cb][:, m, :],
                                r=["Sob%d" % cb], semkey="out", x=True)
                pOv = ps[pO][:, :].rearrange("p (h t) -> p h t", h=4)[:, :, 0:ntok]
                o_v = o_sb[:, 0:4 * ntok].rearrange("p (h t) -> p h t", h=4)
                add("act", call("copy", out=o_v, in_=pOv), r=[pk(pO)], w=["o_sb"], x=True)
                add("act", call("activation", out=osq[:, 0:4 * ntok], in_=o_sb[:, 0:4 * ntok], func=AF.Square), r=["o_sb"], w=["osq"], x=True)
                pN = psum("M")
                add("pe", call("matmul", ps[pN][:, 0:4 * ntok], lhsT=ones_b[:], rhs=osq[:, 0:4 * ntok], start=True, stop=True),
                    r=["osq", "ones_b"], w=[pk(pN)], x=True)
                add("act", call("activation", out=sdm[:, 0:4 * ntok], in_=ps[pN][:, 0:4 * ntok], func=AF.Sqrt, bias=EPS, scale=1.0 / 128.0),
                    r=[pk(pN)], w=["sdm"], x=True)
                add("dve", call("reciprocal", out=ps[pN][:, 0:4 * ntok], in_=sdm[:, 0:4 * ntok]), r=["sdm"], w=[pk(pN)], x=True)
                add("dve", call("scalar_tensor_tensor", out=t1b[:, 0:4 * ntok], in0=o_sb[:, 0:4 * ntok], scalar=gnorm[:, 0:1], in1=ps[pN][:, 0:4 * ntok],
                                op0=ALU.mult, op1=ALU.mult), r=["o_sb", "gnorm", pk(pN)], w=["t1b"], x=True)
                t1v = t1b[:, 0:4 * ntok].rearrange("p (h t) -> p h t", h=4)
                add("dve", call("tensor_tensor", out=cat[:, 4:8, tc0:tc0 + ntok], in0=t1v, in1=sgT[:, :, tc0:tc0 + ntok], op=ALU.mult),
                    r=["t1b", "sgT0", "sgT1", "sgT2", "sgT3"], w=["cat_o%d" % tt], x=True)

            gla_part1(0)
            for tt in range(1, ntile):
                gla_part1(tt)
                gla_part2(tt - 1)
            gla_part2(ntile - 1)
            if (not sample) and pass_idx == npass - 1:
                for m in range(2):
                    dma("sp", pgla_o[2 * m:2 * m + 2, :, :].rearrange("h d e -> (h d) e"), S_f[:, m, :], r=["S_f"], semkey="out")

            listA = P.capture_end()
            P.capture_begin()
            rot_ovr[0] = {"A": [6], "B": [7], "C": [6, 7], "M": [7]}
            ws_ovr[0] = [3]
            for gq in range(4):
                pi = psum("A")
                pbf = ps[pi][:].bitcast(BF16)
                for gg in range(8):
                    g = gq * 8 + gg
                    add("pe", call("transpose", out=pbf[:, gg * 64:gg * 64 + ncc], in_=ucm[0:ncc, g, :, :].rearrange("c j h -> c (j h)"),
                                                                          identity=ident_b[0:ncc, 0:ncc]),
                        r=["ucm", "ident_b"], w=[pk(pi)], x=True)
                src = pbf[:, 0:512].rearrange("p (g c) -> p g c", g=8)[:, :, 0:ncc]
                copy_op(evac_eng(), Ug[:, gq * 8:(gq + 1) * 8, 0:ncc], src, r=[pk(pi)], w=["Ug"])
            Bdst = Ssm if sample else Bst
            Bdk = "Ssm" if sample else "Bst"
            for q in range(4):
                pi = psum("B")
                for hfp in range(2):
                    for g4 in range(4):
                        gl = q * 4 + g4
                        g = hfp * 16 + gl
                        for ri in range(2):
                            col = (ri * 4 + g4) * 64
                            add("pe", call("matmul",
                                ps[pi][hfp * 64:hfp * 64 + 64, col:col + ncc], lhsT=W1[:, g, ri * 64:(ri + 1) * 64], rhs=Ug[:, g, 0:ncc],
                                start=True, stop=True), r=["W1", "Ug"], w=[pk(pi)], x=True)
                src = ps[pi][:, :].rearrange("p (r g c) -> p r g c", r=2, g=4)[:, :, :, 0:ncc]
                if sample:
                    dst = Ssm[:, :, q * 4:(q + 1) * 4, 0:ncc]
                else:
                    dst = Bst[:, :, q * 4:(q + 1) * 4, 1:1 + ncc]
                copy_op(evac_eng(), dst, src, r=[pk(pi)], w=[Bdk])
            if sample:
                pcs = 0
                for ri, src_d in enumerate((s5re_in, s5im_in)):
                    srcv = src_d.rearrange("b (a g) p -> b g a p", a=2)
                    pi = psum("A")
                    for q4 in range(4):
                        bb = pcs % 2
                        pcs += 1
                        for g4 in range(4):
                            dma("sp", h0t[bb][g4 * 16:(g4 + 1) * 16, :, :], srcv[:, q4 * 4 + g4, :, :], w=["h0t%d" % bb], semkey="h0t%d" % bb, x=True)
                        add("pe", call("transpose", out=ps[pi][:, q4 * 64:(q4 + 1) * 64], in_=h0t[bb][0:64, :, :].rearrange("r a p -> r (a p)"),
                                       identity=ident_f[0:64, 0:64]), r=["h0t%d" % bb, "ident_f"], w=[pk(pi)], x=True)
                    copy_op("dve", Bsm[:, ri, :, :], ps[pi][:, 0:256].rearrange("p (g b) -> p g b", g=16), r=[pk(pi)], w=["Bsm"])
                bcb = lambda a: a.unsqueeze(3).to_broadcast([128, 2, 16, 16])
                sw = lambda t: (t[:, 1, :, :], t[:, 0, :, :])
                T0, T1 = sm_t

                def cmul(dst, dk, src, sk, cr_, ci_neg_pos, ck):
                    add("dve", call("tensor_tensor", out=dst[:, :, :, :], in0=src[:, :, :, :], in1=bcb(cr_[:, :, :]), op=ALU.mult), r=[sk, ck], w=[dk], x=True)
                    add("dve", call("tensor_tensor", out=T1[:, 0, :, :], in0=src[:, 1, :, :], in1=ci_neg_pos[:, 0, :].unsqueeze(2).to_broadcast([128, 16, 16]), op=ALU.mult), r=[sk, ck], w=["smT1"], x=True)
                    add("dve", call("tensor_tensor", out=T1[:, 1, :, :], in0=src[:, 0, :, :], in1=ci_neg_pos[:, 1, :].unsqueeze(2).to_broadcast([128, 16, 16]), op=ALU.mult), r=[sk, ck], w=["smT1"], x=True)
                    add("dve", call("tensor_tensor", out=dst[:, :, :, :], in0=dst[:, :, :, :], in1=T1[:, :, :, :], op=ALU.add), r=[dk, "smT1"], w=[dk], x=True)

                cmul(T0, "smT0", Bsm, "Bsm", A1, A2, "A1A2")
                add("dve", call("tensor_tensor", out=T0[:, :, :, :], in0=T0[:, :, :, :], in1=Ssm[:, :, :, :], op=ALU.add), r=["smT0", "Ssm"], w=["smT0"], x=True)
                add("dve", call("tensor_copy", out=sc_t1[:, 0, :], in_=L4[:, 0, :]), r=["L4"], w=["sc_t1"], x=True)
                add("dve", call("tensor_copy", out=sc_t1[:, 1, :], in_=L4[:, 0, :]), r=["L4"], w=["sc_t1"], x=True)
                add("dve", call("tensor_scalar", out=sc_t2[:, 0, :], in0=L4[:, 1, :], scalar1=-1.0, scalar2=None, op0=ALU.mult), r=["L4"], w=["sc_t2"], x=True)
                add("dve", call("tensor_copy", out=sc_t2[:, 1, :], in_=L4[:, 1, :]), r=["L4"], w=["sc_t2"], x=True)
                cmul(Hs3, "Hs3", T0, "smT0", sc_t1, sc_t2, "sc_t2")
                pcs = 0
                for ri, dst_d in enumerate((sre_o, sim_o)):
                    dstv = dst_d.rearrange("b (a g) p -> b g a p", a=2)
                    for q4 in range(4):
                        bb = pcs % 2
                        pcs += 1
                        pi = psum("B")
                        add("pe", call("transpose", out=ps[pi][0:64, 0:128], in_=Hs3[:, ri, q4 * 4:(q4 + 1) * 4, :].rearrange("p g b -> p (g b)"), identity=ident_f[:]),
                            r=["Hs3", "ident_f"], w=[pk(pi)], x=True)
                        copy_op(evac_eng(), hot[bb][0:64, :, :], ps[pi][0:64, 0:128].rearrange("r (a p) -> r a p", a=2), r=[pk(pi)], w=["hot%d" % bb])
                        for g4 in range(4):
                            dma("sp", dstv[:, q4 * 4 + g4, :, :], hot[bb][g4 * 16:(g4 + 1) * 16, :, :], r=["hot%d" % bb], semkey="out", x=True)
            else:
                nsb = ncc // 8
                bc8 = lambda a: a.unsqueeze(3).to_broadcast([128, 2, 16, nsb])
                bc8h = lambda a: a.unsqueeze(2).to_broadcast([128, 16, nsb])
                for j in range(1, 8):
                    src = Bst[:, :, :, j:j + 8 * (nsb - 1) + 1:8]
                    dst = Bst[:, :, :, j + 1:j + 1 + 8 * (nsb - 1) + 1:8]
                    add("dve", call("tensor_tensor", out=sct1[:, :, :, 0:nsb], in0=src, in1=bc8(A1[:, :, :]), op=ALU.mult), r=["Bst", "A1A2"], w=["sct1"], x=True)
                    add("dve", call("tensor_tensor", out=sct2[:, 0, :, 0:nsb], in0=Bst[:, 1, :, j:j + 8 * (nsb - 1) + 1:8], in1=bc8h(A2[:, 0, :]), op=ALU.mult), r=["Bst", "A1A2"], w=["sct2"], x=True)
                    add("dve", call("tensor_tensor", out=sct2[:, 1, :, 0:nsb], in0=Bst[:, 0, :, j:j + 8 * (nsb - 1) + 1:8], in1=bc8h(A2[:, 1, :]), op=ALU.mult), r=["Bst", "A1A2"], w=["sct2"], x=True)
                    add("dve", call("tensor_tensor", out=dst, in0=dst, in1=sct1[:, :, :, 0:nsb], op=ALU.add), r=["Bst", "sct1"], w=["Bst"], x=True)
                    add("dve", call("tensor_tensor", out=dst, in0=dst, in1=sct2[:, :, :, 0:nsb], op=ALU.add), r=["Bst", "sct2"], w=["Bst"], x=True)
                for sbk in range(nsb):
                    car = Bst[:, :, :, 8 * sbk]
                    dst = Bst[:, :, :, 8 * sbk + 1:8 * sbk + 9]
                    add("dve", call("tensor_tensor", out=sct1[:, :, :, 0:8], in0=PA1[:, :, :, :], in1=car.unsqueeze(3).to_broadcast([128, 2, 16, 8]), op=ALU.mult), r=["Bst", "PA"], w=["sct1"], x=True)
                    add("dve", call("tensor_tensor", out=sct2[:, 0, :, 0:8], in0=PA2[:, 0, :, :], in1=Bst[:, 1, :, 8 * sbk].unsqueeze(2).to_broadcast([128, 16, 8]), op=ALU.mult), r=["Bst", "PA"], w=["sct2"], x=True)
                    add("dve", call("tensor_tensor", out=sct2[:, 1, :, 0:8], in0=PA2[:, 1, :, :], in1=Bst[:, 0, :, 8 * sbk].unsqueeze(2).to_broadcast([128, 16, 8]), op=ALU.mult), r=["Bst", "PA"], w=["sct2"], x=True)
                    add("dve", call("tensor_tensor", out=dst, in0=dst, in1=sct1[:, :, :, 0:8], op=ALU.add), r=["Bst", "sct1"], w=["Bst"], x=True)
                    add("dve", call("tensor_tensor", out=dst, in0=dst, in1=sct2[:, :, :, 0:8], op=ALU.add), r=["Bst", "sct2"], w=["Bst"], x=True)
            for hz in range(2):
                hsrc, hkey = (Bsm[:, :, :, 0:ncc], "Bsm") if sample else (Bst[:, :, :, 0:ncc], "Bst")
                add("dve", call("tensor_scalar", out=Hbz[hz][:, :, :, 0:ncc], in0=hsrc, scalar1=sel64[:, hz:hz + 1], scalar2=None, op0=ALU.mult),
                    r=[hkey, "const"], w=["Hbz"], x=True)
            for gq in range(8):
                pi = psum("C")
                for g4 in range(4):
                    g = gq * 4 + g4
                    hfp, gl = g // 16, g % 16
                    pr = slice(hfp * 64, hfp * 64 + 64)
                    osl = ps[pi][0:ncc, g4 * 128:(g4 + 1) * 128]
                    add("pe", call("matmul", osl, lhsT=Hbz[hfp][:, 0, gl, 0:ncc], rhs=W3[:, gl, 0, :], start=True, stop=False),
                        r=["Hbz", "W3"], w=[pk(pi)], x=True)
                    add("pe", call("matmul", osl, lhsT=Hbz[hfp][:, 1, gl, 0:ncc], rhs=W3[:, gl, 1, :], start=False, stop=False),
                        r=["Hbz", "W3"], w=[pk(pi)], x=True)
                    add("pe", call("matmul", osl, lhsT=Ug[:, g, 0:ncc], rhs=W4[:, g, :], start=False, stop=True),
                        r=["Ug", "W4"], w=[pk(pi)], x=True)
                src = ps[pi][0:ncc, :].rearrange("c (g j h) -> c j g h", g=4, j=8)
                dst = zcm[0:ncc, :, gq * 64:(gq + 1) * 64].rearrange("c j (g h) -> c j g h", g=4)
                add("act", call("activation", out=dst, in_=src, func=AF.Gelu_apprx_tanh), r=[pk(pi), "Ug"], w=["ucm"], x=True)
            if (not sample) and pass_idx == npass - 1:
                for hfp in range(2):
                    pr = slice(hfp * 64, hfp * 64 + 64)
                    gs = slice(hfp * 16, hfp * 16 + 16)
                    dma("sp", pre_o[gs, :].rearrange("g p -> p g"), Bst[pr, 0, :, ncc], r=["Bst"], semkey="out", slow=True)
                    dma("sp", pim_o[gs, :].rearrange("g p -> p g"), Bst[pr, 1, :, ncc], r=["Bst"], semkey="out", slow=True)
            if not sample:
                add("dve", call("tensor_copy", out=Bst[:, :, :, 0], in_=Bst[:, :, :, ncc]), r=["Bst"], w=["Bst"], x=True)
            for m in range(4):
                pi = psum("A")
                pbf = ps[pi][:].bitcast(BF16)
                for j in range(J):
                    add("pe", call("transpose", out=pbf[:, j * 64:j * 64 + ncc], in_=zcm[0:ncc, j, m * 128:(m + 1) * 128],
                                                                        identity=ident_b[0:ncc, 0:ncc]),
                        r=["ucm", "ident_b"], w=[pk(pi)], x=True)
                src = pbf[:, 0:J * 64].rearrange("p (j c) -> p j c", j=J)[:, :, 0:ncc]
                dst = zT[:, m, 0:n].rearrange("p (c j) -> p j c", j=J)
                copy_op(evac_eng(), dst, src, r=[pk(pi)], w=["zT%d" % m])
            s_g = wload(lambda t: t[:].rearrange("p (k f) -> p k f", k=4), glu_w_d.rearrange("(k p) f -> p k f", p=128))
            gv = wslot[s_g][:].rearrange("p (k f) -> p k f", k=4)
            for m in range(4):
                pi = psum("A")
                for k in range(4):
                    add("pe", call("matmul", ps[pi][:, 0:n], lhsT=gv[:, k, m * 128:(m + 1) * 128], rhs=zT[:, k, 0:n], start=(k == 0), stop=(k == 3)),
                        r=["ws%d" % s_g, "zT%d" % k], w=[pk(pi)], x=True)
                b = m % 2
                add("act", call("activation", out=sig[b][:, 0:n], in_=ps[pi][:, 0:n], func=AF.Sigmoid, bias=glub[:, m:m + 1]),
                    r=[pk(pi), "glub"], w=["sig"], x=True)
                add("dve", call("tensor_tensor", out=cat[:, m, 0:n], in0=zT[:, m, 0:n], in1=sig[b][:, 0:n], op=ALU.mult),
                    r=["zT%d" % m, "sig"], w=["cat_z%d" % m], x=True)
            listB = P.capture_end()
            rot_ovr[0] = None
            ws_ovr[0] = None
            P.merge([listA, listB], spans=[MERGE_SPAN_A, 1.0])
            woutv = w_out_d.rearrange("(k p) f -> p k f", p=128)
            for dp in range(4):
                s_ = wload(v8, woutv[:, :, dp * 256:(dp + 1) * 256])
                for dd in range(2):
                    d = dp * 2 + dd
                    pi = psum("C")
                    for k in range(8):
                        add("pe", call("matmul", ps[pi][:, 0:n], lhsT=v8(wslot[s_])[:, k, dd * 128:(dd + 1) * 128], rhs=cat[:, k, 0:n],
                                                                                start=(k == 0), stop=(k == 7)),
                            r=["ws%d" % s_, ("cat_z%d" % k) if k < 4 else None] + (["cat_o%d" % t_ for t_ in range(ntile)] if k >= 4 else []), w=[pk(pi)], x=True)
                    add("dve", call("tensor_tensor", out=xT[:, d, c0:c0 + n], in0=ps[pi][:, 0:n], in1=xT[:, d, c0:c0 + n], op=ALU.add),
                        r=[pk(pi), "xT%d" % d], w=["xT%d" % d], x=True)

        P.ops
        add("dve", call("memset", scr[:, 1:2], 0.0), r=["tri64", "triu64", "tri4", "triu4", "sel64", "sel4"], w=["const"])
        side_pc = None
        if do_mixer:
            P.capture_begin()
            rot_ovr[0] = {"A": [6, 7], "B": [6, 7], "C": [6, 7], "M": [6, 7]}
            precompute()
            add("dve", call("memset", scr[:, 2:3], 0.0), r=["A1", "A2"], w=["A1A2"], x=True)
            side_pc = P.capture_end()
            rot_ovr[0] = None
            if not do_ffn:
                P.ops.extend(side_pc)
                side_pc = None
        pending_store = [None]
        for pidx in range(npass):
            chunks = [(0, PT)]
            if pidx == 0:
                chunks = [(0, (PT + NSMP) // 2), ((PT + NSMP) // 2, (PT + NSMP) // 2)]
            P.capture_begin()
            rot_ovr[0] = {"A": [0, 1], "B": [2, 3], "C": [4, 5], "M": [6], "U": [2, 3]}
            for tt in range(PT // 128):
                r0 = pidx * PT + tt * 128
                load_x(xp[r0:r0 + 128, :], 128, tt * 128)
            if pidx == 0:
                load_x(xs[:, :], NSMP, PT)
            if do_ffn:
                rmsnorm(0, chunks, xn, "xn")
            rot_ovr[0] = None
            l_load = P.capture_end()
            if pending_store[0]:
                P.merge([pending_store[0], l_load])
                pending_store[0] = None
            else:
                P.ops.extend(l_load)
            if do_ffn:
                ffn(*w_ffn[0], chunks, side=(side_pc if pidx == 0 else None))
            if do_mixer:
                barrier()
                mixer("prompt", 0, PT, pidx)
                if pidx == 0:
                    mixer("sample", PT, NSMP, pidx)
                barrier()
            if do_ffn:
                rmsnorm(2, chunks, xn, "xn")
                ffn(*w_ffn[1], chunks)
            rmsnorm(3, chunks, yT, "yT")
            P.capture_begin()
            rot_ovr[0] = {"A": [0, 1], "B": [2, 3], "C": [4, 5], "M": [7], "U": [2, 3]}
            for tt in range(PT // 128):
                r0 = pidx * PT + tt * 128
                store_out(yp[r0:r0 + 128, :], 128, tt * 128)
            if pidx == 0:
                store_out(ys[:, :], NSMP, PT)
            rot_ovr[0] = None
            pending_store[0] = P.capture_end()
        if pending_store[0]:
            P.ops.extend(pending_store[0])
        tapsrc = {"xT": (xT[:], ["xT%d" % q_ for q_ in range(8)]), "yT": (yT[:, :, :], ["yT%d" % q_ for q_ in range(8)]), "sdb": (sdb[:, :], ["sdb"]),
                  "A1": (A1[:], ["A1"]), "A2": (A2[:], ["A2"]), "L4": (L4[:], ["L4"]), "Bst": (Bst[:], ["Bst"]), "S_f": (S_f[:], ["S_f"]),
                  "qT": (qT[:, :, :], ["qT0"]), "kT": (kT[:, :, :], ["kT0"]), "lf": (lf[:, :, :], ["lf0"]), "k_tm": (k_tm[:, :, :], ["k_tm0"]),
                  "xin0": (xin[0][:, :], ["xin0"]), "xin1": (xin[1][:, :], ["xin1"]), "pc_er": (pc_er[:, :, :], ["pc_er"]), "pc_ei": (pc_ei[:, :, :], ["pc_ei"])}
        for (tname, tshape) in taps:
            src, keys = tapsrc[tname]
            dma("sp", tap_out[tname], src, r=keys, semkey="out")
        P.emit(nc, final_semkeys=["out"])
    return nc


def _consts():
    c = {}
    c["c_ident"] = np.eye(128, dtype=np.float32)
    s = np.arange(128)
    same64 = (s[:, None] // 64) == (s[None, :] // 64)
    c["c_tri64"] = (same64 & (s[:, None] <= s[None, :])).astype(np.float32)
    c["c_triu64"] = (same64 & (s[:, None] > s[None, :])).astype(np.float32)
    s4 = np.arange(64)
    same4 = (s4[:, None] // 4) == (s4[None, :] // 4)
    c["c_tri4"] = (same4 & (s4[:, None] <= s4[None, :])).astype(np.float32)
    c["c_triu4"] = (same4 & (s4[:, None] > s4[None, :])).astype(np.float32)
    c["c_sel64"] = (s[:, None] // 64 == np.arange(2)[None, :]).astype(np.float32)
    c["c_sel4"] = (s4[:, None] // 4 == np.arange(16)[None, :]).astype(np.float32)
    i_ = s // 16
    c["c_maskw4"] = (i_[:, None] <= i_[None, :]).astype(np.float32)
    c["c_evals"] = np.array(EVALS, dtype=np.float32)
    return c


_NC_CACHE = {}


def kernel(x_prompt, x_sample, state_s5_re, state_s5_im, state_gla, norm_ffn1, ffn1_gate, ffn1_up,
           ffn1_down, norm_mix, w_in, s5_lam_re, s5_lam_im, s5_log_dt, s5_b_re, s5_b_im, s5_c_re,
           s5_c_im, s5_d, s5_glu_w, s5_glu_b, gla_gate_w, gla_gate_b, gla_norm, w_out, norm_ffn2,
           ffn2_gate, ffn2_up, ffn2_down, norm_final, _npass=NPASS_FULL, _do_mixer=True, _do_ffn=True, _taps=(), _stage=9):
    f = lambda a: np.ascontiguousarray(np.asarray(a, dtype=np.float32))
    key = (_npass, _do_mixer, _do_ffn, str(_taps), _stage)
    if key not in _NC_CACHE:
        _NC_CACHE[key] = build_program(npass=_npass, do_mixer=_do_mixer, do_ffn=_do_ffn, taps=_taps, stage=_stage)
    nc = _NC_CACHE[key]
    shared = {
        "gains": f(np.stack([np.asarray(norm_ffn1)[0], np.asarray(norm_mix)[0], np.asarray(norm_ffn2)[0], np.asarray(norm_final)])),
        "ffn1_gate": f(ffn1_gate[0]), "ffn1_up": f(ffn1_up[0]), "ffn1_down": f(ffn1_down[0]),
        "ffn2_gate": f(ffn2_gate[0]), "ffn2_up": f(ffn2_up[0]), "ffn2_down": f(ffn2_down[0]),
        "w_in": f(w_in[0]), "w_out": f(w_out[0]), "glu_w": f(s5_glu_w[0]), "glu_b": f(s5_glu_b[0]),
        "gate_w": f(gla_gate_w[0]), "gate_b": f(gla_gate_b[0]), "gla_norm": f(gla_norm[0]),
        "lam_re": f(s5_lam_re[0]), "lam_im": f(s5_lam_im[0]), "log_dt": f(s5_log_dt[0]),
        "b_re": f(s5_b_re[0]), "b_im": f(s5_b_im[0]), "c_re": f(s5_c_re[0]), "c_im": f(s5_c_im[0]),
        "s5_d": f(s5_d[0]),
    }
    shared.update(_consts())
    xp_ = np.asarray(x_prompt, dtype=np.float32)
    xs_ = np.asarray(x_sample, dtype=np.float32)
    sre = np.asarray(state_s5_re, dtype=np.float32)[0]
    sim = np.asarray(state_s5_im, dtype=np.float32)[0]
    sgl = np.asarray(state_gla, dtype=np.float32)[0]
    in_maps = []
    for i in range(NCORES):
        m = dict(shared)
        m["xp"] = f(xp_[i])
        m["xs"] = f(xs_[16 * i:16 * i + 16].reshape(NSMP, D))
        m["s5re_in"] = f(sre[16 * i:16 * i + 16])
        m["s5im_in"] = f(sim[16 * i:16 * i + 16])
        m["gla_in"] = f(sgl[16 * i:16 * i + 16])
        in_maps.append(m)
    res = run_bass_kernel_spmd(nc, in_maps, core_ids=list(range(NCORES)))
    R = res.results
    y_prompt = np.stack([R[i]["yp"] for i in range(NCORES)]).astype(np.float32)
    y_sample = np.concatenate([R[i]["ys"].reshape(16, 4, D) for i in range(NCORES)]).astype(np.float32)
    p_re = np.stack([R[i]["pre"] for i in range(NCORES)])[None].astype(np.float32)
    p_im = np.stack([R[i]["pim"] for i in range(NCORES)])[None].astype(np.float32)
    p_gla = np.stack([R[i]["pgla"] for i in range(NCORES)])[None].astype(np.float32)
    s_re = np.concatenate([R[i]["sre"] for i in range(NCORES)])[None].astype(np.float32)
    s_im = np.concatenate([R[i]["sim"] for i in range(NCORES)])[None].astype(np.float32)
    s_gla = np.concatenate([R[i]["sgla"] for i in range(NCORES)])[None].astype(np.float32)
    if _taps:
        return (y_prompt, y_sample, p_re, p_im, p_gla, s_re, s_im, s_gla), {t[0]: R[0]["tap_" + t[0]] for t in _taps}
    return (y_prompt, y_sample, p_re, p_im, p_gla, s_re, s_im, s_gla)
```

```python
import contextlib
import math
import numpy as np
import concourse.bass as bass
import concourse.mybir as mybir
from concourse.bass_utils import run_bass_kernel_spmd

F32 = mybir.dt.float32
BF16 = mybir.dt.bfloat16
I32 = mybir.dt.int32
AF = mybir.ActivationFunctionType
ALU = mybir.AluOpType

ENGS = ["pe", "act", "dve", "pool", "sp"]
EPS = 1e-6
NCORES = 8
D = 1024
DFF = 2816
NFT = 22
SEQ = 2048
PT = 512
NPASS_FULL = SEQ // PT
NSMP = 64
INW = 2064
EVALS = [0, 1, 2, 3, 4, 5, 6, 7, 8,
         0, -1, -2, -3, -4, -5, -6, -7,
         7, 6, 5, 4, 3, 2, 1, 0,
         -4,
         16, 24, 32, 40, 48, 56, 64]
NEV = len(EVALS)
MERGE_SPAN_A = 0.5


class Op:
    __slots__ = ("eng", "fn", "r", "w", "dma", "semkey", "deps", "raw", "need_inc", "incval", "dmaval", "pos", "prewait")


class Prog:
    def __init__(self):
        self.ops = []

    def add(self, eng, fn, r=(), w=(), dma=False, semkey=None):
        op = Op()
        op.eng, op.fn, op.r, op.w, op.dma, op.semkey = eng, fn, list(r), list(w), dma, semkey
        op.deps, op.need_inc, op.incval, op.dmaval = [], False, 0, 0
        op.raw = set()
        op.prewait = 0
        op.pos = -1
        self.ops.append(op)
        return op

    def capture_begin(self):
        self._saved = getattr(self, "_saved", [])
        self._saved.append(self.ops)
        self.ops = []

    def capture_end(self):
        lst = self.ops
        self.ops = self._saved.pop()
        return lst

    def merge(self, lists, spans=None):
        if spans is None:
            spans = [1.0] * len(lists)
        keep = [i for i, l in enumerate(lists) if l]
        lists = [lists[i] for i in keep]
        spans = [spans[i] for i in keep]
        idx = [0] * len(lists)
        total = sum(len(l) for l in lists)
        for _ in range(total):
            best, bf = None, None
            for i, l in enumerate(lists):
                if idx[i] < len(l):
                    f = idx[i] / len(l) * spans[i]
                    if bf is None or f < bf:
                        best, bf = i, f
            self.ops.append(lists[best][idx[best]])
            idx[best] += 1

    def analyze(self):
        for i, op in enumerate(self.ops):
            op.pos = i
        last_w = {}
        rd_eng = {}
        rd_dma = {}
        for op in self.ops:
            deps = set()
            for k in op.r:
                if k in last_w:
                    deps.add(last_w[k])
                    op.raw.add(last_w[k])
            for k in op.w:
                if k in last_w:
                    deps.add(last_w[k])
                for p in rd_eng.get(k, {}).values():
                    deps.add(p)
                for p in rd_dma.get(k, ()):
                    deps.add(p)
            deps.discard(op.pos)
            op.deps = sorted(deps)
            for k in op.w:
                last_w[k] = op.pos
                rd_eng[k] = {}
                rd_dma[k] = []
            for k in op.r:
                if op.dma:
                    rd_dma.setdefault(k, []).append(op.pos)
                else:
                    rd_eng.setdefault(k, {})[op.eng] = op.pos
        for op in self.ops:
            for d in op.deps:
                a = self.ops[d]
                if a.dma:
                    continue
                if a.eng != op.eng or op.dma or a.eng != "pe":
                    a.need_inc = True
        cnt = {e: 0 for e in ENGS}
        dcnt = {}
        self.dma_hist = {}
        for op in self.ops:
            if op.dma:
                dcnt[op.semkey] = dcnt.get(op.semkey, 0) + 16
                op.dmaval = dcnt[op.semkey]
                self.dma_hist.setdefault(op.semkey, []).append((op.pos, op.dmaval))
            elif op.need_inc:
                cnt[op.eng] += 1
                op.incval = cnt[op.eng]
        self.semkeys = list(dcnt.keys())
        self.dma_total = dcnt

    def _dma_wait_val(self, semkey, pos):
        v = 0
        for p, c in self.dma_hist[semkey]:
            if p < pos:
                v = c
            else:
                break
        return v

    def emit(self, nc, final_semkeys=()):
        self.analyze()
        maxw = {}
        for op in self.ops:
            if op.dma:
                op.prewait = maxw.get(op.semkey, 0)
            for d in op.deps:
                a = self.ops[d]
                if a.dma:
                    v = self._dma_wait_val(a.semkey, op.pos)
                    if v > maxw.get(a.semkey, 0):
                        maxw[a.semkey] = v
        with contextlib.ExitStack() as st:
            esem = {e: st.enter_context(nc.semaphore("s_" + e)) for e in ENGS}
            dsem = {k: st.enter_context(nc.semaphore("d_%d" % i)) for i, k in enumerate(self.semkeys)}
            block = st.enter_context(nc.Block())
            per_eng = {e: [op for op in self.ops if op.eng == e] for e in ENGS}

            def run(ename, eobj):
                waited = {}
                for op in per_eng[ename]:
                    need = {}
                    for d in op.deps:
                        a = self.ops[d]
                        if a.dma:
                            key = ("d", a.semkey)
                            val = self._dma_wait_val(a.semkey, op.pos)
                            sem = dsem[a.semkey]
                        else:
                            if a.eng == ename and not op.dma and ename == "pe":
                                continue
                            key = ("e", a.eng)
                            val = a.incval
                            sem = esem[a.eng]
                        if val > need.get(key, (0, None))[0]:
                            need[key] = (val, sem)
                    for key, (val, sem) in need.items():
                        if waited.get(key, 0) >= val:
                            continue
                        eobj.wait_ge(sem, val)
                        waited[key] = val
                    if op.dma and op.prewait > waited.get(("d", op.semkey), 0):
                        eobj.wait_ge(dsem[op.semkey], op.prewait)
                        waited[("d", op.semkey)] = op.prewait
                    ins = op.fn(eobj)
                    if op.dma:
                        ins.then_inc(dsem[op.semkey], 16)
                    elif op.need_inc:
                        ins.then_inc(esem[ename], 1)
                if ename == "sp":
                    for k in final_semkeys:
                        if k in dsem:
                            eobj.wait_ge(dsem[k], self.dma_total[k])

            @block.tensor
            def _(e):
                run("pe", e)

            @block.scalar
            def _(e):
                run("act", e)

            @block.vector
            def _(e):
                run("dve", e)

            @block.gpsimd
            def _(e):
                run("pool", e)

            @block.sync
            def _(e):
                run("sp", e)


def build_program(npass=NPASS_FULL, do_mixer=True, do_ffn=True, taps=(), stage=9):
    nc = bass.Bass("TRN2", target_bir_lowering=False, dynamic_dma_scratch_size=4096)
    P = Prog()

    def din(name, shape):
        return nc.dram_tensor(name, list(shape), F32, kind="ExternalInput").ap()

    def dout(name, shape):
        return nc.dram_tensor(name, list(shape), F32, kind="ExternalOutput").ap()

    xp = din("xp", [SEQ, D])
    xs = din("xs", [NSMP, D])
    s5re_in = din("s5re_in", [16, 32, 64])
    s5im_in = din("s5im_in", [16, 32, 64])
    gla_in = din("gla_in", [16, 4, 64, 128])
    gains_d = din("gains", [4, D])
    w_ffn = [(din("ffn1_gate", [D, DFF]), din("ffn1_up", [D, DFF]), din("ffn1_down", [DFF, D])),
             (din("ffn2_gate", [D, DFF]), din("ffn2_up", [D, DFF]), din("ffn2_down", [DFF, D]))]
    w_in_d = din("w_in", [D, INW])
    w_out_d = din("w_out", [D, D])
    glu_w_d = din("glu_w", [512, 512])
    glu_b_d = din("glu_b", [512])
    gate_w_d = din("gate_w", [16, 256])
    gate_b_d = din("gate_b", [256])
    gla_norm_d = din("gla_norm", [128])
    lam_re_d = din("lam_re", [32, 64])
    lam_im_d = din("lam_im", [32, 64])
    log_dt_d = din("log_dt", [32])
    b_re_d = din("b_re", [32, 64, 16])
    b_im_d = din("b_im", [32, 64, 16])
    c_re_d = din("c_re", [32, 16, 64])
    c_im_d = din("c_im", [32, 16, 64])
    s5_d_d = din("s5_d", [512])
    ident_d = din("c_ident", [128, 128])
    tri64_d = din("c_tri64", [128, 128])
    triu64_d = din("c_triu64", [128, 128])
    tri4_d = din("c_tri4", [64, 64])
    triu4_d = din("c_triu4", [64, 64])
    sel64_d = din("c_sel64", [128, 2])
    sel4_d = din("c_sel4", [64, 16])
    maskw4_d = din("c_maskw4", [128, 128])
    evals_d = din("c_evals", [NEV])

    yp = dout("yp", [SEQ, D])
    ys = dout("ys", [NSMP, D])
    pre_o = dout("pre", [32, 64])
    pim_o = dout("pim", [32, 64])
    pgla_o = dout("pgla", [4, 64, 128])
    sre_o = dout("sre", [16, 32, 64])
    sim_o = dout("sim", [16, 32, 64])
    sgla_o = dout("sgla", [16, 4, 64, 128])
    tap_out = {}
    for (tname, tshape) in taps:
        tap_out[tname] = dout("tap_" + tname, tshape)

    st = contextlib.ExitStack()
    with st:
        def sb(name, shape, dt):
            return st.enter_context(nc.sbuf_tensor("sb_" + name, list(shape), dt))

        NMAX = PT + NSMP
        xT = sb("xT", [128, 8, NMAX], F32)
        xn = sb("xn", [128, 8, NMAX], BF16)
        NW = 4
        wslot = [sb("wslot%d" % i, [128, 2048], BF16) for i in range(NW)]
        W1 = sb("W1", [128, 32, 128], BF16)
        W3 = sb("W3", [128, 16, 2, 128], BF16)
        W4 = sb("W4", [128, 32, 128], BF16)
        A1 = sb("A1", [128, 2, 16], F32)
        A2 = sb("A2", [128, 2, 16], F32)
        L4 = sb("L4", [128, 2, 16], F32)
        Bst = sb("Bst", [128, 2, 16, 65], F32)
        ident_f = sb("ident_f", [128, 128], F32)
        ident_b = sb("ident_b", [128, 128], BF16)
        ones_b = sb("ones_b", [128, 128], BF16)
        tri64 = sb("tri64", [128, 128], F32)
        triu64 = sb("triu64", [128, 128], F32)
        tri4 = sb("tri4", [64, 64], F32)
        triu4 = sb("triu4", [64, 64], F32)
        sel64 = sb("sel64", [128, 2], F32)
        sel4 = sb("sel4", [64, 16], F32)
        gains = sb("gains", [128, 4, 8], F32)
        glub = sb("glub", [128, 4], F32)
        gnorm = sb("gnorm", [128, 1], F32)
        gatew = sb("gatew", [16, 256], F32)
        gatew_b = sb("gatew_b", [16, 256], BF16)
        gateb = sb("gateb", [128, 256], F32)
        S_f = sb("S_f", [128, 2, 128], F32)
        PA1 = sb("PA1", [128, 2, 16, 8], F32)
        PA2 = sb("PA2", [128, 2, 16, 8], F32)
        scr = sb("scr", [128, 8], F32)

        XW = 26368
        arena = sb("arena", [128, XW], F32)

        class Carver:
            def __init__(self, base=0):
                self.off = base

            def get(self, shape, dt):
                n = 1
                for s in shape:
                    n *= s
                words = (n * (2 if dt == BF16 else 4) + 3) // 4
                words = (words + 7) // 8 * 8
                a = arena[:, self.off:self.off + words]
                self.off += words
                assert self.off <= XW, "arena overflow %d" % self.off
                if dt == BF16:
                    a = a.bitcast(BF16)[:, 0:n]
                elif dt == I32:
                    a = a.bitcast(I32)[:, 0:n]
                else:
                    a = a[:, 0:n]
                if len(shape) == 2:
                    return a.rearrange("p (a b) -> p a b", a=shape[0])
                if len(shape) == 3:
                    return a.rearrange("p (a b c) -> p a b c", a=shape[0], b=shape[1])
                return a

        cf = Carver(0)
        hT = cf.get([NFT, NMAX], BF16)
        sgb = [cf.get([512], F32) for _ in range(2)]
        sqb = [cf.get([512], BF16) for _ in range(2)]
        sdb = cf.get([512], F32)
        xin = [cf.get([1024], F32) for _ in range(2)]
        yT = Carver(0).get([8, NMAX], F32)
        ffn_end = cf.off
        cst = Carver(ffn_end)
        xst = [cst.get([1024], F32) for _ in range(2)]
        cp = Carver(ffn_end)
        pc_lr = cp.get([16], F32)
        pc_li = cp.get([16], F32)
        pc_dt = cp.get([16], F32)
        pc_ev = cp.get([NEV], F32)
        pc_er = cp.get([16, NEV], F32)
        pc_ei = cp.get([16, NEV], F32)
        pc_t0 = cp.get([16, NEV], F32)
        pc_t1 = cp.get([16, NEV], F32)
        pc_t2 = cp.get([16, NEV], F32)
        pc_ti = cp.get([16, NEV], I32)
        pc_fr = cp.get([16], F32)
        pc_fi = cp.get([16], F32)
        pc_s0 = cp.get([16], F32)
        pc_s1 = cp.get([16], F32)
        pc_s2 = cp.get([16], F32)
        pc_pr = cp.get([16, NEV], F32)
        pc_pi = cp.get([16, NEV], F32)
        pc_br = cp.get([16, 16], F32)
        pc_bi = cp.get([16, 16], F32)
        pc_cr = cp.get([16, 16], F32)
        pc_ci = cp.get([16, 16], F32)
        pc_dcol = cp.get([32], F32)
        pc_A = cp.get([16, 128], F32)
        pc_B = cp.get([16, 128], F32)
        pc_C = cp.get([16, 128], F32)
        pc_D = cp.get([16, 128], F32)
        pc_E = cp.get([16, 128], F32)
        pc_w4t = cp.get([128], F32)
        pc_csb = [pc_E[:, 2 * i_:2 * i_ + 2, :].rearrange("p a (b c) -> p a b c", b=2) for i_ in range(2)]
        cm = Carver(0)
        ucm_flat = cm.get([4096], BF16)
        ucm = ucm_flat.rearrange("p (g j h) -> p g j h", g=32, j=8)
        zcm = ucm_flat.rearrange("p (j c) -> p j c", j=8)
        qT = cm.get([2, 512], F32)
        kT = cm.get([2, 512], F32)
        k_tm = cm.get([4, 256], F32)
        v_tm = cm.get([4, 512], BF16)
        sgT = cm.get([4, 512], BF16)
        aT = cm.get([512], BF16)
        lf = cm.get([4, 256], F32)
        Ug = cm.get([32, 64], BF16)
        cat = cm.get([8, 512], BF16)
        zT = cm.get([4, 512], BF16)
        Hbz = [cm.get([2, 16, 64], BF16) for _ in range(2)]
        ebb = [cm.get([2, 128], F32) for _ in range(2)]
        enb = [cm.get([2, 128], F32) for _ in range(2)]
        erem = [cm.get([256], F32) for _ in range(2)]
        qs = [cm.get([2, 128], BF16) for _ in range(2)]
        ki = [cm.get([2, 128], BF16) for _ in range(2)]
        kiz = [cm.get([4, 128], BF16) for _ in range(2)]
        Sz = [cm.get([4, 128], BF16) for _ in range(2)]
        kend = [cm.get([256], BF16) for _ in range(2)]
        kendm = [cm.get([256], BF16) for _ in range(4)]
        attT = [cm.get([4, 128], BF16) for _ in range(2)]
        o_sb = cm.get([512], F32)
        osq = cm.get([512], BF16)
        t1b = cm.get([512], F32)
        etmp = t1b[:, 0:256]
        sdm = cm.get([512], F32)
        sig0_ = cm.get([512], F32)
        sig = [sig0_, sig0_]
        S0b = [cm.get([2, 128], F32) for _ in range(2)]
        Sob = [cm.get([2, 128], F32) for _ in range(2)]
        sct1 = cm.get([2, 16, 8], F32)
        sct2 = cm.get([2, 16, 8], F32)
        sc_t1 = cm.get([2, 16], F32)
        sc_t2 = cm.get([2, 16], F32)
        Bsm = cm.get([2, 16, 16], F32)
        Ssm = cm.get([2, 16, 16], F32)
        Hs3 = cm.get([2, 16, 16], F32)
        sm_t = [cm.get([2, 16, 16], F32) for _ in range(2)]
        h0t = [cm.get([2, 64], F32) for _ in range(2)]
        hot = [cm.get([2, 64], F32) for _ in range(2)]

        ps = [st.enter_context(nc.psum_tensor("ps%d" % i, [128, 512], F32)) for i in range(8)]
        rot = {"A": [0, 1], "B": [2, 3], "C": [4, 5], "M": [6, 7], "U": [2, 3]}
        rot_i = {k: 0 for k in rot}

        rot_ovr = [None]
        ws_ovr = [None]

        def psum(group):
            banks = rot_ovr[0][group] if rot_ovr[0] is not None else rot[group]
            i = banks[rot_i[group] % len(banks)]
            rot_i[group] += 1
            return i

        def pk(i):
            return "ps%d" % i

        XB = "Xbar"

        def call(name, *a, **kw):
            return lambda e: getattr(e, name)(*a, **kw)

        def add(eng, fn, r=(), w=(), x=False):
            r = [k_ for k_ in r if k_ is not None]
            if x:
                r.append(XB)
            return P.add(eng, fn, r=r, w=w)

        def dma(eng, out, in_, r=(), w=(), semkey=None, x=False, slow=False):
            r = list(r)
            if x:
                r.append(XB)
            if slow:
                fn = call("dma_start", out=out, in_=in_, allow_slow_non_contiguous=True)
            else:
                fn = call("dma_start", out=out, in_=in_)
            return P.add(eng, fn, r=r, w=w, dma=True, semkey=semkey)

        def barrier():
            P.add("pool", call("memset", scr[:, 0:1], 0.0), r=[], w=[XB])

        ws_i = [0]

        ws_hist = []

        def wload(out_view_fn, in_ap):
            pool_ = ws_ovr[0] if ws_ovr[0] is not None else list(range(NW))
            s = pool_[ws_i[0] % len(pool_)]
            ws_i[0] += 1
            extra = ["ws%d" % ws_hist[-2]] if len(ws_hist) >= 2 and ws_hist[-2] != s else []
            ws_hist.append(s)
            dma("pool", out_view_fn(wslot[s]), in_ap, r=extra, w=["ws%d" % s], semkey="ws%d" % s)
            return s

        ev_i = [0]

        def evac_eng():
            ev_i[0] += 1
            return "act" if ev_i[0] % 2 == 0 else "dve"

        def copy_op(eng, out, in_, r, w, x=True):
            if eng == "act":
                add("act", call("copy", out=out, in_=in_), r=r, w=w, x=x)
            else:
                add(eng, call("tensor_copy", out=out, in_=in_), r=r, w=w, x=x)

        cload = [(ident_f[:], ident_d, "ident_f"), (tri64[:], tri64_d, "tri64"), (triu64[:], triu64_d, "triu64"),
                 (tri4[:], tri4_d, "tri4"), (triu4[:], triu4_d, "triu4"), (sel64[:], sel64_d, "sel64"),
                 (sel4[:], sel4_d, "sel4"), (gatew[:], gate_w_d, "gatew"),
                 (gateb[:], gate_b_d.partition_broadcast(128), "gateb")]
        for (o, i, k) in cload:
            dma("sp", o, i, w=[k], semkey="const")
        dma("sp", gains[:], gains_d.rearrange("n (k p) -> p n k", p=128), w=["gains"], semkey="const", slow=True)
        dma("sp", glub[:], glu_b_d.rearrange("(m p) -> p m", p=128), w=["glub"], semkey="const", slow=True)
        dma("sp", gnorm[:], gla_norm_d.rearrange("(p o) -> p o", o=1), w=["gnorm"], semkey="const", slow=True)
        add("dve", call("tensor_copy", out=ident_b[:], in_=ident_f[:]), r=["ident_f"], w=["ident_b"])
        add("dve", call("tensor_copy", out=gatew_b[:], in_=gatew[:]), r=["gatew"], w=["gatew_b"])
        add("act", call("activation", out=gateb[:], in_=gateb[:], func=AF.Exp, scale=-1.0), r=["gateb"], w=["gateb"])
        add("dve", call("memset", ones_b[:], 1.0), w=["ones_b"])
        add("dve", call("memset", S_f[:], 0.0), w=["S_f"])
        add("dve", call("memset", Bst[:], 0.0), w=["Bst"])

        xin_i = [0]

        def load_x(src_rows, ntok, col0):
            b = xin_i[0] % 2
            xin_i[0] += 1
            kx = "xin%d" % b
            dma("sp", xin[b][0:ntok, :], src_rows, w=[kx], semkey=kx, x=True)
            for half in range(2):
                pi = psum("M")
                for kk in range(4):
                    k = half * 4 + kk
                    add("pe", call("transpose",
                        out=ps[pi][:, kk * 128:kk * 128 + ntok], in_=xin[b][0:ntok, k * 128:(k + 1) * 128],
                        identity=ident_f[0:ntok, 0:ntok]), r=[kx, "ident_f"], w=[pk(pi)], x=True)
                src = ps[pi][:].rearrange("p (a b) -> p a b", a=4)[:, :, 0:ntok]
                dst = xT[:, half * 4:half * 4 + 4, col0:col0 + ntok]
                copy_op(evac_eng(), dst, src, r=[pk(pi)], w=["xT%d" % (half * 4 + q_) for q_ in range(4)], x=True)

        def rmsnorm(gi, chunks, dst, dst_key):
            for (c0, n) in chunks:
                pi = psum("M")
                for k in range(8):
                    b = k % 2
                    add("act", call("activation", out=sqb[b][:, 0:n], in_=xT[:, k, c0:c0 + n],
                                                               func=AF.Square),
                        r=["xT%d" % k], w=["sq%d" % b], x=True)
                    add("pe", call("matmul", ps[pi][:, 0:n], lhsT=ones_b[:], rhs=sqb[b][:, 0:n],
                                                                 start=(k == 0), stop=(k == 7)),
                        r=["sq%d" % b, "ones_b"], w=[pk(pi)], x=True)
                add("act", call("activation", out=sdb[:, 0:n], in_=ps[pi][:, 0:n], func=AF.Ln,
                                                        bias=EPS, scale=1.0 / D),
                    r=[pk(pi)], w=["sdb"], x=True)
                add("act", call("activation", out=ps[pi][:, 0:n], in_=sdb[:, 0:n], func=AF.Exp, scale=-0.5),
                    r=["sdb"], w=[pk(pi)], x=True)
                for k in range(8):
                    add("dve", call("scalar_tensor_tensor",
                        out=dst[:, k, c0:c0 + n], in0=xT[:, k, c0:c0 + n], scalar=gains[:, gi, k:k + 1],
                        in1=ps[pi][:, 0:n], op0=ALU.mult, op1=ALU.mult),
                        r=["xT%d" % k, "gains", pk(pi)], w=[dst_key + str(k)] + (["hT%d" % f_ for f_ in range(NFT)] if dst_key == "yT" else []), x=True)

        def ffn(wg, wu, wd, chunks, side=None):
            nside = [0]
            npts = NFT // 2 + 8

            side_dma = [o for o in side if o.dma] if side else []
            side_cmp = [o for o in side if not o.dma] if side else []
            if side_dma:
                P.ops.extend(side_dma)
            first_pt = npts // 2 + 2

            def inject(ip):
                if side_cmp and ip >= first_pt:
                    hi = len(side_cmp) * (ip - first_pt + 1) // (npts - first_pt)
                    P.ops.extend(side_cmp[nside[0]:hi])
                    nside[0] = hi
            wgv = wg.rearrange("(k p) f -> p k f", p=128)
            wuv = wu.rearrange("(k p) f -> p k f", p=128)
            wdv = wd.rearrange("(t p) d -> p t d", p=128)
            v8 = lambda t: t[:].rearrange("p (k f) -> p k f", k=8)
            for fp in range(NFT // 2):
                if fp > 0:
                    inject(fp - 1)
                sg_ = wload(v8, wgv[:, :, fp * 256:(fp + 1) * 256])
                su_ = wload(v8, wuv[:, :, fp * 256:(fp + 1) * 256])
                for hf in range(2):
                    f = fp * 2 + hf
                    for (c0, n) in chunks:
                        pa = psum("A")
                        pb = psum("B")
                        for (pi, s_) in ((pa, sg_), (pb, su_)):
                            for k in range(8):
                                add("pe", call("matmul",
                                    ps[pi][:, 0:n], lhsT=v8(wslot[s_])[:, k, hf * 128:(hf + 1) * 128],
                                    rhs=xn[:, k, c0:c0 + n], start=(k == 0), stop=(k == 7)),
                                    r=["ws%d" % s_, "xn%d" % k], w=[pk(pi)], x=True)
                        b = f % 2
                        add("act", call("activation", out=sgb[b][:, 0:n], in_=ps[pa][:, 0:n],
                                                                          func=AF.Silu),
                            r=[pk(pa)], w=["sg%d" % b], x=True)
                        add("dve", call("tensor_tensor",
                            out=hT[:, f, c0:c0 + n], in0=sgb[b][:, 0:n], in1=ps[pb][:, 0:n], op=ALU.mult),
                            r=["sg%d" % b, pk(pb)], w=["hT%d" % f] + (["yT%d" % q_ for q_ in range(8)] if (f == 0 and c0 == 0) else []), x=True)
            v11 = lambda t: t[:, 0:1408].rearrange("p (t c) -> p t c", t=11)
            for d in range(8):
                inject(NFT // 2 + d)
                sl = [wload(v11, wdv[:, hh * 11:(hh + 1) * 11, d * 128:(d + 1) * 128]) for hh in range(2)]
                for (c0, n) in chunks:
                    pi = psum("C")
                    for f in range(NFT):
                        s_ = sl[f // 11]
                        add("pe", call("matmul",
                            ps[pi][:, 0:n], lhsT=v11(wslot[s_])[:, f % 11, :], rhs=hT[:, f, c0:c0 + n],
                            start=(f == 0), stop=(f == NFT - 1)),
                            r=["ws%d" % s_, "hT%d" % f], w=[pk(pi)], x=True)
                    add("dve", call("scalar_tensor_tensor",
                        out=xT[:, d, c0:c0 + n], in0=ps[pi][:, 0:n], scalar=0.5, in1=xT[:, d, c0:c0 + n],
                        op0=ALU.mult, op1=ALU.add),
                        r=[pk(pi), "xT%d" % d], w=["xT%d" % d], x=True)
            if side_cmp:
                P.ops.extend(side_cmp[nside[0]:])
                nside[0] = len(side_cmp)

        xst_i = [0]

        def store_out(dst_rows, ntok, col0):
            b = xst_i[0] % 2
            xst_i[0] += 1
            kx = "xst%d" % b
            for half in range(2):
                pi = psum("M")
                for kk in range(4):
                    k = half * 4 + kk
                    add("pe", call("transpose", out=ps[pi][0:ntok, kk * 128:(kk + 1) * 128], in_=yT[:, k, col0:col0 + ntok],
                                   identity=ident_f[:]), r=["yT%d" % k, "ident_f"], w=[pk(pi)], x=True)
                copy_op(evac_eng(), xst[b][0:ntok, half * 512:(half + 1) * 512], ps[pi][0:ntok, :],
                        r=[pk(pi)], w=[kx], x=True)
            dma("sp", dst_rows, xst[b][0:ntok, :], r=[kx], semkey="out", x=True)

        def precompute():
            x_ = True
            for hfp in range(2):
                pr = slice(hfp * 64, hfp * 64 + 64)
                gs = slice(hfp * 16, hfp * 16 + 16)
                dma("sp", pc_lr[pr, :], lam_re_d[gs, :].rearrange("g p -> p g"), w=["pc_lr"], semkey="pc", x=x_, slow=True)
                dma("sp", pc_li[pr, :], lam_im_d[gs, :].rearrange("g p -> p g"), w=["pc_li"], semkey="pc", x=x_, slow=True)
                dma("sp", pc_dt[pr, :], log_dt_d[gs].partition_broadcast(64), w=["pc_dt"], semkey="pc", x=x_, slow=True)
                dma("sp", pc_br[pr, :, :], b_re_d[gs, :, :].rearrange("g p h -> p g h"), w=["pc_br"], semkey="pc", x=x_, slow=True)
                dma("sp", pc_bi[pr, :, :], b_im_d[gs, :, :].rearrange("g p h -> p g h"), w=["pc_bi"], semkey="pc", x=x_, slow=True)
            piC = psum("A")
            for t_, src_d in enumerate((c_re_d, c_im_d)):
                for glhi in range(2):
                    dma("sp", pc_csb[t_][:, glhi, :, :], src_d.rearrange("g h p -> (g h) p").rearrange("(a b q) p -> q b a p", a=2, b=2)[:, glhi, :, :],
                        w=["pc_csb%d" % t_], semkey="pc", x=x_)
            for t_ in range(2):
                for glhi in range(2):
                    add("pe", call("transpose", out=ps[piC][:, t_ * 256 + glhi * 128:t_ * 256 + (glhi + 1) * 128],
                                   in_=pc_csb[t_][:, glhi, :, :].rearrange("q a p -> q (a p)"), identity=ident_f[:]),
                        r=["pc_csb%d" % t_, "ident_f"], w=[pk(piC)], x=True)
            copy_op("dve", pc_cr[:, :, :], ps[piC][:, 0:256].rearrange("p (g h) -> p g h", g=16), r=[pk(piC)], w=["pc_cr"])
            copy_op("dve", pc_ci[:, :, :], ps[piC][:, 256:512].rearrange("p (g h) -> p g h", g=16), r=[pk(piC)], w=["pc_ci"])
            dma("sp", pc_ev[:, :], evals_d.partition_broadcast(128), w=["pc_ev"], semkey="pc", x=x_, slow=True)
            for i in range(8):
                dma("sp", pc_dcol[i * 16:(i + 1) * 16, :], s5_d_d.rearrange("(g h) -> h g", h=16), w=["pc_dcol"],
                    semkey="pc", x=x_, slow=True)
            dma("sp", pc_w4t[:, :], maskw4_d, w=["maskw4"], semkey="pc", x=x_)

            def V(eng, fn, r, w):
                add(eng, fn, r=r, w=w, x=True)

            bc_g = lambda a: a.unsqueeze(2).to_broadcast([128, 16, NEV])
            bc_e = lambda a: a.unsqueeze(1).to_broadcast([128, 16, NEV])
            V("act", call("activation", out=pc_dt[:, :], in_=pc_dt[:, :], func=AF.Exp), ["pc_dt"], ["pc_dt"])
            V("dve", call("tensor_tensor", out=pc_s0[:, :], in0=pc_lr[:, :], in1=pc_dt[:, :], op=ALU.mult), ["pc_lr", "pc_dt"], ["pc_s0"])
            V("dve", call("tensor_tensor", out=pc_s1[:, :], in0=pc_li[:, :], in1=pc_dt[:, :], op=ALU.mult), ["pc_li", "pc_dt"], ["pc_s1"])
            V("dve", call("tensor_tensor", out=pc_t0[:, :, :], in0=bc_g(pc_s0[:, :]), in1=bc_e(pc_ev[:, :]), op=ALU.mult), ["pc_s0", "pc_ev"], ["pc_t0"])
            V("dve", call("tensor_tensor", out=pc_t1[:, :, :], in0=bc_g(pc_s1[:, :]), in1=bc_e(pc_ev[:, :]), op=ALU.mult), ["pc_s1", "pc_ev"], ["pc_t1"])
            V("act", call("activation", out=pc_t0[:, :, :], in_=pc_t0[:, :, :], func=AF.Exp), ["pc_t0"], ["pc_t0"])
            C1 = 6.28125
            C2 = 2.0 * math.pi - C1
            V("dve", call("tensor_scalar", out=pc_t2[:, :, :], in0=pc_t1[:, :, :], scalar1=1.0 / (2.0 * math.pi), scalar2=None, op0=ALU.mult), ["pc_t1"], ["pc_t2"])
            V("dve", call("tensor_copy", out=pc_ti[:, :, :], in_=pc_t2[:, :, :]), ["pc_t2"], ["pc_ti"])
            V("dve", call("tensor_copy", out=pc_t2[:, :, :], in_=pc_ti[:, :, :]), ["pc_ti"], ["pc_t2"])
            V("dve", call("scalar_tensor_tensor", out=pc_t1[:, :, :], in0=pc_t2[:, :, :], scalar=-C1, in1=pc_t1[:, :, :], op0=ALU.mult, op1=ALU.add), ["pc_t2", "pc_t1"], ["pc_t1"])
            V("dve", call("scalar_tensor_tensor", out=pc_t1[:, :, :], in0=pc_t2[:, :, :], scalar=-C2, in1=pc_t1[:, :, :], op0=ALU.mult, op1=ALU.add), ["pc_t2", "pc_t1"], ["pc_t1"])
            V("act", call("activation", out=pc_t2[:, :, :], in_=pc_t1[:, :, :], func=AF.Sin, scale=0.5), ["pc_t1"], ["pc_t2"])
            V("act", call("activation", out=pc_t1[:, :, :], in_=pc_t1[:, :, :], func=AF.Sin, scale=0.25), ["pc_t1"], ["pc_t1"])
            V("dve", call("tensor_tensor", out=pc_t1[:, :, :], in0=pc_t1[:, :, :], in1=pc_t1[:, :, :], op=ALU.mult), ["pc_t1"], ["pc_t1"])
            V("dve", call("tensor_scalar", out=pc_t1[:, :, :], in0=pc_t1[:, :, :], scalar1=-2.0, scalar2=1.0, op0=ALU.mult, op1=ALU.add), ["pc_t1"], ["pc_t1"])
            V("dve", call("tensor_tensor", out=pc_t1[:, :, :], in0=pc_t1[:, :, :], in1=pc_t2[:, :, :], op=ALU.mult), ["pc_t1", "pc_t2"], ["pc_t1"])
            V("dve", call("scalar_tensor_tensor", out=pc_ei[:, :, :], in0=pc_t1[:, :, :], scalar=2.0, in1=pc_t0[:, :, :], op0=ALU.mult, op1=ALU.mult), ["pc_t1", "pc_t0"], ["pc_ei"])
            V("dve", call("tensor_tensor", out=pc_t2[:, :, :], in0=pc_t2[:, :, :], in1=pc_t2[:, :, :], op=ALU.mult), ["pc_t2"], ["pc_t2"])
            V("dve", call("tensor_scalar", out=pc_t2[:, :, :], in0=pc_t2[:, :, :], scalar1=-2.0, scalar2=1.0, op0=ALU.mult, op1=ALU.add), ["pc_t2"], ["pc_t2"])
            V("dve", call("tensor_tensor", out=pc_er[:, :, :], in0=pc_t2[:, :, :], in1=pc_t0[:, :, :], op=ALU.mult), ["pc_t2", "pc_t0"], ["pc_er"])
            abr = pc_er[:, :, 1]
            abi = pc_ei[:, :, 1]
            V("dve", call("tensor_tensor", out=pc_s0[:, :], in0=pc_lr[:, :], in1=pc_lr[:, :], op=ALU.mult), ["pc_lr"], ["pc_s0"])
            V("dve", call("tensor_tensor", out=pc_s1[:, :], in0=pc_li[:, :], in1=pc_li[:, :], op=ALU.mult), ["pc_li"], ["pc_s1"])
            V("dve", call("tensor_tensor", out=pc_s0[:, :], in0=pc_s0[:, :], in1=pc_s1[:, :], op=ALU.add), ["pc_s0", "pc_s1"], ["pc_s0"])
            V("dve", call("reciprocal", out=pc_s0[:, :], in_=pc_s0[:, :]), ["pc_s0"], ["pc_s0"])
            V("dve", call("tensor_scalar", out=pc_s1[:, :], in0=abr, scalar1=-1.0, scalar2=None, op0=ALU.add), ["pc_er"], ["pc_s1"])
            V("dve", call("tensor_tensor", out=pc_fr[:, :], in0=pc_s1[:, :], in1=pc_lr[:, :], op=ALU.mult), ["pc_s1", "pc_lr"], ["pc_fr"])
            V("dve", call("tensor_tensor", out=pc_s2[:, :], in0=abi, in1=pc_li[:, :], op=ALU.mult), ["pc_ei", "pc_li"], ["pc_s2"])
            V("dve", call("tensor_tensor", out=pc_fr[:, :], in0=pc_fr[:, :], in1=pc_s2[:, :], op=ALU.add), ["pc_fr", "pc_s2"], ["pc_fr"])
            V("dve", call("tensor_tensor", out=pc_fr[:, :], in0=pc_fr[:, :], in1=pc_s0[:, :], op=ALU.mult), ["pc_fr", "pc_s0"], ["pc_fr"])
            V("dve", call("tensor_tensor", out=pc_fi[:, :], in0=abi, in1=pc_lr[:, :], op=ALU.mult), ["pc_ei", "pc_lr"], ["pc_fi"])
            V("dve", call("tensor_tensor", out=pc_s2[:, :], in0=pc_s1[:, :], in1=pc_li[:, :], op=ALU.mult), ["pc_s1", "pc_li"], ["pc_s2"])
            V("dve", call("tensor_tensor", out=pc_fi[:, :], in0=pc_fi[:, :], in1=pc_s2[:, :], op=ALU.subtract), ["pc_fi", "pc_s2"], ["pc_fi"])
            V("dve", call("tensor_tensor", out=pc_fi[:, :], in0=pc_fi[:, :], in1=pc_s0[:, :], op=ALU.mult), ["pc_fi", "pc_s0"], ["pc_fi"])
            V("dve", call("tensor_tensor", out=pc_pr[:, :, :], in0=pc_er[:, :, :], in1=bc_g(pc_fr[:, :]), op=ALU.mult), ["pc_er", "pc_fr"], ["pc_pr"])
            V("dve", call("tensor_tensor", out=pc_t0[:, :, :], in0=pc_ei[:, :, :], in1=bc_g(pc_fi[:, :]), op=ALU.mult), ["pc_ei", "pc_fi"], ["pc_t0"])
            V("dve", call("tensor_tensor", out=pc_pr[:, :, :], in0=pc_pr[:, :, :], in1=pc_t0[:, :, :], op=ALU.subtract), ["pc_pr", "pc_t0"], ["pc_pr"])
            V("dve", call("tensor_tensor", out=pc_pi[:, :, :], in0=pc_er[:, :, :], in1=bc_g(pc_fi[:, :]), op=ALU.mult), ["pc_er", "pc_fi"], ["pc_pi"])
            V("dve", call("tensor_tensor", out=pc_t0[:, :, :], in0=pc_ei[:, :, :], in1=bc_g(pc_fr[:, :]), op=ALU.mult), ["pc_ei", "pc_fr"], ["pc_t0"])
            V("dve", call("tensor_tensor", out=pc_pi[:, :, :], in0=pc_pi[:, :, :], in1=pc_t0[:, :, :], op=ALU.add), ["pc_pi", "pc_t0"], ["pc_pi"])
            V("dve", call("tensor_copy", out=A1[:, 0, :], in_=pc_er[:, :, 8]), ["pc_er"], ["A1"])
            V("dve", call("tensor_copy", out=A1[:, 1, :], in_=pc_er[:, :, 8]), ["pc_er"], ["A1"])
            V("dve", call("tensor_scalar", out=A2[:, 0, :], in0=pc_ei[:, :, 8], scalar1=-1.0, scalar2=None, op0=ALU.mult), ["pc_ei"], ["A2"])
            V("dve", call("tensor_copy", out=A2[:, 1, :], in_=pc_ei[:, :, 8]), ["pc_ei"], ["A2"])
            V("dve", call("tensor_copy", out=L4[:, 0, :], in_=pc_er[:, :, 25]), ["pc_er"], ["L4"])
            V("dve", call("tensor_copy", out=L4[:, 1, :], in_=pc_ei[:, :, 25]), ["pc_ei"], ["L4"])
            for k_, idx_ in enumerate([8, 26, 27, 28, 29, 30, 31, 32]):
                V("dve", call("tensor_copy", out=PA1[:, 0, :, k_], in_=pc_er[:, :, idx_]), ["pc_er"], ["PA"])
                V("dve", call("tensor_copy", out=PA1[:, 1, :, k_], in_=pc_er[:, :, idx_]), ["pc_er"], ["PA"])
                V("dve", call("tensor_scalar", out=PA2[:, 0, :, k_], in0=pc_ei[:, :, idx_], scalar1=-1.0, scalar2=None, op0=ALU.mult), ["pc_ei"], ["PA"])
                V("dve", call("tensor_copy", out=PA2[:, 1, :, k_], in_=pc_ei[:, :, idx_]), ["pc_ei"], ["PA"])

            def v4(t):
                return t[:, :, :].rearrange("p g (i h) -> p g i h", i=8)

            def bt(T, e0):
                return T[:, :, e0:e0 + 8].unsqueeze(3).to_broadcast([128, 16, 8, 16])

            def bv(Vv):
                return Vv[:, :, :].unsqueeze(2).to_broadcast([128, 16, 8, 16])

            def cplx(outr, outi, Tr, Ti, e0, Vr, Vi, keys_t, keys_v, neg_im=False, only=None):
                kr, ki_ = keys_t
                vr, vi = keys_v
                if outr is not None:
                    okr = outr[1]
                    V("dve", call("tensor_tensor", out=v4(outr[0]), in0=bt(Tr, e0), in1=bv(Vr), op=ALU.mult), [kr, vr], [okr])
                    V("dve", call("tensor_tensor", out=v4(pc_E), in0=bt(Ti, e0), in1=bv(Vi), op=ALU.mult), [ki_, vi], ["pc_E"])
                    V("dve", call("tensor_tensor", out=outr[0][:, :, :], in0=outr[0][:, :, :], in1=pc_E[:, :, :], op=ALU.subtract), [okr, "pc_E"], [okr])
                if outi is not None:
                    oki = outi[1]
                    V("dve", call("tensor_tensor", out=v4(outi[0]), in0=bt(Tr, e0), in1=bv(Vi), op=ALU.mult), [kr, vi], [oki])
                    V("dve", call("tensor_tensor", out=v4(pc_E), in0=bt(Ti, e0), in1=bv(Vr), op=ALU.mult), [ki_, vr], ["pc_E"])
                    if neg_im:
                        V("dve", call("scalar_tensor_tensor", out=outi[0][:, :, :], in0=outi[0][:, :, :], scalar=-1.0, in1=pc_E[:, :, :], op0=ALU.mult, op1=ALU.subtract), [oki, "pc_E"], [oki])
                    else:
                        V("dve", call("tensor_tensor", out=outi[0][:, :, :], in0=outi[0][:, :, :], in1=pc_E[:, :, :], op=ALU.add), [oki, "pc_E"], [oki])

            cplx((pc_A, "pc_A"), (pc_B, "pc_B"), pc_er, pc_ei, 1, pc_cr, pc_ci, ("pc_er", "pc_ei"), ("pc_cr", "pc_ci"), neg_im=True)
            V("dve", call("tensor_copy", out=W3[:, :, 0, :], in_=pc_A[:, :, :]), ["pc_A"], ["W3"])
            V("dve", call("tensor_copy", out=W3[:, :, 1, :], in_=pc_B[:, :, :]), ["pc_B"], ["W3"])
            cplx((pc_A, "pc_A"), (pc_B, "pc_B"), pc_er, pc_ei, 0, pc_cr, pc_ci, ("pc_er", "pc_ei"), ("pc_cr", "pc_ci"))
            cplx((pc_C, "pc_C"), (pc_D, "pc_D"), pc_pr, pc_pi, 9, pc_br, pc_bi, ("pc_pr", "pc_pi"), ("pc_br", "pc_bi"), neg_im=True)
            for g in range(32):
                hfp, gl = g // 16, g % 16
                pr = slice(hfp * 64, hfp * 64 + 64)
                pi = psum("A")
                add("pe", call("matmul", ps[pi][:, 0:128], lhsT=pc_C[pr, gl, :], rhs=pc_A[pr, gl, :], start=True, stop=False),
                    r=["pc_C", "pc_A"], w=[pk(pi)], x=True)
                add("pe", call("matmul", ps[pi][:, 0:128], lhsT=pc_D[pr, gl, :], rhs=pc_B[pr, gl, :], start=False, stop=True),
                    r=["pc_D", "pc_B"], w=[pk(pi)], x=True)
                add("dve", call("tensor_tensor", out=ps[pi][:, 128:256], in0=ps[pi][:, 0:128], in1=pc_w4t[:, :], op=ALU.mult),
                    r=[pk(pi), "maskw4"], w=[pk(pi)], x=True)
                add("dve", call("scalar_tensor_tensor", out=W4[:, g, :], in0=ident_f[:], scalar=pc_dcol[:, g:g + 1], in1=ps[pi][:, 128:256], op0=ALU.mult, op1=ALU.add),
                    r=[pk(pi), "ident_f", "pc_dcol"], w=["W4"], x=True)
            cplx((pc_C, "pc_C"), (pc_D, "pc_D"), pc_pr, pc_pi, 17, pc_br, pc_bi, ("pc_pr", "pc_pi"), ("pc_br", "pc_bi"))
            for g in range(32):
                hfp, gl = g // 16, g % 16
                pr = slice(hfp * 64, hfp * 64 + 64)
                pi = psum("B")
                for ri, (T, tk) in enumerate(((pc_C, "pc_C"), (pc_D, "pc_D"))):
                    add("pe", call("transpose", out=ps[pi][:, ri * 64:(ri + 1) * 64], in_=T[pr, gl, :], identity=ident_f[pr, pr]),
                        r=[tk, "ident_f"], w=[pk(pi)], x=True)
                copy_op(evac_eng(), W1[:, g, :], ps[pi][:, 0:128], r=[pk(pi)], w=["W1"], x=True)

        def mixer(kind, c0, n, pass_idx):
            sample = (kind == "sample")
            ntile = 1 if sample else n // 128
            ntok = 64 if sample else 128
            L = 4 if sample else 64
            nch = ntok // L
            ncc = 16 if sample else n // 8
            J = 4 if sample else 8
            TRI = tri4 if sample else tri64
            TRIU = triu4 if sample else triu64
            SEL = sel4 if sample else sel64
            rmsnorm(1, [(c0, n)], xn, "xn")
            if stage < 1:
                return
            v8 = lambda t: t[:].rearrange("p (k f) -> p k f", k=8)
            winv = w_in_d.rearrange("(k p) f -> p k f", p=128)

            def wl_in(col0, ncol):
                return wload(lambda t: t[:, 0:8 * ncol].rearrange("p (k f) -> p k f", k=8), winv[:, :, col0:col0 + ncol])

            def vin(s_, ncol):
                return wslot[s_][:, 0:8 * ncol].rearrange("p (k f) -> p k f", k=8)

            rot_ovr[0] = {"A": [0, 1, 2, 3], "B": [4, 5], "C": [4, 5], "M": [6, 7], "U": [2, 3]}
            if sample:
                add("dve", call("memset", ucm_flat[:, :], 0.0), w=["ucm"], x=True)
            for gq in range(2):
                s_ = wl_in(gq * 256, 256)
                for j in range(J):
                    pi = psum("A")
                    if sample:
                        lsel = lambda k, j=j: xn[:, k, c0 + j:c0 + n:4]
                    else:
                        lsel = lambda k, j=j: xn[:, k, c0 + j:c0 + n:8]
                    for k in range(8):
                        add("pe", call("matmul",
                            ps[pi][0:ncc, 0:256], lhsT=lsel(k), rhs=vin(s_, 256)[:, k, :], start=(k == 0), stop=(k == 7)),
                            r=["ws%d" % s_, "xn%d" % k], w=[pk(pi)], x=True)
                    copy_op(evac_eng(), ucm[0:ncc, gq * 16:(gq + 1) * 16, j, :], ps[pi][0:ncc, 0:256].rearrange("c (g h) -> c g h", g=16), r=[pk(pi)], w=["ucm"])
            P.capture_begin()
            rot_ovr[0] = {"A": [0, 1, 4], "B": [2, 3, 5], "C": [4], "M": [5]}
            ws_ovr[0] = [0, 1, 2]
            s_q = wl_in(512, 256)
            for m in range(2):
                pi = psum("A")
                for k in range(8):
                    add("pe", call("matmul", ps[pi][:, 0:n], lhsT=vin(s_q, 256)[:, k, m * 128:(m + 1) * 128],
                                                                   rhs=xn[:, k, c0:c0 + n], start=(k == 0), stop=(k == 7)),
                        r=["ws%d" % s_q, "xn%d" % k], w=[pk(pi)], x=True)
                copy_op(evac_eng(), qT[:, m, 0:n], ps[pi][:, 0:n], r=[pk(pi)], w=["qT%d" % m])
            s_k = wl_in(768, 256)
            for m in range(2):
                pi = psum("A")
                for k in range(8):
                    add("pe", call("matmul", ps[pi][:, 0:n], lhsT=vin(s_k, 256)[:, k, m * 128:(m + 1) * 128],
                                                                   rhs=xn[:, k, c0:c0 + n], start=(k == 0), stop=(k == 7)),
                        r=["ws%d" % s_k, "xn%d" % k], w=[pk(pi)], x=True)
                copy_op(evac_eng(), kT[:, m, 0:n], ps[pi][:, 0:n], r=[pk(pi)], w=["kT%d" % m])
            for tt in range(ntile):
                pi = psum("B")
                for k in range(8):
                    add("pe", call("matmul", ps[pi][0:ntok, 0:256], lhsT=xn[:, k, c0 + tt * ntok:c0 + (tt + 1) * ntok],
                                                                     rhs=vin(s_k, 256)[:, k, :], start=(k == 0), stop=(k == 7)),
                        r=["ws%d" % s_k, "xn%d" % k], w=[pk(pi)], x=True)
                copy_op(evac_eng(), k_tm[0:ntok, tt, :], ps[pi][0:ntok, 0:256], r=[pk(pi)], w=["k_tm%d" % tt])
            for gq in range(2):
                s_ = wl_in(1024 + gq * 256, 256)
                for tt in range(ntile):
                    pi = psum("B")
                    for k in range(8):
                        add("pe", call("matmul", ps[pi][0:ntok, 0:256], lhsT=xn[:, k, c0 + tt * ntok:c0 + (tt + 1) * ntok],
                                                                                rhs=vin(s_, 256)[:, k, :], start=(k == 0), stop=(k == 7)),
                            r=["ws%d" % s_, "xn%d" % k], w=[pk(pi)], x=True)
                    copy_op(evac_eng(), v_tm[0:ntok, tt, gq * 256:(gq + 1) * 256], ps[pi][0:ntok, 0:256], r=[pk(pi)], w=["v_tm%d_%d" % (tt, gq)])
            for gq in range(2):
                s_ = wl_in(1536 + gq * 256, 256)
                for mm in range(2):
                    m = gq * 2 + mm
                    pi = psum("A")
                    for k in range(8):
                        add("pe", call("matmul", ps[pi][:, 0:n], lhsT=vin(s_, 256)[:, k, mm * 128:(mm + 1) * 128],
                                                                                rhs=xn[:, k, c0:c0 + n], start=(k == 0), stop=(k == 7)),
                            r=["ws%d" % s_, "xn%d" % k], w=[pk(pi)], x=True)
                    add("act", call("activation", out=sgT[:, m, 0:n], in_=ps[pi][:, 0:n], func=AF.Silu),
                        r=[pk(pi)], w=["sgT%d" % m], x=True)
            s_a = wl_in(2048, 16)
            pi = psum("A")
            for k in range(8):
                add("pe", call("matmul", ps[pi][0:16, 0:n], lhsT=vin(s_a, 16)[:, k, :], rhs=xn[:, k, c0:c0 + n],
                                                          start=(k == 0), stop=(k == 7)),
                    r=["ws%d" % s_a, "xn%d" % k], w=[pk(pi)], x=True)
            copy_op(evac_eng(), aT[0:16, 0:n], ps[pi][0:16, 0:n], r=[pk(pi)], w=["aT"])
            for tt in range(ntile):
                pi = psum("B")
                add("pe", call("matmul", ps[pi][0:ntok, 0:256], lhsT=aT[0:16, tt * ntok:(tt + 1) * ntok], rhs=gatew_b[:, :], start=True, stop=True),
                    r=["aT", "gatew_b"], w=[pk(pi)], x=True)
                add("act", call("activation", out=etmp[0:ntok, :], in_=ps[pi][0:ntok, 0:256], func=AF.Exp, scale=-1.0),
                    r=[pk(pi)], w=["etmp"], x=True)
                add("dve", call("tensor_tensor", out=etmp[0:ntok, :], in0=etmp[0:ntok, :], in1=gateb[0:ntok, :], op=ALU.mult),
                    r=["etmp", "gateb"], w=["etmp"], x=True)
                add("act", call("activation", out=lf[0:ntok, tt, :], in_=etmp[0:ntok, :], func=AF.Ln, bias=1.0),
                    r=["etmp"], w=["lf%d" % tt], x=True)

            rot_ovr[0] = {"A": [0], "B": [1], "C": [2, 3], "U": [4, 5], "M": [1]}
            pOs, pUs = {}, {}

            def gla_part1(tt):
                bi = tt % 2
                eb = ebb[bi]
                ebk = "eb%d" % bi
                tc0 = tt * ntok
                pC = psum("A")
                for m in range(2):
                    add("pe", call("matmul", ps[pC][:, m * 128:m * 128 + ntok], lhsT=lf[0:ntok, tt, m * 128:(m + 1) * 128],
                                   rhs=TRI[0:ntok, 0:ntok], start=True, stop=True), r=["lf%d" % tt, "const"], w=[pk(pC)], x=True)
                pR = psum("B")
                add("pe", call("matmul", ps[pR][0:ntok, 0:256], lhsT=TRIU[0:ntok, 0:ntok], rhs=lf[0:ntok, tt, :], start=True, stop=True),
                    r=["lf%d" % tt, "const"], w=[pk(pR)], x=True)
                pCv = ps[pC][:, 0:256].rearrange("p (m t) -> p m t", m=2)[:, :, 0:ntok]
                add("act", call("activation", out=eb[:, :, 0:ntok], in_=pCv, func=AF.Exp, scale=-1.0 / 16.0), r=[pk(pC)], w=[ebk], x=True)
                add("act", call("activation", out=enb[bi][:, :, 0:ntok], in_=pCv, func=AF.Exp, scale=1.0 / 16.0), r=[pk(pC)], w=["enb%d" % bi], x=True)
                add("act", call("activation", out=erem[bi][0:ntok, :], in_=ps[pR][0:ntok, 0:256], func=AF.Exp, scale=-1.0 / 16.0),
                    r=[pk(pR)], w=["erem%d" % bi], x=True)
                add("dve", call("scalar_tensor_tensor", out=qs[bi][:, :, 0:ntok], in0=qT[:, :, tc0:tc0 + ntok], scalar=0.125, in1=eb[:, :, 0:ntok],
                                op0=ALU.mult, op1=ALU.mult), r=["qT0", "qT1", ebk], w=["qs%d" % bi], x=True)
                add("dve", call("tensor_tensor", out=ki[bi][:, :, 0:ntok], in0=kT[:, :, tc0:tc0 + ntok], in1=enb[bi][:, :, 0:ntok], op=ALU.mult),
                    r=["kT0", "kT1", "enb%d" % bi], w=["ki%d" % bi], x=True)
                add("dve", call("tensor_tensor", out=kend[bi][0:ntok, :], in0=k_tm[0:ntok, tt, :], in1=erem[bi][0:ntok, :], op=ALU.mult),
                    r=["k_tm%d" % tt, "erem%d" % bi], w=["kend%d" % bi], x=True)
                pA = psum("A")
                for h in range(4):
                    m = h // 2
                    add("dve", call("tensor_scalar", out=kiz[bi][:, h, 0:ntok], in0=ki[bi][:, m, 0:ntok], scalar1=sel64[:, (h % 2):(h % 2) + 1], scalar2=None, op0=ALU.mult),
                        r=["ki%d" % bi, "const"], w=["kiz%d" % bi], x=True)
                for h in range(4):
                    m = h // 2
                    add("pe", call("matmul", ps[pA][0:ntok, h * 128:h * 128 + ntok], lhsT=kiz[bi][:, h, 0:ntok], rhs=qs[bi][:, m, 0:ntok], start=True, stop=True),
                        r=["kiz%d" % bi, "qs%d" % bi], w=[pk(pA)], x=True)
                for h in range(4):
                    add("dve", call("tensor_tensor", out=attT[bi][0:ntok, h, 0:ntok], in0=ps[pA][0:ntok, h * 128:h * 128 + ntok], in1=TRI[0:ntok, 0:ntok], op=ALU.mult),
                        r=[pk(pA), "const"], w=["attT%d" % bi], x=True)
                pO = psum("C")
                pOs[tt] = pO
                for h in range(4):
                    add("pe", call("matmul", ps[pO][:, h * 128:h * 128 + ntok], lhsT=v_tm[0:ntok, tt, h * 128:(h + 1) * 128],
                                   rhs=attT[bi][0:ntok, h, 0:ntok], start=(h == 0), stop=False, skip_group_check=True),
                        r=["v_tm%d_%d" % (tt, h // 2), "attT%d" % bi], w=[pk(pO)], x=True)
                if not sample:
                    pU = psum("U")
                    pUs[tt] = pU
                    for c in range(nch):
                        km = kendm[bi * 2 + c]
                        kmk = "kendm%d" % (bi * 2 + c)
                        add("dve", call("tensor_scalar", out=km[0:ntok, :], in0=kend[bi][0:ntok, :], scalar1=SEL[0:ntok, c:c + 1], scalar2=None, op0=ALU.mult),
                            r=["kend%d" % bi, "const"], w=[kmk], x=True)
                        for h in range(4):
                            m, po = h // 2, 64 * (h % 2)
                            add("pe", call("matmul", ps[pU][po:po + 64, c * 256 + m * 128:c * 256 + (m + 1) * 128], lhsT=km[0:ntok, h * 64:(h + 1) * 64],
                                           rhs=v_tm[0:ntok, tt, h * 128:(h + 1) * 128], start=True, stop=True),
                                r=[kmk, "v_tm%d_%d" % (tt, h // 2)], w=[pk(pU)], x=True)

            def gla_part2(tt):
                bi = tt % 2
                eb = ebb[bi]
                ebk = "eb%d" % bi
                tc0 = tt * ntok
                pO = pOs[tt]
                for c in range(nch):
                    cb = c % 2
                    if sample:
                        for m in range(2):
                            dma("sp", S0b[cb][:, m, :], gla_in[c, 2 * m:2 * m + 2, :, :].rearrange("h d e -> (h d) e"),
                                w=["S0b%d" % cb], semkey="S0b%d" % cb, x=True)
                        Ssrc, Skey = S0b[cb], "S0b%d" % cb
                    else:
                        Ssrc, Skey = S_f, "S_f"
                    Szc = Sz[cb]
                    Szk = "Sz%d" % cb
                    for h in range(4):
                        m = h // 2
                        add("act", call("activation", out=Szc[:, h, :], in_=Ssrc[:, m, :], func=AF.Identity, scale=sel64[:, (h % 2):(h % 2) + 1]),
                            r=[Skey, "const"], w=[Szk], x=True)
                    for h in range(4):
                        m = h // 2
                        last = (c == nch - 1) and (h == 3)
                        add("pe", call("matmul", ps[pO][:, h * 128 + c * L:h * 128 + (c + 1) * L], lhsT=Szc[:, h, :],
                                       rhs=qs[bi][:, m, c * L:(c + 1) * L], start=False, stop=last, skip_group_check=True),
                            r=[Szk, "qs%d" % bi], w=[pk(pO)], x=True)
                    if sample:
                        km = kendm[cb]
                        kmk = "kendm%d" % cb
                        add("dve", call("tensor_scalar", out=km[0:ntok, :], in0=kend[bi][0:ntok, :], scalar1=SEL[0:ntok, c:c + 1], scalar2=None, op0=ALU.mult),
                            r=["kend%d" % bi, "const"], w=[kmk], x=True)
                        pU = psum("U")
                        ucol = 0
                        for h in range(4):
                            m, po = h // 2, 64 * (h % 2)
                            add("pe", call("matmul", ps[pU][po:po + 64, m * 128:(m + 1) * 128], lhsT=km[0:ntok, h * 64:(h + 1) * 64],
                                           rhs=v_tm[0:ntok, tt, h * 128:(h + 1) * 128], start=True, stop=True),
                                r=[kmk, "v_tm%d_%d" % (tt, h // 2)], w=[pk(pU)], x=True)
                        Sdst, Sdk = Sob[cb], "Sob%d" % cb
                    else:
                        pU = pUs[tt]
                        ucol = c * 256
                        Sdst, Sdk = S_f, "S_f"
                    col_last = c * L + L - 1
                    for m in range(2):
                        add("dve", call("scalar_tensor_tensor", out=Sdst[:, m, :], in0=Ssrc[:, m, :], scalar=eb[:, m, col_last:col_last + 1],
                                        in1=ps[pU][:, ucol + m * 128:ucol + (m + 1) * 128], op0=ALU.mult, op1=ALU.add),
                            r=[Skey, ebk, pk(pU)], w=[Sdk], x=True)
                    if sample:
                        for m in range(2):
                            dma("sp", sgla_o[c, 2 * m:2 * m + 2, :, :].rearrange("h d e -> (h d) e"), Sob[cb][:, m, :],
                                r=["Sob%d" % cb], semkey="out", x=True)
                pOv = ps[pO][:, :].rearrange("p (h t) -> p h t", h=4)[:, :, 0:ntok]
                o_v = o_sb[:, 0:4 * ntok].rearrange("p (h t) -> p h t", h=4)
                add("act", call("copy", out=o_v, in_=pOv), r=[pk(pO)], w=["o_sb"], x=True)
                add("act", call("activation", out=osq[:, 0:4 * ntok], in_=o_sb[:, 0:4 * ntok], func=AF.Square), r=["o_sb"], w=["osq"], x=True)
                pN = psum("M")
                add("pe", call("matmul", ps[pN][:, 0:4 * ntok], lhsT=ones_b[:], rhs=osq[:, 0:4 * ntok], start=True, stop=True),
                    r=["osq", "ones_b"], w=[pk(pN)], x=True)
                add("act", call("activation", out=sdm[:, 0:4 * ntok], in_=ps[pN][:, 0:4 * ntok], func=AF.Ln, bias=EPS, scale=1.0 / 128.0),
                    r=[pk(pN)], w=["sdm"], x=True)
                add("act", call("activation", out=ps[pN][:, 0:4 * ntok], in_=sdm[:, 0:4 * ntok], func=AF.Exp, scale=-0.5), r=["sdm"], w=[pk(pN)], x=True)
                add("dve", call("scalar_tensor_tensor", out=t1b[:, 0:4 * ntok], in0=o_sb[:, 0:4 * ntok], scalar=gnorm[:, 0:1], in1=ps[pN][:, 0:4 * ntok],
                                op0=ALU.mult, op1=ALU.mult), r=["o_sb", "gnorm", pk(pN)], w=["t1b"], x=True)
                t1v = t1b[:, 0:4 * ntok].rearrange("p (h t) -> p h t", h=4)
                add("dve", call("tensor_tensor", out=cat[:, 4:8, tc0:tc0 + ntok], in0=t1v, in1=sgT[:, :, tc0:tc0 + ntok], op=ALU.mult),
                    r=["t1b", "sgT0", "sgT1", "sgT2", "sgT3"], w=["cat_o%d" % tt], x=True)

            gla_part1(0)
            for tt in range(1, ntile):
                gla_part1(tt)
                gla_part2(tt - 1)
            gla_part2(ntile - 1)
            if (not sample) and pass_idx == npass - 1:
                for m in range(2):
                    dma("sp", pgla_o[2 * m:2 * m + 2, :, :].rearrange("h d e -> (h d) e"), S_f[:, m, :], r=["S_f"], semkey="out")

            listA = P.capture_end()
            P.capture_begin()
            rot_ovr[0] = {"A": [6], "B": [7], "C": [6, 7], "M": [7]}
            ws_ovr[0] = [3]
            for gq in range(4):
                pi = psum("A")
                pbf = ps[pi][:].bitcast(BF16)
                for gg in range(8):
                    g = gq * 8 + gg
                    add("pe", call("transpose", out=pbf[:, gg * 64:gg * 64 + ncc], in_=ucm[0:ncc, g, :, :].rearrange("c j h -> c (j h)"),
                                                                          identity=ident_b[0:ncc, 0:ncc]),
                        r=["ucm", "ident_b"], w=[pk(pi)], x=True)
                src = pbf[:, 0:512].rearrange("p (g c) -> p g c", g=8)[:, :, 0:ncc]
                copy_op(evac_eng(), Ug[:, gq * 8:(gq + 1) * 8, 0:ncc], src, r=[pk(pi)], w=["Ug"])
            Bdst = Ssm if sample else Bst
            Bdk = "Ssm" if sample else "Bst"
            for q in range(4):
                pi = psum("B")
                for hfp in range(2):
                    for g4 in range(4):
                        gl = q * 4 + g4
                        g = hfp * 16 + gl
                        for ri in range(2):
                            col = (ri * 4 + g4) * 64
                            add("pe", call("matmul",
                                ps[pi][hfp * 64:hfp * 64 + 64, col:col + ncc], lhsT=W1[:, g, ri * 64:(ri + 1) * 64], rhs=Ug[:, g, 0:ncc],
                                start=True, stop=True), r=["W1", "Ug"], w=[pk(pi)], x=True)
                src = ps[pi][:, :].rearrange("p (r g c) -> p r g c", r=2, g=4)[:, :, :, 0:ncc]
                if sample:
                    dst = Ssm[:, :, q * 4:(q + 1) * 4, 0:ncc]
                else:
                    dst = Bst[:, :, q * 4:(q + 1) * 4, 1:1 + ncc]
                copy_op(evac_eng(), dst, src, r=[pk(pi)], w=[Bdk])
            if sample:
                pcs = 0
                for ri, src_d in enumerate((s5re_in, s5im_in)):
                    srcv = src_d.rearrange("b (a g) p -> b g a p", a=2)
                    pi = psum("A")
                    for q4 in range(4):
                        bb = pcs % 2
                        pcs += 1
                        for g4 in range(4):
                            dma("sp", h0t[bb][g4 * 16:(g4 + 1) * 16, :, :], srcv[:, q4 * 4 + g4, :, :], w=["h0t%d" % bb], semkey="h0t%d" % bb, x=True)
                        add("pe", call("transpose", out=ps[pi][:, q4 * 64:(q4 + 1) * 64], in_=h0t[bb][0:64, :, :].rearrange("r a p -> r (a p)"),
                                       identity=ident_f[0:64, 0:64]), r=["h0t%d" % bb, "ident_f"], w=[pk(pi)], x=True)
                    copy_op("dve", Bsm[:, ri, :, :], ps[pi][:, 0:256].rearrange("p (g b) -> p g b", g=16), r=[pk(pi)], w=["Bsm"])
                bcb = lambda a: a.unsqueeze(3).to_broadcast([128, 2, 16, 16])
                sw = lambda t: (t[:, 1, :, :], t[:, 0, :, :])
                T0, T1 = sm_t

                def cmul(dst, dk, src, sk, cr_, ci_neg_pos, ck):
                    add("dve", call("tensor_tensor", out=dst[:, :, :, :], in0=src[:, :, :, :], in1=bcb(cr_[:, :, :]), op=ALU.mult), r=[sk, ck], w=[dk], x=True)
                    add("dve", call("tensor_tensor", out=T1[:, 0, :, :], in0=src[:, 1, :, :], in1=ci_neg_pos[:, 0, :].unsqueeze(2).to_broadcast([128, 16, 16]), op=ALU.mult), r=[sk, ck], w=["smT1"], x=True)
                    add("dve", call("tensor_tensor", out=T1[:, 1, :, :], in0=src[:, 0, :, :], in1=ci_neg_pos[:, 1, :].unsqueeze(2).to_broadcast([128, 16, 16]), op=ALU.mult), r=[sk, ck], w=["smT1"], x=True)
                    add("dve", call("tensor_tensor", out=dst[:, :, :, :], in0=dst[:, :, :, :], in1=T1[:, :, :, :], op=ALU.add), r=[dk, "smT1"], w=[dk], x=True)

                cmul(T0, "smT0", Bsm, "Bsm", A1, A2, "A1A2")
                add("dve", call("tensor_tensor", out=T0[:, :, :, :], in0=T0[:, :, :, :], in1=Ssm[:, :, :, :], op=ALU.add), r=["smT0", "Ssm"], w=["smT0"], x=True)
                add("dve", call("tensor_copy", out=sc_t1[:, 0, :], in_=L4[:, 0, :]), r=["L4"], w=["sc_t1"], x=True)
                add("dve", call("tensor_copy", out=sc_t1[:, 1, :], in_=L4[:, 0, :]), r=["L4"], w=["sc_t1"], x=True)
                add("dve", call("tensor_scalar", out=sc_t2[:, 0, :], in0=L4[:, 1, :], scalar1=-1.0, scalar2=None, op0=ALU.mult), r=["L4"], w=["sc_t2"], x=True)
                add("dve", call("tensor_copy", out=sc_t2[:, 1, :], in_=L4[:, 1, :]), r=["L4"], w=["sc_t2"], x=True)
                cmul(Hs3, "Hs3", T0, "smT0", sc_t1, sc_t2, "sc_t2")
                pcs = 0
                for ri, dst_d in enumerate((sre_o, sim_o)):
                    dstv = dst_d.rearrange("b (a g) p -> b g a p", a=2)
                    for q4 in range(4):
                        bb = pcs % 2
                        pcs += 1
                        pi = psum("B")
                        add("pe", call("transpose", out=ps[pi][0:64, 0:128], in_=Hs3[:, ri, q4 * 4:(q4 + 1) * 4, :].rearrange("p g b -> p (g b)"), identity=ident_f[:]),
                            r=["Hs3", "ident_f"], w=[pk(pi)], x=True)
                        copy_op(evac_eng(), hot[bb][0:64, :, :], ps[pi][0:64, 0:128].rearrange("r (a p) -> r a p", a=2), r=[pk(pi)], w=["hot%d" % bb])
                        for g4 in range(4):
                            dma("sp", dstv[:, q4 * 4 + g4, :, :], hot[bb][g4 * 16:(g4 + 1) * 16, :, :], r=["hot%d" % bb], semkey="out", x=True)
            else:
                nsb = ncc // 8
                bc8 = lambda a: a.unsqueeze(3).to_broadcast([128, 2, 16, nsb])
                bc8h = lambda a: a.unsqueeze(2).to_broadcast([128, 16, nsb])
                for j in range(1, 8):
                    src = Bst[:, :, :, j:j + 8 * (nsb - 1) + 1:8]
                    dst = Bst[:, :, :, j + 1:j + 1 + 8 * (nsb - 1) + 1:8]
                    add("dve", call("tensor_tensor", out=sct1[:, :, :, 0:nsb], in0=src, in1=bc8(A1[:, :, :]), op=ALU.mult), r=["Bst", "A1A2"], w=["sct1"], x=True)
                    add("dve", call("tensor_tensor", out=sct2[:, 0, :, 0:nsb], in0=Bst[:, 1, :, j:j + 8 * (nsb - 1) + 1:8], in1=bc8h(A2[:, 0, :]), op=ALU.mult), r=["Bst", "A1A2"], w=["sct2"], x=True)
                    add("dve", call("tensor_tensor", out=sct2[:, 1, :, 0:nsb], in0=Bst[:, 0, :, j:j + 8 * (nsb - 1) + 1:8], in1=bc8h(A2[:, 1, :]), op=ALU.mult), r=["Bst", "A1A2"], w=["sct2"], x=True)
                    add("dve", call("tensor_tensor", out=dst, in0=dst, in1=sct1[:, :, :, 0:nsb], op=ALU.add), r=["Bst", "sct1"], w=["Bst"], x=True)
                    add("dve", call("tensor_tensor", out=dst, in0=dst, in1=sct2[:, :, :, 0:nsb], op=ALU.add), r=["Bst", "sct2"], w=["Bst"], x=True)
                for sbk in range(nsb):
                    car = Bst[:, :, :, 8 * sbk]
                    dst = Bst[:, :, :, 8 * sbk + 1:8 * sbk + 9]
                    add("dve", call("tensor_tensor", out=sct1[:, :, :, 0:8], in0=PA1[:, :, :, :], in1=car.unsqueeze(3).to_broadcast([128, 2, 16, 8]), op=ALU.mult), r=["Bst", "PA"], w=["sct1"], x=True)
                    add("dve", call("tensor_tensor", out=sct2[:, 0, :, 0:8], in0=PA2[:, 0, :, :], in1=Bst[:, 1, :, 8 * sbk].unsqueeze(2).to_broadcast([128, 16, 8]), op=ALU.mult), r=["Bst", "PA"], w=["sct2"], x=True)
                    add("dve", call("tensor_tensor", out=sct2[:, 1, :, 0:8], in0=PA2[:, 1, :, :], in1=Bst[:, 0, :, 8 * sbk].unsqueeze(2).to_broadcast([128, 16, 8]), op=ALU.mult), r=["Bst", "PA"], w=["sct2"], x=True)
                    add("dve", call("tensor_tensor", out=dst, in0=dst, in1=sct1[:, :, :, 0:8], op=ALU.add), r=["Bst", "sct1"], w=["Bst"], x=True)
                    add("dve", call("tensor_tensor", out=dst, in0=dst, in1=sct2[:, :, :, 0:8], op=ALU.add), r=["Bst", "sct2"], w=["Bst"], x=True)
            for hz in range(2):
                hsrc, hkey = (Bsm[:, :, :, 0:ncc], "Bsm") if sample else (Bst[:, :, :, 0:ncc], "Bst")
                add("dve", call("tensor_scalar", out=Hbz[hz][:, :, :, 0:ncc], in0=hsrc, scalar1=sel64[:, hz:hz + 1], scalar2=None, op0=ALU.mult),
                    r=[hkey, "const"], w=["Hbz"], x=True)
            for gq in range(8):
                pi = psum("C")
                for g4 in range(4):
                    g = gq * 4 + g4
                    hfp, gl = g // 16, g % 16
                    pr = slice(hfp * 64, hfp * 64 + 64)
                    osl = ps[pi][0:ncc, g4 * 128:(g4 + 1) * 128]
                    add("pe", call("matmul", osl, lhsT=Hbz[hfp][:, 0, gl, 0:ncc], rhs=W3[:, gl, 0, :], start=True, stop=False),
                        r=["Hbz", "W3"], w=[pk(pi)], x=True)
                    add("pe", call("matmul", osl, lhsT=Hbz[hfp][:, 1, gl, 0:ncc], rhs=W3[:, gl, 1, :], start=False, stop=False),
                        r=["Hbz", "W3"], w=[pk(pi)], x=True)
                    add("pe", call("matmul", osl, lhsT=Ug[:, g, 0:ncc], rhs=W4[:, g, :], start=False, stop=True),
                        r=["Ug", "W4"], w=[pk(pi)], x=True)
                src = ps[pi][0:ncc, :].rearrange("c (g j h) -> c j g h", g=4, j=8)
                dst = zcm[0:ncc, :, gq * 64:(gq + 1) * 64].rearrange("c j (g h) -> c j g h", g=4)
                add("act", call("activation", out=dst, in_=src, func=AF.Gelu_apprx_tanh), r=[pk(pi), "Ug"], w=["ucm"], x=True)
            if (not sample) and pass_idx == npass - 1:
                for hfp in range(2):
                    pr = slice(hfp * 64, hfp * 64 + 64)
                    gs = slice(hfp * 16, hfp * 16 + 16)
                    dma("sp", pre_o[gs, :].rearrange("g p -> p g"), Bst[pr, 0, :, ncc], r=["Bst"], semkey="out", slow=True)
                    dma("sp", pim_o[gs, :].rearrange("g p -> p g"), Bst[pr, 1, :, ncc], r=["Bst"], semkey="out", slow=True)
            if not sample:
                add("dve", call("tensor_copy", out=Bst[:, :, :, 0], in_=Bst[:, :, :, ncc]), r=["Bst"], w=["Bst"], x=True)
            for m in range(4):
                pi = psum("A")
                pbf = ps[pi][:].bitcast(BF16)
                for j in range(J):
                    add("pe", call("transpose", out=pbf[:, j * 64:j * 64 + ncc], in_=zcm[0:ncc, j, m * 128:(m + 1) * 128],
                                                                        identity=ident_b[0:ncc, 0:ncc]),
                        r=["ucm", "ident_b"], w=[pk(pi)], x=True)
                src = pbf[:, 0:J * 64].rearrange("p (j c) -> p j c", j=J)[:, :, 0:ncc]
                dst = zT[:, m, 0:n].rearrange("p (c j) -> p j c", j=J)
                copy_op(evac_eng(), dst, src, r=[pk(pi)], w=["zT%d" % m])
            s_g = wload(lambda t: t[:].rearrange("p (k f) -> p k f", k=4), glu_w_d.rearrange("(k p) f -> p k f", p=128))
            gv = wslot[s_g][:].rearrange("p (k f) -> p k f", k=4)
            for m in range(4):
                pi = psum("A")
                for k in range(4):
                    add("pe", call("matmul", ps[pi][:, 0:n], lhsT=gv[:, k, m * 128:(m + 1) * 128], rhs=zT[:, k, 0:n], start=(k == 0), stop=(k == 3)),
                        r=["ws%d" % s_g, "zT%d" % k], w=[pk(pi)], x=True)
                b = m % 2
                add("act", call("activation", out=sig[b][:, 0:n], in_=ps[pi][:, 0:n], func=AF.Sigmoid, bias=glub[:, m:m + 1]),
                    r=[pk(pi), "glub"], w=["sig"], x=True)
                add("dve", call("tensor_tensor", out=cat[:, m, 0:n], in0=zT[:, m, 0:n], in1=sig[b][:, 0:n], op=ALU.mult),
                    r=["zT%d" % m, "sig"], w=["cat_z%d" % m], x=True)
            listB = P.capture_end()
            rot_ovr[0] = None
            ws_ovr[0] = None
            P.merge([listA, listB], spans=[MERGE_SPAN_A, 1.0])
            woutv = w_out_d.rearrange("(k p) f -> p k f", p=128)
            for dp in range(4):
                s_ = wload(v8, woutv[:, :, dp * 256:(dp + 1) * 256])
                for dd in range(2):
                    d = dp * 2 + dd
                    pi = psum("C")
                    for k in range(8):
                        add("pe", call("matmul", ps[pi][:, 0:n], lhsT=v8(wslot[s_])[:, k, dd * 128:(dd + 1) * 128], rhs=cat[:, k, 0:n],
                                                                                start=(k == 0), stop=(k == 7)),
                            r=["ws%d" % s_, ("cat_z%d" % k) if k < 4 else None] + (["cat_o%d" % t_ for t_ in range(ntile)] if k >= 4 else []), w=[pk(pi)], x=True)
                    add("dve", call("tensor_tensor", out=xT[:, d, c0:c0 + n], in0=ps[pi][:, 0:n], in1=xT[:, d, c0:c0 + n], op=ALU.add),
                        r=[pk(pi), "xT%d" % d], w=["xT%d" % d], x=True)

        P.ops
        add("dve", call("memset", scr[:, 1:2], 0.0), r=["tri64", "triu64", "tri4", "triu4", "sel64", "sel4"], w=["const"])
        side_pc = None
        if do_mixer:
            P.capture_begin()
            rot_ovr[0] = {"A": [6, 7], "B": [6, 7], "C": [6, 7], "M": [6, 7]}
            precompute()
            add("dve", call("memset", scr[:, 2:3], 0.0), r=["A1", "A2"], w=["A1A2"], x=True)
            side_pc = P.capture_end()
            rot_ovr[0] = None
            if not do_ffn:
                P.ops.extend(side_pc)
                side_pc = None
        pending_store = [None]
        for pidx in range(npass):
            chunks = [(0, PT)]
            if pidx == 0:
                chunks = [(0, (PT + NSMP) // 2), ((PT + NSMP) // 2, (PT + NSMP) // 2)]
            P.capture_begin()
            rot_ovr[0] = {"A": [0, 1], "B": [2, 3], "C": [4, 5], "M": [6], "U": [2, 3]}
            for tt in range(PT // 128):
                r0 = pidx * PT + tt * 128
                load_x(xp[r0:r0 + 128, :], 128, tt * 128)
            if pidx == 0:
                load_x(xs[:, :], NSMP, PT)
            if do_ffn:
                rmsnorm(0, chunks, xn, "xn")
            rot_ovr[0] = None
            l_load = P.capture_end()
            if pending_store[0]:
                P.merge([pending_store[0], l_load])
                pending_store[0] = None
            else:
                P.ops.extend(l_load)
            if do_ffn:
                ffn(*w_ffn[0], chunks, side=(side_pc if pidx == 0 else None))
            if do_mixer:
                barrier()
                mixer("prompt", 0, PT, pidx)
                if pidx == 0:
                    mixer("sample", PT, NSMP, pidx)
                barrier()
            if do_ffn:
                rmsnorm(2, chunks, xn, "xn")
                ffn(*w_ffn[1], chunks)
            rmsnorm(3, chunks, yT, "yT")
            P.capture_begin()
            rot_ovr[0] = {"A": [0, 1], "B": [2, 3], "C": [4, 5], "M": [7], "U": [2, 3]}
            for tt in range(PT // 128):
                r0 = pidx * PT + tt * 128
                store_out(yp[r0:r0 + 128, :], 128, tt * 128)
            if pidx == 0:
                store_out(ys[:, :], NSMP, PT)
            rot_ovr[0] = None
            pending_store[0] = P.capture_end()
        if pending_store[0]:
            P.ops.extend(pending_store[0])
        tapsrc = {"xT": (xT[:], ["xT%d" % q_ for q_ in range(8)]), "yT": (yT[:, :, :], ["yT%d" % q_ for q_ in range(8)]), "sdb": (sdb[:, :], ["sdb"]),
                  "A1": (A1[:], ["A1"]), "A2": (A2[:], ["A2"]), "L4": (L4[:], ["L4"]), "Bst": (Bst[:], ["Bst"]), "S_f": (S_f[:], ["S_f"]),
                  "qT": (qT[:, :, :], ["qT0"]), "kT": (kT[:, :, :], ["kT0"]), "lf": (lf[:, :, :], ["lf0"]), "k_tm": (k_tm[:, :, :], ["k_tm0"]),
                  "xin0": (xin[0][:, :], ["xin0"]), "xin1": (xin[1][:, :], ["xin1"]), "pc_er": (pc_er[:, :, :], ["pc_er"]), "pc_ei": (pc_ei[:, :, :], ["pc_ei"])}
        for (tname, tshape) in taps:
            src, keys = tapsrc[tname]
            dma("sp", tap_out[tname], src, r=keys, semkey="out")
        P.emit(nc, final_semkeys=["out"])
    return nc


def _consts():
    c = {}
    c["c_ident"] = np.eye(128, dtype=np.float32)
    s = np.arange(128)
    same64 = (s[:, None] // 64) == (s[None, :] // 64)
    c["c_tri64"] = (same64 & (s[:, None] <= s[None, :])).astype(np.float32)
    c["c_triu64"] = (same64 & (s[:, None] > s[None, :])).astype(np.float32)
    s4 = np.arange(64)
    same4 = (s4[:, None] // 4) == (s4[None, :] // 4)
    c["c_tri4"] = (same4 & (s4[:, None] <= s4[None, :])).astype(np.float32)
    c["c_triu4"] = (same4 & (s4[:, None] > s4[None, :])).astype(np.float32)
    c["c_sel64"] = (s[:, None] // 64 == np.arange(2)[None, :]).astype(np.float32)
    c["c_sel4"] = (s4[:, None] // 4 == np.arange(16)[None, :]).astype(np.float32)
    i_ = s // 16
    c["c_maskw4"] = (i_[:, None] <= i_[None, :]).astype(np.float32)
    c["c_evals"] = np.array(EVALS, dtype=np.float32)
    return c


_NC_CACHE = {}


def kernel(x_prompt, x_sample, state_s5_re, state_s5_im, state_gla, norm_ffn1, ffn1_gate, ffn1_up,
           ffn1_down, norm_mix, w_in, s5_lam_re, s5_lam_im, s5_log_dt, s5_b_re, s5_b_im, s5_c_re,
           s5_c_im, s5_d, s5_glu_w, s5_glu_b, gla_gate_w, gla_gate_b, gla_norm, w_out, norm_ffn2,
           ffn2_gate, ffn2_up, ffn2_down, norm_final, _npass=NPASS_FULL, _do_mixer=True, _do_ffn=True, _taps=(), _stage=9):
    f = lambda a: np.ascontiguousarray(np.asarray(a, dtype=np.float32))
    key = (_npass, _do_mixer, _do_ffn, str(_taps), _stage)
    if key not in _NC_CACHE:
        _NC_CACHE[key] = build_program(npass=_npass, do_mixer=_do_mixer, do_ffn=_do_ffn, taps=_taps, stage=_stage)
    nc = _NC_CACHE[key]
    shared = {
        "gains": f(np.stack([np.asarray(norm_ffn1)[0], np.asarray(norm_mix)[0], np.asarray(norm_ffn2)[0], np.asarray(norm_final)])),
        "ffn1_gate": f(ffn1_gate[0]), "ffn1_up": f(ffn1_up[0]), "ffn1_down": f(ffn1_down[0]),
        "ffn2_gate": f(ffn2_gate[0]), "ffn2_up": f(ffn2_up[0]), "ffn2_down": f(ffn2_down[0]),
        "w_in": f(w_in[0]), "w_out": f(w_out[0]), "glu_w": f(s5_glu_w[0]), "glu_b": f(s5_glu_b[0]),
        "gate_w": f(gla_gate_w[0]), "gate_b": f(gla_gate_b[0]), "gla_norm": f(gla_norm[0]),
        "lam_re": f(s5_lam_re[0]), "lam_im": f(s5_lam_im[0]), "log_dt": f(s5_log_dt[0]),
        "b_re": f(s5_b_re[0]), "b_im": f(s5_b_im[0]), "c_re": f(s5_c_re[0]), "c_im": f(s5_c_im[0]),
        "s5_d": f(s5_d[0]),
    }
    shared.update(_consts())
    xp_ = np.asarray(x_prompt, dtype=np.float32)
    xs_ = np.asarray(x_sample, dtype=np.float32)
    sre = np.asarray(state_s5_re, dtype=np.float32)[0]
    sim = np.asarray(state_s5_im, dtype=np.float32)[0]
    sgl = np.asarray(state_gla, dtype=np.float32)[0]
    in_maps = []
    for i in range(NCORES):
        m = dict(shared)
        m["xp"] = f(xp_[i])
        m["xs"] = f(xs_[16 * i:16 * i + 16].reshape(NSMP, D))
        m["s5re_in"] = f(sre[16 * i:16 * i + 16])
        m["s5im_in"] = f(sim[16 * i:16 * i + 16])
        m["gla_in"] = f(sgl[16 * i:16 * i + 16])
        in_maps.append(m)
    res = run_bass_kernel_spmd(nc, in_maps, core_ids=list(range(NCORES)))
    R = res.results
    y_prompt = np.stack([R[i]["yp"] for i in range(NCORES)]).astype(np.float32)
    y_sample = np.concatenate([R[i]["ys"].reshape(16, 4, D) for i in range(NCORES)]).astype(np.float32)
    p_re = np.stack([R[i]["pre"] for i in range(NCORES)])[None].astype(np.float32)
    p_im = np.stack([R[i]["pim"] for i in range(NCORES)])[None].astype(np.float32)
    p_gla = np.stack([R[i]["pgla"] for i in range(NCORES)])[None].astype(np.float32)
    s_re = np.concatenate([R[i]["sre"] for i in range(NCORES)])[None].astype(np.float32)
    s_im = np.concatenate([R[i]["sim"] for i in range(NCORES)])[None].astype(np.float32)
    s_gla = np.concatenate([R[i]["sgla"] for i in range(NCORES)])[None].astype(np.float32)
    if _taps:
        return (y_prompt, y_sample, p_re, p_im, p_gla, s_re, s_im, s_gla), {t[0]: R[0]["tap_" + t[0]] for t in _taps}
    return (y_prompt, y_sample, p_re, p_im, p_gla, s_re, s_im, s_gla)
```

```python
import contextlib
import math
import numpy as np
import concourse.bass as bass
import concourse.mybir as mybir
from concourse.bass_utils import run_bass_kernel_spmd

F32 = mybir.dt.float32
BF16 = mybir.dt.bfloat16
I32 = mybir.dt.int32
AF = mybir.ActivationFunctionType
ALU = mybir.AluOpType

ENGS = ["pe", "act", "dve", "pool", "sp"]
EPS = 1e-6
NCORES = 8
D = 1024
DFF = 2816
NFT = 22
SEQ = 2048
PT = 512
NPASS_FULL = SEQ // PT
NSMP = 64
INW = 2064
EVALS = [0, 1, 2, 3, 4, 5, 6, 7, 8,
         0, -1, -2, -3, -4, -5, -6, -7,
         7, 6, 5, 4, 3, 2, 1, 0,
         -4,
         16, 24, 32, 40, 48, 56, 64]
NEV = len(EVALS)
MERGE_SPAN_A = 0.5


class Op:
    __slots__ = ("eng", "fn", "r", "w", "dma", "semkey", "deps", "raw", "need_inc", "incval", "dmaval", "pos", "prewait")


class Prog:
    def __init__(self):
        self.ops = []

    def add(self, eng, fn, r=(), w=(), dma=False, semkey=None):
        op = Op()
        op.eng, op.fn, op.r, op.w, op.dma, op.semkey = eng, fn, list(r), list(w), dma, semkey
        op.deps, op.need_inc, op.incval, op.dmaval = [], False, 0, 0
        op.raw = set()
        op.prewait = 0
        op.pos = -1
        self.ops.append(op)
        return op

    def capture_begin(self):
        self._saved = getattr(self, "_saved", [])
        self._saved.append(self.ops)
        self.ops = []

    def capture_end(self):
        lst = self.ops
        self.ops = self._saved.pop()
        return lst

    def merge(self, lists, spans=None):
        if spans is None:
            spans = [1.0] * len(lists)
        keep = [i for i, l in enumerate(lists) if l]
        lists = [lists[i] for i in keep]
        spans = [spans[i] for i in keep]
        idx = [0] * len(lists)
        total = sum(len(l) for l in lists)
        for _ in range(total):
            best, bf = None, None
            for i, l in enumerate(lists):
                if idx[i] < len(l):
                    f = idx[i] / len(l) * spans[i]
                    if bf is None or f < bf:
                        best, bf = i, f
            self.ops.append(lists[best][idx[best]])
            idx[best] += 1

    def analyze(self):
        for i, op in enumerate(self.ops):
            op.pos = i
        last_w = {}
        rd_eng = {}
        rd_dma = {}
        for op in self.ops:
            deps = set()
            for k in op.r:
                if k in last_w:
                    deps.add(last_w[k])
                    op.raw.add(last_w[k])
            for k in op.w:
                if k in last_w:
                    deps.add(last_w[k])
                for p in rd_eng.get(k, {}).values():
                    deps.add(p)
                for p in rd_dma.get(k, ()):
                    deps.add(p)
            deps.discard(op.pos)
            op.deps = sorted(deps)
            for k in op.w:
                last_w[k] = op.pos
                rd_eng[k] = {}
                rd_dma[k] = []
            for k in op.r:
                if op.dma:
                    rd_dma.setdefault(k, []).append(op.pos)
                else:
                    rd_eng.setdefault(k, {})[op.eng] = op.pos
        for op in self.ops:
            for d in op.deps:
                a = self.ops[d]
                if a.dma:
                    continue
                if a.eng != op.eng or op.dma or a.eng != "pe":
                    a.need_inc = True
        cnt = {e: 0 for e in ENGS}
        dcnt = {}
        self.dma_hist = {}
        for op in self.ops:
            if op.dma:
                dcnt[op.semkey] = dcnt.get(op.semkey, 0) + 16
                op.dmaval = dcnt[op.semkey]
                self.dma_hist.setdefault(op.semkey, []).append((op.pos, op.dmaval))
            elif op.need_inc:
                cnt[op.eng] += 1
                op.incval = cnt[op.eng]
        self.semkeys = list(dcnt.keys())
        self.dma_total = dcnt

    def _dma_wait_val(self, semkey, pos):
        v = 0
        for p, c in self.dma_hist[semkey]:
            if p < pos:
                v = c
            else:
                break
        return v

    def emit(self, nc, final_semkeys=()):
        self.analyze()
        maxw = {}
        for op in self.ops:
            if op.dma:
                op.prewait = maxw.get(op.semkey, 0)
            for d in op.deps:
                a = self.ops[d]
                if a.dma:
                    v = self._dma_wait_val(a.semkey, op.pos)
                    if v > maxw.get(a.semkey, 0):
                        maxw[a.semkey] = v
        with contextlib.ExitStack() as st:
            esem = {e: st.enter_context(nc.semaphore("s_" + e)) for e in ENGS}
            dsem = {k: st.enter_context(nc.semaphore("d_%d" % i)) for i, k in enumerate(self.semkeys)}
            block = st.enter_context(nc.Block())
            per_eng = {e: [op for op in self.ops if op.eng == e] for e in ENGS}

            def run(ename, eobj):
                waited = {}
                for op in per_eng[ename]:
                    need = {}
                    for d in op.deps:
                        a = self.ops[d]
                        if a.dma:
                            key = ("d", a.semkey)
                            val = self._dma_wait_val(a.semkey, op.pos)
                            sem = dsem[a.semkey]
                        else:
                            if a.eng == ename and not op.dma and ename == "pe":
                                continue
                            key = ("e", a.eng)
                            val = a.incval
                            sem = esem[a.eng]
                        if val > need.get(key, (0, None))[0]:
                            need[key] = (val, sem)
                    for key, (val, sem) in need.items():
                        if waited.get(key, 0) >= val:
                            continue
                        eobj.wait_ge(sem, val)
                        waited[key] = val
                    if op.dma and op.prewait > waited.get(("d", op.semkey), 0):
                        eobj.wait_ge(dsem[op.semkey], op.prewait)
                        waited[("d", op.semkey)] = op.prewait
                    ins = op.fn(eobj)
                    if op.dma:
                        ins.then_inc(dsem[op.semkey], 16)
                    elif op.need_inc:
                        ins.then_inc(esem[ename], 1)
                if ename == "sp":
                    for k in final_semkeys:
                        if k in dsem:
                            eobj.wait_ge(dsem[k], self.dma_total[k])

            @block.tensor
            def _(e):
                run("pe", e)

            @block.scalar
            def _(e):
                run("act", e)

            @block.vector
            def _(e):
                run("dve", e)

            @block.gpsimd
            def _(e):
                run("pool", e)

            @block.sync
            def _(e):
                run("sp", e)


def build_program(npass=NPASS_FULL, do_mixer=True, do_ffn=True, taps=(), stage=9):
    nc = bass.Bass("TRN2", target_bir_lowering=False, dynamic_dma_scratch_size=4096)
    P = Prog()

    def din(name, shape):
        return nc.dram_tensor(name, list(shape), F32, kind="ExternalInput").ap()

    def dout(name, shape):
        return nc.dram_tensor(name, list(shape), F32, kind="ExternalOutput").ap()

    xp = din("xp", [SEQ, D])
    xs = din("xs", [NSMP, D])
    s5re_in = din("s5re_in", [16, 32, 64])
    s5im_in = din("s5im_in", [16, 32, 64])
    gla_in = din("gla_in", [16, 4, 64, 128])
    gains_d = din("gains", [4, D])
    w_ffn = [(din("ffn1_gate", [D, DFF]), din("ffn1_up", [D, DFF]), din("ffn1_down", [DFF, D])),
             (din("ffn2_gate", [D, DFF]), din("ffn2_up", [D, DFF]), din("ffn2_down", [DFF, D]))]
    w_in_d = din("w_in", [D, INW])
    w_out_d = din("w_out", [D, D])
    glu_w_d = din("glu_w", [512, 512])
    glu_b_d = din("glu_b", [512])
    gate_w_d = din("gate_w", [16, 256])
    gate_b_d = din("gate_b", [256])
    gla_norm_d = din("gla_norm", [128])
    lam_re_d = din("lam_re", [32, 64])
    lam_im_d = din("lam_im", [32, 64])
    log_dt_d = din("log_dt", [32])
    b_re_d = din("b_re", [32, 64, 16])
    b_im_d = din("b_im", [32, 64, 16])
    c_re_d = din("c_re", [32, 16, 64])
    c_im_d = din("c_im", [32, 16, 64])
    s5_d_d = din("s5_d", [512])
    ident_d = din("c_ident", [128, 128])
    tri64_d = din("c_tri64", [128, 128])
    triu64_d = din("c_triu64", [128, 128])
    tri4_d = din("c_tri4", [64, 64])
    triu4_d = din("c_triu4", [64, 64])
    sel64_d = din("c_sel64", [128, 2])
    sel4_d = din("c_sel4", [64, 16])
    maskw4_d = din("c_maskw4", [128, 128])
    evals_d = din("c_evals", [NEV])

    yp = dout("yp", [SEQ, D])
    ys = dout("ys", [NSMP, D])
    pre_o = dout("pre", [32, 64])
    pim_o = dout("pim", [32, 64])
    pgla_o = dout("pgla", [4, 64, 128])
    sre_o = dout("sre", [16, 32, 64])
    sim_o = dout("sim", [16, 32, 64])
    sgla_o = dout("sgla", [16, 4, 64, 128])
    tap_out = {}
    for (tname, tshape) in taps:
        tap_out[tname] = dout("tap_" + tname, tshape)

    st = contextlib.ExitStack()
    with st:
        def sb(name, shape, dt):
            return st.enter_context(nc.sbuf_tensor("sb_" + name, list(shape), dt))

        NMAX = PT + NSMP
        xT = sb("xT", [128, 8, NMAX], F32)
        xn = sb("xn", [128, 8, NMAX], BF16)
        NW = 4
        wslot = [sb("wslot%d" % i, [128, 2048], BF16) for i in range(NW)]
        W1 = sb("W1", [128, 32, 128], BF16)
        W3 = sb("W3", [128, 16, 2, 128], BF16)
        W4 = sb("W4", [128, 32, 128], BF16)
        A1 = sb("A1", [128, 2, 16], F32)
        A2 = sb("A2", [128, 2, 16], F32)
        L4 = sb("L4", [128, 2, 16], F32)
        Bst = sb("Bst", [128, 2, 16, 65], F32)
        ident_f = sb("ident_f", [128, 128], F32)
        ident_b = sb("ident_b", [128, 128], BF16)
        ones_b = sb("ones_b", [128, 128], BF16)
        tri64 = sb("tri64", [128, 128], F32)
        triu64 = sb("triu64", [128, 128], F32)
        tri4 = sb("tri4", [64, 64], F32)
        triu4 = sb("triu4", [64, 64], F32)
        sel64 = sb("sel64", [128, 2], F32)
        sel4 = sb("sel4", [64, 16], F32)
        gains = sb("gains", [128, 4, 8], F32)
        glub = sb("glub", [128, 4], F32)
        gnorm = sb("gnorm", [128, 1], F32)
        gatew = sb("gatew", [16, 256], F32)
        gatew_b = sb("gatew_b", [16, 256], BF16)
        gateb = sb("gateb", [128, 256], F32)
        S_f = sb("S_f", [128, 2, 128], F32)
        PA1 = sb("PA1", [128, 2, 16, 8], F32)
        PA2 = sb("PA2", [128, 2, 16, 8], F32)
        scr = sb("scr", [128, 8], F32)

        XW = 26368
        arena = sb("arena", [128, XW], F32)

        class Carver:
            def __init__(self, base=0):
                self.off = base

            def get(self, shape, dt):
                n = 1
                for s in shape:
                    n *= s
                words = (n * (2 if dt == BF16 else 4) + 3) // 4
                words = (words + 7) // 8 * 8
                a = arena[:, self.off:self.off + words]
                self.off += words
                assert self.off <= XW, "arena overflow %d" % self.off
                if dt == BF16:
                    a = a.bitcast(BF16)[:, 0:n]
                elif dt == I32:
                    a = a.bitcast(I32)[:, 0:n]
                else:
                    a = a[:, 0:n]
                if len(shape) == 2:
                    return a.rearrange("p (a b) -> p a b", a=shape[0])
                if len(shape) == 3:
                    return a.rearrange("p (a b c) -> p a b c", a=shape[0], b=shape[1])
                return a

        cf = Carver(0)
        hT = cf.get([NFT, NMAX], BF16)
        sgb = [cf.get([512], F32) for _ in range(2)]
        sqb = [cf.get([512], BF16) for _ in range(2)]
        sdb = cf.get([512], F32)
        xin = [cf.get([1024], F32) for _ in range(2)]
        yT = Carver(0).get([8, NMAX], F32)
        ffn_end = cf.off
        cst = Carver(ffn_end)
        xst = [cst.get([1024], F32) for _ in range(2)]
        cp = Carver(ffn_end)
        pc_lr = cp.get([16], F32)
        pc_li = cp.get([16], F32)
        pc_dt = cp.get([16], F32)
        pc_ev = cp.get([NEV], F32)
        pc_er = cp.get([16, NEV], F32)
        pc_ei = cp.get([16, NEV], F32)
        pc_t0 = cp.get([16, NEV], F32)
        pc_t1 = cp.get([16, NEV], F32)
        pc_t2 = cp.get([16, NEV], F32)
        pc_ti = cp.get([16, NEV], I32)
        pc_fr = cp.get([16], F32)
        pc_fi = cp.get([16], F32)
        pc_s0 = cp.get([16], F32)
        pc_s1 = cp.get([16], F32)
        pc_s2 = cp.get([16], F32)
        pc_pr = cp.get([16, NEV], F32)
        pc_pi = cp.get([16, NEV], F32)
        pc_br = cp.get([16, 16], F32)
        pc_bi = cp.get([16, 16], F32)
        pc_cr = cp.get([16, 16], F32)
        pc_ci = cp.get([16, 16], F32)
        pc_dcol = cp.get([32], F32)
        pc_A = cp.get([16, 128], F32)
        pc_B = cp.get([16, 128], F32)
        pc_C = cp.get([16, 128], F32)
        pc_D = cp.get([16, 128], F32)
        pc_E = cp.get([16, 128], F32)
        pc_w4t = cp.get([128], F32)
        pc_csb = [pc_E[:, 2 * i_:2 * i_ + 2, :].rearrange("p a (b c) -> p a b c", b=2) for i_ in range(2)]
        cm = Carver(0)
        ucm_flat = cm.get([4096], BF16)
        ucm = ucm_flat.rearrange("p (g j h) -> p g j h", g=32, j=8)
        zcm = ucm_flat.rearrange("p (j c) -> p j c", j=8)
        qT = cm.get([2, 512], F32)
        kT = cm.get([2, 512], F32)
        k_tm = cm.get([4, 256], F32)
        v_tm = cm.get([4, 512], BF16)
        sgT = cm.get([4, 512], BF16)
        aT = cm.get([512], BF16)
        lf = cm.get([4, 256], F32)
        Ug = cm.get([32, 64], BF16)
        cat = cm.get([8, 512], BF16)
        zT = cm.get([4, 512], BF16)
        Hbz = [cm.get([2, 16, 64], BF16) for _ in range(2)]
        ebb = [cm.get([2, 128], F32) for _ in range(2)]
        enb = [cm.get([2, 128], F32) for _ in range(2)]
        erem = [cm.get([256], F32) for _ in range(2)]
        qs = [cm.get([2, 128], BF16) for _ in range(2)]
        ki = [cm.get([2, 128], BF16) for _ in range(2)]
        kiz = [cm.get([4, 128], BF16) for _ in range(2)]
        Sz = [cm.get([4, 128], BF16) for _ in range(2)]
        kend = [cm.get([256], BF16) for _ in range(2)]
        kendm = [cm.get([256], BF16) for _ in range(4)]
        attT = [cm.get([4, 128], BF16) for _ in range(2)]
        o_sb = cm.get([512], F32)
        osq = cm.get([512], BF16)
        t1b = cm.get([512], F32)
        etmp = t1b[:, 0:256]
        sdm = cm.get([512], F32)
        sig0_ = cm.get([512], F32)
        sig = [sig0_, sig0_]
        S0b = [cm.get([2, 128], F32) for _ in range(2)]
        Sob = [cm.get([2, 128], F32) for _ in range(2)]
        sct1 = cm.get([2, 16, 8], F32)
        sct2 = cm.get([2, 16, 8], F32)
        sc_t1 = cm.get([2, 16], F32)
        sc_t2 = cm.get([2, 16], F32)
        Bsm = cm.get([2, 16, 16], F32)
        Ssm = cm.get([2, 16, 16], F32)
        Hs3 = cm.get([2, 16, 16], F32)
        sm_t = [cm.get([2, 16, 16], F32) for _ in range(2)]
        h0t = [cm.get([2, 64], F32) for _ in range(2)]
        hot = [cm.get([2, 64], F32) for _ in range(2)]

        ps = [st.enter_context(nc.psum_tensor("ps%d" % i, [128, 512], F32)) for i in range(8)]
        rot = {"A": [0, 1], "B": [2, 3], "C": [4, 5], "M": [6, 7], "U": [2, 3]}
        rot_i = {k: 0 for k in rot}

        rot_ovr = [None]
        ws_ovr = [None]

        def psum(group):
            banks = rot_ovr[0][group] if rot_ovr[0] is not None else rot[group]
            i = banks[rot_i[group] % len(banks)]
            rot_i[group] += 1
            return i

        def pk(i):
            return "ps%d" % i

        XB = "Xbar"

        def call(name, *a, **kw):
            return lambda e: getattr(e, name)(*a, **kw)

        def add(eng, fn, r=(), w=(), x=False):
            r = [k_ for k_ in r if k_ is not None]
            if x:
                r.append(XB)
            return P.add(eng, fn, r=r, w=w)

        def dma(eng, out, in_, r=(), w=(), semkey=None, x=False, slow=False):
            r = list(r)
            if x:
                r.append(XB)
            if slow:
                fn = call("dma_start", out=out, in_=in_, allow_slow_non_contiguous=True)
            else:
                fn = call("dma_start", out=out, in_=in_)
            return P.add(eng, fn, r=r, w=w, dma=True, semkey=semkey)

        def barrier():
            P.add("pool", call("memset", scr[:, 0:1], 0.0), r=[], w=[XB])

        ws_i = [0]

        ws_hist = []

        def wload(out_view_fn, in_ap):
            pool_ = ws_ovr[0] if ws_ovr[0] is not None else list(range(NW))
            s = pool_[ws_i[0] % len(pool_)]
            ws_i[0] += 1
            extra = ["ws%d" % ws_hist[-2]] if len(ws_hist) >= 2 and ws_hist[-2] != s else []
            ws_hist.append(s)
            dma("pool", out_view_fn(wslot[s]), in_ap, r=extra, w=["ws%d" % s], semkey="ws%d" % s)
            return s

        ev_i = [0]

        def evac_eng():
            ev_i[0] += 1
            return "act" if ev_i[0] % 2 == 0 else "dve"

        def copy_op(eng, out, in_, r, w, x=True):
            if eng == "act":
                add("act", call("copy", out=out, in_=in_), r=r, w=w, x=x)
            else:
                add(eng, call("tensor_copy", out=out, in_=in_), r=r, w=w, x=x)

        cload = [(ident_f[:], ident_d, "ident_f"), (tri64[:], tri64_d, "tri64"), (triu64[:], triu64_d, "triu64"),
                 (tri4[:], tri4_d, "tri4"), (triu4[:], triu4_d, "triu4"), (sel64[:], sel64_d, "sel64"),
                 (sel4[:], sel4_d, "sel4"), (gatew[:], gate_w_d, "gatew"),
                 (gateb[:], gate_b_d.partition_broadcast(128), "gateb")]
        for (o, i, k) in cload:
            dma("sp", o, i, w=[k], semkey="const")
        dma("sp", gains[:], gains_d.rearrange("n (k p) -> p n k", p=128), w=["gains"], semkey="const", slow=True)
        dma("sp", glub[:], glu_b_d.rearrange("(m p) -> p m", p=128), w=["glub"], semkey="const", slow=True)
        dma("sp", gnorm[:], gla_norm_d.rearrange("(p o) -> p o", o=1), w=["gnorm"], semkey="const", slow=True)
        add("dve", call("tensor_copy", out=ident_b[:], in_=ident_f[:]), r=["ident_f"], w=["ident_b"])
        add("dve", call("tensor_copy", out=gatew_b[:], in_=gatew[:]), r=["gatew"], w=["gatew_b"])
        add("act", call("activation", out=gateb[:], in_=gateb[:], func=AF.Exp, scale=-1.0), r=["gateb"], w=["gateb"])
        add("dve", call("memset", ones_b[:], 1.0), w=["ones_b"])
        add("dve", call("memset", S_f[:], 0.0), w=["S_f"])
        add("dve", call("memset", Bst[:], 0.0), w=["Bst"])

        xin_i = [0]

        def load_x(src_rows, ntok, col0):
            b = xin_i[0] % 2
            xin_i[0] += 1
            kx = "xin%d" % b
            dma("act", xin[b][0:ntok, :], src_rows, w=[kx], semkey=kx, x=True)
            for half in range(2):
                pi = psum("M")
                for kk in range(4):
                    k = half * 4 + kk
                    add("pe", call("transpose",
                        out=ps[pi][:, kk * 128:kk * 128 + ntok], in_=xin[b][0:ntok, k * 128:(k + 1) * 128],
                        identity=ident_f[0:ntok, 0:ntok]), r=[kx, "ident_f"], w=[pk(pi)], x=True)
                src = ps[pi][:].rearrange("p (a b) -> p a b", a=4)[:, :, 0:ntok]
                dst = xT[:, half * 4:half * 4 + 4, col0:col0 + ntok]
                copy_op(evac_eng(), dst, src, r=[pk(pi)], w=["xT%d" % (half * 4 + q_) for q_ in range(4)], x=True)

        def rmsnorm(gi, chunks, dst, dst_key):
            for (c0, n) in chunks:
                pi = psum("M")
                for k in range(8):
                    b = k % 2
                    add("act", call("activation", out=sqb[b][:, 0:n], in_=xT[:, k, c0:c0 + n],
                                                               func=AF.Square),
                        r=["xT%d" % k], w=["sq%d" % b], x=True)
                    add("pe", call("matmul", ps[pi][:, 0:n], lhsT=ones_b[:], rhs=sqb[b][:, 0:n],
                                                                 start=(k == 0), stop=(k == 7)),
                        r=["sq%d" % b, "ones_b"], w=[pk(pi)], x=True)
                add("act", call("activation", out=sdb[:, 0:n], in_=ps[pi][:, 0:n], func=AF.Ln,
                                                        bias=EPS, scale=1.0 / D),
                    r=[pk(pi)], w=["sdb"], x=True)
                add("act", call("activation", out=ps[pi][:, 0:n], in_=sdb[:, 0:n], func=AF.Exp, scale=-0.5),
                    r=["sdb"], w=[pk(pi)], x=True)
                for k in range(8):
                    add("dve", call("scalar_tensor_tensor",
                        out=dst[:, k, c0:c0 + n], in0=xT[:, k, c0:c0 + n], scalar=gains[:, gi, k:k + 1],
                        in1=ps[pi][:, 0:n], op0=ALU.mult, op1=ALU.mult),
                        r=["xT%d" % k, "gains", pk(pi)], w=[dst_key + str(k)] + (["hT%d" % f_ for f_ in range(NFT)] if dst_key == "yT" else []), x=True)

        def ffn(wg, wu, wd, chunks, side=None):
            nside = [0]
            npts = NFT // 2 + 8

            side_dma = [o for o in side if o.dma] if side else []
            side_cmp = [o for o in side if not o.dma] if side else []
            if side_dma:
                P.ops.extend(side_dma)
            first_pt = npts // 2 + 2

            def inject(ip):
                if side_cmp and ip >= first_pt:
                    hi = len(side_cmp) * (ip - first_pt + 1) // (npts - first_pt)
                    P.ops.extend(side_cmp[nside[0]:hi])
                    nside[0] = hi
            wgv = wg.rearrange("(k p) f -> p k f", p=128)
            wuv = wu.rearrange("(k p) f -> p k f", p=128)
            wdv = wd.rearrange("(t p) d -> p t d", p=128)
            v8 = lambda t: t[:].rearrange("p (k f) -> p k f", k=8)
            for fp in range(NFT // 2):
                if fp > 0:
                    inject(fp - 1)
                sg_ = wload(v8, wgv[:, :, fp * 256:(fp + 1) * 256])
                su_ = wload(v8, wuv[:, :, fp * 256:(fp + 1) * 256])
                for hf in range(2):
                    f = fp * 2 + hf
                    for (c0, n) in chunks:
                        pa = psum("A")
                        pb = psum("B")
                        for (pi, s_) in ((pa, sg_), (pb, su_)):
                            for k in range(8):
                                add("pe", call("matmul",
                                    ps[pi][:, 0:n], lhsT=v8(wslot[s_])[:, k, hf * 128:(hf + 1) * 128],
                                    rhs=xn[:, k, c0:c0 + n], start=(k == 0), stop=(k == 7)),
                                    r=["ws%d" % s_, "xn%d" % k], w=[pk(pi)], x=True)
                        b = f % 2
                        add("act", call("activation", out=sgb[b][:, 0:n], in_=ps[pa][:, 0:n],
                                                                          func=AF.Silu),
                            r=[pk(pa)], w=["sg%d" % b], x=True)
                        add("dve", call("tensor_tensor",
                            out=hT[:, f, c0:c0 + n], in0=sgb[b][:, 0:n], in1=ps[pb][:, 0:n], op=ALU.mult),
                            r=["sg%d" % b, pk(pb)], w=["hT%d" % f] + (["yT%d" % q_ for q_ in range(8)] if (f == 0 and c0 == 0) else []), x=True)
            v11 = lambda t: t[:, 0:1408].rearrange("p (t c) -> p t c", t=11)
            for d in range(8):
                inject(NFT // 2 + d)
                sl = [wload(v11, wdv[:, hh * 11:(hh + 1) * 11, d * 128:(d + 1) * 128]) for hh in range(2)]
                for (c0, n) in chunks:
                    pi = psum("C")
                    for f in range(NFT):
                        s_ = sl[f // 11]
                        add("pe", call("matmul",
                            ps[pi][:, 0:n], lhsT=v11(wslot[s_])[:, f % 11, :], rhs=hT[:, f, c0:c0 + n],
                            start=(f == 0), stop=(f == NFT - 1)),
                            r=["ws%d" % s_, "hT%d" % f], w=[pk(pi)], x=True)
                    add("dve", call("scalar_tensor_tensor",
                        out=xT[:, d, c0:c0 + n], in0=ps[pi][:, 0:n], scalar=0.5, in1=xT[:, d, c0:c0 + n],
                        op0=ALU.mult, op1=ALU.add),
                        r=[pk(pi), "xT%d" % d], w=["xT%d" % d], x=True)
            if side_cmp:
                P.ops.extend(side_cmp[nside[0]:])
                nside[0] = len(side_cmp)

        xst_i = [0]

        def store_out(dst_rows, ntok, col0):
            b = xst_i[0] % 2
            xst_i[0] += 1
            kx = "xst%d" % b
            for half in range(2):
                pi = psum("M")
                for kk in range(4):
                    k = half * 4 + kk
                    add("pe", call("transpose", out=ps[pi][0:ntok, kk * 128:(kk + 1) * 128], in_=yT[:, k, col0:col0 + ntok],
                                   identity=ident_f[:]), r=["yT%d" % k, "ident_f"], w=[pk(pi)], x=True)
                copy_op(evac_eng(), xst[b][0:ntok, half * 512:(half + 1) * 512], ps[pi][0:ntok, :],
                        r=[pk(pi)], w=[kx], x=True)
            dma("sp", dst_rows, xst[b][0:ntok, :], r=[kx], semkey="out", x=True)

        def precompute():
            x_ = True
            for hfp in range(2):
                pr = slice(hfp * 64, hfp * 64 + 64)
                gs = slice(hfp * 16, hfp * 16 + 16)
                dma("sp", pc_lr[pr, :], lam_re_d[gs, :].rearrange("g p -> p g"), w=["pc_lr"], semkey="pc", x=x_, slow=True)
                dma("sp", pc_li[pr, :], lam_im_d[gs, :].rearrange("g p -> p g"), w=["pc_li"], semkey="pc", x=x_, slow=True)
                dma("sp", pc_dt[pr, :], log_dt_d[gs].partition_broadcast(64), w=["pc_dt"], semkey="pc", x=x_, slow=True)
                dma("sp", pc_br[pr, :, :], b_re_d[gs, :, :].rearrange("g p h -> p g h"), w=["pc_br"], semkey="pc", x=x_, slow=True)
                dma("sp", pc_bi[pr, :, :], b_im_d[gs, :, :].rearrange("g p h -> p g h"), w=["pc_bi"], semkey="pc", x=x_, slow=True)
            piC = psum("A")
            for t_, src_d in enumerate((c_re_d, c_im_d)):
                for glhi in range(2):
                    dma("sp", pc_csb[t_][:, glhi, :, :], src_d.rearrange("g h p -> (g h) p").rearrange("(a b q) p -> q b a p", a=2, b=2)[:, glhi, :, :],
                        w=["pc_csb%d" % t_], semkey="pc", x=x_)
            for t_ in range(2):
                for glhi in range(2):
                    add("pe", call("transpose", out=ps[piC][:, t_ * 256 + glhi * 128:t_ * 256 + (glhi + 1) * 128],
                                   in_=pc_csb[t_][:, glhi, :, :].rearrange("q a p -> q (a p)"), identity=ident_f[:]),
                        r=["pc_csb%d" % t_, "ident_f"], w=[pk(piC)], x=True)
            copy_op("dve", pc_cr[:, :, :], ps[piC][:, 0:256].rearrange("p (g h) -> p g h", g=16), r=[pk(piC)], w=["pc_cr"])
            copy_op("dve", pc_ci[:, :, :], ps[piC][:, 256:512].rearrange("p (g h) -> p g h", g=16), r=[pk(piC)], w=["pc_ci"])
            dma("sp", pc_ev[:, :], evals_d.partition_broadcast(128), w=["pc_ev"], semkey="pc", x=x_, slow=True)
            for i in range(8):
                dma("sp", pc_dcol[i * 16:(i + 1) * 16, :], s5_d_d.rearrange("(g h) -> h g", h=16), w=["pc_dcol"],
                    semkey="pc", x=x_, slow=True)
            dma("sp", pc_w4t[:, :], maskw4_d, w=["maskw4"], semkey="pc", x=x_)

            def V(eng, fn, r, w):
                add(eng, fn, r=r, w=w, x=True)

            bc_g = lambda a: a.unsqueeze(2).to_broadcast([128, 16, NEV])
            bc_e = lambda a: a.unsqueeze(1).to_broadcast([128, 16, NEV])
            V("act", call("activation", out=pc_dt[:, :], in_=pc_dt[:, :], func=AF.Exp), ["pc_dt"], ["pc_dt"])
            V("dve", call("tensor_tensor", out=pc_s0[:, :], in0=pc_lr[:, :], in1=pc_dt[:, :], op=ALU.mult), ["pc_lr", "pc_dt"], ["pc_s0"])
            V("dve", call("tensor_tensor", out=pc_s1[:, :], in0=pc_li[:, :], in1=pc_dt[:, :], op=ALU.mult), ["pc_li", "pc_dt"], ["pc_s1"])
            V("dve", call("tensor_tensor", out=pc_t0[:, :, :], in0=bc_g(pc_s0[:, :]), in1=bc_e(pc_ev[:, :]), op=ALU.mult), ["pc_s0", "pc_ev"], ["pc_t0"])
            V("dve", call("tensor_tensor", out=pc_t1[:, :, :], in0=bc_g(pc_s1[:, :]), in1=bc_e(pc_ev[:, :]), op=ALU.mult), ["pc_s1", "pc_ev"], ["pc_t1"])
            V("act", call("activation", out=pc_t0[:, :, :], in_=pc_t0[:, :, :], func=AF.Exp), ["pc_t0"], ["pc_t0"])
            C1 = 6.28125
            C2 = 2.0 * math.pi - C1
            V("dve", call("tensor_scalar", out=pc_t2[:, :, :], in0=pc_t1[:, :, :], scalar1=1.0 / (2.0 * math.pi), scalar2=None, op0=ALU.mult), ["pc_t1"], ["pc_t2"])
            V("dve", call("tensor_copy", out=pc_ti[:, :, :], in_=pc_t2[:, :, :]), ["pc_t2"], ["pc_ti"])
            V("dve", call("tensor_copy", out=pc_t2[:, :, :], in_=pc_ti[:, :, :]), ["pc_ti"], ["pc_t2"])
            V("dve", call("scalar_tensor_tensor", out=pc_t1[:, :, :], in0=pc_t2[:, :, :], scalar=-C1, in1=pc_t1[:, :, :], op0=ALU.mult, op1=ALU.add), ["pc_t2", "pc_t1"], ["pc_t1"])
            V("dve", call("scalar_tensor_tensor", out=pc_t1[:, :, :], in0=pc_t2[:, :, :], scalar=-C2, in1=pc_t1[:, :, :], op0=ALU.mult, op1=ALU.add), ["pc_t2", "pc_t1"], ["pc_t1"])
            V("act", call("activation", out=pc_t2[:, :, :], in_=pc_t1[:, :, :], func=AF.Sin, scale=0.5), ["pc_t1"], ["pc_t2"])
            V("act", call("activation", out=pc_t1[:, :, :], in_=pc_t1[:, :, :], func=AF.Sin, scale=0.25), ["pc_t1"], ["pc_t1"])
            V("dve", call("tensor_tensor", out=pc_t1[:, :, :], in0=pc_t1[:, :, :], in1=pc_t1[:, :, :], op=ALU.mult), ["pc_t1"], ["pc_t1"])
            V("dve", call("tensor_scalar", out=pc_t1[:, :, :], in0=pc_t1[:, :, :], scalar1=-2.0, scalar2=1.0, op0=ALU.mult, op1=ALU.add), ["pc_t1"], ["pc_t1"])
            V("dve", call("tensor_tensor", out=pc_t1[:, :, :], in0=pc_t1[:, :, :], in1=pc_t2[:, :, :], op=ALU.mult), ["pc_t1", "pc_t2"], ["pc_t1"])
            V("dve", call("scalar_tensor_tensor", out=pc_ei[:, :, :], in0=pc_t1[:, :, :], scalar=2.0, in1=pc_t0[:, :, :], op0=ALU.mult, op1=ALU.mult), ["pc_t1", "pc_t0"], ["pc_ei"])
            V("dve", call("tensor_tensor", out=pc_t2[:, :, :], in0=pc_t2[:, :, :], in1=pc_t2[:, :, :], op=ALU.mult), ["pc_t2"], ["pc_t2"])
            V("dve", call("tensor_scalar", out=pc_t2[:, :, :], in0=pc_t2[:, :, :], scalar1=-2.0, scalar2=1.0, op0=ALU.mult, op1=ALU.add), ["pc_t2"], ["pc_t2"])
            V("dve", call("tensor_tensor", out=pc_er[:, :, :], in0=pc_t2[:, :, :], in1=pc_t0[:, :, :], op=ALU.mult), ["pc_t2", "pc_t0"], ["pc_er"])
            abr = pc_er[:, :, 1]
            abi = pc_ei[:, :, 1]
            V("dve", call("tensor_tensor", out=pc_s0[:, :], in0=pc_lr[:, :], in1=pc_lr[:, :], op=ALU.mult), ["pc_lr"], ["pc_s0"])
            V("dve", call("tensor_tensor", out=pc_s1[:, :], in0=pc_li[:, :], in1=pc_li[:, :], op=ALU.mult), ["pc_li"], ["pc_s1"])
            V("dve", call("tensor_tensor", out=pc_s0[:, :], in0=pc_s0[:, :], in1=pc_s1[:, :], op=ALU.add), ["pc_s0", "pc_s1"], ["pc_s0"])
            V("dve", call("reciprocal", out=pc_s0[:, :], in_=pc_s0[:, :]), ["pc_s0"], ["pc_s0"])
            V("dve", call("tensor_scalar", out=pc_s1[:, :], in0=abr, scalar1=-1.0, scalar2=None, op0=ALU.add), ["pc_er"], ["pc_s1"])
            V("dve", call("tensor_tensor", out=pc_fr[:, :], in0=pc_s1[:, :], in1=pc_lr[:, :], op=ALU.mult), ["pc_s1", "pc_lr"], ["pc_fr"])
            V("dve", call("tensor_tensor", out=pc_s2[:, :], in0=abi, in1=pc_li[:, :], op=ALU.mult), ["pc_ei", "pc_li"], ["pc_s2"])
            V("dve", call("tensor_tensor", out=pc_fr[:, :], in0=pc_fr[:, :], in1=pc_s2[:, :], op=ALU.add), ["pc_fr", "pc_s2"], ["pc_fr"])
            V("dve", call("tensor_tensor", out=pc_fr[:, :], in0=pc_fr[:, :], in1=pc_s0[:, :], op=ALU.mult), ["pc_fr", "pc_s0"], ["pc_fr"])
            V("dve", call("tensor_tensor", out=pc_fi[:, :], in0=abi, in1=pc_lr[:, :], op=ALU.mult), ["pc_ei", "pc_lr"], ["pc_fi"])
            V("dve", call("tensor_tensor", out=pc_s2[:, :], in0=pc_s1[:, :], in1=pc_li[:, :], op=ALU.mult), ["pc_s1", "pc_li"], ["pc_s2"])
            V("dve", call("tensor_tensor", out=pc_fi[:, :], in0=pc_fi[:, :], in1=pc_s2[:, :], op=ALU.subtract), ["pc_fi", "pc_s2"], ["pc_fi"])
            V("dve", call("tensor_tensor", out=pc_fi[:, :], in0=pc_fi[:, :], in1=pc_s0[:, :], op=ALU.mult), ["pc_fi", "pc_s0"], ["pc_fi"])
            V("dve", call("tensor_tensor", out=pc_pr[:, :, :], in0=pc_er[:, :, :], in1=bc_g(pc_fr[:, :]), op=ALU.mult), ["pc_er", "pc_fr"], ["pc_pr"])
            V("dve", call("tensor_tensor", out=pc_t0[:, :, :], in0=pc_ei[:, :, :], in1=bc_g(pc_fi[:, :]), op=ALU.mult), ["pc_ei", "pc_fi"], ["pc_t0"])
            V("dve", call("tensor_tensor", out=pc_pr[:, :, :], in0=pc_pr[:, :, :], in1=pc_t0[:, :, :], op=ALU.subtract), ["pc_pr", "pc_t0"], ["pc_pr"])
            V("dve", call("tensor_tensor", out=pc_pi[:, :, :], in0=pc_er[:, :, :], in1=bc_g(pc_fi[:, :]), op=ALU.mult), ["pc_er", "pc_fi"], ["pc_pi"])
            V("dve", call("tensor_tensor", out=pc_t0[:, :, :], in0=pc_ei[:, :, :], in1=bc_g(pc_fr[:, :]), op=ALU.mult), ["pc_ei", "pc_fr"], ["pc_t0"])
            V("dve", call("tensor_tensor", out=pc_pi[:, :, :], in0=pc_pi[:, :, :], in1=pc_t0[:, :, :], op=ALU.add), ["pc_pi", "pc_t0"], ["pc_pi"])
            V("dve", call("tensor_copy", out=A1[:, 0, :], in_=pc_er[:, :, 8]), ["pc_er"], ["A1"])
            V("dve", call("tensor_copy", out=A1[:, 1, :], in_=pc_er[:, :, 8]), ["pc_er"], ["A1"])
            V("dve", call("tensor_scalar", out=A2[:, 0, :], in0=pc_ei[:, :, 8], scalar1=-1.0, scalar2=None, op0=ALU.mult), ["pc_ei"], ["A2"])
            V("dve", call("tensor_copy", out=A2[:, 1, :], in_=pc_ei[:, :, 8]), ["pc_ei"], ["A2"])
            V("dve", call("tensor_copy", out=L4[:, 0, :], in_=pc_er[:, :, 25]), ["pc_er"], ["L4"])
            V("dve", call("tensor_copy", out=L4[:, 1, :], in_=pc_ei[:, :, 25]), ["pc_ei"], ["L4"])
            for k_, idx_ in enumerate([8, 26, 27, 28, 29, 30, 31, 32]):
                V("dve", call("tensor_copy", out=PA1[:, 0, :, k_], in_=pc_er[:, :, idx_]), ["pc_er"], ["PA"])
                V("dve", call("tensor_copy", out=PA1[:, 1, :, k_], in_=pc_er[:, :, idx_]), ["pc_er"], ["PA"])
                V("dve", call("tensor_scalar", out=PA2[:, 0, :, k_], in0=pc_ei[:, :, idx_], scalar1=-1.0, scalar2=None, op0=ALU.mult), ["pc_ei"], ["PA"])
                V("dve", call("tensor_copy", out=PA2[:, 1, :, k_], in_=pc_ei[:, :, idx_]), ["pc_ei"], ["PA"])

            def v4(t):
                return t[:, :, :].rearrange("p g (i h) -> p g i h", i=8)

            def bt(T, e0):
                return T[:, :, e0:e0 + 8].unsqueeze(3).to_broadcast([128, 16, 8, 16])

            def bv(Vv):
                return Vv[:, :, :].unsqueeze(2).to_broadcast([128, 16, 8, 16])

            def cplx(outr, outi, Tr, Ti, e0, Vr, Vi, keys_t, keys_v, neg_im=False, only=None):
                kr, ki_ = keys_t
                vr, vi = keys_v
                if outr is not None:
                    okr = outr[1]
                    V("dve", call("tensor_tensor", out=v4(outr[0]), in0=bt(Tr, e0), in1=bv(Vr), op=ALU.mult), [kr, vr], [okr])
                    V("dve", call("tensor_tensor", out=v4(pc_E), in0=bt(Ti, e0), in1=bv(Vi), op=ALU.mult), [ki_, vi], ["pc_E"])
                    V("dve", call("tensor_tensor", out=outr[0][:, :, :], in0=outr[0][:, :, :], in1=pc_E[:, :, :], op=ALU.subtract), [okr, "pc_E"], [okr])
                if outi is not None:
                    oki = outi[1]
                    V("dve", call("tensor_tensor", out=v4(outi[0]), in0=bt(Tr, e0), in1=bv(Vi), op=ALU.mult), [kr, vi], [oki])
                    V("dve", call("tensor_tensor", out=v4(pc_E), in0=bt(Ti, e0), in1=bv(Vr), op=ALU.mult), [ki_, vr], ["pc_E"])
                    if neg_im:
                        V("dve", call("scalar_tensor_tensor", out=outi[0][:, :, :], in0=outi[0][:, :, :], scalar=-1.0, in1=pc_E[:, :, :], op0=ALU.mult, op1=ALU.subtract), [oki, "pc_E"], [oki])
                    else:
                        V("dve", call("tensor_tensor", out=outi[0][:, :, :], in0=outi[0][:, :, :], in1=pc_E[:, :, :], op=ALU.add), [oki, "pc_E"], [oki])

            cplx((pc_A, "pc_A"), (pc_B, "pc_B"), pc_er, pc_ei, 1, pc_cr, pc_ci, ("pc_er", "pc_ei"), ("pc_cr", "pc_ci"), neg_im=True)
            V("dve", call("tensor_copy", out=W3[:, :, 0, :], in_=pc_A[:, :, :]), ["pc_A"], ["W3"])
            V("dve", call("tensor_copy", out=W3[:, :, 1, :], in_=pc_B[:, :, :]), ["pc_B"], ["W3"])
            cplx((pc_A, "pc_A"), (pc_B, "pc_B"), pc_er, pc_ei, 0, pc_cr, pc_ci, ("pc_er", "pc_ei"), ("pc_cr", "pc_ci"))
            cplx((pc_C, "pc_C"), (pc_D, "pc_D"), pc_pr, pc_pi, 9, pc_br, pc_bi, ("pc_pr", "pc_pi"), ("pc_br", "pc_bi"), neg_im=True)
            for g in range(32):
                hfp, gl = g // 16, g % 16
                pr = slice(hfp * 64, hfp * 64 + 64)
                pi = psum("A")
                add("pe", call("matmul", ps[pi][:, 0:128], lhsT=pc_C[pr, gl, :], rhs=pc_A[pr, gl, :], start=True, stop=False),
                    r=["pc_C", "pc_A"], w=[pk(pi)], x=True)
                add("pe", call("matmul", ps[pi][:, 0:128], lhsT=pc_D[pr, gl, :], rhs=pc_B[pr, gl, :], start=False, stop=True),
                    r=["pc_D", "pc_B"], w=[pk(pi)], x=True)
                add("dve", call("tensor_tensor", out=ps[pi][:, 128:256], in0=ps[pi][:, 0:128], in1=pc_w4t[:, :], op=ALU.mult),
                    r=[pk(pi), "maskw4"], w=[pk(pi)], x=True)
                add("dve", call("scalar_tensor_tensor", out=W4[:, g, :], in0=ident_f[:], scalar=pc_dcol[:, g:g + 1], in1=ps[pi][:, 128:256], op0=ALU.mult, op1=ALU.add),
                    r=[pk(pi), "ident_f", "pc_dcol"], w=["W4"], x=True)
            cplx((pc_C, "pc_C"), (pc_D, "pc_D"), pc_pr, pc_pi, 17, pc_br, pc_bi, ("pc_pr", "pc_pi"), ("pc_br", "pc_bi"))
            for g in range(32):
                hfp, gl = g // 16, g % 16
                pr = slice(hfp * 64, hfp * 64 + 64)
                pi = psum("B")
                for ri, (T, tk) in enumerate(((pc_C, "pc_C"), (pc_D, "pc_D"))):
                    add("pe", call("transpose", out=ps[pi][:, ri * 64:(ri + 1) * 64], in_=T[pr, gl, :], identity=ident_f[pr, pr]),
                        r=[tk, "ident_f"], w=[pk(pi)], x=True)
                copy_op(evac_eng(), W1[:, g, :], ps[pi][:, 0:128], r=[pk(pi)], w=["W1"], x=True)

        def mixer(kind, c0, n, pass_idx):
            sample = (kind == "sample")
            ntile = 1 if sample else n // 128
            ntok = 64 if sample else 128
            L = 4 if sample else 64
            nch = ntok // L
            ncc = 16 if sample else n // 8
            J = 4 if sample else 8
            TRI = tri4 if sample else tri64
            TRIU = triu4 if sample else triu64
            SEL = sel4 if sample else sel64
            rmsnorm(1, [(c0, n)], xn, "xn")
            if stage < 1:
                return
            v8 = lambda t: t[:].rearrange("p (k f) -> p k f", k=8)
            winv = w_in_d.rearrange("(k p) f -> p k f", p=128)

            def wl_in(col0, ncol):
                return wload(lambda t: t[:, 0:8 * ncol].rearrange("p (k f) -> p k f", k=8), winv[:, :, col0:col0 + ncol])

            def vin(s_, ncol):
                return wslot[s_][:, 0:8 * ncol].rearrange("p (k f) -> p k f", k=8)

            rot_ovr[0] = {"A": [0, 1, 2, 3], "B": [4, 5], "C": [4, 5], "M": [6, 7], "U": [2, 3]}
            if sample:
                add("dve", call("memset", ucm_flat[:, :], 0.0), w=["ucm"], x=True)
            for gq in range(2):
                s_ = wl_in(gq * 256, 256)
                for j in range(J):
                    pi = psum("A")
                    if sample:
                        lsel = lambda k, j=j: xn[:, k, c0 + j:c0 + n:4]
                    else:
                        lsel = lambda k, j=j: xn[:, k, c0 + j:c0 + n:8]
                    for k in range(8):
                        add("pe", call("matmul",
                            ps[pi][0:ncc, 0:256], lhsT=lsel(k), rhs=vin(s_, 256)[:, k, :], start=(k == 0), stop=(k == 7)),
                            r=["ws%d" % s_, "xn%d" % k], w=[pk(pi)], x=True)
                    copy_op(evac_eng(), ucm[0:ncc, gq * 16:(gq + 1) * 16, j, :], ps[pi][0:ncc, 0:256].rearrange("c (g h) -> c g h", g=16), r=[pk(pi)], w=["ucm"])
            P.capture_begin()
            rot_ovr[0] = {"A": [0, 1, 4], "B": [2, 3, 5], "C": [4], "M": [5]}
            ws_ovr[0] = [0, 1, 2]
            s_q = wl_in(512, 256)
            for m in range(2):
                pi = psum("A")
                for k in range(8):
                    add("pe", call("matmul", ps[pi][:, 0:n], lhsT=vin(s_q, 256)[:, k, m * 128:(m + 1) * 128],
                                                                   rhs=xn[:, k, c0:c0 + n], start=(k == 0), stop=(k == 7)),
                        r=["ws%d" % s_q, "xn%d" % k], w=[pk(pi)], x=True)
                copy_op(evac_eng(), qT[:, m, 0:n], ps[pi][:, 0:n], r=[pk(pi)], w=["qT%d" % m])
            s_k = wl_in(768, 256)
            for m in range(2):
                pi = psum("A")
                for k in range(8):
                    add("pe", call("matmul", ps[pi][:, 0:n], lhsT=vin(s_k, 256)[:, k, m * 128:(m + 1) * 128],
                                                                   rhs=xn[:, k, c0:c0 + n], start=(k == 0), stop=(k == 7)),
                        r=["ws%d" % s_k, "xn%d" % k], w=[pk(pi)], x=True)
                copy_op(evac_eng(), kT[:, m, 0:n], ps[pi][:, 0:n], r=[pk(pi)], w=["kT%d" % m])
            for tt in range(ntile):
                pi = psum("B")
                for k in range(8):
                    add("pe", call("matmul", ps[pi][0:ntok, 0:256], lhsT=xn[:, k, c0 + tt * ntok:c0 + (tt + 1) * ntok],
                                                                     rhs=vin(s_k, 256)[:, k, :], start=(k == 0), stop=(k == 7)),
                        r=["ws%d" % s_k, "xn%d" % k], w=[pk(pi)], x=True)
                copy_op(evac_eng(), k_tm[0:ntok, tt, :], ps[pi][0:ntok, 0:256], r=[pk(pi)], w=["k_tm%d" % tt])
            for gq in range(2):
                s_ = wl_in(1024 + gq * 256, 256)
                for tt in range(ntile):
                    pi = psum("B")
                    for k in range(8):
                        add("pe", call("matmul", ps[pi][0:ntok, 0:256], lhsT=xn[:, k, c0 + tt * ntok:c0 + (tt + 1) * ntok],
                                                                                rhs=vin(s_, 256)[:, k, :], start=(k == 0), stop=(k == 7)),
                            r=["ws%d" % s_, "xn%d" % k], w=[pk(pi)], x=True)
                    copy_op(evac_eng(), v_tm[0:ntok, tt, gq * 256:(gq + 1) * 256], ps[pi][0:ntok, 0:256], r=[pk(pi)], w=["v_tm%d_%d" % (tt, gq)])
            for gq in range(2):
                s_ = wl_in(1536 + gq * 256, 256)
                for mm in range(2):
                    m = gq * 2 + mm
                    pi = psum("A")
                    for k in range(8):
                        add("pe", call("matmul", ps[pi][:, 0:n], lhsT=vin(s_, 256)[:, k, mm * 128:(mm + 1) * 128],
                                                                                rhs=xn[:, k, c0:c0 + n], start=(k == 0), stop=(k == 7)),
                            r=["ws%d" % s_, "xn%d" % k], w=[pk(pi)], x=True)
                    add("act", call("activation", out=sgT[:, m, 0:n], in_=ps[pi][:, 0:n], func=AF.Silu),
                        r=[pk(pi)], w=["sgT%d" % m], x=True)
            s_a = wl_in(2048, 16)
            pi = psum("A")
            for k in range(8):
                add("pe", call("matmul", ps[pi][0:16, 0:n], lhsT=vin(s_a, 16)[:, k, :], rhs=xn[:, k, c0:c0 + n],
                                                          start=(k == 0), stop=(k == 7)),
                    r=["ws%d" % s_a, "xn%d" % k], w=[pk(pi)], x=True)
            copy_op(evac_eng(), aT[0:16, 0:n], ps[pi][0:16, 0:n], r=[pk(pi)], w=["aT"])
            for tt in range(ntile):
                pi = psum("B")
                add("pe", call("matmul", ps[pi][0:ntok, 0:256], lhsT=aT[0:16, tt * ntok:(tt + 1) * ntok], rhs=gatew_b[:, :], start=True, stop=True),
                    r=["aT", "gatew_b"], w=[pk(pi)], x=True)
                add("act", call("activation", out=etmp[0:ntok, :], in_=ps[pi][0:ntok, 0:256], func=AF.Exp, scale=-1.0),
                    r=[pk(pi)], w=["etmp"], x=True)
                add("dve", call("tensor_tensor", out=etmp[0:ntok, :], in0=etmp[0:ntok, :], in1=gateb[0:ntok, :], op=ALU.mult),
                    r=["etmp", "gateb"], w=["etmp"], x=True)
                add("act", call("activation", out=lf[0:ntok, tt, :], in_=etmp[0:ntok, :], func=AF.Ln, bias=1.0),
                    r=["etmp"], w=["lf%d" % tt], x=True)

            rot_ovr[0] = {"A": [0], "B": [1], "C": [2, 3], "U": [4, 5], "M": [1]}
            pOs, pUs = {}, {}

            def gla_part1(tt):
                bi = tt % 2
                eb = ebb[bi]
                ebk = "eb%d" % bi
                tc0 = tt * ntok
                pC = psum("A")
                for m in range(2):
                    add("pe", call("matmul", ps[pC][:, m * 128:m * 128 + ntok], lhsT=lf[0:ntok, tt, m * 128:(m + 1) * 128],
                                   rhs=TRI[0:ntok, 0:ntok], start=True, stop=True), r=["lf%d" % tt, "const"], w=[pk(pC)], x=True)
                pR = psum("B")
                add("pe", call("matmul", ps[pR][0:ntok, 0:256], lhsT=TRIU[0:ntok, 0:ntok], rhs=lf[0:ntok, tt, :], start=True, stop=True),
                    r=["lf%d" % tt, "const"], w=[pk(pR)], x=True)
                pCv = ps[pC][:, 0:256].rearrange("p (m t) -> p m t", m=2)[:, :, 0:ntok]
                add("act", call("activation", out=eb[:, :, 0:ntok], in_=pCv, func=AF.Exp, scale=-1.0 / 16.0), r=[pk(pC)], w=[ebk], x=True)
                add("act", call("activation", out=enb[bi][:, :, 0:ntok], in_=pCv, func=AF.Exp, scale=1.0 / 16.0), r=[pk(pC)], w=["enb%d" % bi], x=True)
                add("act", call("activation", out=erem[bi][0:ntok, :], in_=ps[pR][0:ntok, 0:256], func=AF.Exp, scale=-1.0 / 16.0),
                    r=[pk(pR)], w=["erem%d" % bi], x=True)
                add("dve", call("scalar_tensor_tensor", out=qs[bi][:, :, 0:ntok], in0=qT[:, :, tc0:tc0 + ntok], scalar=0.125, in1=eb[:, :, 0:ntok],
                                op0=ALU.mult, op1=ALU.mult), r=["qT0", "qT1", ebk], w=["qs%d" % bi], x=True)
                add("dve", call("tensor_tensor", out=ki[bi][:, :, 0:ntok], in0=kT[:, :, tc0:tc0 + ntok], in1=enb[bi][:, :, 0:ntok], op=ALU.mult),
                    r=["kT0", "kT1", "enb%d" % bi], w=["ki%d" % bi], x=True)
                add("dve", call("tensor_tensor", out=kend[bi][0:ntok, :], in0=k_tm[0:ntok, tt, :], in1=erem[bi][0:ntok, :], op=ALU.mult),
                    r=["k_tm%d" % tt, "erem%d" % bi], w=["kend%d" % bi], x=True)
                pA = psum("A")
                for h in range(4):
                    m = h // 2
                    add("dve", call("tensor_scalar", out=kiz[bi][:, h, 0:ntok], in0=ki[bi][:, m, 0:ntok], scalar1=sel64[:, (h % 2):(h % 2) + 1], scalar2=None, op0=ALU.mult),
                        r=["ki%d" % bi, "const"], w=["kiz%d" % bi], x=True)
                for h in range(4):
                    m = h // 2
                    add("pe", call("matmul", ps[pA][0:ntok, h * 128:h * 128 + ntok], lhsT=kiz[bi][:, h, 0:ntok], rhs=qs[bi][:, m, 0:ntok], start=True, stop=True),
                        r=["kiz%d" % bi, "qs%d" % bi], w=[pk(pA)], x=True)
                for h in range(4):
                    add("dve", call("tensor_tensor", out=attT[bi][0:ntok, h, 0:ntok], in0=ps[pA][0:ntok, h * 128:h * 128 + ntok], in1=TRI[0:ntok, 0:ntok], op=ALU.mult),
                        r=[pk(pA), "const"], w=["attT%d" % bi], x=True)
                pO = psum("C")
                pOs[tt] = pO
                for h in range(4):
                    add("pe", call("matmul", ps[pO][:, h * 128:h * 128 + ntok], lhsT=v_tm[0:ntok, tt, h * 128:(h + 1) * 128],
                                   rhs=attT[bi][0:ntok, h, 0:ntok], start=(h == 0), stop=False, skip_group_check=True),
                        r=["v_tm%d_%d" % (tt, h // 2), "attT%d" % bi], w=[pk(pO)], x=True)
                if not sample:
                    pU = psum("U")
                    pUs[tt] = pU
                    for c in range(nch):
                        km = kendm[bi * 2 + c]
                        kmk = "kendm%d" % (bi * 2 + c)
                        add("dve", call("tensor_scalar", out=km[0:ntok, :], in0=kend[bi][0:ntok, :], scalar1=SEL[0:ntok, c:c + 1], scalar2=None, op0=ALU.mult),
                            r=["kend%d" % bi, "const"], w=[kmk], x=True)
                        for h in range(4):
                            m, po = h // 2, 64 * (h % 2)
                            add("pe", call("matmul", ps[pU][po:po + 64, c * 256 + m * 128:c * 256 + (m + 1) * 128], lhsT=km[0:ntok, h * 64:(h + 1) * 64],
                                           rhs=v_tm[0:ntok, tt, h * 128:(h + 1) * 128], start=True, stop=True),
                                r=[kmk, "v_tm%d_%d" % (tt, h // 2)], w=[pk(pU)], x=True)

            def gla_part2(tt):
                bi = tt % 2
                eb = ebb[bi]
                ebk = "eb%d" % bi
                tc0 = tt * ntok
                pO = pOs[tt]
                for c in range(nch):
                    cb = c % 2
                    if sample:
                        for m in range(2):
                            dma("sp", S0b[cb][:, m, :], gla_in[c, 2 * m:2 * m + 2, :, :].rearrange("h d e -> (h d) e"),
                                w=["S0b%d" % cb], semkey="S0b%d" % cb, x=True)
                        Ssrc, Skey = S0b[cb], "S0b%d" % cb
                    else:
                        Ssrc, Skey = S_f, "S_f"
                    Szc = Sz[cb]
                    Szk = "Sz%d" % cb
                    for h in range(4):
                        m = h // 2
                        add("act", call("activation", out=Szc[:, h, :], in_=Ssrc[:, m, :], func=AF.Identity, scale=sel64[:, (h % 2):(h % 2) + 1]),
                            r=[Skey, "const"], w=[Szk], x=True)
                    for h in range(4):
                        m = h // 2
                        last = (c == nch - 1) and (h == 3)
                        add("pe", call("matmul", ps[pO][:, h * 128 + c * L:h * 128 + (c + 1) * L], lhsT=Szc[:, h, :],
                                       rhs=qs[bi][:, m, c * L:(c + 1) * L], start=False, stop=last, skip_group_check=True),
                            r=[Szk, "qs%d" % bi], w=[pk(pO)], x=True)
                    if sample:
                        km = kendm[cb]
                        kmk = "kendm%d" % cb
                        add("dve", call("tensor_scalar", out=km[0:ntok, :], in0=kend[bi][0:ntok, :], scalar1=SEL[0:ntok, c:c + 1], scalar2=None, op0=ALU.mult),
                            r=["kend%d" % bi, "const"], w=[kmk], x=True)
                        pU = psum("U")
                        ucol = 0
                        for h in range(4):
                            m, po = h // 2, 64 * (h % 2)
                            add("pe", call("matmul", ps[pU][po:po + 64, m * 128:(m + 1) * 128], lhsT=km[0:ntok, h * 64:(h + 1) * 64],
                                           rhs=v_tm[0:ntok, tt, h * 128:(h + 1) * 128], start=True, stop=True),
                                r=[kmk, "v_tm%d_%d" % (tt, h // 2)], w=[pk(pU)], x=True)
                        Sdst, Sdk = Sob[cb], "Sob%d" % cb
                    else:
                        pU = pUs[tt]
                        ucol = c * 256
                        Sdst, Sdk = S_f, "S_f"
                    col_last = c * L + L - 1
                    for m in range(2):
                        add("dve", call("scalar_tensor_tensor", out=Sdst[:, m, :], in0=Ssrc[:, m, :], scalar=eb[:, m, col_last:col_last + 1],
                                        in1=ps[pU][:, ucol + m * 128:ucol + (m + 1) * 128], op0=ALU.mult, op1=ALU.add),
                            r=[Skey, ebk, pk(pU)], w=[Sdk], x=True)
                    if sample:
                        for m in range(2):
                            dma("sp", sgla_o[c, 2 * m:2 * m + 2, :, :].rearrange("h d e -> (h d) e"), Sob[cb][:, m, :],
                                r=["Sob%d" % cb], semkey="out", x=True)
                pOv = ps[pO][:, :].rearrange("p (h t) -> p h t", h=4)[:, :, 0:ntok]
                o_v = o_sb[:, 0:4 * ntok].rearrange("p (h t) -> p h t", h=4)
                add("act", call("copy", out=o_v, in_=pOv), r=[pk(pO)], w=["o_sb"], x=True)
                add("act", call("activation", out=osq[:, 0:4 * ntok], in_=o_sb[:, 0:4 * ntok], func=AF.Square), r=["o_sb"], w=["osq"], x=True)
                pN = psum("M")
                add("pe", call("matmul", ps[pN][:, 0:4 * ntok], lhsT=ones_b[:], rhs=osq[:, 0:4 * ntok], start=True, stop=True),
                    r=["osq", "ones_b"], w=[pk(pN)], x=True)
                add("act", call("activation", out=sdm[:, 0:4 * ntok], in_=ps[pN][:, 0:4 * ntok], func=AF.Ln, bias=EPS, scale=1.0 / 128.0),
                    r=[pk(pN)], w=["sdm"], x=True)
                add("act", call("activation", out=ps[pN][:, 0:4 * ntok], in_=sdm[:, 0:4 * ntok], func=AF.Exp, scale=-0.5), r=["sdm"], w=[pk(pN)], x=True)
                add("dve", call("scalar_tensor_tensor", out=t1b[:, 0:4 * ntok], in0=o_sb[:, 0:4 * ntok], scalar=gnorm[:, 0:1], in1=ps[pN][:, 0:4 * ntok],
                                op0=ALU.mult, op1=ALU.mult), r=["o_sb", "gnorm", pk(pN)], w=["t1b"], x=True)
                t1v = t1b[:, 0:4 * ntok].rearrange("p (h t) -> p h t", h=4)
                add("dve", call("tensor_tensor", out=cat[:, 4:8, tc0:tc0 + ntok], in0=t1v, in1=sgT[:, :, tc0:tc0 + ntok], op=ALU.mult),
                    r=["t1b", "sgT0", "sgT1", "sgT2", "sgT3"], w=["cat_o%d" % tt], x=True)

            gla_part1(0)
            for tt in range(1, ntile):
                gla_part1(tt)
                gla_part2(tt - 1)
            gla_part2(ntile - 1)
            if (not sample) and pass_idx == npass - 1:
                for m in range(2):
                    dma("sp", pgla_o[2 * m:2 * m + 2, :, :].rearrange("h d e -> (h d) e"), S_f[:, m, :], r=["S_f"], semkey="out")

            listA = P.capture_end()
            P.capture_begin()
            rot_ovr[0] = {"A": [6], "B": [7], "C": [6, 7], "M": [7]}
            ws_ovr[0] = [3]
            for gq in range(4):
                pi = psum("A")
                pbf = ps[pi][:].bitcast(BF16)
                for gg in range(8):
                    g = gq * 8 + gg
                    add("pe", call("transpose", out=pbf[:, gg * 64:gg * 64 + ncc], in_=ucm[0:ncc, g, :, :].rearrange("c j h -> c (j h)"),
                                                                          identity=ident_b[0:ncc, 0:ncc]),
                        r=["ucm", "ident_b"], w=[pk(pi)], x=True)
                src = pbf[:, 0:512].rearrange("p (g c) -> p g c", g=8)[:, :, 0:ncc]
                copy_op(evac_eng(), Ug[:, gq * 8:(gq + 1) * 8, 0:ncc], src, r=[pk(pi)], w=["Ug"])
            Bdst = Ssm if sample else Bst
            Bdk = "Ssm" if sample else "Bst"
            for q in range(4):
                pi = psum("B")
                for hfp in range(2):
                    for g4 in range(4):
                        gl = q * 4 + g4
                        g = hfp * 16 + gl
                        for ri in range(2):
                            col = (ri * 4 + g4) * 64
                            add("pe", call("matmul",
                                ps[pi][hfp * 64:hfp * 64 + 64, col:col + ncc], lhsT=W1[:, g, ri * 64:(ri + 1) * 64], rhs=Ug[:, g, 0:ncc],
                                start=True, stop=True), r=["W1", "Ug"], w=[pk(pi)], x=True)
                src = ps[pi][:, :].rearrange("p (r g c) -> p r g c", r=2, g=4)[:, :, :, 0:ncc]
                if sample:
                    dst = Ssm[:, :, q * 4:(q + 1) * 4, 0:ncc]
                else:
                    dst = Bst[:, :, q * 4:(q + 1) * 4, 1:1 + ncc]
                copy_op(evac_eng(), dst, src, r=[pk(pi)], w=[Bdk])
            if sample:
                pcs = 0
                for ri, src_d in enumerate((s5re_in, s5im_in)):
                    srcv = src_d.rearrange("b (a g) p -> b g a p", a=2)
                    pi = psum("A")
                    for q4 in range(4):
                        bb = pcs % 2
                        pcs += 1
                        for g4 in range(4):
                            dma("sp", h0t[bb][g4 * 16:(g4 + 1) * 16, :, :], srcv[:, q4 * 4 + g4, :, :], w=["h0t%d" % bb], semkey="h0t%d" % bb, x=True)
                        add("pe", call("transpose", out=ps[pi][:, q4 * 64:(q4 + 1) * 64], in_=h0t[bb][0:64, :, :].rearrange("r a p -> r (a p)"),
                                       identity=ident_f[0:64, 0:64]), r=["h0t%d" % bb, "ident_f"], w=[pk(pi)], x=True)
                    copy_op("dve", Bsm[:, ri, :, :], ps[pi][:, 0:256].rearrange("p (g b) -> p g b", g=16), r=[pk(pi)], w=["Bsm"])
                bcb = lambda a: a.unsqueeze(3).to_broadcast([128, 2, 16, 16])
                sw = lambda t: (t[:, 1, :, :], t[:, 0, :, :])
                T0, T1 = sm_t

                def cmul(dst, dk, src, sk, cr_, ci_neg_pos, ck):
                    add("dve", call("tensor_tensor", out=dst[:, :, :, :], in0=src[:, :, :, :], in1=bcb(cr_[:, :, :]), op=ALU.mult), r=[sk, ck], w=[dk], x=True)
                    add("dve", call("tensor_tensor", out=T1[:, 0, :, :], in0=src[:, 1, :, :], in1=ci_neg_pos[:, 0, :].unsqueeze(2).to_broadcast([128, 16, 16]), op=ALU.mult), r=[sk, ck], w=["smT1"], x=True)
                    add("dve", call("tensor_tensor", out=T1[:, 1, :, :], in0=src[:, 0, :, :], in1=ci_neg_pos[:, 1, :].unsqueeze(2).to_broadcast([128, 16, 16]), op=ALU.mult), r=[sk, ck], w=["smT1"], x=True)
                    add("dve", call("tensor_tensor", out=dst[:, :, :, :], in0=dst[:, :, :, :], in1=T1[:, :, :, :], op=ALU.add), r=[dk, "smT1"], w=[dk], x=True)

                cmul(T0, "smT0", Bsm, "Bsm", A1, A2, "A1A2")
                add("dve", call("tensor_tensor", out=T0[:, :, :, :], in0=T0[:, :, :, :], in1=Ssm[:, :, :, :], op=ALU.add), r=["smT0", "Ssm"], w=["smT0"], x=True)
                add("dve", call("tensor_copy", out=sc_t1[:, 0, :], in_=L4[:, 0, :]), r=["L4"], w=["sc_t1"], x=True)
                add("dve", call("tensor_copy", out=sc_t1[:, 1, :], in_=L4[:, 0, :]), r=["L4"], w=["sc_t1"], x=True)
                add("dve", call("tensor_scalar", out=sc_t2[:, 0, :], in0=L4[:, 1, :], scalar1=-1.0, scalar2=None, op0=ALU.mult), r=["L4"], w=["sc_t2"], x=True)
                add("dve", call("tensor_copy", out=sc_t2[:, 1, :], in_=L4[:, 1, :]), r=["L4"], w=["sc_t2"], x=True)
                cmul(Hs3, "Hs3", T0, "smT0", sc_t1, sc_t2, "sc_t2")
                pcs = 0
                for ri, dst_d in enumerate((sre_o, sim_o)):
                    dstv = dst_d.rearrange("b (a g) p -> b g a p", a=2)
                    for q4 in range(4):
                        bb = pcs % 2
                        pcs += 1
                        pi = psum("B")
                        add("pe", call("transpose", out=ps[pi][0:64, 0:128], in_=Hs3[:, ri, q4 * 4:(q4 + 1) * 4, :].rearrange("p g b -> p (g b)"), identity=ident_f[:]),
                            r=["Hs3", "ident_f"], w=[pk(pi)], x=True)
                        copy_op(evac_eng(), hot[bb][0:64, :, :], ps[pi][0:64, 0:128].rearrange("r (a p) -> r a p", a=2), r=[pk(pi)], w=["hot%d" % bb])
                        for g4 in range(4):
                            dma("sp", dstv[:, q4 * 4 + g4, :, :], hot[bb][g4 * 16:(g4 + 1) * 16, :, :], r=["hot%d" % bb], semkey="out", x=True)
            else:
                nsb = ncc // 8
                bc8 = lambda a: a.unsqueeze(3).to_broadcast([128, 2, 16, nsb])
                bc8h = lambda a: a.unsqueeze(2).to_broadcast([128, 16, nsb])
                for j in range(1, 8):
                    src = Bst[:, :, :, j:j + 8 * (nsb - 1) + 1:8]
                    dst = Bst[:, :, :, j + 1:j + 1 + 8 * (nsb - 1) + 1:8]
                    add("dve", call("tensor_tensor", out=sct1[:, :, :, 0:nsb], in0=src, in1=bc8(A1[:, :, :]), op=ALU.mult), r=["Bst", "A1A2"], w=["sct1"], x=True)
                    add("dve", call("tensor_tensor", out=sct2[:, 0, :, 0:nsb], in0=Bst[:, 1, :, j:j + 8 * (nsb - 1) + 1:8], in1=bc8h(A2[:, 0, :]), op=ALU.mult), r=["Bst", "A1A2"], w=["sct2"], x=True)
                    add("dve", call("tensor_tensor", out=sct2[:, 1, :, 0:nsb], in0=Bst[:, 0, :, j:j + 8 * (nsb - 1) + 1:8], in1=bc8h(A2[:, 1, :]), op=ALU.mult), r=["Bst", "A1A2"], w=["sct2"], x=True)
                    add("dve", call("tensor_tensor", out=dst, in0=dst, in1=sct1[:, :, :, 0:nsb], op=ALU.add), r=["Bst", "sct1"], w=["Bst"], x=True)
                    add("dve", call("tensor_tensor", out=dst, in0=dst, in1=sct2[:, :, :, 0:nsb], op=ALU.add), r=["Bst", "sct2"], w=["Bst"], x=True)
                for sbk in range(nsb):
                    car = Bst[:, :, :, 8 * sbk]
                    dst = Bst[:, :, :, 8 * sbk + 1:8 * sbk + 9]
                    add("dve", call("tensor_tensor", out=sct1[:, :, :, 0:8], in0=PA1[:, :, :, :], in1=car.unsqueeze(3).to_broadcast([128, 2, 16, 8]), op=ALU.mult), r=["Bst", "PA"], w=["sct1"], x=True)
                    add("dve", call("tensor_tensor", out=sct2[:, 0, :, 0:8], in0=PA2[:, 0, :, :], in1=Bst[:, 1, :, 8 * sbk].unsqueeze(2).to_broadcast([128, 16, 8]), op=ALU.mult), r=["Bst", "PA"], w=["sct2"], x=True)
                    add("dve", call("tensor_tensor", out=sct2[:, 1, :, 0:8], in0=PA2[:, 1, :, :], in1=Bst[:, 0, :, 8 * sbk].unsqueeze(2).to_broadcast([128, 16, 8]), op=ALU.mult), r=["Bst", "PA"], w=["sct2"], x=True)
                    add("dve", call("tensor_tensor", out=dst, in0=dst, in1=sct1[:, :, :, 0:8], op=ALU.add), r=["Bst", "sct1"], w=["Bst"], x=True)
                    add("dve", call("tensor_tensor", out=dst, in0=dst, in1=sct2[:, :, :, 0:8], op=ALU.add), r=["Bst", "sct2"], w=["Bst"], x=True)
            for hz in range(2):
                hsrc, hkey = (Bsm[:, :, :, 0:ncc], "Bsm") if sample else (Bst[:, :, :, 0:ncc], "Bst")
                add("dve", call("tensor_scalar", out=Hbz[hz][:, :, :, 0:ncc], in0=hsrc, scalar1=sel64[:, hz:hz + 1], scalar2=None, op0=ALU.mult),
                    r=[hkey, "const"], w=["Hbz"], x=True)
            for gq in range(8):
                pi = psum("C")
                for g4 in range(4):
                    g = gq * 4 + g4
                    hfp, gl = g // 16, g % 16
                    pr = slice(hfp * 64, hfp * 64 + 64)
                    osl = ps[pi][0:ncc, g4 * 128:(g4 + 1) * 128]
                    add("pe", call("matmul", osl, lhsT=Hbz[hfp][:, 0, gl, 0:ncc], rhs=W3[:, gl, 0, :], start=True, stop=False),
                        r=["Hbz", "W3"], w=[pk(pi)], x=True)
                    add("pe", call("matmul", osl, lhsT=Hbz[hfp][:, 1, gl, 0:ncc], rhs=W3[:, gl, 1, :], start=False, stop=False),
                        r=["Hbz", "W3"], w=[pk(pi)], x=True)
                    add("pe", call("matmul", osl, lhsT=Ug[:, g, 0:ncc], rhs=W4[:, g, :], start=False, stop=True),
                        r=["Ug", "W4"], w=[pk(pi)], x=True)
                src = ps[pi][0:ncc, :].rearrange("c (g j h) -> c j g h", g=4, j=8)
                dst = zcm[0:ncc, :, gq * 64:(gq + 1) * 64].rearrange("c j (g h) -> c j g h", g=4)
                add("act", call("activation", out=dst, in_=src, func=AF.Gelu_apprx_tanh), r=[pk(pi), "Ug"], w=["ucm"], x=True)
            if (not sample) and pass_idx == npass - 1:
                for hfp in range(2):
                    pr = slice(hfp * 64, hfp * 64 + 64)
                    gs = slice(hfp * 16, hfp * 16 + 16)
                    dma("sp", pre_o[gs, :].rearrange("g p -> p g"), Bst[pr, 0, :, ncc], r=["Bst"], semkey="out", slow=True)
                    dma("sp", pim_o[gs, :].rearrange("g p -> p g"), Bst[pr, 1, :, ncc], r=["Bst"], semkey="out", slow=True)
            if not sample:
                add("dve", call("tensor_copy", out=Bst[:, :, :, 0], in_=Bst[:, :, :, ncc]), r=["Bst"], w=["Bst"], x=True)
            for m in range(4):
                pi = psum("A")
                pbf = ps[pi][:].bitcast(BF16)
                for j in range(J):
                    add("pe", call("transpose", out=pbf[:, j * 64:j * 64 + ncc], in_=zcm[0:ncc, j, m * 128:(m + 1) * 128],
                                                                        identity=ident_b[0:ncc, 0:ncc]),
                        r=["ucm", "ident_b"], w=[pk(pi)], x=True)
                src = pbf[:, 0:J * 64].rearrange("p (j c) -> p j c", j=J)[:, :, 0:ncc]
                dst = zT[:, m, 0:n].rearrange("p (c j) -> p j c", j=J)
                copy_op(evac_eng(), dst, src, r=[pk(pi)], w=["zT%d" % m])
            s_g = wload(lambda t: t[:].rearrange("p (k f) -> p k f", k=4), glu_w_d.rearrange("(k p) f -> p k f", p=128))
            gv = wslot[s_g][:].rearrange("p (k f) -> p k f", k=4)
            for m in range(4):
                pi = psum("A")
                for k in range(4):
                    add("pe", call("matmul", ps[pi][:, 0:n], lhsT=gv[:, k, m * 128:(m + 1) * 128], rhs=zT[:, k, 0:n], start=(k == 0), stop=(k == 3)),
                        r=["ws%d" % s_g, "zT%d" % k], w=[pk(pi)], x=True)
                b = m % 2
                add("act", call("activation", out=sig[b][:, 0:n], in_=ps[pi][:, 0:n], func=AF.Sigmoid, bias=glub[:, m:m + 1]),
                    r=[pk(pi), "glub"], w=["sig"], x=True)
                add("dve", call("tensor_tensor", out=cat[:, m, 0:n], in0=zT[:, m, 0:n], in1=sig[b][:, 0:n], op=ALU.mult),
                    r=["zT%d" % m, "sig"], w=["cat_z%d" % m], x=True)
            listB = P.capture_end()
            rot_ovr[0] = None
            ws_ovr[0] = None
            P.merge([listA, listB], spans=[MERGE_SPAN_A, 1.0])
            woutv = w_out_d.rearrange("(k p) f -> p k f", p=128)
            for dp in range(4):
                s_ = wload(v8, woutv[:, :, dp * 256:(dp + 1) * 256])
                for dd in range(2):
                    d = dp * 2 + dd
                    pi = psum("C")
                    for k in range(8):
                        add("pe", call("matmul", ps[pi][:, 0:n], lhsT=v8(wslot[s_])[:, k, dd * 128:(dd + 1) * 128], rhs=cat[:, k, 0:n],
                                                                                start=(k == 0), stop=(k == 7)),
                            r=["ws%d" % s_, ("cat_z%d" % k) if k < 4 else None] + (["cat_o%d" % t_ for t_ in range(ntile)] if k >= 4 else []), w=[pk(pi)], x=True)
                    add("dve", call("tensor_tensor", out=xT[:, d, c0:c0 + n], in0=ps[pi][:, 0:n], in1=xT[:, d, c0:c0 + n], op=ALU.add),
                        r=[pk(pi), "xT%d" % d], w=["xT%d" % d], x=True)

        P.ops
        add("dve", call("memset", scr[:, 1:2], 0.0), r=["tri64", "triu64", "tri4", "triu4", "sel64", "sel4"], w=["const"])
        side_pc = None
        if do_mixer:
            P.capture_begin()
            rot_ovr[0] = {"A": [6, 7], "B": [6, 7], "C": [6, 7], "M": [6, 7]}
            precompute()
            add("dve", call("memset", scr[:, 2:3], 0.0), r=["A1", "A2"], w=["A1A2"], x=True)
            side_pc = P.capture_end()
            rot_ovr[0] = None
            if not do_ffn:
                P.ops.extend(side_pc)
                side_pc = None
        pending_store = [None]
        for pidx in range(npass):
            chunks = [(0, PT)]
            if pidx == 0:
                chunks = [(0, (PT + NSMP) // 2), ((PT + NSMP) // 2, (PT + NSMP) // 2)]
            P.capture_begin()
            rot_ovr[0] = {"A": [0, 1], "B": [2, 3], "C": [4, 5], "M": [6], "U": [2, 3]}
            for tt in range(PT // 128):
                r0 = pidx * PT + tt * 128
                load_x(xp[r0:r0 + 128, :], 128, tt * 128)
            if pidx == 0:
                load_x(xs[:, :], NSMP, PT)
            if do_ffn:
                rmsnorm(0, chunks, xn, "xn")
            rot_ovr[0] = None
            l_load = P.capture_end()
            if pending_store[0]:
                P.merge([pending_store[0], l_load])
                pending_store[0] = None
            else:
                P.ops.extend(l_load)
            if do_ffn:
                ffn(*w_ffn[0], chunks, side=(side_pc if pidx == 0 else None))
            if do_mixer:
                barrier()
                mixer("prompt", 0, PT, pidx)
                if pidx == 0:
                    mixer("sample", PT, NSMP, pidx)
                barrier()
            if do_ffn:
                rmsnorm(2, chunks, xn, "xn")
                ffn(*w_ffn[1], chunks)
            rmsnorm(3, chunks, yT, "yT")
            P.capture_begin()
            rot_ovr[0] = {"A": [0, 1], "B": [2, 3], "C": [4, 5], "M": [7], "U": [2, 3]}
            for tt in range(PT // 128):
                r0 = pidx * PT + tt * 128
                store_out(yp[r0:r0 + 128, :], 128, tt * 128)
            if pidx == 0:
                store_out(ys[:, :], NSMP, PT)
            rot_ovr[0] = None
            pending_store[0] = P.capture_end()
        if pending_store[0]:
            P.ops.extend(pending_store[0])
        tapsrc = {"xT": (xT[:], ["xT%d" % q_ for q_ in range(8)]), "yT": (yT[:, :, :], ["yT%d" % q_ for q_ in range(8)]), "sdb": (sdb[:, :], ["sdb"]),
                  "A1": (A1[:], ["A1"]), "A2": (A2[:], ["A2"]), "L4": (L4[:], ["L4"]), "Bst": (Bst[:], ["Bst"]), "S_f": (S_f[:], ["S_f"]),
                  "qT": (qT[:, :, :], ["qT0"]), "kT": (kT[:, :, :], ["kT0"]), "lf": (lf[:, :, :], ["lf0"]), "k_tm": (k_tm[:, :, :], ["k_tm0"]),
                  "xin0": (xin[0][:, :], ["xin0"]), "xin1": (xin[1][:, :], ["xin1"]), "pc_er": (pc_er[:, :, :], ["pc_er"]), "pc_ei": (pc_ei[:, :, :], ["pc_ei"])}
        for (tname, tshape) in taps:
            src, keys = tapsrc[tname]
            dma("sp", tap_out[tname], src, r=keys, semkey="out")
        P.emit(nc, final_semkeys=["out"])
    return nc


def _consts():
    c = {}
    c["c_ident"] = np.eye(128, dtype=np.float32)
    s = np.arange(128)
    same64 = (s[:, None] // 64) == (s[None, :] // 64)
    c["c_tri64"] = (same64 & (s[:, None] <= s[None, :])).astype(np.float32)
    c["c_triu64"] = (same64 & (s[:, None] > s[None, :])).astype(np.float32)
    s4 = np.arange(64)
    same4 = (s4[:, None] // 4) == (s4[None, :] // 4)
    c["c_tri4"] = (same4 & (s4[:, None] <= s4[None, :])).astype(np.float32)
    c["c_triu4"] = (same4 & (s4[:, None] > s4[None, :])).astype(np.float32)
    c["c_sel64"] = (s[:, None] // 64 == np.arange(2)[None, :]).astype(np.float32)
    c["c_sel4"] = (s4[:, None] // 4 == np.arange(16)[None, :]).astype(np.float32)
    i_ = s // 16
    c["c_maskw4"] = (i_[:, None] <= i_[None, :]).astype(np.float32)
    c["c_evals"] = np.array(EVALS, dtype=np.float32)
    return c


_NC_CACHE = {}


def kernel(x_prompt, x_sample, state_s5_re, state_s5_im, state_gla, norm_ffn1, ffn1_gate, ffn1_up,
           ffn1_down, norm_mix, w_in, s5_lam_re, s5_lam_im, s5_log_dt, s5_b_re, s5_b_im, s5_c_re,
           s5_c_im, s5_d, s5_glu_w, s5_glu_b, gla_gate_w, gla_gate_b, gla_norm, w_out, norm_ffn2,
           ffn2_gate, ffn2_up, ffn2_down, norm_final, _npass=NPASS_FULL, _do_mixer=True, _do_ffn=True, _taps=(), _stage=9):
    f = lambda a: np.ascontiguousarray(np.asarray(a, dtype=np.float32))
    key = (_npass, _do_mixer, _do_ffn, str(_taps), _stage)
    if key not in _NC_CACHE:
        _NC_CACHE[key] = build_program(npass=_npass, do_mixer=_do_mixer, do_ffn=_do_ffn, taps=_taps, stage=_stage)
    nc = _NC_CACHE[key]
    shared = {
        "gains": f(np.stack([np.asarray(norm_ffn1)[0], np.asarray(norm_mix)[0], np.asarray(norm_ffn2)[0], np.asarray(norm_final)])),
        "ffn1_gate": f(ffn1_gate[0]), "ffn1_up": f(ffn1_up[0]), "ffn1_down": f(ffn1_down[0]),
        "ffn2_gate": f(ffn2_gate[0]), "ffn2_up": f(ffn2_up[0]), "ffn2_down": f(ffn2_down[0]),
        "w_in": f(w_in[0]), "w_out": f(w_out[0]), "glu_w": f(s5_glu_w[0]), "glu_b": f(s5_glu_b[0]),
        "gate_w": f(gla_gate_w[0]), "gate_b": f(gla_gate_b[0]), "gla_norm": f(gla_norm[0]),
        "lam_re": f(s5_lam_re[0]), "lam_im": f(s5_lam_im[0]), "log_dt": f(s5_log_dt[0]),
        "b_re": f(s5_b_re[0]), "b_im": f(s5_b_im[0]), "c_re": f(s5_c_re[0]), "c_im": f(s5_c_im[0]),
        "s5_d": f(s5_d[0]),
    }
    shared.update(_consts())
    xp_ = np.asarray(x_prompt, dtype=np.float32)
    xs_ = np.asarray(x_sample, dtype=np.float32)
    sre = np.asarray(state_s5_re, dtype=np.float32)[0]
    sim = np.asarray(state_s5_im, dtype=np.float32)[0]
    sgl = np.asarray(state_gla, dtype=np.float32)[0]
    in_maps = []
    for i in range(NCORES):
        m = dict(shared)
        m["xp"] = f(xp_[i])
        m["xs"] = f(xs_[16 * i:16 * i + 16].reshape(NSMP, D))
        m["s5re_in"] = f(sre[16 * i:16 * i + 16])
        m["s5im_in"] = f(sim[16 * i:16 * i + 16])
        m["gla_in"] = f(sgl[16 * i:16 * i + 16])
        in_maps.append(m)
    res = run_bass_kernel_spmd(nc, in_maps, core_ids=list(range(NCORES)))
    R = res.results
    y_prompt = np.stack([R[i]["yp"] for i in range(NCORES)]).astype(np.float32)
    y_sample = np.concatenate([R[i]["ys"].reshape(16, 4, D) for i in range(NCORES)]).astype(np.float32)
    p_re = np.stack([R[i]["pre"] for i in range(NCORES)])[None].astype(np.float32)
    p_im = np.stack([R[i]["pim"] for i in range(NCORES)])[None].astype(np.float32)
    p_gla = np.stack([R[i]["pgla"] for i in range(NCORES)])[None].astype(np.float32)
    s_re = np.concatenate([R[i]["sre"] for i in range(NCORES)])[None].astype(np.float32)
    s_im = np.concatenate([R[i]["sim"] for i in range(NCORES)])[None].astype(np.float32)
    s_gla = np.concatenate([R[i]["sgla"] for i in range(NCORES)])[None].astype(np.float32)
    if _taps:
        return (y_prompt, y_sample, p_re, p_im, p_gla, s_re, s_im, s_gla), {t[0]: R[0]["tap_" + t[0]] for t in _taps}
    return (y_prompt, y_sample, p_re, p_im, p_gla, s_re, s_im, s_gla)
```

```python
import contextlib
import math
import numpy as np
import concourse.bass as bass
import concourse.mybir as mybir
from concourse.bass_utils import run_bass_kernel_spmd

F32 = mybir.dt.float32
BF16 = mybir.dt.bfloat16
I32 = mybir.dt.int32
AF = mybir.ActivationFunctionType
ALU = mybir.AluOpType

ENGS = ["pe", "act", "dve", "pool", "sp"]
EPS = 1e-6
NCORES = 8
D = 1024
DFF = 2816
NFT = 22
SEQ = 2048
PT = 512
NPASS_FULL = SEQ // PT
NSMP = 64
INW = 2064
EVALS = [0, 1, 2, 3, 4, 5, 6, 7, 8,
         0, -1, -2, -3, -4, -5, -6, -7,
         7, 6, 5, 4, 3, 2, 1, 0,
         -4,
         16, 24, 32, 40, 48, 56, 64]
NEV = len(EVALS)
MERGE_SPAN_A = 0.5


class Op:
    __slots__ = ("eng", "fn", "r", "w", "dma", "semkey", "deps", "raw", "need_inc", "incval", "dmaval", "pos", "prewait")


class Prog:
    def __init__(self):
        self.ops = []

    def add(self, eng, fn, r=(), w=(), dma=False, semkey=None):
        op = Op()
        op.eng, op.fn, op.r, op.w, op.dma, op.semkey = eng, fn, list(r), list(w), dma, semkey
        op.deps, op.need_inc, op.incval, op.dmaval = [], False, 0, 0
        op.raw = set()
        op.prewait = 0
        op.pos = -1
        self.ops.append(op)
        return op

    def capture_begin(self):
        self._saved = getattr(self, "_saved", [])
        self._saved.append(self.ops)
        self.ops = []

    def capture_end(self):
        lst = self.ops
        self.ops = self._saved.pop()
        return lst

    def merge(self, lists, spans=None):
        if spans is None:
            spans = [1.0] * len(lists)
        keep = [i for i, l in enumerate(lists) if l]
        lists = [lists[i] for i in keep]
        spans = [spans[i] for i in keep]
        idx = [0] * len(lists)
        total = sum(len(l) for l in lists)
        for _ in range(total):
            best, bf = None, None
            for i, l in enumerate(lists):
                if idx[i] < len(l):
                    f = idx[i] / len(l) * spans[i]
                    if bf is None or f < bf:
                        best, bf = i, f
            self.ops.append(lists[best][idx[best]])
            idx[best] += 1

    def analyze(self):
        for i, op in enumerate(self.ops):
            op.pos = i
        last_w = {}
        rd_eng = {}
        rd_dma = {}
        for op in self.ops:
            deps = set()
            for k in op.r:
                if k in last_w:
                    deps.add(last_w[k])
                    op.raw.add(last_w[k])
            for k in op.w:
                if k in last_w:
                    deps.add(last_w[k])
                for p in rd_eng.get(k, {}).values():
                    deps.add(p)
                for p in rd_dma.get(k, ()):
                    deps.add(p)
            deps.discard(op.pos)
            op.deps = sorted(deps)
            for k in op.w:
                last_w[k] = op.pos
                rd_eng[k] = {}
                rd_dma[k] = []
            for k in op.r:
                if op.dma:
                    rd_dma.setdefault(k, []).append(op.pos)
                else:
                    rd_eng.setdefault(k, {})[op.eng] = op.pos
        for op in self.ops:
            for d in op.deps:
                a = self.ops[d]
                if a.dma:
                    continue
                if a.eng != op.eng or op.dma or a.eng != "pe":
                    a.need_inc = True
        cnt = {e: 0 for e in ENGS}
        dcnt = {}
        self.dma_hist = {}
        for op in self.ops:
            if op.dma:
                dcnt[op.semkey] = dcnt.get(op.semkey, 0) + 16
                op.dmaval = dcnt[op.semkey]
                self.dma_hist.setdefault(op.semkey, []).append((op.pos, op.dmaval))
            elif op.need_inc:
                cnt[op.eng] += 1
                op.incval = cnt[op.eng]
        self.semkeys = list(dcnt.keys())
        self.dma_total = dcnt

    def _dma_wait_val(self, semkey, pos):
        v = 0
        for p, c in self.dma_hist[semkey]:
            if p < pos:
                v = c
            else:
                break
        return v

    def emit(self, nc, final_semkeys=()):
        self.analyze()
        maxw = {}
        for op in self.ops:
            if op.dma:
                op.prewait = maxw.get(op.semkey, 0)
            for d in op.deps:
                a = self.ops[d]
                if a.dma:
                    v = self._dma_wait_val(a.semkey, op.pos)
                    if v > maxw.get(a.semkey, 0):
                        maxw[a.semkey] = v
        with contextlib.ExitStack() as st:
            esem = {e: st.enter_context(nc.semaphore("s_" + e)) for e in ENGS}
            dsem = {k: st.enter_context(nc.semaphore("d_%d" % i)) for i, k in enumerate(self.semkeys)}
            block = st.enter_context(nc.Block())
            per_eng = {e: [op for op in self.ops if op.eng == e] for e in ENGS}

            def run(ename, eobj):
                waited = {}
                for op in per_eng[ename]:
                    need = {}
                    for d in op.deps:
                        a = self.ops[d]
                        if a.dma:
                            key = ("d", a.semkey)
                            val = self._dma_wait_val(a.semkey, op.pos)
                            sem = dsem[a.semkey]
                        else:
                            if a.eng == ename and not op.dma and ename == "pe":
                                continue
                            key = ("e", a.eng)
                            val = a.incval
                            sem = esem[a.eng]
                        if val > need.get(key, (0, None))[0]:
                            need[key] = (val, sem)
                    for key, (val, sem) in need.items():
                        if waited.get(key, 0) >= val:
                            continue
                        eobj.wait_ge(sem, val)
                        waited[key] = val
                    if op.dma and op.prewait > waited.get(("d", op.semkey), 0):
                        eobj.wait_ge(dsem[op.semkey], op.prewait)
                        waited[("d", op.semkey)] = op.prewait
                    ins = op.fn(eobj)
                    if op.dma:
                        ins.then_inc(dsem[op.semkey], 16)
                    elif op.need_inc:
                        ins.then_inc(esem[ename], 1)
                if ename == "sp":
                    for k in final_semkeys:
                        if k in dsem:
                            eobj.wait_ge(dsem[k], self.dma_total[k])

            @block.tensor
            def _(e):
                run("pe", e)

            @block.scalar
            def _(e):
                run("act", e)

            @block.vector
            def _(e):
                run("dve", e)

            @block.gpsimd
            def _(e):
                run("pool", e)

            @block.sync
            def _(e):
                run("sp", e)


def build_program(npass=NPASS_FULL, do_mixer=True, do_ffn=True, taps=(), stage=9):
    nc = bass.Bass("TRN2", target_bir_lowering=False, dynamic_dma_scratch_size=4096)
    P = Prog()

    def din(name, shape):
        return nc.dram_tensor(name, list(shape), F32, kind="ExternalInput").ap()

    def dout(name, shape):
        return nc.dram_tensor(name, list(shape), F32, kind="ExternalOutput").ap()

    xp = din("xp", [SEQ, D])
    xs = din("xs", [NSMP, D])
    s5re_in = din("s5re_in", [16, 32, 64])
    s5im_in = din("s5im_in", [16, 32, 64])
    gla_in = din("gla_in", [16, 4, 64, 128])
    gains_d = din("gains", [4, D])
    w_ffn = [(din("ffn1_gate", [D, DFF]), din("ffn1_up", [D, DFF]), din("ffn1_down", [DFF, D])),
             (din("ffn2_gate", [D, DFF]), din("ffn2_up", [D, DFF]), din("ffn2_down", [DFF, D]))]
    w_in_d = din("w_in", [D, INW])
    w_out_d = din("w_out", [D, D])
    glu_w_d = din("glu_w", [512, 512])
    glu_b_d = din("glu_b", [512])
    gate_w_d = din("gate_w", [16, 256])
    gate_b_d = din("gate_b", [256])
    gla_norm_d = din("gla_norm", [128])
    lam_re_d = din("lam_re", [32, 64])
    lam_im_d = din("lam_im", [32, 64])
    log_dt_d = din("log_dt", [32])
    b_re_d = din("b_re", [32, 64, 16])
    b_im_d = din("b_im", [32, 64, 16])
    c_re_d = din("c_re", [32, 16, 64])
    c_im_d = din("c_im", [32, 16, 64])
    s5_d_d = din("s5_d", [512])
    ident_d = din("c_ident", [128, 128])
    tri64_d = din("c_tri64", [128, 128])
    triu64_d = din("c_triu64", [128, 128])
    tri4_d = din("c_tri4", [64, 64])
    triu4_d = din("c_triu4", [64, 64])
    sel64_d = din("c_sel64", [128, 2])
    sel4_d = din("c_sel4", [64, 16])
    maskw4_d = din("c_maskw4", [128, 128])
    evals_d = din("c_evals", [NEV])

    yp = dout("yp", [SEQ, D])
    ys = dout("ys", [NSMP, D])
    pre_o = dout("pre", [32, 64])
    pim_o = dout("pim", [32, 64])
    pgla_o = dout("pgla", [4, 64, 128])
    sre_o = dout("sre", [16, 32, 64])
    sim_o = dout("sim", [16, 32, 64])
    sgla_o = dout("sgla", [16, 4, 64, 128])
    tap_out = {}
    for (tname, tshape) in taps:
        tap_out[tname] = dout("tap_" + tname, tshape)

    st = contextlib.ExitStack()
    with st:
        def sb(name, shape, dt):
            return st.enter_context(nc.sbuf_tensor("sb_" + name, list(shape), dt))

        NMAX = PT + NSMP
        xT = sb("xT", [128, 8, NMAX], F32)
        xn = sb("xn", [128, 8, NMAX], BF16)
        NW = 4
        wslot = [sb("wslot%d" % i, [128, 2048], BF16) for i in range(NW)]
        W1 = sb("W1", [128, 32, 128], BF16)
        W3 = sb("W3", [128, 16, 2, 128], BF16)
        W4 = sb("W4", [128, 32, 128], BF16)
        A1 = sb("A1", [128, 2, 16], F32)
        A2 = sb("A2", [128, 2, 16], F32)
        L4 = sb("L4", [128, 2, 16], F32)
        Bst = sb("Bst", [128, 2, 16, 65], F32)
        ident_f = sb("ident_f", [128, 128], F32)
        ident_b = sb("ident_b", [128, 128], BF16)
        ones_b = sb("ones_b", [128, 128], BF16)
        tri64 = sb("tri64", [128, 128], F32)
        triu64 = sb("triu64", [128, 128], F32)
        tri4 = sb("tri4", [64, 64], F32)
        triu4 = sb("triu4", [64, 64], F32)
        sel64 = sb("sel64", [128, 2], F32)
        sel4 = sb("sel4", [64, 16], F32)
        gains = sb("gains", [128, 4, 8], F32)
        glub = sb("glub", [128, 4], F32)
        gnorm = sb("gnorm", [128, 1], F32)
        gatew = sb("gatew", [16, 256], F32)
        gatew_b = sb("gatew_b", [16, 256], BF16)
        gateb = sb("gateb", [128, 256], F32)
        S_f = sb("S_f", [128, 2, 128], F32)
        PA1 = sb("PA1", [128, 2, 16, 8], F32)
        PA2 = sb("PA2", [128, 2, 16, 8], F32)
        scr = sb("scr", [128, 8], F32)

        XW = 26368
        arena = sb("arena", [128, XW], F32)

        class Carver:
            def __init__(self, base=0):
                self.off = base

            def get(self, shape, dt):
                n = 1
                for s in shape:
                    n *= s
                words = (n * (2 if dt == BF16 else 4) + 3) // 4
                words = (words + 7) // 8 * 8
                a = arena[:, self.off:self.off + words]
                self.off += words
                assert self.off <= XW, "arena overflow %d" % self.off
                if dt == BF16:
                    a = a.bitcast(BF16)[:, 0:n]
                elif dt == I32:
                    a = a.bitcast(I32)[:, 0:n]
                else:
                    a = a[:, 0:n]
                if len(shape) == 2:
                    return a.rearrange("p (a b) -> p a b", a=shape[0])
                if len(shape) == 3:
                    return a.rearrange("p (a b c) -> p a b c", a=shape[0], b=shape[1])
                return a

        cf = Carver(0)
        hT = cf.get([NFT, NMAX], BF16)
        sgb = [cf.get([512], F32) for _ in range(2)]
        sqb = [cf.get([512], BF16) for _ in range(2)]
        sdb = cf.get([512], F32)
        xin = [cf.get([1024], F32) for _ in range(2)]
        yT = Carver(0).get([8, NMAX], F32)
        ffn_end = cf.off
        cst = Carver(ffn_end)
        xst = [cst.get([1024], F32) for _ in range(2)]
        cp = Carver(ffn_end)
        pc_lr = cp.get([16], F32)
        pc_li = cp.get([16], F32)
        pc_dt = cp.get([16], F32)
        pc_ev = cp.get([NEV], F32)
        pc_er = cp.get([16, NEV], F32)
        pc_ei = cp.get([16, NEV], F32)
        pc_t0 = cp.get([16, NEV], F32)
        pc_t1 = cp.get([16, NEV], F32)
        pc_t2 = cp.get([16, NEV], F32)
        pc_ti = cp.get([16, NEV], I32)
        pc_fr = cp.get([16], F32)
        pc_fi = cp.get([16], F32)
        pc_s0 = cp.get([16], F32)
        pc_s1 = cp.get([16], F32)
        pc_s2 = cp.get([16], F32)
        pc_pr = cp.get([16, NEV], F32)
        pc_pi = cp.get([16, NEV], F32)
        pc_br = cp.get([16, 16], F32)
        pc_bi = cp.get([16, 16], F32)
        pc_cr = cp.get([16, 16], F32)
        pc_ci = cp.get([16, 16], F32)
        pc_dcol = cp.get([32], F32)
        pc_A = cp.get([16, 128], F32)
        pc_B = cp.get([16, 128], F32)
        pc_C = cp.get([16, 128], F32)
        pc_D = cp.get([16, 128], F32)
        pc_E = cp.get([16, 128], F32)
        pc_w4t = cp.get([128], F32)
        pc_csb = [pc_E[:, 2 * i_:2 * i_ + 2, :].rearrange("p a (b c) -> p a b c", b=2) for i_ in range(2)]
        cm = Carver(0)
        ucm_flat = cm.get([4096], BF16)
        ucm = ucm_flat.rearrange("p (g j h) -> p g j h", g=32, j=8)
        zcm = ucm_flat.rearrange("p (j c) -> p j c", j=8)
        qT = cm.get([2, 512], F32)
        kT = cm.get([2, 512], F32)
        k_tm = cm.get([4, 256], F32)
        v_tm = cm.get([4, 512], BF16)
        sgT = cm.get([4, 512], BF16)
        aT = cm.get([512], BF16)
        lf = cm.get([4, 256], F32)
        Ug = cm.get([32, 64], BF16)
        cat = cm.get([8, 512], BF16)
        zT = cm.get([4, 512], BF16)
        Hbz = [cm.get([2, 16, 64], BF16) for _ in range(2)]
        ebb = [cm.get([2, 128], F32) for _ in range(2)]
        enb = [cm.get([2, 128], F32) for _ in range(2)]
        erem = [cm.get([256], F32) for _ in range(2)]
        qs = [cm.get([2, 128], BF16) for _ in range(2)]
        ki = [cm.get([2, 128], BF16) for _ in range(2)]
        kiz = [cm.get([4, 128], BF16) for _ in range(2)]
        Sz = [cm.get([4, 128], BF16) for _ in range(2)]
        kend = [cm.get([256], BF16) for _ in range(2)]
        kendm = [cm.get([256], BF16) for _ in range(4)]
        attT = [cm.get([4, 128], BF16) for _ in range(2)]
        o_sb = cm.get([512], F32)
        osq = cm.get([512], BF16)
        t1b = cm.get([512], F32)
        etmp = t1b[:, 0:256]
        sdm = cm.get([512], F32)
        sig0_ = cm.get([512], F32)
        sig = [sig0_, sig0_]
        S0b = [cm.get([2, 128], F32) for _ in range(2)]
        Sob = [cm.get([2, 128], F32) for _ in range(2)]
        sct1 = cm.get([2, 16, 8], F32)
        sct2 = cm.get([2, 16, 8], F32)
        sc_t1 = cm.get([2, 16], F32)
        sc_t2 = cm.get([2, 16], F32)
        Bsm = cm.get([2, 16, 16], F32)
        Ssm = cm.get([2, 16, 16], F32)
        Hs3 = cm.get([2, 16, 16], F32)
        sm_t = [cm.get([2, 16, 16], F32) for _ in range(2)]
        h0t = [cm.get([2, 64], F32) for _ in range(2)]
        hot = [cm.get([2, 64], F32) for _ in range(2)]

        ps = [st.enter_context(nc.psum_tensor("ps%d" % i, [128, 512], F32)) for i in range(8)]
        rot = {"A": [0, 1], "B": [2, 3], "C": [4, 5], "M": [6, 7], "U": [2, 3]}
        rot_i = {k: 0 for k in rot}

        rot_ovr = [None]
        ws_ovr = [None]

        def psum(group):
            banks = rot_ovr[0][group] if rot_ovr[0] is not None else rot[group]
            i = banks[rot_i[group] % len(banks)]
            rot_i[group] += 1
            return i

        def pk(i):
            return "ps%d" % i

        XB = "Xbar"

        def call(name, *a, **kw):
            return lambda e: getattr(e, name)(*a, **kw)

        def add(eng, fn, r=(), w=(), x=False):
            r = [k_ for k_ in r if k_ is not None]
            if x:
                r.append(XB)
            return P.add(eng, fn, r=r, w=w)

        def dma(eng, out, in_, r=(), w=(), semkey=None, x=False, slow=False):
            r = list(r)
            if x:
                r.append(XB)
            if slow:
                fn = call("dma_start", out=out, in_=in_, allow_slow_non_contiguous=True)
            else:
                fn = call("dma_start", out=out, in_=in_)
            return P.add(eng, fn, r=r, w=w, dma=True, semkey=semkey)

        def barrier():
            P.add("pool", call("memset", scr[:, 0:1], 0.0), r=[], w=[XB])

        ws_i = [0]

        ws_hist = []

        def wload(out_view_fn, in_ap):
            pool_ = ws_ovr[0] if ws_ovr[0] is not None else list(range(NW))
            s = pool_[ws_i[0] % len(pool_)]
            ws_i[0] += 1
            extra = ["ws%d" % ws_hist[-2]] if len(ws_hist) >= 2 and ws_hist[-2] != s else []
            ws_hist.append(s)
            dma("pool", out_view_fn(wslot[s]), in_ap, r=extra, w=["ws%d" % s], semkey="ws%d" % s)
            return s

        ev_i = [0]

        def evac_eng():
            ev_i[0] += 1
            return "act" if ev_i[0] % 2 == 0 else "dve"

        def copy_op(eng, out, in_, r, w, x=True):
            if eng == "act":
                add("act", call("copy", out=out, in_=in_), r=r, w=w, x=x)
            else:
                add(eng, call("tensor_copy", out=out, in_=in_), r=r, w=w, x=x)

        cload = [(ident_f[:], ident_d, "ident_f"), (tri64[:], tri64_d, "tri64"), (triu64[:], triu64_d, "triu64"),
                 (tri4[:], tri4_d, "tri4"), (triu4[:], triu4_d, "triu4"), (sel64[:], sel64_d, "sel64"),
                 (sel4[:], sel4_d, "sel4"), (gatew[:], gate_w_d, "gatew"),
                 (gateb[:], gate_b_d.partition_broadcast(128), "gateb")]
        for (o, i, k) in cload:
            dma("sp", o, i, w=[k], semkey="const")
        dma("sp", gains[:], gains_d.rearrange("n (k p) -> p n k", p=128), w=["gains"], semkey="const", slow=True)
        dma("sp", glub[:], glu_b_d.rearrange("(m p) -> p m", p=128), w=["glub"], semkey="const", slow=True)
        dma("sp", gnorm[:], gla_norm_d.rearrange("(p o) -> p o", o=1), w=["gnorm"], semkey="const", slow=True)
        add("dve", call("tensor_copy", out=ident_b[:], in_=ident_f[:]), r=["ident_f"], w=["ident_b"])
        add("dve", call("tensor_copy", out=gatew_b[:], in_=gatew[:]), r=["gatew"], w=["gatew_b"])
        add("act", call("activation", out=gateb[:], in_=gateb[:], func=AF.Exp, scale=-1.0), r=["gateb"], w=["gateb"])
        add("dve", call("memset", ones_b[:], 1.0), w=["ones_b"])
        add("dve", call("memset", S_f[:], 0.0), w=["S_f"])
        add("dve", call("memset", Bst[:], 0.0), w=["Bst"])

        xin_i = [0]

        def load_x(src_rows, ntok, col0):
            b = xin_i[0] % 2
            xin_i[0] += 1
            kx = "xin%d" % b
            dma("act", xin[b][0:ntok, :], src_rows, w=[kx], semkey=kx, x=True)
            for half in range(2):
                pi = psum("M")
                for kk in range(4):
                    k = half * 4 + kk
                    add("pe", call("transpose",
                        out=ps[pi][:, kk * 128:kk * 128 + ntok], in_=xin[b][0:ntok, k * 128:(k + 1) * 128],
                        identity=ident_f[0:ntok, 0:ntok]), r=[kx, "ident_f"], w=[pk(pi)], x=True)
                src = ps[pi][:].rearrange("p (a b) -> p a b", a=4)[:, :, 0:ntok]
                dst = xT[:, half * 4:half * 4 + 4, col0:col0 + ntok]
                copy_op(evac_eng(), dst, src, r=[pk(pi)], w=["xT%d" % (half * 4 + q_) for q_ in range(4)], x=True)

        def rmsnorm(gi, chunks, dst, dst_key):
            for (c0, n) in chunks:
                pi = psum("M")
                for k in range(8):
                    b = k % 2
                    add("act", call("activation", out=sqb[b][:, 0:n], in_=xT[:, k, c0:c0 + n],
                                                               func=AF.Square),
                        r=["xT%d" % k], w=["sq%d" % b], x=True)
                    add("pe", call("matmul", ps[pi][:, 0:n], lhsT=ones_b[:], rhs=sqb[b][:, 0:n],
                                                                 start=(k == 0), stop=(k == 7)),
                        r=["sq%d" % b, "ones_b"], w=[pk(pi)], x=True)
                add("act", call("activation", out=sdb[:, 0:n], in_=ps[pi][:, 0:n], func=AF.Ln,
                                                        bias=EPS, scale=1.0 / D),
                    r=[pk(pi)], w=["sdb"], x=True)
                add("act", call("activation", out=ps[pi][:, 0:n], in_=sdb[:, 0:n], func=AF.Exp, scale=-0.5),
                    r=["sdb"], w=[pk(pi)], x=True)
                for k in range(8):
                    add("dve", call("scalar_tensor_tensor",
                        out=dst[:, k, c0:c0 + n], in0=xT[:, k, c0:c0 + n], scalar=gains[:, gi, k:k + 1],
                        in1=ps[pi][:, 0:n], op0=ALU.mult, op1=ALU.mult),
                        r=["xT%d" % k, "gains", pk(pi)], w=[dst_key + str(k)] + (["hT%d" % f_ for f_ in range(NFT)] if dst_key == "yT" else []), x=True)

        def ffn(wg, wu, wd, chunks, side=None):
            nside = [0]
            npts = NFT // 2 + 8

            side_dma = [o for o in side if o.dma] if side else []
            side_cmp = [o for o in side if not o.dma] if side else []
            if side_dma:
                P.ops.extend(side_dma)
            first_pt = npts // 2 + 2

            def inject(ip):
                if side_cmp and ip >= first_pt:
                    hi = len(side_cmp) * (ip - first_pt + 1) // (npts - first_pt)
                    P.ops.extend(side_cmp[nside[0]:hi])
                    nside[0] = hi
            wgv = wg.rearrange("(k p) f -> p k f", p=128)
            wuv = wu.rearrange("(k p) f -> p k f", p=128)
            wdv = wd.rearrange("(t p) d -> p t d", p=128)
            v8 = lambda t: t[:].rearrange("p (k f) -> p k f", k=8)
            for fp in range(NFT // 2):
                if fp > 0:
                    inject(fp - 1)
                sg_ = wload(v8, wgv[:, :, fp * 256:(fp + 1) * 256])
                su_ = wload(v8, wuv[:, :, fp * 256:(fp + 1) * 256])
                for hf in range(2):
                    f = fp * 2 + hf
                    for (c0, n) in chunks:
                        pa = psum("A")
                        pb = psum("B")
                        for (pi, s_) in ((pa, sg_), (pb, su_)):
                            for k in range(8):
                                add("pe", call("matmul",
                                    ps[pi][:, 0:n], lhsT=v8(wslot[s_])[:, k, hf * 128:(hf + 1) * 128],
                                    rhs=xn[:, k, c0:c0 + n], start=(k == 0), stop=(k == 7)),
                                    r=["ws%d" % s_, "xn%d" % k], w=[pk(pi)], x=True)
                        b = f % 2
                        add("act", call("activation", out=sgb[b][:, 0:n], in_=ps[pa][:, 0:n],
                                                                          func=AF.Silu),
                            r=[pk(pa)], w=["sg%d" % b], x=True)
                        add("dve", call("tensor_tensor",
                            out=hT[:, f, c0:c0 + n], in0=sgb[b][:, 0:n], in1=ps[pb][:, 0:n], op=ALU.mult),
                            r=["sg%d" % b, pk(pb)], w=["hT%d" % f] + (["yT%d" % q_ for q_ in range(8)] if (f == 0 and c0 == 0) else []), x=True)
            v11 = lambda t: t[:, 0:1408].rearrange("p (t c) -> p t c", t=11)
            for d in range(8):
                inject(NFT // 2 + d)
                sl = [wload(v11, wdv[:, hh * 11:(hh + 1) * 11, d * 128:(d + 1) * 128]) for hh in range(2)]
                for (c0, n) in chunks:
                    pi = psum("C")
                    for f in range(NFT):
                        s_ = sl[f // 11]
                        add("pe", call("matmul",
                            ps[pi][:, 0:n], lhsT=v11(wslot[s_])[:, f % 11, :], rhs=hT[:, f, c0:c0 + n],
                            start=(f == 0), stop=(f == NFT - 1)),
                            r=["ws%d" % s_, "hT%d" % f], w=[pk(pi)], x=True)
                    add("dve", call("scalar_tensor_tensor",
                        out=xT[:, d, c0:c0 + n], in0=ps[pi][:, 0:n], scalar=0.5, in1=xT[:, d, c0:c0 + n],
                        op0=ALU.mult, op1=ALU.add),
                        r=[pk(pi), "xT%d" % d], w=["xT%d" % d], x=True)
            if side_cmp:
                P.ops.extend(side_cmp[nside[0]:])
                nside[0] = len(side_cmp)

        xst_i = [0]

        def store_out(dst_rows, ntok, col0):
            b = xst_i[0] % 2
            xst_i[0] += 1
            kx = "xst%d" % b
            for half in range(2):
                pi = psum("M")
                for kk in range(4):
                    k = half * 4 + kk
                    add("pe", call("transpose", out=ps[pi][0:ntok, kk * 128:(kk + 1) * 128], in_=yT[:, k, col0:col0 + ntok],
                                   identity=ident_f[:]), r=["yT%d" % k, "ident_f"], w=[pk(pi)], x=True)
                copy_op(evac_eng(), xst[b][0:ntok, half * 512:(half + 1) * 512], ps[pi][0:ntok, :],
                        r=[pk(pi)], w=[kx], x=True)
            dma("sp", dst_rows, xst[b][0:ntok, :], r=[kx], semkey="out", x=True)

        def precompute():
            x_ = True
            for hfp in range(2):
                pr = slice(hfp * 64, hfp * 64 + 64)
                gs = slice(hfp * 16, hfp * 16 + 16)
                dma("sp", pc_lr[pr, :], lam_re_d[gs, :].rearrange("g p -> p g"), w=["pc_lr"], semkey="pc", x=x_, slow=True)
                dma("sp", pc_li[pr, :], lam_im_d[gs, :].rearrange("g p -> p g"), w=["pc_li"], semkey="pc", x=x_, slow=True)
                dma("sp", pc_dt[pr, :], log_dt_d[gs].partition_broadcast(64), w=["pc_dt"], semkey="pc", x=x_, slow=True)
                dma("sp", pc_br[pr, :, :], b_re_d[gs, :, :].rearrange("g p h -> p g h"), w=["pc_br"], semkey="pc", x=x_, slow=True)
                dma("sp", pc_bi[pr, :, :], b_im_d[gs, :, :].rearrange("g p h -> p g h"), w=["pc_bi"], semkey="pc", x=x_, slow=True)
            piC = psum("A")
            for t_, src_d in enumerate((c_re_d, c_im_d)):
                for glhi in range(2):
                    dma("sp", pc_csb[t_][:, glhi, :, :], src_d.rearrange("g h p -> (g h) p").rearrange("(a b q) p -> q b a p", a=2, b=2)[:, glhi, :, :],
                        w=["pc_csb%d" % t_], semkey="pc", x=x_)
            for t_ in range(2):
                for glhi in range(2):
                    add("pe", call("transpose", out=ps[piC][:, t_ * 256 + glhi * 128:t_ * 256 + (glhi + 1) * 128],
                                   in_=pc_csb[t_][:, glhi, :, :].rearrange("q a p -> q (a p)"), identity=ident_f[:]),
                        r=["pc_csb%d" % t_, "ident_f"], w=[pk(piC)], x=True)
            copy_op("dve", pc_cr[:, :, :], ps[piC][:, 0:256].rearrange("p (g h) -> p g h", g=16), r=[pk(piC)], w=["pc_cr"])
            copy_op("dve", pc_ci[:, :, :], ps[piC][:, 256:512].rearrange("p (g h) -> p g h", g=16), r=[pk(piC)], w=["pc_ci"])
            dma("sp", pc_ev[:, :], evals_d.partition_broadcast(128), w=["pc_ev"], semkey="pc", x=x_, slow=True)
            for i in range(8):
                dma("sp", pc_dcol[i * 16:(i + 1) * 16, :], s5_d_d.rearrange("(g h) -> h g", h=16), w=["pc_dcol"],
                    semkey="pc", x=x_, slow=True)
            dma("sp", pc_w4t[:, :], maskw4_d, w=["maskw4"], semkey="pc", x=x_)

            def V(eng, fn, r, w):
                add(eng, fn, r=r, w=w, x=True)

            bc_g = lambda a: a.unsqueeze(2).to_broadcast([128, 16, NEV])
            bc_e = lambda a: a.unsqueeze(1).to_broadcast([128, 16, NEV])
            V("act", call("activation", out=pc_dt[:, :], in_=pc_dt[:, :], func=AF.Exp), ["pc_dt"], ["pc_dt"])
            V("dve", call("tensor_tensor", out=pc_s0[:, :], in0=pc_lr[:, :], in1=pc_dt[:, :], op=ALU.mult), ["pc_lr", "pc_dt"], ["pc_s0"])
            V("dve", call("tensor_tensor", out=pc_s1[:, :], in0=pc_li[:, :], in1=pc_dt[:, :], op=ALU.mult), ["pc_li", "pc_dt"], ["pc_s1"])
            V("dve", call("tensor_tensor", out=pc_t0[:, :, :], in0=bc_g(pc_s0[:, :]), in1=bc_e(pc_ev[:, :]), op=ALU.mult), ["pc_s0", "pc_ev"], ["pc_t0"])
            V("dve", call("tensor_tensor", out=pc_t1[:, :, :], in0=bc_g(pc_s1[:, :]), in1=bc_e(pc_ev[:, :]), op=ALU.mult), ["pc_s1", "pc_ev"], ["pc_t1"])
            V("act", call("activation", out=pc_t0[:, :, :], in_=pc_t0[:, :, :], func=AF.Exp), ["pc_t0"], ["pc_t0"])
            C1 = 6.28125
            C2 = 2.0 * math.pi - C1
            V("dve", call("tensor_scalar", out=pc_t2[:, :, :], in0=pc_t1[:, :, :], scalar1=1.0 / (2.0 * math.pi), scalar2=None, op0=ALU.mult), ["pc_t1"], ["pc_t2"])
            V("dve", call("tensor_copy", out=pc_ti[:, :, :], in_=pc_t2[:, :, :]), ["pc_t2"], ["pc_ti"])
            V("dve", call("tensor_copy", out=pc_t2[:, :, :], in_=pc_ti[:, :, :]), ["pc_ti"], ["pc_t2"])
            V("dve", call("scalar_tensor_tensor", out=pc_t1[:, :, :], in0=pc_t2[:, :, :], scalar=-C1, in1=pc_t1[:, :, :], op0=ALU.mult, op1=ALU.add), ["pc_t2", "pc_t1"], ["pc_t1"])
            V("dve", call("scalar_tensor_tensor", out=pc_t1[:, :, :], in0=pc_t2[:, :, :], scalar=-C2, in1=pc_t1[:, :, :], op0=ALU.mult, op1=ALU.add), ["pc_t2", "pc_t1"], ["pc_t1"])
            V("act", call("activation", out=pc_t2[:, :, :], in_=pc_t1[:, :, :], func=AF.Sin, scale=0.5), ["pc_t1"], ["pc_t2"])
            V("act", call("activation", out=pc_t1[:, :, :], in_=pc_t1[:, :, :], func=AF.Sin, scale=0.25), ["pc_t1"], ["pc_t1"])
            V("dve", call("tensor_tensor", out=pc_t1[:, :, :], in0=pc_t1[:, :, :], in1=pc_t1[:, :, :], op=ALU.mult), ["pc_t1"], ["pc_t1"])
            V("dve", call("tensor_scalar", out=pc_t1[:, :, :], in0=pc_t1[:, :, :], scalar1=-2.0, scalar2=1.0, op0=ALU.mult, op1=ALU.add), ["pc_t1"], ["pc_t1"])
            V("dve", call("tensor_tensor", out=pc_t1[:, :, :], in0=pc_t1[:, :, :], in1=pc_t2[:, :, :], op=ALU.mult), ["pc_t1", "pc_t2"], ["pc_t1"])
            V("dve", call("scalar_tensor_tensor", out=pc_ei[:, :, :], in0=pc_t1[:, :, :], scalar=2.0, in1=pc_t0[:, :, :], op0=ALU.mult, op1=ALU.mult), ["pc_t1", "pc_t0"], ["pc_ei"])
            V("dve", call("tensor_tensor", out=pc_t2[:, :, :], in0=pc_t2[:, :, :], in1=pc_t2[:, :, :], op=ALU.mult), ["pc_t2"], ["pc_t2"])
            V("dve", call("tensor_scalar", out=pc_t2[:, :, :], in0=pc_t2[:, :, :], scalar1=-2.0, scalar2=1.0, op0=ALU.mult, op1=ALU.add), ["pc_t2"], ["pc_t2"])
            V("dve", call("tensor_tensor", out=pc_er[:, :, :], in0=pc_t2[:, :, :], in1=pc_t0[:, :, :], op=ALU.mult), ["pc_t2", "pc_t0"], ["pc_er"])
            abr = pc_er[:, :, 1]
            abi = pc_ei[:, :, 1]
            V("dve", call("tensor_tensor", out=pc_s0[:, :], in0=pc_lr[:, :], in1=pc_lr[:, :], op=ALU.mult), ["pc_lr"], ["pc_s0"])
            V("dve", call("tensor_tensor", out=pc_s1[:, :], in0=pc_li[:, :], in1=pc_li[:, :], op=ALU.mult), ["pc_li"], ["pc_s1"])
            V("dve", call("tensor_tensor", out=pc_s0[:, :], in0=pc_s0[:, :], in1=pc_s1[:, :], op=ALU.add), ["pc_s0", "pc_s1"], ["pc_s0"])
            V("dve", call("reciprocal", out=pc_s0[:, :], in_=pc_s0[:, :]), ["pc_s0"], ["pc_s0"])
            V("dve", call("tensor_scalar", out=pc_s1[:, :], in0=abr, scalar1=-1.0, scalar2=None, op0=ALU.add), ["pc_er"], ["pc_s1"])
            V("dve", call("tensor_tensor", out=pc_fr[:, :], in0=pc_s1[:, :], in1=pc_lr[:, :], op=ALU.mult), ["pc_s1", "pc_lr"], ["pc_fr"])
            V("dve", call("tensor_tensor", out=pc_s2[:, :], in0=abi, in1=pc_li[:, :], op=ALU.mult), ["pc_ei", "pc_li"], ["pc_s2"])
            V("dve", call("tensor_tensor", out=pc_fr[:, :], in0=pc_fr[:, :], in1=pc_s2[:, :], op=ALU.add), ["pc_fr", "pc_s2"], ["pc_fr"])
            V("dve", call("tensor_tensor", out=pc_fr[:, :], in0=pc_fr[:, :], in1=pc_s0[:, :], op=ALU.mult), ["pc_fr", "pc_s0"], ["pc_fr"])
            V("dve", call("tensor_tensor", out=pc_fi[:, :], in0=abi, in1=pc_lr[:, :], op=ALU.mult), ["pc_ei", "pc_lr"], ["pc_fi"])
            V("dve", call("tensor_tensor", out=pc_s2[:, :], in0=pc_s1[:, :], in1=pc_li[:, :], op=ALU.mult), ["pc_s1", "pc_li"], ["pc_s2"])
            V("dve", call("tensor_tensor", out=pc_fi[:, :], in0=pc_fi[:, :], in1=pc_s2[:, :], op=ALU.subtract), ["pc_fi", "pc_s2"], ["pc_fi"])
            V("dve", call("tensor_tensor", out=pc_fi[:, :], in0=pc_fi[:, :], in1=pc_s0[:, :], op=ALU.mult), ["pc_fi", "pc_s0"], ["pc_fi"])
            V("dve", call("tensor_tensor", out=pc_pr[:, :, :], in0=pc_er[:, :, :], in1=bc_g(pc_fr[:, :]), op=ALU.mult), ["pc_er", "pc_fr"], ["pc_pr"])
            V("dve", call("tensor_tensor", out=pc_t0[:, :, :], in0=pc_ei[:, :, :], in1=bc_g(pc_fi[:, :]), op=ALU.mult), ["pc_ei", "pc_fi"], ["pc_t0"])
            V("dve", call("tensor_tensor", out=pc_pr[:, :, :], in0=pc_pr[:, :, :], in1=pc_t0[:, :, :], op=ALU.subtract), ["pc_pr", "pc_t0"], ["pc_pr"])
            V("dve", call("tensor_tensor", out=pc_pi[:, :, :], in0=pc_er[:, :, :], in1=bc_g(pc_fi[:, :]), op=ALU.mult), ["pc_er", "pc_fi"], ["pc_pi"])
            V("dve", call("tensor_tensor", out=pc_t0[:, :, :], in0=pc_ei[:, :, :], in1=bc_g(pc_fr[:, :]), op=ALU.mult), ["pc_ei", "pc_fr"], ["pc_t0"])
            V("dve", call("tensor_tensor", out=pc_pi[:, :, :], in0=pc_pi[:, :, :], in1=pc_t0[:, :, :], op=ALU.add), ["pc_pi", "pc_t0"], ["pc_pi"])
            V("dve", call("tensor_copy", out=A1[:, 0, :], in_=pc_er[:, :, 8]), ["pc_er"], ["A1"])
            V("dve", call("tensor_copy", out=A1[:, 1, :], in_=pc_er[:, :, 8]), ["pc_er"], ["A1"])
            V("dve", call("tensor_scalar", out=A2[:, 0, :], in0=pc_ei[:, :, 8], scalar1=-1.0, scalar2=None, op0=ALU.mult), ["pc_ei"], ["A2"])
            V("dve", call("tensor_copy", out=A2[:, 1, :], in_=pc_ei[:, :, 8]), ["pc_ei"], ["A2"])
            V("dve", call("tensor_copy", out=L4[:, 0, :], in_=pc_er[:, :, 25]), ["pc_er"], ["L4"])
            V("dve", call("tensor_copy", out=L4[:, 1, :], in_=pc_ei[:, :, 25]), ["pc_ei"], ["L4"])
            for k_, idx_ in enumerate([8, 26, 27, 28, 29, 30, 31, 32]):
                V("dve", call("tensor_copy", out=PA1[:, 0, :, k_], in_=pc_er[:, :, idx_]), ["pc_er"], ["PA"])
                V("dve", call("tensor_copy", out=PA1[:, 1, :, k_], in_=pc_er[:, :, idx_]), ["pc_er"], ["PA"])
                V("dve", call("tensor_scalar", out=PA2[:, 0, :, k_], in0=pc_ei[:, :, idx_], scalar1=-1.0, scalar2=None, op0=ALU.mult), ["pc_ei"], ["PA"])
                V("dve", call("tensor_copy", out=PA2[:, 1, :, k_], in_=pc_ei[:, :, idx_]), ["pc_ei"], ["PA"])

            def v4(t):
                return t[:, :, :].rearrange("p g (i h) -> p g i h", i=8)

            def bt(T, e0):
                return T[:, :, e0:e0 + 8].unsqueeze(3).to_broadcast([128, 16, 8, 16])

            def bv(Vv):
                return Vv[:, :, :].unsqueeze(2).to_broadcast([128, 16, 8, 16])

            def cplx(outr, outi, Tr, Ti, e0, Vr, Vi, keys_t, keys_v, neg_im=False, only=None):
                kr, ki_ = keys_t
                vr, vi = keys_v
                if outr is not None:
                    okr = outr[1]
                    V("dve", call("tensor_tensor", out=v4(outr[0]), in0=bt(Tr, e0), in1=bv(Vr), op=ALU.mult), [kr, vr], [okr])
                    V("dve", call("tensor_tensor", out=v4(pc_E), in0=bt(Ti, e0), in1=bv(Vi), op=ALU.mult), [ki_, vi], ["pc_E"])
                    V("dve", call("tensor_tensor", out=outr[0][:, :, :], in0=outr[0][:, :, :], in1=pc_E[:, :, :], op=ALU.subtract), [okr, "pc_E"], [okr])
                if outi is not None:
                    oki = outi[1]
                    V("dve", call("tensor_tensor", out=v4(outi[0]), in0=bt(Tr, e0), in1=bv(Vi), op=ALU.mult), [kr, vi], [oki])
                    V("dve", call("tensor_tensor", out=v4(pc_E), in0=bt(Ti, e0), in1=bv(Vr), op=ALU.mult), [ki_, vr], ["pc_E"])
                    if neg_im:
                        V("dve", call("scalar_tensor_tensor", out=outi[0][:, :, :], in0=outi[0][:, :, :], scalar=-1.0, in1=pc_E[:, :, :], op0=ALU.mult, op1=ALU.subtract), [oki, "pc_E"], [oki])
                    else:
                        V("dve", call("tensor_tensor", out=outi[0][:, :, :], in0=outi[0][:, :, :], in1=pc_E[:, :, :], op=ALU.add), [oki, "pc_E"], [oki])

            cplx((pc_A, "pc_A"), (pc_B, "pc_B"), pc_er, pc_ei, 1, pc_cr, pc_ci, ("pc_er", "pc_ei"), ("pc_cr", "pc_ci"), neg_im=True)
            V("dve", call("tensor_copy", out=W3[:, :, 0, :], in_=pc_A[:, :, :]), ["pc_A"], ["W3"])
            V("dve", call("tensor_copy", out=W3[:, :, 1, :], in_=pc_B[:, :, :]), ["pc_B"], ["W3"])
            cplx((pc_A, "pc_A"), (pc_B, "pc_B"), pc_er, pc_ei, 0, pc_cr, pc_ci, ("pc_er", "pc_ei"), ("pc_cr", "pc_ci"))
            cplx((pc_C, "pc_C"), (pc_D, "pc_D"), pc_pr, pc_pi, 9, pc_br, pc_bi, ("pc_pr", "pc_pi"), ("pc_br", "pc_bi"), neg_im=True)
            for g in range(32):
                hfp, gl = g // 16, g % 16
                pr = slice(hfp * 64, hfp * 64 + 64)
                pi = psum("A")
                add("pe", call("matmul", ps[pi][:, 0:128], lhsT=pc_C[pr, gl, :], rhs=pc_A[pr, gl, :], start=True, stop=False),
                    r=["pc_C", "pc_A"], w=[pk(pi)], x=True)
                add("pe", call("matmul", ps[pi][:, 0:128], lhsT=pc_D[pr, gl, :], rhs=pc_B[pr, gl, :], start=False, stop=True),
                    r=["pc_D", "pc_B"], w=[pk(pi)], x=True)
                add("dve", call("tensor_tensor", out=ps[pi][:, 128:256], in0=ps[pi][:, 0:128], in1=pc_w4t[:, :], op=ALU.mult),
                    r=[pk(pi), "maskw4"], w=[pk(pi)], x=True)
                add("dve", call("scalar_tensor_tensor", out=W4[:, g, :], in0=ident_f[:], scalar=pc_dcol[:, g:g + 1], in1=ps[pi][:, 128:256], op0=ALU.mult, op1=ALU.add),
                    r=[pk(pi), "ident_f", "pc_dcol"], w=["W4"], x=True)
            cplx((pc_C, "pc_C"), (pc_D, "pc_D"), pc_pr, pc_pi, 17, pc_br, pc_bi, ("pc_pr", "pc_pi"), ("pc_br", "pc_bi"))
            for g in range(32):
                hfp, gl = g // 16, g % 16
                pr = slice(hfp * 64, hfp * 64 + 64)
                pi = psum("B")
                for ri, (T, tk) in enumerate(((pc_C, "pc_C"), (pc_D, "pc_D"))):
                    add("pe", call("transpose", out=ps[pi][:, ri * 64:(ri + 1) * 64], in_=T[pr, gl, :], identity=ident_f[pr, pr]),
                        r=[tk, "ident_f"], w=[pk(pi)], x=True)
                copy_op(evac_eng(), W1[:, g, :], ps[pi][:, 0:128], r=[pk(pi)], w=["W1"], x=True)

        def mixer(kind, c0, n, pass_idx):
            sample = (kind == "sample")
            ntile = 1 if sample else n // 128
            ntok = 64 if sample else 128
            L = 4 if sample else 64
            nch = ntok // L
            ncc = 16 if sample else n // 8
            J = 4 if sample else 8
            TRI = tri4 if sample else tri64
            TRIU = triu4 if sample else triu64
            SEL = sel4 if sample else sel64
            rmsnorm(1, [(c0, n)], xn, "xn")
            if stage < 1:
                return
            v8 = lambda t: t[:].rearrange("p (k f) -> p k f", k=8)
            winv = w_in_d.rearrange("(k p) f -> p k f", p=128)

            def wl_in(col0, ncol):
                return wload(lambda t: t[:, 0:8 * ncol].rearrange("p (k f) -> p k f", k=8), winv[:, :, col0:col0 + ncol])

            def vin(s_, ncol):
                return wslot[s_][:, 0:8 * ncol].rearrange("p (k f) -> p k f", k=8)

            rot_ovr[0] = {"A": [0, 1, 2, 3], "B": [4, 5], "C": [4, 5], "M": [6, 7], "U": [2, 3]}
            if sample:
                add("dve", call("memset", ucm_flat[:, :], 0.0), w=["ucm"], x=True)
            for gq in range(2):
                s_ = wl_in(gq * 256, 256)
                for j in range(J):
                    pi = psum("A")
                    if sample:
                        lsel = lambda k, j=j: xn[:, k, c0 + j:c0 + n:4]
                    else:
                        lsel = lambda k, j=j: xn[:, k, c0 + j:c0 + n:8]
                    for k in range(8):
                        add("pe", call("matmul",
                            ps[pi][0:ncc, 0:256], lhsT=lsel(k), rhs=vin(s_, 256)[:, k, :], start=(k == 0), stop=(k == 7)),
                            r=["ws%d" % s_, "xn%d" % k], w=[pk(pi)], x=True)
                    copy_op(evac_eng(), ucm[0:ncc, gq * 16:(gq + 1) * 16, j, :], ps[pi][0:ncc, 0:256].rearrange("c (g h) -> c g h", g=16), r=[pk(pi)], w=["ucm"])
            P.capture_begin()
            rot_ovr[0] = {"A": [0, 1, 4], "B": [2, 3, 5], "C": [4], "M": [5]}
            ws_ovr[0] = [0, 1, 2]
            s_q = wl_in(512, 256)
            for m in range(2):
                pi = psum("A")
                for k in range(8):
                    add("pe", call("matmul", ps[pi][:, 0:n], lhsT=vin(s_q, 256)[:, k, m * 128:(m + 1) * 128],
                                                                   rhs=xn[:, k, c0:c0 + n], start=(k == 0), stop=(k == 7)),
                        r=["ws%d" % s_q, "xn%d" % k], w=[pk(pi)], x=True)
                copy_op(evac_eng(), qT[:, m, 0:n], ps[pi][:, 0:n], r=[pk(pi)], w=["qT%d" % m])
            s_k = wl_in(768, 256)
            for m in range(2):
                pi = psum("A")
                for k in range(8):
                    add("pe", call("matmul", ps[pi][:, 0:n], lhsT=vin(s_k, 256)[:, k, m * 128:(m + 1) * 128],
                                                                   rhs=xn[:, k, c0:c0 + n], start=(k == 0), stop=(k == 7)),
                        r=["ws%d" % s_k, "xn%d" % k], w=[pk(pi)], x=True)
                copy_op(evac_eng(), kT[:, m, 0:n], ps[pi][:, 0:n], r=[pk(pi)], w=["kT%d" % m])
            for tt in range(ntile):
                pi = psum("B")
                for k in range(8):
                    add("pe", call("matmul", ps[pi][0:ntok, 0:256], lhsT=xn[:, k, c0 + tt * ntok:c0 + (tt + 1) * ntok],
                                                                     rhs=vin(s_k, 256)[:, k, :], start=(k == 0), stop=(k == 7)),
                        r=["ws%d" % s_k, "xn%d" % k], w=[pk(pi)], x=True)
                copy_op(evac_eng(), k_tm[0:ntok, tt, :], ps[pi][0:ntok, 0:256], r=[pk(pi)], w=["k_tm%d" % tt])
            for gq in range(2):
                s_ = wl_in(1024 + gq * 256, 256)
                for tt in range(ntile):
                    pi = psum("B")
                    for k in range(8):
                        add("pe", call("matmul", ps[pi][0:ntok, 0:256], lhsT=xn[:, k, c0 + tt * ntok:c0 + (tt + 1) * ntok],
                                                                                rhs=vin(s_, 256)[:, k, :], start=(k == 0), stop=(k == 7)),
                            r=["ws%d" % s_, "xn%d" % k], w=[pk(pi)], x=True)
                    copy_op(evac_eng(), v_tm[0:ntok, tt, gq * 256:(gq + 1) * 256], ps[pi][0:ntok, 0:256], r=[pk(pi)], w=["v_tm%d_%d" % (tt, gq)])
            for gq in range(2):
                s_ = wl_in(1536 + gq * 256, 256)
                for mm in range(2):
                    m = gq * 2 + mm
                    pi = psum("A")
                    for k in range(8):
                        add("pe", call("matmul", ps[pi][:, 0:n], lhsT=vin(s_, 256)[:, k, mm * 128:(mm + 1) * 128],
                                                                                rhs=xn[:, k, c0:c0 + n], start=(k == 0), stop=(k == 7)),
                            r=["ws%d" % s_, "xn%d" % k], w=[pk(pi)], x=True)
                    add("act", call("activation", out=sgT[:, m, 0:n], in_=ps[pi][:, 0:n], func=AF.Silu),
                        r=[pk(pi)], w=["sgT%d" % m], x=True)
            s_a = wl_in(2048, 16)
            pi = psum("A")
            for k in range(8):
                add("pe", call("matmul", ps[pi][0:16, 0:n], lhsT=vin(s_a, 16)[:, k, :], rhs=xn[:, k, c0:c0 + n],
                                                          start=(k == 0), stop=(k == 7)),
                    r=["ws%d" % s_a, "xn%d" % k], w=[pk(pi)], x=True)
            copy_op(evac_eng(), aT[0:16, 0:n], ps[pi][0:16, 0:n], r=[pk(pi)], w=["aT"])
            for tt in range(ntile):
                pi = psum("B")
                add("pe", call("matmul", ps[pi][0:ntok, 0:256], lhsT=aT[0:16, tt * ntok:(tt + 1) * ntok], rhs=gatew_b[:, :], start=True, stop=True),
                    r=["aT", "gatew_b"], w=[pk(pi)], x=True)
                add("act", call("activation", out=etmp[0:ntok, :], in_=ps[pi][0:ntok, 0:256], func=AF.Exp, scale=-1.0),
                    r=[pk(pi)], w=["etmp"], x=True)
                add("dve", call("tensor_tensor", out=etmp[0:ntok, :], in0=etmp[0:ntok, :], in1=gateb[0:ntok, :], op=ALU.mult),
                    r=["etmp", "gateb"], w=["etmp"], x=True)
                add("act", call("activation", out=lf[0:ntok, tt, :], in_=etmp[0:ntok, :], func=AF.Ln, bias=1.0),
                    r=["etmp"], w=["lf%d" % tt], x=True)

            rot_ovr[0] = {"A": [0], "B": [1], "C": [2, 3], "U": [4, 5], "M": [1]}
            pOs, pUs = {}, {}

            def gla_part1(tt):
                bi = tt % 2
                eb = ebb[bi]
                ebk = "eb%d" % bi
                tc0 = tt * ntok
                pC = psum("A")
                for m in range(2):
                    add("pe", call("matmul", ps[pC][:, m * 128:m * 128 + ntok], lhsT=lf[0:ntok, tt, m * 128:(m + 1) * 128],
                                   rhs=TRI[0:ntok, 0:ntok], start=True, stop=True), r=["lf%d" % tt, "const"], w=[pk(pC)], x=True)
                pR = psum("B")
                add("pe", call("matmul", ps[pR][0:ntok, 0:256], lhsT=TRIU[0:ntok, 0:ntok], rhs=lf[0:ntok, tt, :], start=True, stop=True),
                    r=["lf%d" % tt, "const"], w=[pk(pR)], x=True)
                pCv = ps[pC][:, 0:256].rearrange("p (m t) -> p m t", m=2)[:, :, 0:ntok]
                add("act", call("activation", out=eb[:, :, 0:ntok], in_=pCv, func=AF.Exp, scale=-1.0 / 16.0), r=[pk(pC)], w=[ebk], x=True)
                add("act", call("activation", out=enb[bi][:, :, 0:ntok], in_=pCv, func=AF.Exp, scale=1.0 / 16.0), r=[pk(pC)], w=["enb%d" % bi], x=True)
                add("act", call("activation", out=erem[bi][0:ntok, :], in_=ps[pR][0:ntok, 0:256], func=AF.Exp, scale=-1.0 / 16.0),
                    r=[pk(pR)], w=["erem%d" % bi], x=True)
                add("dve", call("scalar_tensor_tensor", out=qs[bi][:, :, 0:ntok], in0=qT[:, :, tc0:tc0 + ntok], scalar=0.125, in1=eb[:, :, 0:ntok],
                                op0=ALU.mult, op1=ALU.mult), r=["qT0", "qT1", ebk], w=["qs%d" % bi], x=True)
                add("dve", call("tensor_tensor", out=kend[bi][0:ntok, :], in0=k_tm[0:ntok, tt, :], in1=erem[bi][0:ntok, :], op=ALU.mult),
                    r=["k_tm%d" % tt, "erem%d" % bi], w=["kend%d" % bi], x=True)
                pA = psum("A")
                for h in range(4):
                    m = h // 2
                    add("dve", call("scalar_tensor_tensor", out=kiz[bi][:, h, 0:ntok], in0=kT[:, m, tc0:tc0 + ntok], scalar=sel64[:, (h % 2):(h % 2) + 1],
                                    in1=enb[bi][:, m, 0:ntok], op0=ALU.mult, op1=ALU.mult),
                        r=["kT%d" % m, "enb%d" % bi, "const"], w=["kiz%d" % bi], x=True)
                for h in range(4):
                    m = h // 2
                    add("pe", call("matmul", ps[pA][0:ntok, h * 128:h * 128 + ntok], lhsT=kiz[bi][:, h, 0:ntok], rhs=qs[bi][:, m, 0:ntok], start=True, stop=True),
                        r=["kiz%d" % bi, "qs%d" % bi], w=[pk(pA)], x=True)
                for h in range(4):
                    add("dve", call("tensor_tensor", out=attT[bi][0:ntok, h, 0:ntok], in0=ps[pA][0:ntok, h * 128:h * 128 + ntok], in1=TRI[0:ntok, 0:ntok], op=ALU.mult),
                        r=[pk(pA), "const"], w=["attT%d" % bi], x=True)
                pO = psum("C")
                pOs[tt] = pO
                for h in range(4):
                    add("pe", call("matmul", ps[pO][:, h * 128:h * 128 + ntok], lhsT=v_tm[0:ntok, tt, h * 128:(h + 1) * 128],
                                   rhs=attT[bi][0:ntok, h, 0:ntok], start=(h == 0), stop=False, skip_group_check=True),
                        r=["v_tm%d_%d" % (tt, h // 2), "attT%d" % bi], w=[pk(pO)], x=True)
                if not sample:
                    pU = psum("U")
                    pUs[tt] = pU
                    for c in range(nch):
                        km = kendm[bi * 2 + c]
                        kmk = "kendm%d" % (bi * 2 + c)
                        add("dve", call("tensor_scalar", out=km[0:ntok, :], in0=kend[bi][0:ntok, :], scalar1=SEL[0:ntok, c:c + 1], scalar2=None, op0=ALU.mult),
                            r=["kend%d" % bi, "const"], w=[kmk], x=True)
                        for h in range(4):
                            m, po = h // 2, 64 * (h % 2)
                            add("pe", call("matmul", ps[pU][po:po + 64, c * 256 + m * 128:c * 256 + (m + 1) * 128], lhsT=km[0:ntok, h * 64:(h + 1) * 64],
                                           rhs=v_tm[0:ntok, tt, h * 128:(h + 1) * 128], start=True, stop=True),
                                r=[kmk, "v_tm%d_%d" % (tt, h // 2)], w=[pk(pU)], x=True)

            def gla_part2(tt):
                bi = tt % 2
                eb = ebb[bi]
                ebk = "eb%d" % bi
                tc0 = tt * ntok
                pO = pOs[tt]
                for c in range(nch):
                    cb = c % 2
                    if sample:
                        for m in range(2):
                            dma("sp", S0b[cb][:, m, :], gla_in[c, 2 * m:2 * m + 2, :, :].rearrange("h d e -> (h d) e"),
                                w=["S0b%d" % cb], semkey="S0b%d" % cb, x=True)
                        Ssrc, Skey = S0b[cb], "S0b%d" % cb
                    else:
                        Ssrc, Skey = S_f, "S_f"
                    Szc = Sz[cb]
                    Szk = "Sz%d" % cb
                    for h in range(4):
                        m = h // 2
                        add("act", call("activation", out=Szc[:, h, :], in_=Ssrc[:, m, :], func=AF.Identity, scale=sel64[:, (h % 2):(h % 2) + 1]),
                            r=[Skey, "const"], w=[Szk], x=True)
                    for h in range(4):
                        m = h // 2
                        last = (c == nch - 1) and (h == 3)
                        add("pe", call("matmul", ps[pO][:, h * 128 + c * L:h * 128 + (c + 1) * L], lhsT=Szc[:, h, :],
                                       rhs=qs[bi][:, m, c * L:(c + 1) * L], start=False, stop=last, skip_group_check=True),
                            r=[Szk, "qs%d" % bi], w=[pk(pO)], x=True)
                    if sample:
                        km = kendm[cb]
                        kmk = "kendm%d" % cb
                        add("dve", call("tensor_scalar", out=km[0:ntok, :], in0=kend[bi][0:ntok, :], scalar1=SEL[0:ntok, c:c + 1], scalar2=None, op0=ALU.mult),
                            r=["kend%d" % bi, "const"], w=[kmk], x=True)
                        pU = psum("U")
                        ucol = 0
                        for h in range(4):
                            m, po = h // 2, 64 * (h % 2)
                            add("pe", call("matmul", ps[pU][po:po + 64, m * 128:(m + 1) * 128], lhsT=km[0:ntok, h * 64:(h + 1) * 64],
                                           rhs=v_tm[0:ntok, tt, h * 128:(h + 1) * 128], start=True, stop=True),
                                r=[kmk, "v_tm%d_%d" % (tt, h // 2)], w=[pk(pU)], x=True)
                        Sdst, Sdk = Sob[cb], "Sob%d" % cb
                    else:
                        pU = pUs[tt]
                        ucol = c * 256
                        Sdst, Sdk = S_f, "S_f"
                    col_last = c * L + L - 1
                    for m in range(2):
                        add("dve", call("scalar_tensor_tensor", out=Sdst[:, m, :], in0=Ssrc[:, m, :], scalar=eb[:, m, col_last:col_last + 1],
                                        in1=ps[pU][:, ucol + m * 128:ucol + (m + 1) * 128], op0=ALU.mult, op1=ALU.add),
                            r=[Skey, ebk, pk(pU)], w=[Sdk], x=True)
                    if sample:
                        for m in range(2):
                            dma("sp", sgla_o[c, 2 * m:2 * m + 2, :, :].rearrange("h d e -> (h d) e"), Sob[cb][:, m, :],
                                r=["Sob%d" % cb], semkey="out", x=True)
                pOv = ps[pO][:, :].rearrange("p (h t) -> p h t", h=4)[:, :, 0:ntok]
                o_v = o_sb[:, 0:4 * ntok].rearrange("p (h t) -> p h t", h=4)
                add("act", call("copy", out=o_v, in_=pOv), r=[pk(pO)], w=["o_sb"], x=True)
                add("act", call("activation", out=osq[:, 0:4 * ntok], in_=o_sb[:, 0:4 * ntok], func=AF.Square), r=["o_sb"], w=["osq"], x=True)
                pN = psum("M")
                add("pe", call("matmul", ps[pN][:, 0:4 * ntok], lhsT=ones_b[:], rhs=osq[:, 0:4 * ntok], start=True, stop=True),
                    r=["osq", "ones_b"], w=[pk(pN)], x=True)
                add("act", call("activation", out=sdm[:, 0:4 * ntok], in_=ps[pN][:, 0:4 * ntok], func=AF.Ln, bias=EPS, scale=1.0 / 128.0),
                    r=[pk(pN)], w=["sdm"], x=True)
                add("act", call("activation", out=ps[pN][:, 0:4 * ntok], in_=sdm[:, 0:4 * ntok], func=AF.Exp, scale=-0.5), r=["sdm"], w=[pk(pN)], x=True)
                add("dve", call("scalar_tensor_tensor", out=t1b[:, 0:4 * ntok], in0=o_sb[:, 0:4 * ntok], scalar=gnorm[:, 0:1], in1=ps[pN][:, 0:4 * ntok],
                                op0=ALU.mult, op1=ALU.mult), r=["o_sb", "gnorm", pk(pN)], w=["t1b"], x=True)
                t1v = t1b[:, 0:4 * ntok].rearrange("p (h t) -> p h t", h=4)
                add("dve", call("tensor_tensor", out=cat[:, 4:8, tc0:tc0 + ntok], in0=t1v, in1=sgT[:, :, tc0:tc0 + ntok], op=ALU.mult),
                    r=["t1b", "sgT0", "sgT1", "sgT2", "sgT3"], w=["cat_o%d" % tt], x=True)

            gla_part1(0)
            for tt in range(1, ntile):
                gla_part1(tt)
                gla_part2(tt - 1)
            gla_part2(ntile - 1)
            if (not sample) and pass_idx == npass - 1:
                for m in range(2):
                    dma("sp", pgla_o[2 * m:2 * m + 2, :, :].rearrange("h d e -> (h d) e"), S_f[:, m, :], r=["S_f"], semkey="out")

            listA = P.capture_end()
            P.capture_begin()
            rot_ovr[0] = {"A": [6], "B": [7], "C": [6, 7], "M": [7]}
            ws_ovr[0] = [3]
            for gq in range(4):
                pi = psum("A")
                pbf = ps[pi][:].bitcast(BF16)
                for gg in range(8):
                    g = gq * 8 + gg
                    add("pe", call("transpose", out=pbf[:, gg * 64:gg * 64 + ncc], in_=ucm[0:ncc, g, :, :].rearrange("c j h -> c (j h)"),
                                                                          identity=ident_b[0:ncc, 0:ncc]),
                        r=["ucm", "ident_b"], w=[pk(pi)], x=True)
                src = pbf[:, 0:512].rearrange("p (g c) -> p g c", g=8)[:, :, 0:ncc]
                copy_op(evac_eng(), Ug[:, gq * 8:(gq + 1) * 8, 0:ncc], src, r=[pk(pi)], w=["Ug"])
            Bdst = Ssm if sample else Bst
            Bdk = "Ssm" if sample else "Bst"
            for q in range(4):
                pi = psum("B")
                for hfp in range(2):
                    for g4 in range(4):
                        gl = q * 4 + g4
                        g = hfp * 16 + gl
                        for ri in range(2):
                            col = (ri * 4 + g4) * 64
                            add("pe", call("matmul",
                                ps[pi][hfp * 64:hfp * 64 + 64, col:col + ncc], lhsT=W1[:, g, ri * 64:(ri + 1) * 64], rhs=Ug[:, g, 0:ncc],
                                start=True, stop=True), r=["W1", "Ug"], w=[pk(pi)], x=True)
                src = ps[pi][:, :].rearrange("p (r g c) -> p r g c", r=2, g=4)[:, :, :, 0:ncc]
                if sample:
                    dst = Ssm[:, :, q * 4:(q + 1) * 4, 0:ncc]
                else:
                    dst = Bst[:, :, q * 4:(q + 1) * 4, 1:1 + ncc]
                copy_op(evac_eng(), dst, src, r=[pk(pi)], w=[Bdk])
            if sample:
                pcs = 0
                for ri, src_d in enumerate((s5re_in, s5im_in)):
                    srcv = src_d.rearrange("b (a g) p -> b g a p", a=2)
                    pi = psum("A")
                    for q4 in range(4):
                        bb = pcs % 2
                        pcs += 1
                        for g4 in range(4):
                            dma("sp", h0t[bb][g4 * 16:(g4 + 1) * 16, :, :], srcv[:, q4 * 4 + g4, :, :], w=["h0t%d" % bb], semkey="h0t%d" % bb, x=True)
                        add("pe", call("transpose", out=ps[pi][:, q4 * 64:(q4 + 1) * 64], in_=h0t[bb][0:64, :, :].rearrange("r a p -> r (a p)"),
                                       identity=ident_f[0:64, 0:64]), r=["h0t%d" % bb, "ident_f"], w=[pk(pi)], x=True)
                    copy_op("dve", Bsm[:, ri, :, :], ps[pi][:, 0:256].rearrange("p (g b) -> p g b", g=16), r=[pk(pi)], w=["Bsm"])
                bcb = lambda a: a.unsqueeze(3).to_broadcast([128, 2, 16, 16])
                sw = lambda t: (t[:, 1, :, :], t[:, 0, :, :])
                T0, T1 = sm_t

                def cmul(dst, dk, src, sk, cr_, ci_neg_pos, ck):
                    add("dve", call("tensor_tensor", out=dst[:, :, :, :], in0=src[:, :, :, :], in1=bcb(cr_[:, :, :]), op=ALU.mult), r=[sk, ck], w=[dk], x=True)
                    add("dve", call("tensor_tensor", out=T1[:, 0, :, :], in0=src[:, 1, :, :], in1=ci_neg_pos[:, 0, :].unsqueeze(2).to_broadcast([128, 16, 16]), op=ALU.mult), r=[sk, ck], w=["smT1"], x=True)
                    add("dve", call("tensor_tensor", out=T1[:, 1, :, :], in0=src[:, 0, :, :], in1=ci_neg_pos[:, 1, :].unsqueeze(2).to_broadcast([128, 16, 16]), op=ALU.mult), r=[sk, ck], w=["smT1"], x=True)
                    add("dve", call("tensor_tensor", out=dst[:, :, :, :], in0=dst[:, :, :, :], in1=T1[:, :, :, :], op=ALU.add), r=[dk, "smT1"], w=[dk], x=True)

                cmul(T0, "smT0", Bsm, "Bsm", A1, A2, "A1A2")
                add("dve", call("tensor_tensor", out=T0[:, :, :, :], in0=T0[:, :, :, :], in1=Ssm[:, :, :, :], op=ALU.add), r=["smT0", "Ssm"], w=["smT0"], x=True)
                add("dve", call("tensor_copy", out=sc_t1[:, 0, :], in_=L4[:, 0, :]), r=["L4"], w=["sc_t1"], x=True)
                add("dve", call("tensor_copy", out=sc_t1[:, 1, :], in_=L4[:, 0, :]), r=["L4"], w=["sc_t1"], x=True)
                add("dve", call("tensor_scalar", out=sc_t2[:, 0, :], in0=L4[:, 1, :], scalar1=-1.0, scalar2=None, op0=ALU.mult), r=["L4"], w=["sc_t2"], x=True)
                add("dve", call("tensor_copy", out=sc_t2[:, 1, :], in_=L4[:, 1, :]), r=["L4"], w=["sc_t2"], x=True)
                cmul(Hs3, "Hs3", T0, "smT0", sc_t1, sc_t2, "sc_t2")
                pcs = 0
                for ri, dst_d in enumerate((sre_o, sim_o)):
                    dstv = dst_d.rearrange("b (a g) p -> b g a p", a=2)
                    for q4 in range(4):
                        bb = pcs % 2
                        pcs += 1
                        pi = psum("B")
                        add("pe", call("transpose", out=ps[pi][0:64, 0:128], in_=Hs3[:, ri, q4 * 4:(q4 + 1) * 4, :].rearrange("p g b -> p (g b)"), identity=ident_f[:]),
                            r=["Hs3", "ident_f"], w=[pk(pi)], x=True)
                        copy_op(evac_eng(), hot[bb][0:64, :, :], ps[pi][0:64, 0:128].rearrange("r (a p) -> r a p", a=2), r=[pk(pi)], w=["hot%d" % bb])
                        for g4 in range(4):
                            dma("sp", dstv[:, q4 * 4 + g4, :, :], hot[bb][g4 * 16:(g4 + 1) * 16, :, :], r=["hot%d" % bb], semkey="out", x=True)
            else:
                nsb = ncc // 8
                bc8 = lambda a: a.unsqueeze(3).to_broadcast([128, 2, 16, nsb])
                bc8h = lambda a: a.unsqueeze(2).to_broadcast([128, 16, nsb])
                for j in range(1, 8):
                    src = Bst[:, :, :, j:j + 8 * (nsb - 1) + 1:8]
                    dst = Bst[:, :, :, j + 1:j + 1 + 8 * (nsb - 1) + 1:8]
                    add("dve", call("tensor_tensor", out=sct1[:, :, :, 0:nsb], in0=src, in1=bc8(A1[:, :, :]), op=ALU.mult), r=["Bst", "A1A2"], w=["sct1"], x=True)
                    add("dve", call("tensor_tensor", out=sct2[:, 0, :, 0:nsb], in0=Bst[:, 1, :, j:j + 8 * (nsb - 1) + 1:8], in1=bc8h(A2[:, 0, :]), op=ALU.mult), r=["Bst", "A1A2"], w=["sct2"], x=True)
                    add("dve", call("tensor_tensor", out=sct2[:, 1, :, 0:nsb], in0=Bst[:, 0, :, j:j + 8 * (nsb - 1) + 1:8], in1=bc8h(A2[:, 1, :]), op=ALU.mult), r=["Bst", "A1A2"], w=["sct2"], x=True)
                    add("dve", call("tensor_tensor", out=dst, in0=dst, in1=sct1[:, :, :, 0:nsb], op=ALU.add), r=["Bst", "sct1"], w=["Bst"], x=True)
                    add("dve", call("tensor_tensor", out=dst, in0=dst, in1=sct2[:, :, :, 0:nsb], op=ALU.add), r=["Bst", "sct2"], w=["Bst"], x=True)
                for sbk in range(nsb):
                    car = Bst[:, :, :, 8 * sbk]
                    dst = Bst[:, :, :, 8 * sbk + 1:8 * sbk + 9]
                    add("dve", call("tensor_tensor", out=sct1[:, :, :, 0:8], in0=PA1[:, :, :, :], in1=car.unsqueeze(3).to_broadcast([128, 2, 16, 8]), op=ALU.mult), r=["Bst", "PA"], w=["sct1"], x=True)
                    add("dve", call("tensor_tensor", out=sct2[:, 0, :, 0:8], in0=PA2[:, 0, :, :], in1=Bst[:, 1, :, 8 * sbk].unsqueeze(2).to_broadcast([128, 16, 8]), op=ALU.mult), r=["Bst", "PA"], w=["sct2"], x=True)
                    add("dve", call("tensor_tensor", out=sct2[:, 1, :, 0:8], in0=PA2[:, 1, :, :], in1=Bst[:, 0, :, 8 * sbk].unsqueeze(2).to_broadcast([128, 16, 8]), op=ALU.mult), r=["Bst", "PA"], w=["sct2"], x=True)
                    add("dve", call("tensor_tensor", out=dst, in0=dst, in1=sct1[:, :, :, 0:8], op=ALU.add), r=["Bst", "sct1"], w=["Bst"], x=True)
                    add("dve", call("tensor_tensor", out=dst, in0=dst, in1=sct2[:, :, :, 0:8], op=ALU.add), r=["Bst", "sct2"], w=["Bst"], x=True)
            for hz in range(2):
                hsrc, hkey = (Bsm[:, :, :, 0:ncc], "Bsm") if sample else (Bst[:, :, :, 0:ncc], "Bst")
                add("dve", call("tensor_scalar", out=Hbz[hz][:, :, :, 0:ncc], in0=hsrc, scalar1=sel64[:, hz:hz + 1], scalar2=None, op0=ALU.mult),
                    r=[hkey, "const"], w=["Hbz"], x=True)
            for gq in range(8):
                pi = psum("C")
                for g4 in range(4):
                    g = gq * 4 + g4
                    hfp, gl = g // 16, g % 16
                    pr = slice(hfp * 64, hfp * 64 + 64)
                    osl = ps[pi][0:ncc, g4 * 128:(g4 + 1) * 128]
                    add("pe", call("matmul", osl, lhsT=Hbz[hfp][:, 0, gl, 0:ncc], rhs=W3[:, gl, 0, :], start=True, stop=False),
                        r=["Hbz", "W3"], w=[pk(pi)], x=True)
                    add("pe", call("matmul", osl, lhsT=Hbz[hfp][:, 1, gl, 0:ncc], rhs=W3[:, gl, 1, :], start=False, stop=False),
                        r=["Hbz", "W3"], w=[pk(pi)], x=True)
                    add("pe", call("matmul", osl, lhsT=Ug[:, g, 0:ncc], rhs=W4[:, g, :], start=False, stop=True),
                        r=["Ug", "W4"], w=[pk(pi)], x=True)
                src = ps[pi][0:ncc, :].rearrange("c (g j h) -> c j g h", g=4, j=8)
                dst = zcm[0:ncc, :, gq * 64:(gq + 1) * 64].rearrange("c j (g h) -> c j g h", g=4)
                add("act", call("activation", out=dst, in_=src, func=AF.Gelu_apprx_tanh), r=[pk(pi), "Ug"], w=["ucm"], x=True)
            if (not sample) and pass_idx == npass - 1:
                for hfp in range(2):
                    pr = slice(hfp * 64, hfp * 64 + 64)
                    gs = slice(hfp * 16, hfp * 16 + 16)
                    dma("sp", pre_o[gs, :].rearrange("g p -> p g"), Bst[pr, 0, :, ncc], r=["Bst"], semkey="out", slow=True)
                    dma("sp", pim_o[gs, :].rearrange("g p -> p g"), Bst[pr, 1, :, ncc], r=["Bst"], semkey="out", slow=True)
            if not sample:
                add("dve", call("tensor_copy", out=Bst[:, :, :, 0], in_=Bst[:, :, :, ncc]), r=["Bst"], w=["Bst"], x=True)
            for m in range(4):
                pi = psum("A")
                pbf = ps[pi][:].bitcast(BF16)
                for j in range(J):
                    add("pe", call("transpose", out=pbf[:, j * 64:j * 64 + ncc], in_=zcm[0:ncc, j, m * 128:(m + 1) * 128],
                                                                        identity=ident_b[0:ncc, 0:ncc]),
                        r=["ucm", "ident_b"], w=[pk(pi)], x=True)
                src = pbf[:, 0:J * 64].rearrange("p (j c) -> p j c", j=J)[:, :, 0:ncc]
                dst = zT[:, m, 0:n].rearrange("p (c j) -> p j c", j=J)
                copy_op(evac_eng(), dst, src, r=[pk(pi)], w=["zT%d" % m])
            s_g = wload(lambda t: t[:].rearrange("p (k f) -> p k f", k=4), glu_w_d.rearrange("(k p) f -> p k f", p=128))
            gv = wslot[s_g][:].rearrange("p (k f) -> p k f", k=4)
            for m in range(4):
                pi = psum("A")
                for k in range(4):
                    add("pe", call("matmul", ps[pi][:, 0:n], lhsT=gv[:, k, m * 128:(m + 1) * 128], rhs=zT[:, k, 0:n], start=(k == 0), stop=(k == 3)),
                        r=["ws%d" % s_g, "zT%d" % k], w=[pk(pi)], x=True)
                b = m % 2
                add("act", call("activation", out=sig[b][:, 0:n], in_=ps[pi][:, 0:n], func=AF.Sigmoid, bias=glub[:, m:m + 1]),
                    r=[pk(pi), "glub"], w=["sig"], x=True)
                add("dve", call("tensor_tensor", out=cat[:, m, 0:n], in0=zT[:, m, 0:n], in1=sig[b][:, 0:n], op=ALU.mult),
                    r=["zT%d" % m, "sig"], w=["cat_z%d" % m], x=True)
            listB = P.capture_end()
            rot_ovr[0] = None
            ws_ovr[0] = None
            P.merge([listA, listB], spans=[MERGE_SPAN_A, 1.0])
            woutv = w_out_d.rearrange("(k p) f -> p k f", p=128)
            for dp in range(4):
                s_ = wload(v8, woutv[:, :, dp * 256:(dp + 1) * 256])
                for dd in range(2):
                    d = dp * 2 + dd
                    pi = psum("C")
                    for k in range(8):
                        add("pe", call("matmul", ps[pi][:, 0:n], lhsT=v8(wslot[s_])[:, k, dd * 128:(dd + 1) * 128], rhs=cat[:, k, 0:n],
                                                                                start=(k == 0), stop=(k == 7)),
                            r=["ws%d" % s_, ("cat_z%d" % k) if k < 4 else None] + (["cat_o%d" % t_ for t_ in range(ntile)] if k >= 4 else []), w=[pk(pi)], x=True)
                    add("dve", call("tensor_tensor", out=xT[:, d, c0:c0 + n], in0=ps[pi][:, 0:n], in1=xT[:, d, c0:c0 + n], op=ALU.add),
                        r=[pk(pi), "xT%d" % d], w=["xT%d" % d], x=True)

        P.ops
        add("dve", call("memset", scr[:, 1:2], 0.0), r=["tri64", "triu64", "tri4", "triu4", "sel64", "sel4"], w=["const"])
        side_pc = None
        if do_mixer:
            P.capture_begin()
            rot_ovr[0] = {"A": [6, 7], "B": [6, 7], "C": [6, 7], "M": [6, 7]}
            precompute()
            add("dve", call("memset", scr[:, 2:3], 0.0), r=["A1", "A2"], w=["A1A2"], x=True)
            side_pc = P.capture_end()
            rot_ovr[0] = None
            if not do_ffn:
                P.ops.extend(side_pc)
                side_pc = None
        pending_store = [None]
        for pidx in range(npass):
            chunks = [(0, PT)]
            if pidx == 0:
                chunks = [(0, (PT + NSMP) // 2), ((PT + NSMP) // 2, (PT + NSMP) // 2)]
            P.capture_begin()
            rot_ovr[0] = {"A": [0, 1], "B": [2, 3], "C": [4, 5], "M": [6], "U": [2, 3]}
            for tt in range(PT // 128):
                r0 = pidx * PT + tt * 128
                load_x(xp[r0:r0 + 128, :], 128, tt * 128)
            if pidx == 0:
                load_x(xs[:, :], NSMP, PT)
            if do_ffn:
                rmsnorm(0, chunks, xn, "xn")
            rot_ovr[0] = None
            l_load = P.capture_end()
            if pending_store[0]:
                P.merge([pending_store[0], l_load])
                pending_store[0] = None
            else:
                P.ops.extend(l_load)
            if do_ffn:
                ffn(*w_ffn[0], chunks, side=(side_pc if pidx == 0 else None))
            if do_mixer:
                barrier()
                mixer("prompt", 0, PT, pidx)
                if pidx == 0:
                    mixer("sample", PT, NSMP, pidx)
                barrier()
            if do_ffn:
                rmsnorm(2, chunks, xn, "xn")
                ffn(*w_ffn[1], chunks)
            rmsnorm(3, chunks, yT, "yT")
            P.capture_begin()
            rot_ovr[0] = {"A": [0, 1], "B": [2, 3], "C": [4, 5], "M": [7], "U": [2, 3]}
            for tt in range(PT // 128):
                r0 = pidx * PT + tt * 128
                store_out(yp[r0:r0 + 128, :], 128, tt * 128)
            if pidx == 0:
                store_out(ys[:, :], NSMP, PT)
            rot_ovr[0] = None
            pending_store[0] = P.capture_end()
        if pending_store[0]:
            P.ops.extend(pending_store[0])
        tapsrc = {"xT": (xT[:], ["xT%d" % q_ for q_ in range(8)]), "yT": (yT[:, :, :], ["yT%d" % q_ for q_ in range(8)]), "sdb": (sdb[:, :], ["sdb"]),
                  "A1": (A1[:], ["A1"]), "A2": (A2[:], ["A2"]), "L4": (L4[:], ["L4"]), "Bst": (Bst[:], ["Bst"]), "S_f": (S_f[:], ["S_f"]),
                  "qT": (qT[:, :, :], ["qT0"]), "kT": (kT[:, :, :], ["kT0"]), "lf": (lf[:, :, :], ["lf0"]), "k_tm": (k_tm[:, :, :], ["k_tm0"]),
                  "xin0": (xin[0][:, :], ["xin0"]), "xin1": (xin[1][:, :], ["xin1"]), "pc_er": (pc_er[:, :, :], ["pc_er"]), "pc_ei": (pc_ei[:, :, :], ["pc_ei"])}
        for (tname, tshape) in taps:
            src, keys = tapsrc[tname]
            dma("sp", tap_out[tname], src, r=keys, semkey="out")
        P.emit(nc, final_semkeys=["out"])
    return nc


def _consts():
    c = {}
    c["c_ident"] = np.eye(128, dtype=np.float32)
    s = np.arange(128)
    same64 = (s[:, None] // 64) == (s[None, :] // 64)
    c["c_tri64"] = (same64 & (s[:, None] <= s[None, :])).astype(np.float32)
    c["c_triu64"] = (same64 & (s[:, None] > s[None, :])).astype(np.float32)
    s4 = np.arange(64)
    same4 = (s4[:, None] // 4) == (s4[None, :] // 4)
    c["c_tri4"] = (same4 & (s4[:, None] <= s4[None, :])).astype(np.float32)
    c["c_triu4"] = (same4 & (s4[:, None] > s4[None, :])).astype(np.float32)
    c["c_sel64"] = (s[:, None] // 64 == np.arange(2)[None, :]).astype(np.float32)
    c["c_sel4"] = (s4[:, None] // 4 == np.arange(16)[None, :]).astype(np.float32)
    i_ = s // 16
    c["c_maskw4"] = (i_[:, None] <= i_[None, :]).astype(np.float32)
    c["c_evals"] = np.array(EVALS, dtype=np.float32)
    return c


_NC_CACHE = {}


def kernel(x_prompt, x_sample, state_s5_re, state_s5_im, state_gla, norm_ffn1, ffn1_gate, ffn1_up,
           ffn1_down, norm_mix, w_in, s5_lam_re, s5_lam_im, s5_log_dt, s5_b_re, s5_b_im, s5_c_re,
           s5_c_im, s5_d, s5_glu_w, s5_glu_b, gla_gate_w, gla_gate_b, gla_norm, w_out, norm_ffn2,
           ffn2_gate, ffn2_up, ffn2_down, norm_final, _npass=NPASS_FULL, _do_mixer=True, _do_ffn=True, _taps=(), _stage=9):
    f = lambda a: np.ascontiguousarray(np.asarray(a, dtype=np.float32))
    key = (_npass, _do_mixer, _do_ffn, str(_taps), _stage)
    if key not in _NC_CACHE:
        _NC_CACHE[key] = build_program(npass=_npass, do_mixer=_do_mixer, do_ffn=_do_ffn, taps=_taps, stage=_stage)
    nc = _NC_CACHE[key]
    shared = {
        "gains": f(np.stack([np.asarray(norm_ffn1)[0], np.asarray(norm_mix)[0], np.asarray(norm_ffn2)[0], np.asarray(norm_final)])),
        "ffn1_gate": f(ffn1_gate[0]), "ffn1_up": f(ffn1_up[0]), "ffn1_down": f(ffn1_down[0]),
        "ffn2_gate": f(ffn2_gate[0]), "ffn2_up": f(ffn2_up[0]), "ffn2_down": f(ffn2_down[0]),
        "w_in": f(w_in[0]), "w_out": f(w_out[0]), "glu_w": f(s5_glu_w[0]), "glu_b": f(s5_glu_b[0]),
        "gate_w": f(gla_gate_w[0]), "gate_b": f(gla_gate_b[0]), "gla_norm": f(gla_norm[0]),
        "lam_re": f(s5_lam_re[0]), "lam_im": f(s5_lam_im[0]), "log_dt": f(s5_log_dt[0]),
        "b_re": f(s5_b_re[0]), "b_im": f(s5_b_im[0]), "c_re": f(s5_c_re[0]), "c_im": f(s5_c_im[0]),
        "s5_d": f(s5_d[0]),
    }
    shared.update(_consts())
    xp_ = np.asarray(x_prompt, dtype=np.float32)
    xs_ = np.asarray(x_sample, dtype=np.float32)
    sre = np.asarray(state_s5_re, dtype=np.float32)[0]
    sim = np.asarray(state_s5_im, dtype=np.float32)[0]
    sgl = np.asarray(state_gla, dtype=np.float32)[0]
    in_maps = []
    for i in range(NCORES):
        m = dict(shared)
        m["xp"] = f(xp_[i])
        m["xs"] = f(xs_[16 * i:16 * i + 16].reshape(NSMP, D))
        m["s5re_in"] = f(sre[16 * i:16 * i + 16])
        m["s5im_in"] = f(sim[16 * i:16 * i + 16])
        m["gla_in"] = f(sgl[16 * i:16 * i + 16])
        in_maps.append(m)
    res = run_bass_kernel_spmd(nc, in_maps, core_ids=list(range(NCORES)))
    R = res.results
    y_prompt = np.stack([R[i]["yp"] for i in range(NCORES)]).astype(np.float32)
    y_sample = np.concatenate([R[i]["ys"].reshape(16, 4, D) for i in range(NCORES)]).astype(np.float32)
    p_re = np.stack([R[i]["pre"] for i in range(NCORES)])[None].astype(np.float32)
    p_im = np.stack([R[i]["pim"] for i in range(NCORES)])[None].astype(np.float32)
    p_gla = np.stack([R[i]["pgla"] for i in range(NCORES)])[None].astype(np.float32)
    s_re = np.concatenate([R[i]["sre"] for i in range(NCORES)])[None].astype(np.float32)
    s_im = np.concatenate([R[i]["sim"] for i in range(NCORES)])[None].astype(np.float32)
    s_gla = np.concatenate([R[i]["sgla"] for i in range(NCORES)])[None].astype(np.float32)
    if _taps:
        return (y_prompt, y_sample, p_re, p_im, p_gla, s_re, s_im, s_gla), {t[0]: R[0]["tap_" + t[0]] for t in _taps}
    return (y_prompt, y_sample, p_re, p_im, p_gla, s_re, s_im, s_gla)
```
